# Optimizing a Trainium2 kernel written in Bass

```python
import jax, jax.numpy as jnp
from jax import lax
import numpy as np

D_MODEL = 2048
BATCH = 8
SEQ = 2048
DEPTH = 2
DEC_BATCH = 128
DEC_SEQ = 4
PAST_LEN = 8192
PAGE_SIZE = 128

N_A_LAYERS = DEPTH // 2
N_B_LAYERS = DEPTH - N_A_LAYERS
N_META = 16
A_HEADS = 4
A_DV = D_MODEL // A_HEADS
A_DK = A_DV // 2
A_CHUNK = 128
A_GATE_CAP = 15.0
A_IN_COLS = 2 * A_HEADS * A_DK + 2 * A_HEADS * A_DV + 2 * A_HEADS
B_HEADS = 32
B_DH = D_MODEL // B_HEADS
B_KV_HEADS = 4
B_GROUP = B_HEADS // B_KV_HEADS
WINDOW = 128
B_BLOCK = 128
D_FF = ((8 * D_MODEL // 3 + 255) // 256) * 256
EPS = 1e-6

kernel_name = 'yoco_mlstm_swa_sink_macaron_step'


def rms_norm(x, g):
    xf = x.astype(jnp.float32)
    y = xf * lax.rsqrt(jnp.mean(xf * xf, axis=-1, keepdims=True) + EPS)
    return (y * g.astype(jnp.float32)).astype(x.dtype)


def swiglu_ffn(x, w_in, w_out):
    g, u = jnp.split(x @ w_in, 2, axis=-1)
    return (jax.nn.silu(g) * u) @ w_out


def alibi_slopes():
    h = jnp.arange(1, B_HEADS + 1, dtype=jnp.float32)
    return jnp.exp2(-8.0 * h / B_HEADS).reshape(B_KV_HEADS, B_GROUP)


def mlstm_project(xn, w_in, b_gate):
    bsz, t = xn.shape[0], xn.shape[1]
    p = xn @ w_in
    qk = A_HEADS * A_DK
    hv = A_HEADS * A_DV
    cuts = [qk, 2 * qk, 2 * qk + hv, 2 * qk + 2 * hv, 2 * qk + 2 * hv + A_HEADS]
    q, k, v, o, ig, fg = jnp.split(p, cuts, axis=-1)
    q = q.reshape(bsz, t, A_HEADS, A_DK)
    k = k.reshape(bsz, t, A_HEADS, A_DK) * (A_DK ** -0.5)
    v = v.reshape(bsz, t, A_HEADS, A_DV)
    gates = jnp.concatenate([ig, fg], axis=-1).astype(jnp.float32) + b_gate.astype(jnp.float32)
    gates = A_GATE_CAP * jnp.tanh(gates / A_GATE_CAP)
    ig, fg = jnp.split(gates, 2, axis=-1)
    return q, k, v, o, ig, jax.nn.log_sigmoid(fg)


def mlstm_chunk(carry, inp):
    c_prev, n_prev, m_prev = carry
    q, k, v, ig, log_f = inp
    q = q.astype(jnp.float32)
    k = k.astype(jnp.float32)
    v = v.astype(jnp.float32)
    L = q.shape[1]
    b = jnp.cumsum(log_f, axis=1).transpose(0, 2, 1)
    igt = ig.transpose(0, 2, 1)
    causal = jnp.tril(jnp.ones((L, L), dtype=bool))
    d_log = jnp.where(causal, b[..., :, None] - b[..., None, :] + igt[..., None, :], -jnp.inf)
    inter_log = b + m_prev[..., None]
    m_t = jnp.maximum(inter_log, jnp.max(d_log, axis=-1))
    w_intra = jnp.exp(d_log - m_t[..., None])
    w_inter = jnp.exp(inter_log - m_t)
    s = jnp.einsum('blhd,bshd->bhls', q, k) * w_intra
    num = jnp.einsum('bhls,bshv->bhlv', s, v) + w_inter[..., None] * jnp.einsum('blhd,bhdv->bhlv', q, c_prev)
    den = jnp.sum(s, axis=-1) + w_inter * jnp.einsum('blhd,bhd->bhl', q, n_prev)
    den = jnp.maximum(jnp.abs(den), jnp.exp(-m_t))
    h = (num / den[..., None]).transpose(0, 2, 1, 3)
    b_last = b[..., -1]
    w_log = b_last[..., None] - b + igt
    m_new = jnp.maximum(b_last + m_prev, jnp.max(w_log, axis=-1))
    w_state = jnp.exp(w_log - m_new[..., None])
    decay = jnp.exp(b_last + m_prev - m_new)
    c_new = decay[..., None, None] * c_prev + jnp.einsum('bhs,bshd,bshv->bhdv', w_state, k, v)
    n_new = decay[..., None] * n_prev + jnp.einsum('bhs,bshd->bhd', w_state, k)
    return (c_new, n_new, m_new), h


def mlstm_prompt(q, k, v, ig, log_f):
    bsz = q.shape[0]
    zero = (jnp.zeros((bsz, A_HEADS, A_DK, A_DV), jnp.float32),
            jnp.zeros((bsz, A_HEADS, A_DK), jnp.float32),
            jnp.zeros((bsz, A_HEADS), jnp.float32))
    arrs = (q, k, v, ig, log_f)
    carry, h_meta = mlstm_chunk(zero, tuple(a[:, :N_META] for a in arrs))

    def to_chunks(a):
        r = a[:, N_META:]
        nc = r.shape[1] // A_CHUNK
        return jnp.moveaxis(r.reshape((bsz, nc, A_CHUNK) + r.shape[2:]), 1, 0)

    carry, h_seq = lax.scan(mlstm_chunk, carry, tuple(to_chunks(a) for a in arrs))
    h_seq = jnp.moveaxis(h_seq, 0, 1).reshape(bsz, -1, A_HEADS, A_DV)
    return jnp.concatenate([h_meta, h_seq], axis=1), carry


def mlstm_out(h, o, head_gain, w_out, dtype):
    hn = h * lax.rsqrt(jnp.mean(h * h, axis=-1, keepdims=True) + EPS)
    hn = hn.reshape(h.shape[0], h.shape[1], A_HEADS * A_DV) * head_gain.astype(jnp.float32)
    return (hn * jax.nn.sigmoid(o.astype(jnp.float32))).astype(dtype) @ w_out


def shared_kv(h, kv_norm, w_kv, k_norm):
    xn = rms_norm(h, kv_norm)
    kv = (xn @ w_kv).reshape(h.shape[0], h.shape[1], 2, B_KV_HEADS, B_DH)
    return rms_norm(kv[:, :, 0], k_norm), kv[:, :, 1]


def b_queries(xn, w_q, q_norm):
    q = (xn @ w_q).reshape(xn.shape[0], xn.shape[1], B_KV_HEADS, B_GROUP, B_DH)
    return rms_norm(q, q_norm)


def sink_attention(q, k, v, valid, dist, sinks):
    slopes = alibi_slopes()
    s = jnp.einsum('bnqgrd,bnkgd->bngrqk', q, k).astype(jnp.float32) * (B_DH ** -0.5)
    s = s - slopes[None, None, :, :, None, None] * dist[None, :, None, None]
    s = jnp.where(valid[None, :, None, None], s, -jnp.inf)
    sink = sinks.astype(jnp.float32)[None, None, :, :, None]
    mx = jnp.maximum(jnp.max(s, axis=-1), sink)
    p = jnp.exp(s - mx[..., None])
    den = jnp.sum(p, axis=-1) + jnp.exp(sink - mx)
    return jnp.einsum('bngrqk,bnkgd->bnqgrd', p / den[..., None], v.astype(jnp.float32))


def window_attn_prompt(q, k, v, sinks):
    bsz, t = q.shape[0], q.shape[1]
    nb = -(-t // B_BLOCK)
    pad = nb * B_BLOCK - t
    qb = jnp.pad(q, ((0, 0), (0, pad), (0, 0), (0, 0), (0, 0))).reshape(bsz, nb, B_BLOCK, B_KV_HEADS, B_GROUP, B_DH)
    kp = jnp.pad(k, ((0, 0), (B_BLOCK, pad), (0, 0), (0, 0))).reshape(bsz, nb + 1, B_BLOCK, B_KV_HEADS, B_DH)
    vp = jnp.pad(v, ((0, 0), (B_BLOCK, pad), (0, 0), (0, 0))).reshape(bsz, nb + 1, B_BLOCK, B_KV_HEADS, B_DH)
    meta_shape = (bsz, nb, N_META, B_KV_HEADS, B_DH)
    keys = jnp.concatenate([jnp.broadcast_to(k[:, None, :N_META], meta_shape), kp[:, :-1], kp[:, 1:]], axis=2)
    vals = jnp.concatenate([jnp.broadcast_to(v[:, None, :N_META], meta_shape), vp[:, :-1], vp[:, 1:]], axis=2)
    blk = jnp.arange(nb)[:, None]
    t_pos = blk * B_BLOCK + jnp.arange(B_BLOCK)[None, :]
    s_band = (blk - 1) * B_BLOCK + jnp.arange(2 * B_BLOCK)[None, :]
    s_meta = jnp.broadcast_to(jnp.arange(N_META)[None, :], (nb, N_META))
    s_all = jnp.concatenate([s_meta, s_band], axis=1)
    is_meta = jnp.concatenate([jnp.ones((N_META,), bool), jnp.zeros((2 * B_BLOCK,), bool)])
    rel = t_pos[:, :, None] - s_all[:, None, :]
    band_ok = (rel >= 0) & (rel < WINDOW) & (s_all[:, None, :] >= 0) & (s_all[:, None, :] < t)
    valid = jnp.where(is_meta[None, None, :], rel >= WINDOW, band_ok)
    dist = jnp.minimum(rel, WINDOW).astype(jnp.float32)
    o = sink_attention(qb, keys, vals, valid, dist, sinks)
    return o.reshape(bsz, nb * B_BLOCK, B_HEADS * B_DH)[:, :t]


def window_attn_sample(q, k_new, v_new, k_meta, v_meta, k_win, v_win, sinks):
    bsz, s_len = q.shape[0], q.shape[1]
    keys = jnp.concatenate([k_meta, k_win, k_new.astype(k_win.dtype)], axis=1)[:, None]
    vals = jnp.concatenate([v_meta, v_win, v_new.astype(v_win.dtype)], axis=1)[:, None]
    t_pos = PAST_LEN + jnp.arange(s_len)
    s_all = jnp.concatenate([jnp.arange(N_META), PAST_LEN - WINDOW + jnp.arange(WINDOW), PAST_LEN + jnp.arange(s_len)])
    is_meta = jnp.concatenate([jnp.ones((N_META,), bool), jnp.zeros((WINDOW + s_len,), bool)])
    rel = t_pos[:, None] - s_all[None, :]
    valid = jnp.where(is_meta[None, :], rel >= WINDOW, (rel >= 0) & (rel < WINDOW))
    dist = jnp.minimum(rel, WINDOW).astype(jnp.float32)
    o = sink_attention(q[:, None], keys, vals, valid[None], dist[None], sinks)
    return o.reshape(bsz, s_len, B_HEADS * B_DH)


def trunk(h, a_mixer, make_kv, b_mixer, ffn_norm, w_ffn_in, w_ffn_out, mix_norm):
    a_states = []
    kv = None
    for layer in range(DEPTH):
        if layer == N_A_LAYERS:
            kv = make_kv(h)
        h = h + 0.5 * swiglu_ffn(rms_norm(h, ffn_norm[layer, 0]), w_ffn_in[layer, 0], w_ffn_out[layer, 0])
        xn = rms_norm(h, mix_norm[layer])
        if layer < N_A_LAYERS:
            out, st = a_mixer(layer, xn)
            a_states.append(st)
        else:
            out = b_mixer(layer - N_A_LAYERS, xn, kv)
        h = h + out
        h = h + 0.5 * swiglu_ffn(rms_norm(h, ffn_norm[layer, 1]), w_ffn_in[layer, 1], w_ffn_out[layer, 1])
    return h, a_states, kv


def setup_inputs(seed: int = 0) -> dict:
    key = jax.random.key(seed)
    ks = jax.random.split(key, 32)

    def nrm(k, shape, scale):
        return jax.random.normal(k, shape, jnp.float32) * scale

    f_bias = jnp.linspace(3.0, 6.0, A_HEADS, dtype=jnp.float32)[None, :] + nrm(ks[20], (N_A_LAYERS, A_HEADS), 0.1)
    i_bias = nrm(ks[21], (N_A_LAYERS, A_HEADS), 0.1)
    return {
        'x_prompt': nrm(ks[0], (BATCH, SEQ, D_MODEL), 1.0),
        'x_sample': nrm(ks[1], (DEC_BATCH, DEC_SEQ, D_MODEL), 1.0),
        'state_C': nrm(ks[2], (N_A_LAYERS, DEC_BATCH, A_HEADS, A_DK, A_DV), 0.1),
        'state_n': nrm(ks[3], (N_A_LAYERS, DEC_BATCH, A_HEADS, A_DK), 0.1),
        'state_m': nrm(ks[4], (N_A_LAYERS, DEC_BATCH, A_HEADS), 1.0),
        'cache_k_meta': nrm(ks[5], (DEC_BATCH, N_META, B_KV_HEADS, B_DH), 1.0),
        'cache_v_meta': nrm(ks[6], (DEC_BATCH, N_META, B_KV_HEADS, B_DH), 1.0),
        'cache_k_win': nrm(ks[7], (DEC_BATCH, WINDOW, B_KV_HEADS, B_DH), 1.0),
        'cache_v_win': nrm(ks[8], (DEC_BATCH, WINDOW, B_KV_HEADS, B_DH), 1.0),
        'meta_tokens': nrm(ks[9], (N_META, D_MODEL), 1.0),
        'ffn_norm': 1.0 + nrm(ks[10], (DEPTH, 2, D_MODEL), 0.02),
        'w_ffn_in': nrm(ks[11], (DEPTH, 2, D_MODEL, 2 * D_FF), D_MODEL ** -0.5),
        'w_ffn_out': nrm(ks[12], (DEPTH, 2, D_FF, D_MODEL), D_FF ** -0.5),
        'mix_norm': 1.0 + nrm(ks[13], (DEPTH, D_MODEL), 0.02),
        'w_a_in': nrm(ks[14], (N_A_LAYERS, D_MODEL, A_IN_COLS), D_MODEL ** -0.5),
        'b_a_gate': jnp.concatenate([i_bias, f_bias], axis=-1),
        'a_head_norm': 1.0 + nrm(ks[15], (N_A_LAYERS, A_HEADS * A_DV), 0.02),
        'w_a_out': nrm(ks[16], (N_A_LAYERS, A_HEADS * A_DV, D_MODEL), (A_HEADS * A_DV) ** -0.5),
        'kv_norm': 1.0 + nrm(ks[17], (D_MODEL,), 0.02),
        'w_kv': nrm(ks[18], (D_MODEL, 2 * B_KV_HEADS * B_DH), D_MODEL ** -0.5),
        'k_norm': 1.0 + nrm(ks[19], (B_DH,), 0.02),
        'w_q': nrm(ks[22], (N_B_LAYERS, D_MODEL, B_HEADS * B_DH), D_MODEL ** -0.5),
        'q_norm': 1.0 + nrm(ks[23], (N_B_LAYERS, B_DH), 0.02),
        'sinks': nrm(ks[24], (N_B_LAYERS, B_HEADS), 0.5),
        'w_b_out': nrm(ks[25], (N_B_LAYERS, B_HEADS * B_DH, D_MODEL), (B_HEADS * B_DH) ** -0.5),
    }


def reference(x_prompt, x_sample, state_C, state_n, state_m, cache_k_meta, cache_v_meta, cache_k_win, cache_v_win,
              meta_tokens, ffn_norm, w_ffn_in, w_ffn_out, mix_norm, w_a_in, b_a_gate, a_head_norm, w_a_out,
              kv_norm, w_kv, k_norm, w_q, q_norm, sinks, w_b_out):
    def make_kv(h):
        return shared_kv(h, kv_norm, w_kv, k_norm)

    def a_prompt(la, xn):
        q, k, v, o, ig, lf = mlstm_project(xn, w_a_in[la], b_a_gate[la])
        h, st = mlstm_prompt(q, k, v, ig, lf)
        return mlstm_out(h, o, a_head_norm[la], w_a_out[la], xn.dtype), st

    def a_sample(la, xn):
        q, k, v, o, ig, lf = mlstm_project(xn, w_a_in[la], b_a_gate[la])
        carry = (state_C[la].astype(jnp.float32), state_n[la].astype(jnp.float32), state_m[la].astype(jnp.float32))
        st, h = mlstm_chunk(carry, (q, k, v, ig, lf))
        return mlstm_out(h, o, a_head_norm[la], w_a_out[la], xn.dtype), st

    def b_prompt(lb, xn, kv):
        q = b_queries(xn, w_q[lb], q_norm[lb])
        o = window_attn_prompt(q, kv[0], kv[1], sinks[lb].reshape(B_KV_HEADS, B_GROUP))
        return o.astype(xn.dtype) @ w_b_out[lb]

    def b_sample(lb, xn, kv):
        q = b_queries(xn, w_q[lb], q_norm[lb])
        o = window_attn_sample(q, kv[0], kv[1], cache_k_meta, cache_v_meta, cache_k_win, cache_v_win,
                               sinks[lb].reshape(B_KV_HEADS, B_GROUP))
        return o.astype(xn.dtype) @ w_b_out[lb]

    bsz = x_prompt.shape[0]
    meta = jnp.broadcast_to(meta_tokens.astype(x_prompt.dtype)[None], (bsz, N_META, D_MODEL))
    h_p, st_p, kv_p = trunk(jnp.concatenate([meta, x_prompt], axis=1), a_prompt, make_kv, b_prompt,
                            ffn_norm, w_ffn_in, w_ffn_out, mix_norm)
    h_s, st_s, kv_s = trunk(x_sample, a_sample, make_kv, b_sample, ffn_norm, w_ffn_in, w_ffn_out, mix_norm)

    y_prompt = h_p[:, N_META:]
    k_p, v_p = kv_p
    k_s, v_s = kv_s
    c_p = jnp.stack([st[0] for st in st_p]).astype(state_C.dtype)
    n_p = jnp.stack([st[1] for st in st_p]).astype(state_n.dtype)
    m_p = jnp.stack([st[2] for st in st_p]).astype(state_m.dtype)
    c_s = jnp.stack([st[0] for st in st_s]).astype(state_C.dtype)
    n_s = jnp.stack([st[1] for st in st_s]).astype(state_n.dtype)
    m_s = jnp.stack([st[2] for st in st_s]).astype(state_m.dtype)
    k_win_s = jnp.concatenate([cache_k_win, k_s.astype(cache_k_win.dtype)], axis=1)[:, -WINDOW:]
    v_win_s = jnp.concatenate([cache_v_win, v_s.astype(cache_v_win.dtype)], axis=1)[:, -WINDOW:]
    return (y_prompt, h_s, c_p, n_p, m_p, k_p[:, :N_META], v_p[:, :N_META], k_p[:, -WINDOW:], v_p[:, -WINDOW:], c_s, n_s, m_s, k_win_s, v_win_s)
```

```python
from contextlib import ExitStack
import numpy as np
import concourse.bass as bass
import concourse.mybir as mybir
from concourse.bass_utils import run_bass_kernel_spmd

F32 = mybir.dt.float32
BF16 = mybir.dt.bfloat16
AF = mybir.ActivationFunctionType
ALU = mybir.AluOpType
AX = mybir.AxisListType

D = 2048
KC = 16
DFF = 5632
FC = 44
NA = 80
TS = 512
NSEG = 4
NT = NA + TS
NSLOT = 3
EPS = 1e-6
NEG = -1e30
SLOPES = [2.0 ** (-8.0 * (h + 1) / 32.0) for h in range(32)]
B_FFN = 0
B_AQK = 88
B_AV = 92
B_AO = 96
B_AOUT = 100
B_KV = 104
B_Q = 105
B_BOUT = 109
NBLK = 113


class Prog:
    def __init__(self, nc, dry=False):
        self.nc = nc
        self.dry = dry
        self.engs = {'pe': nc.tensor, 'act': nc.scalar, 'dve': nc.vector, 'pool': nc.gpsimd, 'sp': nc.sync}
        self.ops = []
        self.lw = {}
        self.rd = {}
        self.stream_last = {}
        self.fence = {}
        self.last_eng = {}

    def add(self, eng, fn, r=(), w=(), stream=None):
        if self.dry:
            return -1
        i = len(self.ops)
        deps = set()
        for x in r:
            if x in self.lw:
                deps.add(self.lw[x])
        for x in w:
            if x in self.lw:
                deps.add(self.lw[x])
            deps.update(self.rd.get(x, {}).values())
        if stream is not None and stream in self.stream_last:
            deps.add(self.stream_last[stream])
        if eng in self.fence:
            deps.update(self.fence.pop(eng))
        self.ops.append((eng, fn, deps, stream))
        for x in r:
            self.rd.setdefault(x, {})[eng if stream is None else (eng, i)] = i
        for x in w:
            self.lw[x] = i
            self.rd[x] = {}
        if stream is not None:
            self.stream_last[stream] = i
        else:
            self.last_eng[eng] = i
        return i

    def barrier(self, engines=('pe', 'act', 'dve', 'sp')):
        if self.dry:
            return
        front = set(self.last_eng.values()) | set(self.stream_last[s] for s in self.stream_last if not s.startswith('w'))
        for e in engines:
            self.fence.setdefault(e, set()).update(front)

    def emit(self, stack, final_streams):
        EP = 3000
        SEP = 200
        ops = self.ops
        n = len(ops)
        needed = [False] * n
        for i, (e, fn, deps, st) in enumerate(ops):
            for d in deps:
                de, _, _, dst = ops[d]
                if dst is None and not (de == 'pe' and e == 'pe' and st is None):
                    needed[d] = True
        cnt = {}
        ms = [0] * n
        for i, (e, fn, deps, st) in enumerate(ops):
            if st is not None:
                key = ('s', st)
                cnt[key] = cnt.get(key, 0) + 1
                ms[i] = cnt[key]
            elif needed[i]:
                key = ('e', e)
                cnt[key] = cnt.get(key, 0) + 1
                ms[i] = cnt[key]
        sems = {}

        def sem_for(key, count):
            if key[0] == 's':
                ep, v = (count - 1) // SEP, ((count - 1) % SEP + 1) * 16
            else:
                ep, v = (count - 1) // EP, (count - 1) % EP + 1
            k2 = (key, ep)
            if k2 not in sems:
                sems[k2] = stack.enter_context(self.nc.semaphore("sem_%s_%s_%d" % (key[0], str(key[1]), ep)))
            return sems[k2], v

        waited = {e: {} for e in self.engs}
        for i, (e, fn, deps, st) in enumerate(ops):
            E = self.engs[e]
            reqs = {}
            for d in deps:
                de, _, _, dst = ops[d]
                if dst is not None:
                    key = ('s', dst)
                elif de == 'pe' and e == 'pe' and st is None:
                    continue
                else:
                    key = ('e', de)
                if ms[d] > reqs.get(key, 0):
                    reqs[key] = ms[d]
            for key, val in reqs.items():
                if waited[e].get(key, 0) >= val:
                    continue
                sm, v = sem_for(key, val)
                E.wait_ge(sm, v)
                waited[e][key] = val
            ins = fn(E)
            if st is not None:
                sm, v = sem_for(('s', st), ms[i])
                ins.then_inc(sm, 16)
            elif needed[i]:
                sm, v = sem_for(('e', e), ms[i])
                ins.then_inc(sm, 1)
        sp = self.engs['sp']
        for s in final_streams:
            if ('s', s) in cnt:
                sm, v = sem_for(('s', s), cnt[('s', s)])
                sp.wait_ge(sm, v)
        self.stats = dict(n_ops=n, n_sems=len(sems), counts={str(k): v for k, v in cnt.items()})


class WRing:
    def __init__(self, P, slots, schedule=None):
        self.P = P
        self.slots = slots
        self.schedule = schedule
        self.record = [] if schedule is None else None
        self.n_get = 0
        self.n_issued = 0

    def _issue_upto(self, j):
        while self.n_issued <= j and self.n_issued < len(self.schedule):
            k = self.n_issued
            src, nfree, cache, mode, ckey = self.schedule[k]
            s = k % NSLOT
            dst = self.slots[s][:, 0:nfree]
            if mode == 'read':
                self.P.add('pool', (lambda E, dst=dst, cache=cache: E.dma_start(out=dst, in_=cache)),
                           r=[ckey], w=['wslot%d' % s], stream='w%d' % s)
            else:
                self.P.add('pool', (lambda E, dst=dst, src=src: E.dma_start(out=dst, in_=src)),
                           w=['wslot%d' % s], stream='w%d' % s)
                if mode == 'write':
                    self.P.add('sp', (lambda E, dst=dst, cache=cache: E.dma_start(out=cache, in_=dst)),
                               r=['wslot%d' % s], w=[ckey], stream='wb%d' % s)
            self.n_issued += 1

    def get(self, src, nfree, hold=1, cache=None, mode=None, ckey=None):
        k = self.n_get
        self.n_get += 1
        if self.record is not None:
            self.record.append((src, nfree, cache, mode, ckey))
            return self.slots[k % NSLOT][:, 0:nfree], 'wslot%d' % (k % NSLOT)
        self._issue_upto(k - hold + NSLOT)
        return self.slots[k % NSLOT][:, 0:nfree], 'wslot%d' % (k % NSLOT)


def build_program(debug=False):
    nc = bass.Bass("TRN2", target_bir_lowering=False)

    def din(name, shape, dt=F32):
        return nc.dram_tensor(name, list(shape), dt, kind="ExternalInput").ap()

    def dout(name, shape, dt=F32):
        return nc.dram_tensor(name, list(shape), dt, kind="ExternalOutput").ap()

    xA = din("xA", [128, KC, NA])
    xR = din("xR", [NSEG, 128, KC, TS])
    wblk = din("wblk", [NBLK, 128, KC * 512])
    wout = din("wout", [64, 128, FC * 128])
    wgate = din("wgate", [128, KC * 8])
    gains = din("gains", [128, 8, KC])
    smallc = din("smallc", [1, 8 + 64 + 32 + 32])
    qg = din("qg", [128, 1])
    ctab = din("ctab", [128, 12, 128])
    stC = din("stC", [16, 4, 256, 512])
    stn = din("stn", [16, 128, 8])
    stm = din("stm", [16, 4])
    ckm = din("ckm", [16, 16, 256])
    cvm = din("cvm", [16, 16, 256])
    ckw = din("ckw", [16, 128, 256])
    cvw = din("cvw", [16, 128, 256])

    wbf_in = nc.dram_tensor("wbf_in", [88, 128, KC * 512], BF16, kind="Internal").ap()
    wbf_out = nc.dram_tensor("wbf_out", [64, 128, FC * 128], BF16, kind="Internal").ap()

    yA = dout("yA", [128, KC, NA])
    yR = dout("yR", [NSEG, 128, KC, TS])
    oCp = dout("oCp", [4, 256, 512])
    onp = dout("onp", [128, 8])
    omp = dout("omp", [1, 4])
    okmp = dout("okmp", [16, 256])
    ovmp = dout("ovmp", [16, 256])
    okwp = dout("okwp", [128, 256])
    ovwp = dout("ovwp", [128, 256])
    oCs = dout("oCs", [16, 4, 256, 512])
    ons = dout("ons", [16, 128, 8])
    oms = dout("oms", [16, 4])
    okws = dout("okws", [16, 128, 256])
    ovws = dout("ovws", [16, 128, 256])

    with ExitStack() as st:
        def sb(name, shape, dt):
            return st.enter_context(nc.sbuf_tensor(name, list(shape), dt))

        def pst(name, shape, dt):
            return st.enter_context(nc.psum_tensor(name, list(shape), dt))

        hT = sb("hT", [128, KC, NT], F32)
        xn = sb("xn", [128, KC, NT], BF16)
        slots = [sb("wslot%d" % i, [128, KC * 512], BF16) for i in range(NSLOT)]
        scrB = sb("scrB", [128, 21312], BF16)
        scrF = sb("scrF", [128, 2880], F32)
        nSall = sb("nSall", [128, 16, 2, 4], F32)
        Cst = sb("Cst", [128, 4, 2, 512], F32)
        Cbf = sb("Cbf", [128, 4, 2, 512], BF16)
        nst = sb("nst", [128, 2, 4], F32)
        nbf = sb("nbf", [128, 2, 4], BF16)
        mbc = sb("mbc", [128, 4], F32)
        ctf = sb("ctf", [128, 12, 128], F32)
        identb = sb("identb", [128, 128], BF16)
        onesb = sb("onesb", [128, 128], BF16)
        onesD = sb("onesD", [128, 128], BF16)
        bd64 = sb("bd64", [128, 128], BF16)
        onesf = sb("onesf", [128, 128], F32)
        epsb = sb("epsb", [128, 1], F32)
        gn = sb("gn", [128, 8, KC], F32)
        wg = sb("wg", [128, KC * 8], BF16)
        smc = sb("smc", [128, 8 + 64 + 32 + 32], F32)
        esink = sb("esink", [128, 32], F32)
        qgt = sb("qgt", [128, 1], F32)
        kTz = sb("kTz", [128, 8, 5 * 128], BF16)
        kTm = sb("kTm", [128, 8, 16], BF16)
        vext = sb("vext", [128, 6, 4, 65], BF16)
        knA = sb("knA", [128, 256], F32)
        vA = sb("vA", [128, 256], F32)

        psA = [pst("psA%d" % i, [128, 512], F32) for i in range(4)]
        psB = [pst("psB%d" % i, [128, 512], F32) for i in range(3)]
        psT = pst("psT", [128, 1024], BF16)

        identf = ctf[:, 0, :]
        Umat = ctf[:, 1, :]
        maskC = ctf[:, 2, :]
        maskT = ctf[:, 3, :]
        selL = {128: ctf[:, 4, :], 16: ctf[:, 5, :], 4: ctf[:, 6, :]}
        Rcur = ctf[:, 8, :]
        Rprev = ctf[:, 9, :]
        Rmeta0 = ctf[:, 10, :]
        RwinS = ctf[:, 11, 0:4]
        R2S = ctf[:, 11, 4:8]
        bgate_bc = smc[:, 0:8]
        kg_bc = smc[:, 8:8 + 64]
        nslope128 = smc[:, 8 + 64 + 32: 8 + 64 + 64]

        gen_state = {}

        def gen(P, W):
            cntr = [0]

            def rot_psA():
                i = cntr[0] % 4
                cntr[0] += 1
                return psA[i], 'psA%d' % i

            P.add('sp', lambda E: E.dma_start(out=ctf[:], in_=ctab[:, :, :]), w=['ctf'], stream='ld0')
            P.add('sp', lambda E: E.dma_start(out=gn[:], in_=gains[:, :, :]), w=['gn'], stream='ld1')
            P.add('sp', lambda E: E.dma_start(out=smc[:], in_=smallc.partition_broadcast(128)), w=['smc'], stream='ld2')
            P.add('sp', lambda E: E.dma_start(out=qgt[:], in_=qg[:, :]), w=['qgt'], stream='ld3')
            P.add('pool', lambda E: E.dma_start(out=wg[:], in_=wgate[:, :]), w=['wg'], stream='ldw')
            P.add('dve', lambda E: E.memset(onesb[:], 1.0), w=['onesb'])
            P.add('dve', lambda E: E.memset(onesD[:], 1.0 / D), w=['onesD'])
            P.add('dve', lambda E: E.memset(onesf[:], 1.0), w=['onesf'])
            P.add('dve', lambda E: E.memset(epsb[:], EPS), w=['epsb'])
            P.add('dve', lambda E: E.memset(Cst[:], 0.0), w=['C0', 'C1', 'C2', 'C3'])
            P.add('dve', lambda E: E.memset(Cbf[:], 0.0), w=['Cb0', 'Cb1', 'Cb2', 'Cb3'])
            P.add('dve', lambda E: E.memset(nst[:], 0.0), w=['n0', 'n1', 'n2', 'n3'])
            P.add('dve', lambda E: E.memset(nbf[:], 0.0), w=['nb0', 'nb1', 'nb2', 'nb3'])
            P.add('dve', lambda E: E.memset(mbc[:], 0.0), w=['mbc'])
            P.add('dve', lambda E: E.memset(kTz[:], 0.0), w=['kTz'])
            P.add('dve', lambda E: E.memset(kTm[:], 0.0), w=['kTm'])
            P.add('dve', lambda E: E.memset(vext[:], 1.0), w=['vext'])
            P.add('dve', lambda E: E.tensor_copy(out=identb[:], in_=identf), r=['ctf'], w=['identb'])
            P.add('dve', lambda E: E.tensor_copy(out=bd64[:], in_=ctf[:, 7, :]), r=['ctf'], w=['bd64'])
            P.add('act', lambda E: E.activation(out=esink[:], in_=smc[:, 8 + 64: 8 + 64 + 32], func=AF.Exp),
                  r=['smc'], w=['esink'])

            def rmsnorm(gi, blocks, reuse=False):
                sq = scrB[:, 0:KC * NT].rearrange("p (c t) -> p c t", t=NT)
                rstd = scrF[:, 0:NT]
                for (b0, bn) in blocks:
                    bk = 'b%d' % b0
                    if reuse:
                        for c in range(KC):
                            P.add('dve', lambda E, c=c, b0=b0, bn=bn: E.scalar_tensor_tensor(
                                out=xn[:, c, b0:b0 + bn], in0=hT[:, c, b0:b0 + bn], scalar=gn[:, gi, c:c + 1], in1=rstd[:, b0:b0 + bn],
                                op0=ALU.mult, op1=ALU.mult), r=['hT' + bk, 'gn', 'rstd' + bk], w=['xn' + bk])
                        continue
                    P.add('act', lambda E, b0=b0, bn=bn: E.activation(out=sq[:, :, b0:b0 + bn], in_=hT[:, :, b0:b0 + bn], func=AF.Square),
                          r=['hT' + bk], w=['sq' + bk])
                    ps, psn = rot_psA()

                    def mmf(E, b0=b0, bn=bn, ps=ps):
                        for c in range(KC):
                            ins = E.matmul(ps[:, 0:bn], lhsT=onesD[:], rhs=sq[:, c, b0:b0 + bn], start=(c == 0), stop=(c == KC - 1))
                        return ins
                    P.add('pe', mmf, r=['onesD', 'sq' + bk], w=[psn])
                    P.add('act', lambda E, b0=b0, bn=bn, ps=ps: E.activation(out=rstd[:, b0:b0 + bn], in_=ps[:, 0:bn], func=AF.Sqrt, bias=epsb[:], scale=1.0),
                          r=[psn, 'epsb'], w=['rstd' + bk])
                    P.add('dve', lambda E, b0=b0, bn=bn: E.reciprocal(rstd[:, b0:b0 + bn], rstd[:, b0:b0 + bn]), r=['rstd' + bk], w=['rstd' + bk])
                    for c in range(KC):
                        P.add('dve', lambda E, c=c, b0=b0, bn=bn: E.scalar_tensor_tensor(
                            out=xn[:, c, b0:b0 + bn], in0=hT[:, c, b0:b0 + bn], scalar=gn[:, gi, c:c + 1], in1=rstd[:, b0:b0 + bn],
                            op0=ALU.mult, op1=ALU.mult), r=['hT' + bk, 'gn', 'rstd' + bk], w=['xn' + bk])

            def ffn(f, blocks, seg):
                cmode = 'write' if seg == 0 else 'read'
                hidA = scrB[:, 0:36 * NT].rearrange("p (c t) -> p c t", t=NT)
                hidB = scrF[:, 512:512 + 4 * NT].bitcast(BF16).rearrange("p (c t) -> p c t", t=NT)

                def hid_ap(fc, b0, bn):
                    return hidA[:, fc, b0:b0 + bn] if fc < 36 else hidB[:, fc - 36, b0:b0 + bn]
                sg = scrF[:, 0:2 * 256].bitcast(BF16).rearrange("p (a t) -> p a t", a=2)
                it = 0
                for blk in range(22):
                    wv, wr = W.get(wblk[B_FFN + 22 * f + blk], KC * 512, cache=wbf_in[22 * f + blk], mode=cmode, ckey='wbi%d' % (22 * f + blk))
                    wv = wv.rearrange("p (k n) -> p k n", n=512)
                    for j in range(2):
                        fc = 2 * blk + j
                        for (b0, bn) in blocks:
                            bk = 'b%d' % b0
                            pg, pgn = rot_psA()
                            pu, pun = rot_psA()

                            def mmf(E, wv=wv, j=j, b0=b0, bn=bn, pg=pg, pu=pu):
                                for c in range(KC):
                                    E.matmul(pg[:, 0:bn], lhsT=wv[:, c, 128 * j:128 * j + 128], rhs=xn[:, c, b0:b0 + bn], start=(c == 0), stop=(c == KC - 1))
                                for c in range(KC):
                                    ins = E.matmul(pu[:, 0:bn], lhsT=wv[:, c, 256 + 128 * j:256 + 128 * j + 128], rhs=xn[:, c, b0:b0 + bn], start=(c == 0), stop=(c == KC - 1))
                                return ins
                            P.add('pe', mmf, r=[wr, 'xn' + bk], w=[pgn, pun])
                            sgi = it % 2
                            it += 1
                            P.add('act', lambda E, pg=pg, bn=bn, sgi=sgi: E.activation(out=sg[:, sgi, 0:bn], in_=pg[:, 0:bn], func=AF.Silu),
                                  r=[pgn], w=['sg%d' % sgi])
                            P.add('dve', lambda E, pu=pu, bn=bn, sgi=sgi, fc=fc, b0=b0: E.tensor_tensor(
                                out=hid_ap(fc, b0, bn), in0=pu[:, 0:bn], in1=sg[:, sgi, 0:bn], op=ALU.mult),
                                r=[pun, 'sg%d' % sgi], w=['hid%d' % fc + bk])
                for oc in range(KC):
                    wv, wr = W.get(wout[16 * f + oc], FC * 128, cache=wbf_out[16 * f + oc], mode=cmode, ckey='wbo%d' % (16 * f + oc))
                    wv = wv.rearrange("p (k n) -> p k n", n=128)
                    for (b0, bn) in blocks:
                        bk = 'b%d' % b0
                        ps, psn = rot_psA()

                        def mmf(E, wv=wv, b0=b0, bn=bn, ps=ps):
                            for c in range(FC):
                                ins = E.matmul(ps[:, 0:bn], lhsT=wv[:, c, :], rhs=hid_ap(c, b0, bn), start=(c == 0), stop=(c == FC - 1))
                            return ins
                        P.add('pe', mmf, r=[wr] + ['hid%d' % c + bk for c in range(FC)], w=[psn])
                        P.add('dve', lambda E, oc=oc, b0=b0, bn=bn, ps=ps: E.scalar_tensor_tensor(
                            out=hT[:, oc, b0:b0 + bn], in0=ps[:, 0:bn], scalar=0.5, in1=hT[:, oc, b0:b0 + bn], op0=ALU.mult, op1=ALU.add),
                            r=[psn, 'hT' + bk], w=['hT' + bk])

            def out_proj(bbase, src, srcres, blocks):
                for jb in range(4):
                    wv, wr = W.get(wblk[bbase + jb], KC * 512)
                    wv = wv.rearrange("p (k n) -> p k n", n=512)
                    for j in range(4):
                        oc = 4 * jb + j
                        for (b0, bn) in blocks:
                            bk = 'b%d' % b0
                            ps, psn = rot_psA()

                            def mmf(E, wv=wv, j=j, b0=b0, bn=bn, ps=ps):
                                for c in range(KC):
                                    ins = E.matmul(ps[:, 0:bn], lhsT=wv[:, c, 128 * j:128 * j + 128], rhs=src[:, c, b0:b0 + bn], start=(c == 0), stop=(c == KC - 1))
                                return ins
                            P.add('pe', mmf, r=[wr, srcres + bk], w=[psn])
                            P.add('dve', lambda E, oc=oc, b0=b0, bn=bn, ps=ps: E.tensor_tensor(
                                out=hT[:, oc, b0:b0 + bn], in0=ps[:, 0:bn], in1=hT[:, oc, b0:b0 + bn], op=ALU.add),
                                r=[psn, 'hT' + bk], w=['hT' + bk])

            actT = scrB[:, 0:KC * NT].rearrange("p (c t) -> p c t", t=NT)
            o1 = KC * NT
            qTh = scrB[:, o1:o1 + 2 * NT].rearrange("p (c t) -> p c t", t=NT)
            kTh = scrB[:, o1 + 2 * NT:o1 + 4 * NT].rearrange("p (c t) -> p c t", t=NT)
            o2 = o1 + 4 * NT
            vtk = scrB[:, o2:o2 + 512]
            kwt = scrB[:, o2 + 512:o2 + 768]
            STb = scrB[:, o2 + 768:o2 + 896]
            hnb = scrB[:, o2 + 896:o2 + 1408]
            Csb = scrB[:, o2 + 1408:o2 + 2432].rearrange("p (c v) -> p c v", v=512)
            nsb = scrB[:, o2 + 2432:o2 + 2440].rearrange("p (c h) -> p c h", h=4)
            WtBig = scrB[:, o2 + 2560:o2 + 2560 + 4 * 512].rearrange("p (k h l) -> p k h l", h=4, l=128)
            WtSm = scrB[:, o2 + 2560 + 2048:o2 + 2560 + 2048 + 17 * 64].rearrange("p (k h l) -> p k h l", h=4, l=16)

            def Wt_ap(ci, L, h):
                return WtBig[0:L, ci - 17, h, 0:L] if ci >= 17 else WtSm[0:L, ci, h, 0:L]
            gpre = scrF[:, 0:8]
            t1 = scrF[:, 8:16]
            ee = scrF[:, 16:20]
            spl = scrF[:, 20:24]
            gmax = scrF[:, 24:28]
            bgg = scrF[:, 28:36]
            glb = scrF[:, 36:40]
            tmp4 = scrF[:, 40:44]
            den2 = scrF[:, 44:46]
            den = scrF[:, 46:47]
            rden = scrF[:, 47:48]
            ssq = scrF[:, 48:49]
            scl = scrF[:, 49:50]
            mS = scrF[:, 52:56]
            a_all = scrF[:, 64:64 + 84].rearrange("p (k h) -> p k h", h=4)
            g_all = scrF[:, 148:148 + 84].rearrange("p (k h) -> p k h", h=4)
            wi_all = scrF[:, 232:232 + 84].rearrange("p (k h) -> p k h", h=4)
            ws_all = scrF[:, 316:316 + 84].rearrange("p (k h) -> p k h", h=4)
            dc_all = scrF[:, 400:400 + 84].rearrange("p (k h) -> p k h", h=4)
            em_all = scrF[:, 484:484 + 84].rearrange("p (k h) -> p k h", h=4)
            diag = scrF[:, 576:576 + 512].rearrange("p (h l) -> p h l", l=128)
            tmpA = scrF[:, 1088:1088 + 512].rearrange("p (h l) -> p h l", l=128)
            numI = scrF[:, 576:576 + 512]
            numT = scrF[:, 1088:1088 + 512]
            CsF2 = scrF[:, 1600:1600 + 1024].rearrange("p (c v) -> p c v", v=512)
            Csb2 = scrB[:, 19584:19584 + 1024].rearrange("p (c v) -> p c v", v=512)
            nsb2 = scrB[:, o2 + 2440:o2 + 2448].rearrange("p (c h) -> p c h", h=4)
            kA_tok = scrB[:, 20608:20608 + 256]
            vA_tok = scrF[:, 2624:2880].bitcast(BF16)
            CsF = scrB[:, 17536:17536 + 2048].bitcast(F32).rearrange("p (c v) -> p c v", v=512)

            def mlstm_gates(ci, L, t0, mprev, mres, mout, moutres):
                cr = 'ck%d' % ci
                pb0, pb1, pb2 = psB[0], psB[1], psB[2]

                def mm_g(E):
                    for c in range(KC):
                        ins = E.matmul(pb0[0:L, 0:8], lhsT=xn[:, c, t0:t0 + L], rhs=wg[:, 8 * c:8 * c + 8], start=(c == 0), stop=(c == KC - 1))
                    return ins
                P.add('pe', mm_g, r=['xnall', 'wg'], w=['psB0'])
                P.add('dve', lambda E: E.tensor_tensor(out=gpre[0:L, :], in0=pb0[0:L, 0:8], in1=bgate_bc[0:L, :], op=ALU.add), r=['psB0', 'smc'], w=['gpre'])
                P.add('act', lambda E: E.activation(out=t1[0:L, :], in_=gpre[0:L, :], func=AF.Tanh, scale=1.0 / 15.0), r=['gpre'], w=['t1'])
                P.add('act', lambda E: E.activation(out=ee[0:L, :], in_=t1[0:L, 4:8], func=AF.Exp, scale=-15.0), r=['t1'], w=['ee'])
                P.add('act', lambda E: E.activation(out=spl[0:L, :], in_=ee[0:L, :], func=AF.Ln, bias=onesf[0:L, 0:1], scale=1.0), r=['ee', 'onesf'], w=['spl'])
                P.add('pe', lambda E: E.matmul(pb1[0:L, 0:4], lhsT=Umat[0:L, 0:L], rhs=spl[0:L, :], start=True, stop=True), r=['ctf', 'spl'], w=['psB1'])
                P.add('dve', lambda E: E.scalar_tensor_tensor(out=a_all[0:L, ci, :], in0=t1[0:L, 0:4], scalar=15.0, in1=pb1[0:L, 0:4], op0=ALU.mult, op1=ALU.add),
                      r=['t1', 'psB1'], w=['a' + cr])
                for h in range(4):
                    P.add('dve', lambda E, h=h: E.tensor_scalar(diag[0:L, h, 0:L], identf[0:L, 0:L], a_all[0:L, ci, h:h + 1], None, ALU.mult), r=['ctf', 'a' + cr], w=['diag'])
                pb2v = pb2[:, :].rearrange("p (h l) -> p h l", l=128)
                P.add('pe', lambda E: E.matmul(pb2v[0:L, :, 0:L], lhsT=onesf[0:L, 0:L], rhs=diag[0:L, :, 0:L], start=True, stop=True), r=['onesf', 'diag'], w=['psB2'])
                for h in range(4):
                    P.add('dve', lambda E, h=h: E.tensor_tensor(out=tmpA[0:L, h, 0:L], in0=pb2v[0:L, h, 0:L], in1=maskC[0:L, 0:L], op=ALU.add), r=['psB2', 'ctf'], w=['tmpA'])
                P.add('dve', lambda E: E.tensor_reduce(out=gmax[0:L, :], in_=tmpA[0:L, :, 0:L], axis=AX.X, op=ALU.max), r=['tmpA'], w=['gmax'])
                P.add('dve', lambda E: E.tensor_tensor(out=g_all[0:L, ci, :], in0=gmax[0:L, :], in1=mprev[0:L, :], op=ALU.max), r=['gmax', mres], w=['g' + cr])
                for h in range(4):
                    P.add('dve', lambda E, h=h: E.tensor_scalar(diag[0:L, h, 0:L], identf[0:L, 0:L], g_all[0:L, ci, h:h + 1], None, ALU.mult), r=['ctf', 'g' + cr], w=['diag'])
                P.add('pe', lambda E: E.matmul(pb2v[0:L, :, 0:L], lhsT=onesf[0:L, 0:L], rhs=diag[0:L, :, 0:L], start=True, stop=True), r=['onesf', 'diag'], w=['psB2'])
                for h in range(4):
                    P.add('dve', lambda E, h=h: E.scalar_tensor_tensor(out=tmpA[0:L, h, 0:L], in0=pb2v[0:L, h, 0:L], scalar=-1.0, in1=maskT[0:L, 0:L], op0=ALU.mult, op1=ALU.add),
                          r=['psB2', 'ctf'], w=['tmpA'])
                for h in range(4):
                    P.add('act', lambda E, h=h: E.activation(out=Wt_ap(ci, L, h), in_=tmpA[0:L, h, 0:L], func=AF.Exp, bias=a_all[0:L, ci, h:h + 1], scale=1.0),
                          r=['tmpA', 'a' + cr], w=['Wt' + cr])
                P.add('dve', lambda E: E.tensor_tensor(out=bgg[0:L, 0:4], in0=g_all[0:L, ci, :], in1=pb1[0:L, 0:4], op=ALU.subtract), r=['g' + cr, 'psB1'], w=['bgg'])
                P.add('dve', lambda E: E.tensor_copy(out=bgg[0:L, 4:8], in_=g_all[0:L, ci, :]), r=['g' + cr], w=['bgg'])
                P.add('pe', lambda E: E.matmul(pb0[:, 0:8], lhsT=selL[L][0:L, :], rhs=bgg[0:L, :], start=True, stop=True), r=['ctf', 'bgg'], w=['psB0'])
                P.add('dve', lambda E: E.tensor_copy(out=glb[:, :], in_=pb0[:, 4:8]), r=['psB0'], w=['glb'])
                P.add('dve', lambda E: E.tensor_tensor(out=tmp4[0:L, :], in0=mprev[0:L, :], in1=g_all[0:L, ci, :], op=ALU.subtract), r=[mres, 'g' + cr], w=['tmp4'])
                P.add('act', lambda E: E.activation(out=wi_all[0:L, ci, :], in_=tmp4[0:L, :], func=AF.Exp), r=['tmp4'], w=['wi' + cr])
                P.add('dve', lambda E: E.tensor_tensor(out=tmp4[0:L, :], in0=a_all[0:L, ci, :], in1=glb[0:L, :], op=ALU.subtract), r=['a' + cr, 'glb'], w=['tmp4'])
                P.add('act', lambda E: E.activation(out=ws_all[0:L, ci, :], in_=tmp4[0:L, :], func=AF.Exp), r=['tmp4'], w=['ws' + cr])
                P.add('dve', lambda E: E.tensor_tensor(out=tmp4[:, :], in0=mprev[:, :], in1=glb[:, :], op=ALU.subtract), r=[mres, 'glb'], w=['tmp4'])
                P.add('act', lambda E: E.activation(out=dc_all[:, ci, :], in_=tmp4[:, :], func=AF.Exp), r=['tmp4'], w=['dc' + cr])
                P.add('act', lambda E: E.activation(out=em_all[0:L, ci, :], in_=bgg[0:L, 0:4], func=AF.Exp, scale=-1.0), r=['bgg'], w=['em' + cr])
                P.add('dve', lambda E: E.tensor_copy(out=mout, in_=pb0[:, 0:4]), r=['psB0', mres], w=[moutres])

            vtk1 = scrB[:, 17536:17536 + 512]
            kwt1 = scrB[:, 18048:18048 + 256]
            vtks = [vtk, vtk1]
            kwts = [kwt, kwt1]

            def mlstm_chunk_front(h, ci, L, t0, wqk, wqkr, wv_, wvr, par):
                cr = 'ck%d' % ci
                vtk = vtks[par]
                kwt = kwts[par]
                xw = ['CS0'] if par == 1 else []
                pv, pvn = rot_psA()
                pk, pkn = rot_psA()
                if t0 < NA:
                    P.add('pe', lambda E: E.matmul(pv[0:L, 0:512], lhsT=identb[0:NA, t0:t0 + L], rhs=vA_tok[0:NA, :], start=True, stop=True), r=['vA_tok', 'identb'], w=[pvn])
                    P.add('pe', lambda E: E.matmul(pk[0:L, 0:256], lhsT=identb[0:NA, t0:t0 + L], rhs=kA_tok[0:NA, :], start=True, stop=True), r=['kA_tok', 'identb'], w=[pkn])
                else:
                    def mm_v(E):
                        for c in range(KC):
                            ins = E.matmul(pv[0:L, 0:512], lhsT=xn[:, c, t0:t0 + L], rhs=wv_[:, c, :], start=(c == 0), stop=(c == KC - 1))
                        return ins
                    P.add('pe', mm_v, r=['xnall', wvr], w=[pvn])

                    def mm_k(E):
                        for c in range(KC):
                            ins = E.matmul(pk[0:L, 0:256], lhsT=xn[:, c, t0:t0 + L], rhs=wqk[:, c, 256:512], start=(c == 0), stop=(c == KC - 1))
                        return ins
                    P.add('pe', mm_k, r=['xnall', wqkr], w=[pkn])
                P.add('act', lambda E: E.activation(out=vtk[0:L, :], in_=pv[0:L, 0:512], func=AF.Copy), r=[pvn], w=['vtk%d' % par] + xw)
                P.add('dve', lambda E: E.tensor_scalar(kwt[0:L, :], pk[0:L, 0:256], ws_all[0:L, ci, h:h + 1], 1.0 / 16.0, ALU.mult, ALU.mult), r=[pkn, 'ws' + cr], w=['kwt%d' % par] + xw)

            def mlstm_head_chunk(h, ci, L, t0, wqk, wqkr, wv_, wvr, Cf, Cb, nf, nb, cres, par=0, front_done=False, mid_hook=None):
                cr = 'ck%d' % ci
                if not front_done:
                    mlstm_chunk_front(h, ci, L, t0, wqk, wqkr, wv_, wvr, par)
                vtk = vtks[par]
                kwt = kwts[par]
                vtkr = 'vtk%d' % par
                kwtr = 'kwt%d' % par
                pb0, pb1, pb2 = psB[0], psB[1], psB[2]

                def mm_s(E):
                    for c in range(2):
                        ins = E.matmul(pb0[0:L, 0:L], lhsT=kTh[:, c, t0:t0 + L], rhs=qTh[:, c, t0:t0 + L], start=(c == 0), stop=(c == 1))
                    return ins
                P.add('pe', mm_s, r=['qkT'], w=['psB0'])
                P.add('dve', lambda E: E.tensor_tensor(out=STb[0:L, 0:L], in0=pb0[0:L, 0:L], in1=Wt_ap(ci, L, h), op=ALU.mult), r=['psB0', 'Wt' + cr], w=['STb'])
                pn, pnn = rot_psA()
                P.add('pe', lambda E: E.matmul(pn[0:L, 0:512], lhsT=STb[0:L, 0:L], rhs=vtk[0:L, :], start=True, stop=True), r=['STb', vtkr], w=[pnn])

                def mm_d(E):
                    E.matmul(pb1[0:L, 0:1], lhsT=STb[0:L, 0:L], rhs=onesb[0:L, 0:1], start=True, stop=True)
                    for c in range(2):
                        ins = E.matmul(pb1[0:L, 1:2], lhsT=qTh[:, c, t0:t0 + L], rhs=nb[:, c:c + 1], start=(c == 0), stop=(c == 1))
                    return ins
                P.add('pe', mm_d, r=['STb', 'qkT', 'nb' + cres, 'onesb'], w=['psB1'])
                pi, pin = rot_psA()

                def mm_i(E):
                    for c in range(2):
                        ins = E.matmul(pi[0:L, 0:512], lhsT=qTh[:, c, t0:t0 + L], rhs=Cb[:, c, :], start=(c == 0), stop=(c == 1))
                    return ins
                P.add('pe', mm_i, r=['qkT', 'Cb' + cres], w=[pin])
                if mid_hook is not None:
                    mid_hook()
                P.add('act', lambda E: E.activation(out=numI[0:L, :], in_=pn[0:L, 0:512], func=AF.Copy), r=[pnn], w=['diag'])
                P.add('dve', lambda E: E.scalar_tensor_tensor(out=numT[0:L, :], in0=pi[0:L, 0:512], scalar=wi_all[0:L, ci, h:h + 1], in1=numI[0:L, :], op0=ALU.mult, op1=ALU.add),
                      r=[pin, 'wi' + cr, 'diag'], w=['tmpA'])
                P.add('act', lambda E: E.activation(out=den2[0:L, :], in_=pb1[0:L, 0:2], func=AF.Copy), r=['psB1'], w=['den2'])
                P.add('dve', lambda E: E.scalar_tensor_tensor(out=den[0:L, :], in0=den2[0:L, 1:2], scalar=wi_all[0:L, ci, h:h + 1], in1=den2[0:L, 0:1], op0=ALU.mult, op1=ALU.add),
                      r=['den2', 'wi' + cr], w=['den'])
                P.add('act', lambda E: E.activation(out=den[0:L, :], in_=den[0:L, :], func=AF.Abs), r=['den'], w=['den'])
                P.add('dve', lambda E: E.tensor_scalar(den[0:L, :], den[0:L, :], em_all[0:L, ci, h:h + 1], None, ALU.max), r=['den', 'em' + cr], w=['den'])
                P.add('dve', lambda E: E.reciprocal(rden[0:L, :], den[0:L, :]), r=['den'], w=['rden'])
                P.add('act', lambda E: E.activation(out=numI[0:L, :], in_=numT[0:L, :], func=AF.Square, scale=rden[0:L, 0:1], accum_out=ssq[0:L, :]), r=['tmpA', 'rden', 'diag'], w=['diag', 'ssq'])
                P.add('act', lambda E: E.activation(out=ssq[0:L, :], in_=ssq[0:L, :], func=AF.Sqrt, bias=epsb[0:L, :], scale=1.0 / 512.0), r=['ssq', 'epsb'], w=['ssq'])
                P.add('dve', lambda E: E.reciprocal(ssq[0:L, :], ssq[0:L, :]), r=['ssq'], w=['ssq'])
                P.add('dve', lambda E: E.tensor_tensor(out=scl[0:L, :], in0=ssq[0:L, :], in1=rden[0:L, :], op=ALU.mult), r=['ssq', 'rden'], w=['scl'])
                P.add('dve', lambda E: E.tensor_scalar(hnb[0:L, :], numT[0:L, :], scl[0:L, 0:1], None, ALU.mult), r=['tmpA', 'scl'], w=['hnb'])
                psTv = psT[:, 0:512].rearrange("p (j l) -> p j l", l=128)

                def mm_t(E):
                    for j in range(4):
                        ins = E.transpose(psTv[:, j, 0:L], hnb[0:L, 128 * j:128 * j + 128], identb[0:L, 0:L])
                    return ins
                P.add('pe', mm_t, r=['hnb', 'identb'], w=['psT'])
                for j in range(4):
                    P.add('dve', lambda E, j=j: E.scalar_tensor_tensor(out=actT[:, 4 * h + j, t0:t0 + L], in0=psTv[:, j, 0:L], scalar=gn[:, 7, 4 * h + j:4 * h + j + 1],
                                                                       in1=actT[:, 4 * h + j, t0:t0 + L], op0=ALU.mult, op1=ALU.mult),
                          r=['psT', 'gn', 'actT%d' % h], w=['actT%d' % h])
                pc0, pc0n = rot_psA()
                pc1, pc1n = rot_psA()

                def mm_c(E):
                    E.matmul(pc0[:, 0:512], lhsT=kwt[0:L, 0:128], rhs=vtk[0:L, :], start=True, stop=True)
                    ins = E.matmul(pc1[:, 0:512], lhsT=kwt[0:L, 128:256], rhs=vtk[0:L, :], start=True, stop=True)
                    return ins
                P.add('pe', mm_c, r=[kwtr, vtkr], w=[pc0n, pc1n])

                def mm_n(E):
                    E.matmul(pb2[:, 0:1], lhsT=kwt[0:L, 0:128], rhs=onesb[0:L, 0:1], start=True, stop=True)
                    ins = E.matmul(pb2[:, 1:2], lhsT=kwt[0:L, 128:256], rhs=onesb[0:L, 0:1], start=True, stop=True)
                    return ins
                P.add('pe', mm_n, r=[kwtr, 'onesb'], w=['psB2'])
                P.add('dve', lambda E: E.scalar_tensor_tensor(out=Cf[:, 0, :], in0=Cf[:, 0, :], scalar=dc_all[:, ci, h:h + 1], in1=pc0[:, 0:512], op0=ALU.mult, op1=ALU.add),
                      r=['C' + cres, 'dc' + cr, pc0n, 'Cb' + cres], w=['C' + cres])
                P.add('dve', lambda E: E.scalar_tensor_tensor(out=Cf[:, 1, :], in0=Cf[:, 1, :], scalar=dc_all[:, ci, h:h + 1], in1=pc1[:, 0:512], op0=ALU.mult, op1=ALU.add),
                      r=['C' + cres, 'dc' + cr, pc1n, 'Cb' + cres], w=['C' + cres])
                P.add('dve', lambda E: E.scalar_tensor_tensor(out=nf, in0=nf, scalar=dc_all[:, ci, h:h + 1], in1=pb2[:, 0:2], op0=ALU.mult, op1=ALU.add),
                      r=['n' + cres, 'dc' + cr, 'psB2'], w=['n' + cres])
                P.add('act', lambda E: E.activation(out=Cb, in_=Cf, func=AF.Copy), r=['C' + cres], w=['Cb' + cres])
                P.add('act', lambda E: E.activation(out=nb, in_=nf, func=AF.Copy), r=['n' + cres], w=['nb' + cres])

            def mlstm(seg, blocks):
                chunks = []
                if seg == 0:
                    chunks.append((0, 16, 0, 'p', None))
                    for b in range(16):
                        chunks.append((1 + b, 4, 16 + 4 * b, 's', b))
                for i in range(4):
                    chunks.append((17 + i, 128, NA + 128 * i, 'p', None))
                if seg == 0:
                    P.add('sp', lambda E: E.dma_start(out=nSall[:, :, :, :].rearrange("p b c h -> p b (c h)"), in_=stn.rearrange("b p x -> p b x")), w=['nS0', 'nS1'], stream='ldn')
                for (ci, L, t0, kind, b) in chunks:
                    if kind == 'p':
                        mlstm_gates(ci, L, t0, mbc, 'mbc', mbc[:, :], 'mbc')
                    else:
                        P.add('sp', lambda E, b=b: E.dma_start(out=mS[:, :], in_=stm[b:b + 1, :].partition_broadcast(128)), w=['mS'], stream='ldm')
                        mlstm_gates(ci, L, t0, mS, 'mS', mS[:, :], 'mS')
                        P.add('sp', lambda E, b=b: E.dma_start(out=oms[b:b + 1, :], in_=mS[0:1, :]), r=['mS'], stream='stm')
                for h in range(4):
                    wqk, wqkr = W.get(wblk[B_AQK + h], KC * 512)
                    wqk = wqk.rearrange("p (k n) -> p k n", n=512)
                    wv_, wvr = W.get(wblk[B_AV + h], KC * 512, hold=2)
                    wv_ = wv_.rearrange("p (k n) -> p k n", n=512)
                    wo_, wor = W.get(wblk[B_AO + h], KC * 512, hold=3)
                    wo_ = wo_.rearrange("p (k n) -> p k n", n=512)
                    for (b0, bn) in blocks:
                        bk = 'b%d' % b0
                        for c in range(4):
                            ps, psn = rot_psA()

                            def mmf(E, c=c, b0=b0, bn=bn, ps=ps, wqk=wqk):
                                for kc in range(KC):
                                    ins = E.matmul(ps[:, 0:bn], lhsT=wqk[:, kc, 128 * c:128 * c + 128], rhs=xn[:, kc, b0:b0 + bn], start=(kc == 0), stop=(kc == KC - 1))
                                return ins
                            P.add('pe', mmf, r=[wqkr, 'xn' + bk, 'xnall'], w=[psn])
                            if c < 2:
                                P.add('act', lambda E, c=c, b0=b0, bn=bn, ps=ps: E.activation(out=qTh[:, c, b0:b0 + bn], in_=ps[:, 0:bn], func=AF.Copy), r=[psn], w=['qkT'])
                            else:
                                P.add('act', lambda E, c=c, b0=b0, bn=bn, ps=ps: E.activation(out=kTh[:, c - 2, b0:b0 + bn], in_=ps[:, 0:bn], func=AF.Copy, scale=1.0 / 16.0), r=[psn], w=['qkT'])
                        for j in range(4):
                            ps, psn = rot_psA()

                            def mmf(E, j=j, b0=b0, bn=bn, ps=ps, wo_=wo_):
                                for kc in range(KC):
                                    ins = E.matmul(ps[:, 0:bn], lhsT=wo_[:, kc, 128 * j:128 * j + 128], rhs=xn[:, kc, b0:b0 + bn], start=(kc == 0), stop=(kc == KC - 1))
                                return ins
                            P.add('pe', mmf, r=[wor, 'xn' + bk, 'xnall'], w=[psn])
                            P.add('act', lambda E, j=j, b0=b0, bn=bn, ps=ps, h=h: E.activation(out=actT[:, 4 * h + j, b0:b0 + bn], in_=ps[:, 0:bn], func=AF.Sigmoid), r=[psn], w=['actT%d' % h])
                    if seg == 0:
                        pv_, pvn_ = rot_psA()
                        pk_, pkn_ = rot_psA()

                        def mm_va(E, pv_=pv_, pk_=pk_, wv_=wv_, wqk=wqk):
                            for c in range(KC):
                                E.matmul(pv_[0:NA, 0:512], lhsT=xn[:, c, 0:NA], rhs=wv_[:, c, :], start=(c == 0), stop=(c == KC - 1))
                            for c in range(KC):
                                ins = E.matmul(pk_[0:NA, 0:256], lhsT=xn[:, c, 0:NA], rhs=wqk[:, c, 256:512], start=(c == 0), stop=(c == KC - 1))
                            return ins
                        P.add('pe', mm_va, r=['xnall', wvr, wqkr], w=[pvn_, pkn_])
                        P.add('act', lambda E, pv_=pv_: E.activation(out=vA_tok[0:NA, :], in_=pv_[0:NA, 0:512], func=AF.Copy), r=[pvn_], w=['vA_tok'])
                        P.add('act', lambda E, pk_=pk_: E.activation(out=kA_tok[0:NA, :], in_=pk_[0:NA, 0:256], func=AF.Copy), r=[pkn_], w=['kA_tok'])
                    CsFs = [CsF, CsF2]
                    Csbs = [Csb, Csb2]
                    nsbs = [nsb, nsb2]

                    def ld_sample(b, h=h):
                        pb = b % 2
                        P.add('sp', lambda E: E.dma_start(out=CsFs[pb][:, :, :], in_=stC[b, h].rearrange("(c p) v -> p c v", p=128)), w=['CS%d' % pb] + (['vtk1', 'kwt1'] if pb == 0 else []), stream='ldC')
                    real = [c_ for c_ in chunks if c_[2] >= NA]
                    for (ci, L, t0, kind, b) in chunks:
                        if kind == 'p' and t0 >= NA:
                            ri = ci - 17
                            if ri == 0:
                                mlstm_chunk_front(h, ci, L, t0, wqk, wqkr, wv_, wvr, 0)
                            hook = None
                            if ri + 1 < 4:
                                nci, nL, nt0, _, _ = real[ri + 1]
                                hook = (lambda nci=nci, nL=nL, nt0=nt0, npar=(ri + 1) % 2, h=h, wqk=wqk, wqkr=wqkr, wv_=wv_, wvr=wvr:
                                        mlstm_chunk_front(h, nci, nL, nt0, wqk, wqkr, wv_, wvr, npar))
                            mlstm_head_chunk(h, ci, L, t0, wqk, wqkr, wv_, wvr, Cst[:, h, :, :], Cbf[:, h, :, :], nst[:, :, h], nbf[:, :, h], '%d' % h,
                                             par=ri % 2, front_done=True, mid_hook=hook)
                        elif kind == 'p':
                            mlstm_head_chunk(h, ci, L, t0, wqk, wqkr, wv_, wvr, Cst[:, h, :, :], Cbf[:, h, :, :], nst[:, :, h], nbf[:, :, h], '%d' % h)
                        else:
                            pb = b % 2
                            if b == 0:
                                ld_sample(0)
                            if b + 1 < 16:
                                ld_sample(b + 1)
                            nfS = nSall[:, b, :, h]
                            P.add('act', lambda E, pb=pb: E.activation(out=Csbs[pb][:, :, :], in_=CsFs[pb][:, :, :], func=AF.Copy), r=['CS%d' % pb], w=['CbS%d' % pb])
                            P.add('act', lambda E, nfS=nfS, pb=pb: E.activation(out=nsbs[pb][:, :, 0], in_=nfS, func=AF.Copy), r=['nS%d' % pb], w=['nbS%d' % pb])
                            mlstm_head_chunk(h, ci, L, t0, wqk, wqkr, wv_, wvr, CsFs[pb][:, :, :], Csbs[pb][:, :, :], nfS, nsbs[pb][:, :, 0], 'S%d' % pb)
                            P.add('sp', lambda E, b=b, h=h, pb=pb: E.dma_start(out=oCs[b, h].rearrange("(c p) v -> p c v", p=128), in_=CsFs[pb][:, :, :]), r=['CS%d' % pb], stream='stC')

            def mlstm_finish_sample():
                P.add('sp', lambda E: E.dma_start(out=ons.rearrange("b p x -> p b x"), in_=nSall[:, :, :, :].rearrange("p b c h -> p b (c h)")), r=['nS0', 'nS1'], stream='stn')


            def kv_phase(seg):
                wkv, wkvr = W.get(wblk[B_KV], KC * 512)
                wkv = wkv.rearrange("p (k n) -> p k n", n=512)
                kpad = scrB[:, 0:1024].rearrange("p (x d) -> p x d", d=128)
                knf = scrF[:, 600:856]
                vf = scrF[:, 856:1112]
                ssk = scrF[:, 1112:1116]
                junk = scrF[:, 1120:1184]
                psTv = psT[:, 0:1024].rearrange("p (x l) -> p x l", l=128)
                P.add('dve', lambda E: E.memset(kpad[:, :, :], 0.0), w=['kpad'])
                tiles = []
                if seg == 0:
                    tiles.append((NA, 0, 'A'))
                for i in range(4):
                    tiles.append((128, NA + 128 * i, i))
                for (L, t0, sl) in tiles:
                    ps, psn = rot_psA()

                    def mmf(E, L=L, t0=t0, ps=ps):
                        for kc in range(KC):
                            ins = E.matmul(ps[0:L, 0:512], lhsT=xn[:, kc, t0:t0 + L], rhs=wkv[:, kc, :], start=(kc == 0), stop=(kc == KC - 1))
                        return ins
                    P.add('pe', mmf, r=[wkvr, 'xnall'], w=[psn])
                    for g in range(4):
                        P.add('act', lambda E, g=g, L=L, ps=ps: E.activation(out=junk[0:L, :], in_=ps[0:L, 64 * g:64 * g + 64], func=AF.Square, accum_out=ssk[0:L, g:g + 1]),
                              r=[psn, 'junk'], w=['junk', 'ssk'])
                    P.add('act', lambda E, L=L: E.activation(out=ssk[0:L, :], in_=ssk[0:L, :], func=AF.Sqrt, bias=epsb[0:L, :], scale=1.0 / 64.0), r=['ssk', 'epsb'], w=['ssk'])
                    P.add('dve', lambda E, L=L: E.reciprocal(ssk[0:L, :], ssk[0:L, :]), r=['ssk'], w=['ssk'])
                    dk = knA if sl == 'A' else knf
                    dv = vA if sl == 'A' else vf
                    dkr = 'knA' if sl == 'A' else 'knf'
                    dvr = 'vA' if sl == 'A' else 'vf'
                    for g in range(4):
                        P.add('dve', lambda E, g=g, L=L, ps=ps, dk=dk: E.scalar_tensor_tensor(out=dk[0:L, 64 * g:64 * g + 64], in0=ps[0:L, 64 * g:64 * g + 64], scalar=ssk[0:L, g:g + 1],
                                                                                        in1=kg_bc[0:L, :], op0=ALU.mult, op1=ALU.mult), r=[psn, 'ssk', 'smc'], w=[dkr])
                    P.add('act', lambda E, L=L, ps=ps, dv=dv: E.activation(out=dv[0:L, :], in_=ps[0:L, 256:512], func=AF.Copy), r=[psn], w=[dvr])
                    if sl == 'A':
                        P.add('sp', lambda E: E.dma_start(out=okmp[:, :], in_=knA[0:16, :]), r=['knA'], stream='stkv')
                        P.add('sp', lambda E: E.dma_start(out=ovmp[:, :], in_=vA[0:16, :]), r=['vA'], stream='stkv2')
                    if seg == NSEG - 1 and sl == 3:
                        P.add('sp', lambda E: E.dma_start(out=okwp[:, :], in_=knf[:, :]), r=['knf'], stream='stkv')
                        P.add('sp', lambda E: E.dma_start(out=ovwp[:, :], in_=vf[:, :]), r=['vf'], stream='stkv2')
                    Lk = 16 if sl == 'A' else 128
                    dk3 = dk[0:Lk, :].rearrange("p (g d) -> p g d", d=64)
                    dv3 = dv[0:Lk, :].rearrange("p (g d) -> p g d", d=64)
                    kp4 = kpad[0:Lk, :, :].rearrange("p (g e) d -> p g e d", e=2)
                    P.add('dve', lambda E, kp4=kp4, dk3=dk3: E.tensor_copy(out=kp4[:, :, 0, 0:64], in_=dk3), r=[dkr], w=['kpad'])
                    P.add('dve', lambda E, kp4=kp4, dk3=dk3: E.tensor_copy(out=kp4[:, :, 1, 64:128], in_=dk3), r=[dkr], w=['kpad'])

                    def mm_t(E, Lk=Lk):
                        for x in range(8):
                            ins = E.transpose(psTv[:, x, 0:Lk], kpad[0:Lk, x, :], identb[0:Lk, 0:Lk])
                        return ins
                    P.add('pe', mm_t, r=['kpad', 'identb'], w=['psT'])
                    if sl == 'A':
                        P.add('act', lambda E: E.activation(out=kTm[:, :, :], in_=psTv[:, :, 0:16], func=AF.Copy), r=['psT'], w=['kTm'])
                        P.add('dve', lambda E, dv3=dv3: E.tensor_copy(out=vext[0:16, 0, :, 0:64], in_=dv3), r=[dvr], w=['vext0'])
                    else:
                        P.add('act', lambda E, sl=sl: E.activation(out=kTz[:, :, (1 + sl) * 128:(2 + sl) * 128], in_=psTv[:, :, :], func=AF.Copy), r=['psT'], w=['kTz%d' % (1 + sl)])
                        P.add('dve', lambda E, sl=sl, dv3=dv3: E.tensor_copy(out=vext[:, 2 + sl, :, 0:64], in_=dv3), r=[dvr], w=['vext%d' % (2 + sl)])

            qn = scrB[:, 0:KC * NT].rearrange("p (c t) -> p c t", t=NT)
            oT = scrB[:, KC * NT:2 * KC * NT].rearrange("p (c t) -> p c t", t=NT)
            ob = 2 * KC * NT
            otok = scrB[:, ob:ob + 2048]
            Pc = scrB[:, ob + 2048:ob + 2048 + 320]
            sqq = scrF[:, 0:256].bitcast(BF16)
            rq = scrF[:, 256:768]
            tmpS = scrF[:, 768:1280]
            Pcur = scrF[:, 1280:1536].bitcast(BF16)
            Pprev = scrF[:, 1536:1792].bitcast(BF16)
            Pmeta = scrF[:, 1792:2048].bitcast(BF16)
            dn = scrF[:, 2048:2052]
            qg8 = scrF[:, 2056:2057]
            kwin_f = scrF[:, 2064:2320]
            vwin_f = scrF[:, 2320:2576]
            k2_f = scrF[:, 2576:2832]
            sb2 = ob + 2048 + 320

            def q_proj(blocks):
                P.add('dve', lambda E: E.tensor_scalar(qg8[:, :], qgt[:, :], 0.125, None, ALU.mult), r=['qgt'], w=['qg8'])
                for jb in range(4):
                    wv, wr = W.get(wblk[B_Q + jb], KC * 512)
                    wv = wv.rearrange("p (k n) -> p k n", n=512)
                    for j in range(4):
                        c = 4 * jb + j
                        for (b0, bn) in blocks:
                            bk = 'b%d' % b0
                            ps, psn = rot_psA()

                            def mmf(E, wv=wv, j=j, b0=b0, bn=bn, ps=ps):
                                for kc in range(KC):
                                    ins = E.matmul(ps[:, 0:bn], lhsT=wv[:, kc, 128 * j:128 * j + 128], rhs=xn[:, kc, b0:b0 + bn], start=(kc == 0), stop=(kc == KC - 1))
                                return ins
                            P.add('pe', mmf, r=[wr, 'xn' + bk], w=[psn])
                            P.add('act', lambda E, bn=bn, ps=ps: E.activation(out=sqq[:, 0:bn], in_=ps[:, 0:bn], func=AF.Square), r=[psn], w=['sqq'])
                            ps2, ps2n = rot_psA()
                            P.add('pe', lambda E, bn=bn, ps2=ps2: E.matmul(ps2[:, 0:bn], lhsT=bd64[:, :], rhs=sqq[:, 0:bn], start=True, stop=True), r=['bd64', 'sqq'], w=[ps2n])
                            P.add('act', lambda E, bn=bn, ps2=ps2: E.activation(out=rq[:, 0:bn], in_=ps2[:, 0:bn], func=AF.Sqrt, bias=epsb[:, :], scale=1.0), r=[ps2n, 'epsb'], w=['rq'])
                            P.add('dve', lambda E, bn=bn: E.reciprocal(rq[:, 0:bn], rq[:, 0:bn]), r=['rq'], w=['rq'])
                            P.add('dve', lambda E, c=c, b0=b0, bn=bn, ps=ps: E.scalar_tensor_tensor(out=qn[:, c, b0:b0 + bn], in0=ps[:, 0:bn], scalar=qg8[:, 0:1], in1=rq[:, 0:bn],
                                                                                              op0=ALU.mult, op1=ALU.mult), r=[psn, 'qg8', 'rq'], w=['qn' + bk])

            def attn_core(nq, qcols, keysets, res_in, out_rows_ap, meta0=False):
                nk_list = [ks[1] for ks in keysets]
                Pt = [Pcur, Pprev, Pmeta]
                for g in range(4):
                    for e in range(2):
                        kx = 2 * g + e
                        heads = [8 * g + 2 * j + e for j in range(4)]
                        for si, (kT, nk, vx, Rt, kind) in enumerate(keysets):
                            pb = psB[si]
                            pbv = pb[:, 0:4 * nq].rearrange("p (j q) -> p j q", q=nq)
                            P.add('pe', lambda E, kT=kT, nk=nk, pbv=pbv, kx=kx, g=g: E.matmul(pbv[0:nk, :, :], lhsT=kT[:, kx, 0:nk], rhs=qn[:, 4 * g:4 * g + 4, qcols[0]:qcols[1]], start=True, stop=True),
                                  r=res_in + ['qnall'], w=['psB%d' % si])
                            Pv = Pt[si][:, 0:4 * nq].rearrange("p (j q) -> p j q", q=nq)
                            tv = tmpS[:, 0:4 * nq].rearrange("p (j q) -> p j q", q=nq)
                            if kind == 'R':
                                for j in range(4):
                                    P.add('dve', lambda E, j=j, nk=nk, Rt=Rt, pbv=pbv, tv=tv, heads=heads: E.scalar_tensor_tensor(out=tv[0:nk, j, :], in0=Rt[0:nk, 0:nq], scalar=float(SLOPES[heads[j]]),
                                                                                                                in1=pbv[0:nk, j, :], op0=ALU.mult, op1=ALU.add),
                                          r=['psB%d' % si, 'ctf'], w=['tmpS'])
                                P.add('act', lambda E, nk=nk, Pv=Pv, tv=tv: E.activation(out=Pv[0:nk, :, :], in_=tv[0:nk, :, :], func=AF.Exp), r=['tmpS'], w=['P%d' % si])
                            else:
                                for j in range(4):
                                    P.add('act', lambda E, j=j, nk=nk, Pv=Pv, pbv=pbv, heads=heads: E.activation(out=Pv[0:nk, j, :], in_=pbv[0:nk, j, :], func=AF.Exp,
                                                                                             bias=nslope128[0:nk, heads[j]:heads[j] + 1], scale=1.0),
                                          r=['psB%d' % si, 'smc'], w=['P%d' % si])
                        po, pon = rot_psA()

                        def mm_pv(E, po=po, g=g):
                            for j in range(4):
                                for si, (kT, nk, vx, Rt, kind) in enumerate(keysets):
                                    Pv = Pt[si][:, 0:4 * nq].rearrange("p (j q) -> p j q", q=nq)
                                    ins = E.matmul(po[0:nq, 65 * j:65 * j + 65], lhsT=Pv[0:nk, j, :], rhs=vx[0:nk, g, :], start=(si == 0), stop=(si == len(keysets) - 1))
                            return ins
                        P.add('pe', mm_pv, r=['P%d' % si for si in range(len(keysets))] + res_in, w=[pon])
                        pov = po[:, 0:260].rearrange("p (j d) -> p j d", d=65)
                        esv = esink[:, 8 * g:8 * g + 8].rearrange("p (j e) -> p j e", e=2)
                        P.add('dve', lambda E, pov=pov, esv=esv, e=e: E.tensor_tensor(out=dn[0:nq, :], in0=pov[0:nq, :, 64], in1=esv[0:nq, :, e], op=ALU.add), r=[pon, 'esink'], w=['dn'])
                        P.add('dve', lambda E: E.reciprocal(dn[0:nq, :], dn[0:nq, :]), r=['dn'], w=['dn'])
                        for j in range(4):
                            hh = heads[j]
                            P.add('dve', lambda E, j=j, hh=hh, pov=pov: E.tensor_scalar(otok[0:nq, 64 * hh:64 * hh + 64], pov[0:nq, j, 0:64], dn[0:nq, j:j + 1], None, ALU.mult),
                                  r=[pon, 'dn'], w=['otok'])

            def attn_prompt(seg):
                psTv = psT[:, 0:1024].rearrange("p (x l) -> p x l", l=128)
                def attn_block(i):
                    t0 = NA + 128 * i
                    first = (seg == 0 and i == 0)
                    keysets = [(kTz[:, :, (1 + i) * 128:(2 + i) * 128], 128, vext[:, 2 + i, :, :], Rcur, 'R')]
                    if not first:
                        keysets.append((kTz[:, :, i * 128:(1 + i) * 128], 128, vext[:, 1 + i, :, :], Rprev, 'R'))
                        keysets.append((kTm[:, :, :], 16, vext[:, 0, :, :], None, 'C'))
                    else:
                        keysets.append((kTm[:, :, :], 16, vext[:, 0, :, :], Rmeta0, 'R'))
                    psT32 = psT[:, :].bitcast(F32)
                    sets = [([psB[0], psB[1], psB[2]], ['psB0', 'psB1', 'psB2'], psA[3], 'psA3'),
                            ([psA[0], psA[1], psA[2]], ['psA0', 'psA1', 'psA2'], psT32, 'psT')]
                    Psets = [[Pcur, Pprev, Pmeta],
                             [scrF[:, 2064:2320].bitcast(BF16), scrF[:, 2320:2576].bitcast(BF16), scrF[:, 2576:2832].bitcast(BF16)]]
                    tmps = [tmpS, scrF[:, 0:512]]
                    dns = [scrF[:, 2048:2052], scrF[:, 2052:2056]]
                    nks = len(keysets)

                    def st_S(k):
                        g, e, p = k // 2, k % 2, k % 2
                        banks, bnames, _, _ = sets[p]
                        for si, (kT, nk, vx, Rt, kind) in enumerate(keysets):
                            pbv = banks[si][:, 0:512].rearrange("p (j q) -> p j q", q=128)
                            P.add('pe', lambda E, kT=kT, nk=nk, pbv=pbv, g=g, e=e: E.matmul(pbv[0:nk, :, :], lhsT=kT[:, 2 * g + e, 0:nk], rhs=qn[:, 4 * g:4 * g + 4, t0:t0 + 128], start=True, stop=True),
                                  r=['kTzall', 'qnall'], w=[bnames[si]])

                    def st_BX(k):
                        g, e, p = k // 2, k % 2, k % 2
                        banks, bnames, _, _ = sets[p]
                        heads = [8 * g + 2 * j + e for j in range(4)]
                        tv = tmps[p][:, 0:512].rearrange("p (j q) -> p j q", q=128)
                        for si, (kT, nk, vx, Rt, kind) in enumerate(keysets):
                            pbv = banks[si][:, 0:512].rearrange("p (j q) -> p j q", q=128)
                            Pv = Psets[p][si][:, 0:512].rearrange("p (j q) -> p j q", q=128)
                            if kind == 'R':
                                for j in range(4):
                                    P.add('dve', lambda E, j=j, nk=nk, Rt=Rt, pbv=pbv, tv=tv, heads=heads: E.scalar_tensor_tensor(
                                        out=tv[0:nk, j, :], in0=Rt[0:nk, 0:128], scalar=float(SLOPES[heads[j]]), in1=pbv[0:nk, j, :], op0=ALU.mult, op1=ALU.add),
                                        r=[bnames[si], 'ctf'], w=['tmpS%d' % p])
                                P.add('act', lambda E, nk=nk, Pv=Pv, tv=tv: E.activation(out=Pv[0:nk, :, :], in_=tv[0:nk, :, :], func=AF.Exp), r=['tmpS%d' % p], w=['P%d_%d' % (p, si)])
                            else:
                                for j in range(4):
                                    P.add('act', lambda E, j=j, nk=nk, Pv=Pv, pbv=pbv, heads=heads: E.activation(out=Pv[0:nk, j, :], in_=pbv[0:nk, j, :], func=AF.Exp,
                                                                                                          bias=nslope128[0:nk, heads[j]:heads[j] + 1], scale=1.0),
                                          r=[bnames[si], 'smc'], w=['P%d_%d' % (p, si)])

                    def st_PV(k):
                        g, e, p = k // 2, k % 2, k % 2
                        _, _, po, pon = sets[p]

                        def mm_pv(E, po=po, g=g, p=p):
                            for j in range(4):
                                for si, (kT, nk, vx, Rt, kind) in enumerate(keysets):
                                    Pv = Psets[p][si][:, 0:512].rearrange("p (j q) -> p j q", q=128)
                                    ins = E.matmul(po[:, 65 * j:65 * j + 65], lhsT=Pv[0:nk, j, :], rhs=vx[0:nk, g, :], start=(si == 0), stop=(si == nks - 1))
                            return ins
                        P.add('pe', mm_pv, r=['P%d_%d' % (p, si) for si in range(nks)] + ['kTzall'], w=[pon])

                    def st_E(k):
                        g, e, p = k // 2, k % 2, k % 2
                        _, _, po, pon = sets[p]
                        dn_ = dns[p]
                        heads = [8 * g + 2 * j + e for j in range(4)]
                        pov = po[:, 0:260].rearrange("p (j d) -> p j d", d=65)
                        esv = esink[:, 8 * g:8 * g + 8].rearrange("p (j e) -> p j e", e=2)
                        P.add('dve', lambda E, pov=pov, esv=esv, e=e, dn_=dn_: E.tensor_tensor(out=dn_[:, :], in0=pov[:, :, 64], in1=esv[:, :, e], op=ALU.add), r=[pon, 'esink'], w=['dn%d' % p])
                        P.add('dve', lambda E, dn_=dn_: E.reciprocal(dn_[:, :], dn_[:, :]), r=['dn%d' % p], w=['dn%d' % p])
                        for j in range(4):
                            hh = heads[j]
                            P.add('dve', lambda E, j=j, hh=hh, pov=pov, dn_=dn_: E.tensor_scalar(otok[:, 64 * hh:64 * hh + 64], pov[:, j, 0:64], dn_[:, j:j + 1], None, ALU.mult),
                                  r=[pon, 'dn%d' % p], w=['otok%d' % k])

                    st_S(0)
                    st_BX(0)
                    for k in range(8):
                        if k + 1 < 8:
                            st_S(k + 1)
                            st_BX(k + 1)
                        st_PV(k)
                        st_E(k)
                    for half in range(2):
                        def mm_t(E, half=half):
                            for x in range(8):
                                c = 8 * half + x
                                ins = E.transpose(psTv[:, x, :], otok[:, 128 * c:128 * c + 128], identb[:, :])
                            return ins
                        P.add('pe', mm_t, r=['otok%d' % k for k in range(8)] + ['identb'], w=['psT'])
                        P.add('act', lambda E, half=half, t0=t0: E.activation(out=oT[:, 8 * half:8 * half + 8, t0:t0 + 128], in_=psTv[:, :, :], func=AF.Copy), r=['psT'], w=['oT'])
                for i in range(4):
                    attn_block(i)
                P.add('dve', lambda E: E.tensor_copy(out=kTz[:, :, 0:128], in_=kTz[:, :, 512:640]), r=['kTzall'], w=['kTzall'])
                P.add('dve', lambda E: E.tensor_copy(out=vext[:, 1, :, :], in_=vext[:, 5, :, :]), r=['kTzall'], w=['kTzall'])

            def attn_sample():
                psTv = psT[:, 0:1024].rearrange("p (x l) -> p x l", l=128)
                xf = xn[:, :, :].rearrange("p c t -> p (c t)")
                kpadS = xf[:, 0:1024].rearrange("p (x d) -> p x d", d=128)
                kTzS = xf[:, 1024:2048].rearrange("p (x d) -> p x d", d=128)
                kpad2 = xf[:, 2048:3072].rearrange("p (x d) -> p x d", d=128)
                kTz2 = xf[:, 3072:3328].rearrange("p (x d) -> p x d", d=32)
                vxS1 = xf[:, 3328:3588].rearrange("p (g d) -> p g d", d=65)
                vxS2 = xf[:, 3588:3848].rearrange("p (g d) -> p g d", d=65)
                oS_all = xf[:, 3848:5896]
                v2_f = xf[:, 5896:6408].bitcast(F32)
                P.add('dve', lambda E: E.memset(xf[:, 0:3328], 0.0), w=['kpadS', 'kpad2', 'kTzS', 'kTz2'])
                P.add('dve', lambda E: E.memset(xf[:, 3328:3848], 1.0), w=['vxS1', 'vxS2'])
                SR1 = xf[:, 6408:6664].bitcast(F32)
                SR2 = xf[:, 6664:6920].bitcast(F32)
                esP = xf[:, 6920:6984].bitcast(F32)
                dnS = xf[:, 6984:7048].bitcast(F32)
                tmp1 = tmpS[:, 0:128]
                tmp2 = tmpS[:, 128:256]
                P1 = Pcur[:, 0:128]
                P2 = Pprev[:, 0:128]
                for x8 in range(8):
                    g_, e_ = x8 // 2, x8 % 2
                    for j in range(4):
                        hh = 8 * g_ + 2 * j + e_
                        sidx = 4 * x8 + j
                        P.add('dve', lambda E, sidx=sidx, hh=hh: E.tensor_scalar(SR1[:, 4 * sidx:4 * sidx + 4], RwinS, float(SLOPES[hh]), None, ALU.mult), r=['ctf'], w=['SR1'])
                        P.add('dve', lambda E, sidx=sidx, hh=hh: E.tensor_scalar(SR2[0:20, 4 * sidx:4 * sidx + 4], R2S[0:20, :], float(SLOPES[hh]), None, ALU.mult), r=['ctf'], w=['SR2'])
                    esv = esink[:, 8 * g_:8 * g_ + 8].rearrange("p (j e) -> p j e", e=2)
                    P.add('dve', lambda E, x8=x8, esv=esv, e_=e_: E.tensor_copy(out=esP[:, 4 * x8:4 * x8 + 4], in_=esv[:, :, e_]), r=['esink'], w=['esP'])
                banks = [(psA[0], 'psA0'), (psA[1], 'psA1'), (psA[2], 'psA2'), (psA[3], 'psA3'), (psB[2], 'psB2')]
                kpS4 = kpadS[:, :, :].rearrange("p (g e) d -> p g e d", e=2)
                kp24 = kpad2[0:20, :, :].rearrange("p (g e) d -> p g e d", e=2)
                kw3 = kwin_f[:, :].rearrange("p (g d) -> p g d", d=64)
                vw3 = vwin_f[:, :].rearrange("p (g d) -> p g d", d=64)
                k23 = k2_f[0:20, :].rearrange("p (g d) -> p g d", d=64)
                v23 = v2_f[0:20, :].rearrange("p (g d) -> p g d", d=64)
                for b in range(16):
                    r0 = 16 + 4 * b
                    P.add('sp', lambda E, b=b: E.dma_start(out=kwin_f[:, :], in_=ckw[b]), w=['kwin_f'], stream='ldk0')
                    P.add('sp', lambda E, b=b: E.dma_start(out=vwin_f[:, :], in_=cvw[b]), w=['vwin_f'], stream='ldk1')
                    P.add('sp', lambda E, b=b: E.dma_start(out=k2_f[0:16, :], in_=ckm[b]), w=['k2_f'], stream='ldk2')
                    P.add('sp', lambda E, r0=r0: E.dma_start(out=k2_f[16:20, :], in_=knA[r0:r0 + 4, :]), r=['knA'], w=['k2_f'], stream='ldk2')
                    P.add('sp', lambda E, b=b: E.dma_start(out=v2_f[0:16, :], in_=cvm[b]), w=['v2_f'], stream='ldk3')
                    P.add('sp', lambda E, r0=r0: E.dma_start(out=v2_f[16:20, :], in_=vA[r0:r0 + 4, :]), r=['vA'], w=['v2_f'], stream='ldk3')
                    P.add('sp', lambda E, b=b: E.dma_start(out=okws[b, 0:124, :], in_=kwin_f[4:128, :]), r=['kwin_f'], stream='stkw')
                    P.add('sp', lambda E, b=b, r0=r0: E.dma_start(out=okws[b, 124:128, :], in_=knA[r0:r0 + 4, :]), r=['knA'], stream='stkw')
                    P.add('sp', lambda E, b=b: E.dma_start(out=ovws[b, 0:124, :], in_=vwin_f[4:128, :]), r=['vwin_f'], stream='stkw2')
                    P.add('sp', lambda E, b=b, r0=r0: E.dma_start(out=ovws[b, 124:128, :], in_=vA[r0:r0 + 4, :]), r=['vA'], stream='stkw2')
                    P.add('dve', lambda E: E.tensor_copy(out=kpS4[:, :, 0, 0:64], in_=kw3), r=['kwin_f'], w=['kpadS'])
                    P.add('dve', lambda E: E.tensor_copy(out=kpS4[:, :, 1, 64:128], in_=kw3), r=['kwin_f'], w=['kpadS'])
                    P.add('dve', lambda E: E.tensor_copy(out=kp24[:, :, 0, 0:64], in_=k23), r=['k2_f'], w=['kpad2'])
                    P.add('dve', lambda E: E.tensor_copy(out=kp24[:, :, 1, 64:128], in_=k23), r=['k2_f'], w=['kpad2'])

                    def mm_t1(E):
                        for x in range(8):
                            ins = E.transpose(psTv[:, x, :], kpadS[:, x, :], identb[:, :])
                        return ins
                    P.add('pe', mm_t1, r=['kpadS', 'identb'], w=['psT'])
                    P.add('act', lambda E: E.activation(out=kTzS[:, :, :], in_=psTv[:, :, :], func=AF.Copy), r=['psT'], w=['kTzS'])

                    def mm_t2(E):
                        for x in range(8):
                            ins = E.transpose(psTv[:, x, 0:20], kpad2[0:20, x, :], identb[0:20, 0:20])
                        return ins
                    P.add('pe', mm_t2, r=['kpad2', 'identb'], w=['psT'])
                    P.add('act', lambda E: E.activation(out=kTz2[:, :, 0:20], in_=psTv[:, :, 0:20], func=AF.Copy), r=['psT'], w=['kTz2'])
                    P.add('dve', lambda E: E.tensor_copy(out=vxS1[:, :, 0:64], in_=vw3), r=['vwin_f'], w=['vxS1'])
                    P.add('dve', lambda E: E.tensor_copy(out=vxS2[0:20, :, 0:64], in_=v23), r=['v2_f'], w=['vxS2'])
                    def mm_sc(E, r0=r0):
                        for x8 in range(8):
                            g_ = x8 // 2
                            o0 = psB[0][:, 16 * x8:16 * x8 + 16].rearrange("p (j q) -> p j q", q=4)
                            o1 = psB[1][:, 16 * x8:16 * x8 + 16].rearrange("p (j q) -> p j q", q=4)
                            E.matmul(o0[:, :, :], lhsT=kTzS[:, x8, :], rhs=qn[:, 4 * g_:4 * g_ + 4, r0:r0 + 4], start=True, stop=True)
                            ins = E.matmul(o1[0:20, :, :], lhsT=kTz2[:, x8, 0:20], rhs=qn[:, 4 * g_:4 * g_ + 4, r0:r0 + 4], start=True, stop=True)
                        return ins
                    P.add('pe', mm_sc, r=['kTzS', 'kTz2', 'qnall'], w=['psB0', 'psB1'])
                    P.add('dve', lambda E: E.tensor_tensor(out=tmp1[:, :], in0=psB[0][:, 0:128], in1=SR1[:, :], op=ALU.add), r=['psB0', 'SR1'], w=['tmp1'])
                    P.add('dve', lambda E: E.tensor_tensor(out=tmp2[0:20, :], in0=psB[1][0:20, 0:128], in1=SR2[0:20, :], op=ALU.add), r=['psB1', 'SR2'], w=['tmp2'])
                    P.add('act', lambda E: E.activation(out=P1[:, :], in_=tmp1[:, :], func=AF.Exp), r=['tmp1'], w=['P1s'])
                    P.add('act', lambda E: E.activation(out=P2[0:20, :], in_=tmp2[0:20, :], func=AF.Exp), r=['tmp2'], w=['P2s'])

                    def mm_pvs(E):
                        for hh in range(32):
                            g_, j_, e_ = hh // 8, (hh % 8) // 2, hh % 2
                            sidx = 4 * (2 * g_ + e_) + j_
                            bank = banks[hh // 7][0]
                            col = (hh % 7) * 65
                            E.matmul(bank[0:4, col:col + 65], lhsT=P1[:, 4 * sidx:4 * sidx + 4], rhs=vxS1[:, g_, :], start=True, stop=False)
                            ins = E.matmul(bank[0:4, col:col + 65], lhsT=P2[0:20, 4 * sidx:4 * sidx + 4], rhs=vxS2[0:20, g_, :], start=False, stop=True)
                        return ins
                    P.add('pe', mm_pvs, r=['P1s', 'P2s', 'vxS1', 'vxS2'], w=[bn_ for (_, bn_) in banks])
                    for k, (bank, bname) in enumerate(banks):
                        s0 = 7 * k
                        nk_ = min(7, 32 - s0)
                        bv = bank[:, 0:nk_ * 65].rearrange("p (s d) -> p s d", d=65)
                        P.add('dve', lambda E, bv=bv, s0=s0, nk_=nk_: E.tensor_tensor(out=dnS[0:4, s0:s0 + nk_], in0=bv[0:4, :, 64], in1=esink[0:4, s0:s0 + nk_], op=ALU.add),
                              r=[bname, 'esink'], w=['dnS%d' % k])
                    P.add('dve', lambda E: E.reciprocal(dnS[0:4, 0:32], dnS[0:4, 0:32]), r=['dnS%d' % k for k in range(5)], w=['dnSr'])
                    for k, (bank, bname) in enumerate(banks):
                        s0 = 7 * k
                        nk_ = min(7, 32 - s0)
                        bv = bank[:, 0:nk_ * 65].rearrange("p (s d) -> p s d", d=65)
                        otv = otok[0:4, 64 * s0:64 * (s0 + nk_)].rearrange("p (s d) -> p s d", d=64)
                        P.add('dve', lambda E, bv=bv, s0=s0, nk_=nk_, otv=otv: E.tensor_tensor(out=otv, in0=bv[0:4, :, 0:64], in1=dnS[0:4, s0:s0 + nk_, None].broadcast_to([4, nk_, 64]), op=ALU.mult),
                              r=[bname, 'dnSr'], w=['otok'])
                    P.add('sp', lambda E, b=b: E.dma_start(out=oS_all[4 * b:4 * b + 4, :], in_=otok[0:4, :]), r=['otok'], w=['oS_all'], stream='mvo')
                for half in range(2):
                    def mm_t(E, half=half):
                        for x in range(8):
                            c = 8 * half + x
                            ins = E.transpose(psTv[:, x, 0:64], oS_all[0:64, 128 * c:128 * c + 128], identb[0:64, 0:64])
                        return ins
                    P.add('pe', mm_t, r=['oS_all', 'identb'], w=['psT'])
                    P.add('act', lambda E, half=half: E.activation(out=oT[:, 8 * half:8 * half + 8, 16:NA], in_=psTv[:, :, 0:64], func=AF.Copy), r=['psT'], w=['oT'])

            gen_state['kv_phase'] = kv_phase
            gen_state['mlstm'] = mlstm
            gen_state['rmsnorm'] = rmsnorm
            gen_state['ffn'] = ffn
            gen_state['out_proj'] = out_proj

            for seg in range(NSEG):
                blocks = [(NA, TS)] if seg > 0 else [(0, NA), (NA, TS)]
                allres = ['hTb%d' % b0 for (b0, _) in blocks]
                if seg == 0:
                    P.add('sp', lambda E: E.dma_start(out=hT[:, :, 0:NA], in_=xA[:, :, :]), w=['hTb0'], stream='ldx0')
                P.add('sp', lambda E, seg=seg: E.dma_start(out=hT[:, :, NA:NT], in_=xR[seg]), w=['hTb%d' % NA], stream='ldx1')
                P.barrier()
                rmsnorm(0, blocks)
                ffn(0, blocks, seg)
                rmsnorm(4, blocks)
                P.add('dve', lambda E: E.tensor_copy(out=scrF[0:1, 0:1], in_=scrF[0:1, 0:1]), r=['xnb%d' % b0 for (b0, _) in blocks], w=['xnall'])
                P.barrier()
                mlstm(seg, blocks)
                if seg == 0:
                    mlstm_finish_sample()
                P.barrier()
                out_proj(B_AOUT, actT, 'actTall', blocks)
                P.barrier()
                rmsnorm(1, blocks)
                ffn(1, blocks, seg)
                rmsnorm(6, blocks)
                P.add('dve', lambda E: E.tensor_copy(out=scrF[0:1, 0:1], in_=scrF[0:1, 0:1]), r=['xnb%d' % b0 for (b0, _) in blocks], w=['xnall'])
                P.barrier()
                kv_phase(seg)
                P.barrier()
                rmsnorm(2, blocks, reuse=True)
                ffn(2, blocks, seg)
                rmsnorm(5, blocks)
                P.barrier()
                q_proj(blocks)
                P.add('dve', lambda E: E.memset(oT[:, :, 0:NA], 0.0), w=['oT'])
                P.add('dve', lambda E: E.tensor_copy(out=scrF[0:1, 0:1], in_=scrF[0:1, 0:1]), r=['qnb%d' % b0 for (b0, _) in blocks] + ['kTz%d' % x for x in range(1, 5)] + ['vext%d' % x for x in range(2, 6)] + ['kTm', 'vext0'], w=['qnall', 'kTzall'])
                P.barrier()
                attn_prompt(seg)
                if seg == 0:
                    P.barrier()
                    attn_sample()
                P.barrier()
                out_proj(B_BOUT, oT, 'oTall', blocks)
                P.barrier()
                rmsnorm(3, blocks)
                ffn(3, blocks, seg)
                P.barrier()
                if seg == 0:
                    P.add('sp', lambda E: E.dma_start(out=yA[:, :, :], in_=hT[:, :, 0:NA]), r=['hTb0'], stream='sty0')
                P.add('sp', lambda E, seg=seg: E.dma_start(out=yR[seg], in_=hT[:, :, NA:NT]), r=['hTb%d' % NA], stream='sty1')
                P.barrier()
            for h in range(4):
                P.add('sp', lambda E, h=h: E.dma_start(out=oCp[h].rearrange("(c p) v -> p c v", p=128), in_=Cst[:, h, :, :]), r=['C%d' % h], stream='stCp')
            P.add('sp', lambda E: E.dma_start(out=onp[:, :], in_=nst[:, :, :].rearrange("p c h -> p (c h)")), r=['n0', 'n1', 'n2', 'n3'], stream='stnp')
            P.add('sp', lambda E: E.dma_start(out=omp[:, :], in_=mbc[0:1, :]), r=['mbc'], stream='stmp')

        Pd = Prog(nc, dry=True)
        Wd = WRing(Pd, slots)
        gen(Pd, Wd)
        P = Prog(nc)
        W = WRing(P, slots, schedule=Wd.record)
        gen(P, W)
        P.emit(st, final_streams=['sty0', 'sty1', 'stCp', 'stnp', 'stmp', 'stC', 'stn', 'stm', 'stkv', 'stkv2', 'stkw', 'stkw2'])
        build_program.stats = P.stats
    return nc


def _fm(x2d):
    T = x2d.shape[0]
    return np.ascontiguousarray(x2d.T.reshape(KC, 128, T).transpose(1, 0, 2))


def _blk(Wm, cols):
    return Wm[:, cols].reshape(KC, 128, len(cols)).transpose(1, 0, 2).reshape(128, KC * len(cols))


def _const_tables():
    t = np.zeros((128, 12, 128), np.float32)
    p = np.arange(128)[:, None]
    f = np.arange(128)[None, :]
    t[:, 0] = (p == f)
    t[:, 1] = (p <= f)
    t[:, 2] = np.where(f <= p, 0.0, NEG)
    t[:, 3] = np.where(p <= f, 0.0, NEG)
    t[:, 4] = (p == 127) * np.ones((1, 128))
    t[:, 5] = (p == 15) * np.ones((1, 128))
    t[:, 6] = (p == 3) * np.ones((1, 128))
    t[:, 7] = ((p // 64) == (f // 64)) / 64.0
    t[:, 8] = np.where(f >= p, -(f - p).astype(np.float32), NEG)
    t[:, 9] = np.where(p > f, -(f + 128 - p).astype(np.float32), NEG)
    t[:, 10] = -np.minimum(16 + f - p, 128).astype(np.float32)
    i4 = np.arange(4)[None, :]
    t[:, 11, 0:4] = np.where(p > i4, -(128 + i4 - p).astype(np.float32), NEG)
    r2 = np.full((128, 4), -128.0, np.float32)
    for j in range(4):
        r2[16 + j] = np.where(j <= np.arange(4), -(np.arange(4) - j).astype(np.float32), NEG)
    t[:, 11, 4:8] = r2
    return t


def _prep_shared(inp):
    w_in = inp['w_ffn_in']
    w_out = inp['w_ffn_out']
    wblk = np.empty((NBLK, 128, KC * 512), np.float32)
    woutb = np.empty((64, 128, FC * 128), np.float32)
    for l in range(2):
        for i in range(2):
            f = 2 * l + i
            Wm = w_in[l, i]
            for b in range(22):
                cols = np.concatenate([np.arange(256 * b, 256 * b + 256), DFF + np.arange(256 * b, 256 * b + 256)])
                wblk[B_FFN + 22 * f + b] = _blk(Wm, cols)
            Wo = w_out[l, i]
            for oc in range(16):
                woutb[16 * f + oc] = Wo[:, 128 * oc:128 * oc + 128].reshape(FC, 128, 128).transpose(1, 0, 2).reshape(128, FC * 128)
    wa = inp['w_a_in'][0]
    for h in range(4):
        wblk[B_AQK + h] = _blk(wa, np.concatenate([np.arange(256 * h, 256 * h + 256), 1024 + np.arange(256 * h, 256 * h + 256)]))
        wblk[B_AV + h] = _blk(wa, 2048 + np.arange(512 * h, 512 * h + 512))
        wblk[B_AO + h] = _blk(wa, 4096 + np.arange(512 * h, 512 * h + 512))
        wblk[B_AOUT + h] = _blk(inp['w_a_out'][0], np.arange(512 * h, 512 * h + 512))
        wblk[B_Q + h] = _blk(inp['w_q'][0], np.arange(512 * h, 512 * h + 512))
        wblk[B_BOUT + h] = _blk(inp['w_b_out'][0], np.arange(512 * h, 512 * h + 512))
    wblk[B_KV] = _blk(inp['w_kv'], np.arange(512))
    wgate = np.ascontiguousarray(wa[:, 6144:6152].reshape(KC, 128, 8).transpose(1, 0, 2).reshape(128, KC * 8))
    gl = [inp['ffn_norm'][0, 0], inp['ffn_norm'][0, 1], inp['ffn_norm'][1, 0], inp['ffn_norm'][1, 1],
          inp['mix_norm'][0], inp['mix_norm'][1], inp['kv_norm'], inp['a_head_norm'][0]]
    gains = np.ascontiguousarray(np.stack([g.reshape(KC, 128).T for g in gl], axis=1)).astype(np.float32)
    nsl = np.array([-128.0 * s for s in SLOPES], np.float32)
    smallc = np.concatenate([inp['b_a_gate'][0], inp['k_norm'], inp['sinks'][0], nsl]).astype(np.float32)[None, :]
    qg = np.ascontiguousarray(np.tile(inp['q_norm'][0], 2)[:, None]).astype(np.float32)
    return dict(wblk=wblk, wout=woutb, wgate=wgate, gains=gains, smallc=smallc, qg=qg, ctab=_const_tables())


def _prep_core(inp, c):
    xs = inp['x_sample'][16 * c:16 * c + 16].reshape(64, D)
    xA = _fm(np.concatenate([inp['meta_tokens'], xs], axis=0))
    xp = inp['x_prompt'][c]
    xR = np.stack([_fm(xp[TS * s:TS * s + TS]) for s in range(NSEG)])
    stn = inp['state_n'][0, 16 * c:16 * c + 16]
    stn = np.ascontiguousarray(stn.reshape(16, 4, 2, 128).transpose(0, 3, 2, 1).reshape(16, 128, 8))
    return dict(
        xA=xA, xR=xR,
        stC=np.ascontiguousarray(inp['state_C'][0, 16 * c:16 * c + 16]),
        stn=stn,
        stm=np.ascontiguousarray(inp['state_m'][0, 16 * c:16 * c + 16]),
        ckm=np.ascontiguousarray(inp['cache_k_meta'][16 * c:16 * c + 16].reshape(16, 16, 256)),
        cvm=np.ascontiguousarray(inp['cache_v_meta'][16 * c:16 * c + 16].reshape(16, 16, 256)),
        ckw=np.ascontiguousarray(inp['cache_k_win'][16 * c:16 * c + 16].reshape(16, 128, 256)),
        cvw=np.ascontiguousarray(inp['cache_v_win'][16 * c:16 * c + 16].reshape(16, 128, 256)),
    )


def _tm(a):
    T = a.shape[2]
    return a.transpose(1, 0, 2).reshape(D, T).T


def _assemble(results):
    n = len(results)
    y_prompt = np.empty((n, 2048, D), np.float32)
    y_sample = np.empty((16 * n, 4, D), np.float32)
    c_p = np.empty((1, n, 4, 256, 512), np.float32)
    n_p = np.empty((1, n, 4, 256), np.float32)
    m_p = np.empty((1, n, 4), np.float32)
    k_meta_p = np.empty((n, 16, 4, 64), np.float32)
    v_meta_p = np.empty((n, 16, 4, 64), np.float32)
    k_win_p = np.empty((n, 128, 4, 64), np.float32)
    v_win_p = np.empty((n, 128, 4, 64), np.float32)
    c_s = np.empty((1, 16 * n, 4, 256, 512), np.float32)
    n_s = np.empty((1, 16 * n, 4, 256), np.float32)
    m_s = np.empty((1, 16 * n, 4), np.float32)
    k_win_s = np.empty((16 * n, 128, 4, 64), np.float32)
    v_win_s = np.empty((16 * n, 128, 4, 64), np.float32)
    for c, r in enumerate(results):
        for s in range(NSEG):
            y_prompt[c, TS * s:TS * s + TS] = _tm(r['yR'][s])
        y_sample[16 * c:16 * c + 16] = _tm(r['yA'])[16:].reshape(16, 4, D)
        c_p[0, c] = r['oCp']
        n_p[0, c] = r['onp'].reshape(128, 2, 4).transpose(2, 1, 0).reshape(4, 256)
        m_p[0, c] = r['omp'][0]
        k_meta_p[c] = r['okmp'].reshape(16, 4, 64)
        v_meta_p[c] = r['ovmp'].reshape(16, 4, 64)
        k_win_p[c] = r['okwp'].reshape(128, 4, 64)
        v_win_p[c] = r['ovwp'].reshape(128, 4, 64)
        c_s[0, 16 * c:16 * c + 16] = r['oCs']
        n_s[0, 16 * c:16 * c + 16] = r['ons'].reshape(16, 128, 2, 4).transpose(0, 3, 2, 1).reshape(16, 4, 256)
        m_s[0, 16 * c:16 * c + 16] = r['oms']
        k_win_s[16 * c:16 * c + 16] = r['okws'].reshape(16, 128, 4, 64)
        v_win_s[16 * c:16 * c + 16] = r['ovws'].reshape(16, 128, 4, 64)
    return (y_prompt, y_sample, c_p, n_p, m_p, k_meta_p, v_meta_p, k_win_p, v_win_p, c_s, n_s, m_s, k_win_s, v_win_s)


def kernel(**inputs):
    inp = {k: np.asarray(v) for k, v in inputs.items()}
    n = 8
    shared = _prep_shared(inp)
    in_maps = []
    for c in range(n):
        m = dict(shared)
        m.update(_prep_core(inp, c))
        in_maps.append(m)
    nc = build_program()
    res = run_bass_kernel_spmd(nc, in_maps, core_ids=list(range(n)))
    return _assemble(res.results)
```

```python
from contextlib import ExitStack
import numpy as np
import concourse.bass as bass
import concourse.mybir as mybir
from concourse.bass_utils import run_bass_kernel_spmd

F32 = mybir.dt.float32
BF16 = mybir.dt.bfloat16
AF = mybir.ActivationFunctionType
ALU = mybir.AluOpType
AX = mybir.AxisListType

D = 2048
KC = 16
DFF = 5632
FC = 44
NA = 80
TS = 512
NSEG = 4
NT = NA + TS
NSLOT = 3
EPS = 1e-6
NEG = -1e30
SLOPES = [2.0 ** (-8.0 * (h + 1) / 32.0) for h in range(32)]
B_FFN = 0
B_AQK = 88
B_AV = 92
B_AO = 96
B_AOUT = 100
B_KV = 104
B_Q = 105
B_BOUT = 109
NBLK = 113


class Prog:
    def __init__(self, nc, dry=False):
        self.nc = nc
        self.dry = dry
        self.engs = {'pe': nc.tensor, 'act': nc.scalar, 'dve': nc.vector, 'pool': nc.gpsimd, 'sp': nc.sync}
        self.ops = []
        self.lw = {}
        self.rd = {}
        self.stream_last = {}
        self.fence = {}
        self.last_eng = {}

    def add(self, eng, fn, r=(), w=(), stream=None):
        if self.dry:
            return -1
        i = len(self.ops)
        deps = set()
        for x in r:
            if x in self.lw:
                deps.add(self.lw[x])
        for x in w:
            if x in self.lw:
                d = self.lw[x]
                de, _, _, dst = self.ops[d]
                if not (stream is None and dst is None and de == eng):
                    deps.add(d)
            for key, d in self.rd.get(x, {}).items():
                if stream is None and key == eng:
                    continue
                deps.add(d)
        if stream is not None and stream in self.stream_last:
            deps.add(self.stream_last[stream])
        if eng in self.fence:
            deps.update(self.fence.pop(eng))
        self.ops.append((eng, fn, deps, stream))
        for x in r:
            self.rd.setdefault(x, {})[eng if stream is None else (eng, i)] = i
        for x in w:
            self.lw[x] = i
            self.rd[x] = {}
        if stream is not None:
            self.stream_last[stream] = i
        else:
            self.last_eng[eng] = i
        return i

    def barrier(self, engines=('pe', 'act', 'dve', 'sp')):
        if self.dry:
            return
        front = set(self.last_eng.values()) | set(self.stream_last[s] for s in self.stream_last if not s.startswith('w'))
        for e in engines:
            self.fence.setdefault(e, set()).update(front)

    def emit(self, stack, final_streams):
        EP = 3000
        SEP = 200
        ops = self.ops
        n = len(ops)
        needed = [False] * n
        for i, (e, fn, deps, st) in enumerate(ops):
            for d in deps:
                de, _, _, dst = ops[d]
                if dst is None and not (de == 'pe' and e == 'pe' and st is None):
                    needed[d] = True
        cnt = {}
        ms = [0] * n
        for i, (e, fn, deps, st) in enumerate(ops):
            if st is not None:
                key = ('s', st)
                cnt[key] = cnt.get(key, 0) + 1
                ms[i] = cnt[key]
            elif needed[i]:
                key = ('e', e)
                cnt[key] = cnt.get(key, 0) + 1
                ms[i] = cnt[key]
        sems = {}

        def sem_for(key, count):
            if key[0] == 's':
                ep, v = (count - 1) // SEP, ((count - 1) % SEP + 1) * 16
            else:
                ep, v = (count - 1) // EP, (count - 1) % EP + 1
            k2 = (key, ep)
            if k2 not in sems:
                sems[k2] = stack.enter_context(self.nc.semaphore("sem_%s_%s_%d" % (key[0], str(key[1]), ep)))
            return sems[k2], v

        waited = {e: {} for e in self.engs}
        for i, (e, fn, deps, st) in enumerate(ops):
            E = self.engs[e]
            reqs = {}
            for d in deps:
                de, _, _, dst = ops[d]
                if dst is not None:
                    key = ('s', dst)
                elif de == 'pe' and e == 'pe' and st is None:
                    continue
                else:
                    key = ('e', de)
                if ms[d] > reqs.get(key, 0):
                    reqs[key] = ms[d]
            for key, val in reqs.items():
                if waited[e].get(key, 0) >= val:
                    continue
                sm, v = sem_for(key, val)
                E.wait_ge(sm, v)
                waited[e][key] = val
            ins = fn(E)
            if st is not None:
                sm, v = sem_for(('s', st), ms[i])
                ins.then_inc(sm, 16)
            elif needed[i]:
                sm, v = sem_for(('e', e), ms[i])
                ins.then_inc(sm, 1)
        sp = self.engs['sp']
        for s in final_streams:
            if ('s', s) in cnt:
                sm, v = sem_for(('s', s), cnt[('s', s)])
                sp.wait_ge(sm, v)
        self.stats = dict(n_ops=n, n_sems=len(sems), counts={str(k): v for k, v in cnt.items()})


class WRing:
    def __init__(self, P, slots, schedule=None):
        self.P = P
        self.slots = slots
        self.schedule = schedule
        self.record = [] if schedule is None else None
        self.n_get = 0
        self.n_issued = 0

    def _issue_upto(self, j):
        while self.n_issued <= j and self.n_issued < len(self.schedule):
            k = self.n_issued
            src, nfree, cache, mode, ckey = self.schedule[k]
            s = k % NSLOT
            dst = self.slots[s][:, 0:nfree]
            if mode == 'read':
                self.P.add('pool', (lambda E, dst=dst, cache=cache: E.dma_start(out=dst, in_=cache)),
                           r=[ckey], w=['wslot%d' % s], stream='w%d' % s)
            else:
                self.P.add('pool', (lambda E, dst=dst, src=src: E.dma_start(out=dst, in_=src)),
                           w=['wslot%d' % s], stream='w%d' % s)
                if mode == 'write':
                    self.P.add('sp', (lambda E, dst=dst, cache=cache: E.dma_start(out=cache, in_=dst)),
                               r=['wslot%d' % s], w=[ckey], stream='wb%d' % s)
            self.n_issued += 1

    def get(self, src, nfree, hold=1, cache=None, mode=None, ckey=None):
        k = self.n_get
        self.n_get += 1
        if self.record is not None:
            self.record.append((src, nfree, cache, mode, ckey))
            return self.slots[k % NSLOT][:, 0:nfree], 'wslot%d' % (k % NSLOT)
        self._issue_upto(k - hold + NSLOT)
        return self.slots[k % NSLOT][:, 0:nfree], 'wslot%d' % (k % NSLOT)


def build_program(debug=False):
    nc = bass.Bass("TRN2", target_bir_lowering=False)

    def din(name, shape, dt=F32):
        return nc.dram_tensor(name, list(shape), dt, kind="ExternalInput").ap()

    def dout(name, shape, dt=F32):
        return nc.dram_tensor(name, list(shape), dt, kind="ExternalOutput").ap()

    xA = din("xA", [128, KC, NA])
    xR = din("xR", [NSEG, 128, KC, TS])
    wblk = din("wblk", [NBLK, 128, KC * 512])
    wout = din("wout", [64, 128, FC * 128])
    wgate = din("wgate", [128, KC * 8])
    gains = din("gains", [128, 8, KC])
    smallc = din("smallc", [1, 8 + 64 + 32 + 32])
    qg = din("qg", [128, 1])
    ctab = din("ctab", [128, 12, 128])
    stC = din("stC", [16, 4, 256, 512])
    stn = din("stn", [16, 128, 8])
    stm = din("stm", [16, 4])
    ckm = din("ckm", [16, 16, 256])
    cvm = din("cvm", [16, 16, 256])
    ckw = din("ckw", [16, 128, 256])
    cvw = din("cvw", [16, 128, 256])

    wbf_in = nc.dram_tensor("wbf_in", [88, 128, KC * 512], BF16, kind="Internal").ap()
    wbf_out = nc.dram_tensor("wbf_out", [64, 128, FC * 128], BF16, kind="Internal").ap()

    yA = dout("yA", [128, KC, NA])
    yR = dout("yR", [NSEG, 128, KC, TS])
    oCp = dout("oCp", [4, 256, 512])
    onp = dout("onp", [128, 8])
    omp = dout("omp", [1, 4])
    okmp = dout("okmp", [16, 256])
    ovmp = dout("ovmp", [16, 256])
    okwp = dout("okwp", [128, 256])
    ovwp = dout("ovwp", [128, 256])
    oCs = dout("oCs", [16, 4, 256, 512])
    ons = dout("ons", [16, 128, 8])
    oms = dout("oms", [16, 4])
    okws = dout("okws", [16, 128, 256])
    ovws = dout("ovws", [16, 128, 256])

    with ExitStack() as st:
        def sb(name, shape, dt):
            return st.enter_context(nc.sbuf_tensor(name, list(shape), dt))

        def pst(name, shape, dt):
            return st.enter_context(nc.psum_tensor(name, list(shape), dt))

        hT = sb("hT", [128, KC, NT], F32)
        xn = sb("xn", [128, KC, NT], BF16)
        slots = [sb("wslot%d" % i, [128, KC * 512], BF16) for i in range(NSLOT)]
        scrB = sb("scrB", [128, 21312], BF16)
        scrF = sb("scrF", [128, 2880], F32)
        nSall = sb("nSall", [128, 16, 2, 4], F32)
        Cst = sb("Cst", [128, 4, 2, 512], F32)
        Cbf = sb("Cbf", [128, 4, 2, 512], BF16)
        nst = sb("nst", [128, 2, 4], F32)
        nbf = sb("nbf", [128, 2, 4], BF16)
        mbc = sb("mbc", [128, 4], F32)
        ctf = sb("ctf", [128, 12, 128], F32)
        identb = sb("identb", [128, 128], BF16)
        onesb = sb("onesb", [128, 128], BF16)
        onesD = sb("onesD", [128, 128], BF16)
        bd64 = sb("bd64", [128, 128], BF16)
        onesf = sb("onesf", [128, 128], F32)
        epsb = sb("epsb", [128, 1], F32)
        gn = sb("gn", [128, 8, KC], F32)
        wg = sb("wg", [128, KC * 8], BF16)
        smc = sb("smc", [128, 8 + 64 + 32 + 32], F32)
        esink = sb("esink", [128, 32], F32)
        qgt = sb("qgt", [128, 1], F32)
        kTz = sb("kTz", [128, 8, 5 * 128], BF16)
        kTm = sb("kTm", [128, 8, 16], BF16)
        vext = sb("vext", [128, 6, 4, 65], BF16)
        knA = sb("knA", [128, 256], F32)
        vA = sb("vA", [128, 256], F32)

        psA = [pst("psA%d" % i, [128, 512], F32) for i in range(4)]
        psB = [pst("psB%d" % i, [128, 512], F32) for i in range(3)]
        psT = pst("psT", [128, 1024], BF16)

        identf = ctf[:, 0, :]
        Umat = ctf[:, 1, :]
        maskC = ctf[:, 2, :]
        maskT = ctf[:, 3, :]
        selL = {128: ctf[:, 4, :], 16: ctf[:, 5, :], 4: ctf[:, 6, :]}
        Rcur = ctf[:, 8, :]
        Rprev = ctf[:, 9, :]
        Rmeta0 = ctf[:, 10, :]
        RwinS = ctf[:, 11, 0:4]
        R2S = ctf[:, 11, 4:8]
        bgate_bc = smc[:, 0:8]
        kg_bc = smc[:, 8:8 + 64]
        nslope128 = smc[:, 8 + 64 + 32: 8 + 64 + 64]

        gen_state = {}

        def gen(P, W):
            cntr = [0]

            def rot_psA():
                i = cntr[0] % 4
                cntr[0] += 1
                return psA[i], 'psA%d' % i

            P.add('sp', lambda E: E.dma_start(out=ctf[:], in_=ctab[:, :, :]), w=['ctf'], stream='ld0')
            P.add('sp', lambda E: E.dma_start(out=gn[:], in_=gains[:, :, :]), w=['gn'], stream='ld1')
            P.add('sp', lambda E: E.dma_start(out=smc[:], in_=smallc.partition_broadcast(128)), w=['smc'], stream='ld2')
            P.add('sp', lambda E: E.dma_start(out=qgt[:], in_=qg[:, :]), w=['qgt'], stream='ld3')
            P.add('pool', lambda E: E.dma_start(out=wg[:], in_=wgate[:, :]), w=['wg'], stream='ldw')
            P.add('dve', lambda E: E.memset(onesb[:], 1.0), w=['onesb'])
            P.add('dve', lambda E: E.memset(onesD[:], 1.0 / D), w=['onesD'])
            P.add('dve', lambda E: E.memset(onesf[:], 1.0), w=['onesf'])
            P.add('dve', lambda E: E.memset(epsb[:], EPS), w=['epsb'])
            P.add('dve', lambda E: E.memset(Cst[:], 0.0), w=['C0', 'C1', 'C2', 'C3'])
            P.add('dve', lambda E: E.memset(Cbf[:], 0.0), w=['Cb0', 'Cb1', 'Cb2', 'Cb3'])
            P.add('dve', lambda E: E.memset(nst[:], 0.0), w=['n0', 'n1', 'n2', 'n3'])
            P.add('dve', lambda E: E.memset(nbf[:], 0.0), w=['nb0', 'nb1', 'nb2', 'nb3'])
            P.add('dve', lambda E: E.memset(mbc[:], 0.0), w=['mbc'])
            P.add('dve', lambda E: E.memset(kTz[:], 0.0), w=['kTz'])
            P.add('dve', lambda E: E.memset(kTm[:], 0.0), w=['kTm'])
            P.add('dve', lambda E: E.memset(vext[:], 1.0), w=['vext'])
            P.add('dve', lambda E: E.tensor_copy(out=identb[:], in_=identf), r=['ctf'], w=['identb'])
            P.add('dve', lambda E: E.tensor_copy(out=bd64[:], in_=ctf[:, 7, :]), r=['ctf'], w=['bd64'])
            P.add('act', lambda E: E.activation(out=esink[:], in_=smc[:, 8 + 64: 8 + 64 + 32], func=AF.Exp),
                  r=['smc'], w=['esink'])

            def rmsnorm(gi, blocks, reuse=False):
                sq = scrB[:, 0:KC * NT].rearrange("p (c t) -> p c t", t=NT)
                rstd = scrF[:, 0:NT]
                for (b0, bn) in blocks:
                    bk = 'b%d' % b0
                    if reuse:
                        for c in range(KC):
                            P.add('dve', lambda E, c=c, b0=b0, bn=bn: E.scalar_tensor_tensor(
                                out=xn[:, c, b0:b0 + bn], in0=hT[:, c, b0:b0 + bn], scalar=gn[:, gi, c:c + 1], in1=rstd[:, b0:b0 + bn],
                                op0=ALU.mult, op1=ALU.mult), r=['hT' + bk, 'gn', 'rstd' + bk], w=['xn' + bk])
                        continue
                    P.add('act', lambda E, b0=b0, bn=bn: E.activation(out=sq[:, :, b0:b0 + bn], in_=hT[:, :, b0:b0 + bn], func=AF.Square),
                          r=['hT' + bk], w=['sq' + bk])
                    ps, psn = rot_psA()

                    def mmf(E, b0=b0, bn=bn, ps=ps):
                        for c in range(KC):
                            ins = E.matmul(ps[:, 0:bn], lhsT=onesD[:], rhs=sq[:, c, b0:b0 + bn], start=(c == 0), stop=(c == KC - 1))
                        return ins
                    P.add('pe', mmf, r=['onesD', 'sq' + bk], w=[psn])
                    P.add('act', lambda E, b0=b0, bn=bn, ps=ps: E.activation(out=rstd[:, b0:b0 + bn], in_=ps[:, 0:bn], func=AF.Sqrt, bias=epsb[:], scale=1.0),
                          r=[psn, 'epsb'], w=['rstd' + bk])
                    P.add('dve', lambda E, b0=b0, bn=bn: E.reciprocal(rstd[:, b0:b0 + bn], rstd[:, b0:b0 + bn]), r=['rstd' + bk], w=['rstd' + bk])
                    for c in range(KC):
                        P.add('dve', lambda E, c=c, b0=b0, bn=bn: E.scalar_tensor_tensor(
                            out=xn[:, c, b0:b0 + bn], in0=hT[:, c, b0:b0 + bn], scalar=gn[:, gi, c:c + 1], in1=rstd[:, b0:b0 + bn],
                            op0=ALU.mult, op1=ALU.mult), r=['hT' + bk, 'gn', 'rstd' + bk], w=['xn' + bk])

            def ffn(f, blocks, seg):
                cmode = 'write' if seg == 0 else 'read'
                hidA = scrB[:, 0:36 * NT].rearrange("p (c t) -> p c t", t=NT)
                hidB = scrF[:, 512:512 + 4 * NT].bitcast(BF16).rearrange("p (c t) -> p c t", t=NT)

                def hid_ap(fc, b0, bn):
                    return hidA[:, fc, b0:b0 + bn] if fc < 36 else hidB[:, fc - 36, b0:b0 + bn]
                sg = scrF[:, 0:2 * 256].bitcast(BF16).rearrange("p (a t) -> p a t", a=2)
                it = 0
                for blk in range(22):
                    wv, wr = W.get(wblk[B_FFN + 22 * f + blk], KC * 512, cache=wbf_in[22 * f + blk], mode=cmode, ckey='wbi%d' % (22 * f + blk))
                    wv = wv.rearrange("p (k n) -> p k n", n=512)
                    for j in range(2):
                        fc = 2 * blk + j
                        for (b0, bn) in blocks:
                            bk = 'b%d' % b0
                            pg, pgn = rot_psA()
                            pu, pun = rot_psA()

                            def mmf(E, wv=wv, j=j, b0=b0, bn=bn, pg=pg, pu=pu):
                                for c in range(KC):
                                    E.matmul(pg[:, 0:bn], lhsT=wv[:, c, 128 * j:128 * j + 128], rhs=xn[:, c, b0:b0 + bn], start=(c == 0), stop=(c == KC - 1))
                                for c in range(KC):
                                    ins = E.matmul(pu[:, 0:bn], lhsT=wv[:, c, 256 + 128 * j:256 + 128 * j + 128], rhs=xn[:, c, b0:b0 + bn], start=(c == 0), stop=(c == KC - 1))
                                return ins
                            P.add('pe', mmf, r=[wr, 'xn' + bk], w=[pgn, pun])
                            sgi = it % 2
                            it += 1
                            P.add('act', lambda E, pg=pg, bn=bn, sgi=sgi: E.activation(out=sg[:, sgi, 0:bn], in_=pg[:, 0:bn], func=AF.Silu),
                                  r=[pgn], w=['sg%d' % sgi])
                            P.add('dve', lambda E, pu=pu, bn=bn, sgi=sgi, fc=fc, b0=b0: E.tensor_tensor(
                                out=hid_ap(fc, b0, bn), in0=pu[:, 0:bn], in1=sg[:, sgi, 0:bn], op=ALU.mult),
                                r=[pun, 'sg%d' % sgi], w=['hid%d' % fc + bk])
                for oc in range(KC):
                    wv, wr = W.get(wout[16 * f + oc], FC * 128, cache=wbf_out[16 * f + oc], mode=cmode, ckey='wbo%d' % (16 * f + oc))
                    wv = wv.rearrange("p (k n) -> p k n", n=128)
                    for (b0, bn) in blocks:
                        bk = 'b%d' % b0
                        ps, psn = rot_psA()

                        def mmf(E, wv=wv, b0=b0, bn=bn, ps=ps):
                            for c in range(FC):
                                ins = E.matmul(ps[:, 0:bn], lhsT=wv[:, c, :], rhs=hid_ap(c, b0, bn), start=(c == 0), stop=(c == FC - 1))
                            return ins
                        P.add('pe', mmf, r=[wr] + ['hid%d' % c + bk for c in range(FC)], w=[psn])
                        P.add('dve', lambda E, oc=oc, b0=b0, bn=bn, ps=ps: E.scalar_tensor_tensor(
                            out=hT[:, oc, b0:b0 + bn], in0=ps[:, 0:bn], scalar=0.5, in1=hT[:, oc, b0:b0 + bn], op0=ALU.mult, op1=ALU.add),
                            r=[psn, 'hT' + bk], w=['hT' + bk])

            def out_proj(bbase, src, srcres, blocks):
                for jb in range(4):
                    wv, wr = W.get(wblk[bbase + jb], KC * 512)
                    wv = wv.rearrange("p (k n) -> p k n", n=512)
                    for j in range(4):
                        oc = 4 * jb + j
                        for (b0, bn) in blocks:
                            bk = 'b%d' % b0
                            ps, psn = rot_psA()

                            def mmf(E, wv=wv, j=j, b0=b0, bn=bn, ps=ps):
                                for c in range(KC):
                                    ins = E.matmul(ps[:, 0:bn], lhsT=wv[:, c, 128 * j:128 * j + 128], rhs=src[:, c, b0:b0 + bn], start=(c == 0), stop=(c == KC - 1))
                                return ins
                            P.add('pe', mmf, r=[wr, srcres + bk], w=[psn])
                            P.add('dve', lambda E, oc=oc, b0=b0, bn=bn, ps=ps: E.tensor_tensor(
                                out=hT[:, oc, b0:b0 + bn], in0=ps[:, 0:bn], in1=hT[:, oc, b0:b0 + bn], op=ALU.add),
                                r=[psn, 'hT' + bk], w=['hT' + bk])

            actT = scrB[:, 0:KC * NT].rearrange("p (c t) -> p c t", t=NT)
            o1 = KC * NT
            qTh = scrB[:, o1:o1 + 2 * NT].rearrange("p (c t) -> p c t", t=NT)
            kTh = scrB[:, o1 + 2 * NT:o1 + 4 * NT].rearrange("p (c t) -> p c t", t=NT)
            o2 = o1 + 4 * NT
            vtk = scrB[:, o2:o2 + 512]
            kwt = scrB[:, o2 + 512:o2 + 768]
            STb = scrB[:, o2 + 768:o2 + 896]
            hnb = scrB[:, o2 + 896:o2 + 1408]
            Csb = scrB[:, o2 + 1408:o2 + 2432].rearrange("p (c v) -> p c v", v=512)
            nsb = scrB[:, o2 + 2432:o2 + 2440].rearrange("p (c h) -> p c h", h=4)
            WtBig = scrB[:, o2 + 2560:o2 + 2560 + 4 * 512].rearrange("p (k h l) -> p k h l", h=4, l=128)
            WtSm = scrB[:, o2 + 2560 + 2048:o2 + 2560 + 2048 + 17 * 64].rearrange("p (k h l) -> p k h l", h=4, l=16)

            def Wt_ap(ci, L, h):
                return WtBig[0:L, ci - 17, h, 0:L] if ci >= 17 else WtSm[0:L, ci, h, 0:L]
            gpre = scrF[:, 0:8]
            t1 = scrF[:, 8:16]
            ee = scrF[:, 16:20]
            spl = scrF[:, 20:24]
            gmax = scrF[:, 24:28]
            bgg = scrF[:, 28:36]
            glb = scrF[:, 36:40]
            tmp4 = scrF[:, 40:44]
            den2 = scrF[:, 44:46]
            den = scrF[:, 46:47]
            rden = scrF[:, 47:48]
            ssq = scrF[:, 48:49]
            scl = scrF[:, 49:50]
            mS = scrF[:, 52:56]
            a_all = scrF[:, 64:64 + 84].rearrange("p (k h) -> p k h", h=4)
            g_all = scrF[:, 148:148 + 84].rearrange("p (k h) -> p k h", h=4)
            wi_all = scrF[:, 232:232 + 84].rearrange("p (k h) -> p k h", h=4)
            ws_all = scrF[:, 316:316 + 84].rearrange("p (k h) -> p k h", h=4)
            dc_all = scrF[:, 400:400 + 84].rearrange("p (k h) -> p k h", h=4)
            em_all = scrF[:, 484:484 + 84].rearrange("p (k h) -> p k h", h=4)
            diag = scrF[:, 576:576 + 512].rearrange("p (h l) -> p h l", l=128)
            tmpA = scrF[:, 1088:1088 + 512].rearrange("p (h l) -> p h l", l=128)
            numI = scrF[:, 576:576 + 512]
            numT = scrF[:, 1088:1088 + 512]
            CsF2 = scrF[:, 1600:1600 + 1024].rearrange("p (c v) -> p c v", v=512)
            Csb2 = scrB[:, 19584:19584 + 1024].rearrange("p (c v) -> p c v", v=512)
            nsb2 = scrB[:, o2 + 2440:o2 + 2448].rearrange("p (c h) -> p c h", h=4)
            kA_tok = scrB[:, 20608:20608 + 256]
            vA_tok = scrF[:, 2624:2880].bitcast(BF16)
            CsF = scrB[:, 17536:17536 + 2048].bitcast(F32).rearrange("p (c v) -> p c v", v=512)

            def mlstm_gates(ci, L, t0, mprev, mres, mout, moutres):
                cr = 'ck%d' % ci
                pb0, pb1, pb2 = psB[0], psB[1], psB[2]

                def mm_g(E):
                    for c in range(KC):
                        ins = E.matmul(pb0[0:L, 0:8], lhsT=xn[:, c, t0:t0 + L], rhs=wg[:, 8 * c:8 * c + 8], start=(c == 0), stop=(c == KC - 1))
                    return ins
                P.add('pe', mm_g, r=['xnall', 'wg'], w=['psB0'])
                P.add('dve', lambda E: E.tensor_tensor(out=gpre[0:L, :], in0=pb0[0:L, 0:8], in1=bgate_bc[0:L, :], op=ALU.add), r=['psB0', 'smc'], w=['gpre'])
                P.add('act', lambda E: E.activation(out=t1[0:L, :], in_=gpre[0:L, :], func=AF.Tanh, scale=1.0 / 15.0), r=['gpre'], w=['t1'])
                P.add('act', lambda E: E.activation(out=ee[0:L, :], in_=t1[0:L, 4:8], func=AF.Exp, scale=-15.0), r=['t1'], w=['ee'])
                P.add('act', lambda E: E.activation(out=spl[0:L, :], in_=ee[0:L, :], func=AF.Ln, bias=onesf[0:L, 0:1], scale=1.0), r=['ee', 'onesf'], w=['spl'])
                P.add('pe', lambda E: E.matmul(pb1[0:L, 0:4], lhsT=Umat[0:L, 0:L], rhs=spl[0:L, :], start=True, stop=True), r=['ctf', 'spl'], w=['psB1'])
                P.add('dve', lambda E: E.scalar_tensor_tensor(out=a_all[0:L, ci, :], in0=t1[0:L, 0:4], scalar=15.0, in1=pb1[0:L, 0:4], op0=ALU.mult, op1=ALU.add),
                      r=['t1', 'psB1'], w=['a' + cr])
                for h in range(4):
                    P.add('dve', lambda E, h=h: E.tensor_scalar(diag[0:L, h, 0:L], identf[0:L, 0:L], a_all[0:L, ci, h:h + 1], None, ALU.mult), r=['ctf', 'a' + cr], w=['diag'])
                pb2v = pb2[:, :].rearrange("p (h l) -> p h l", l=128)
                P.add('pe', lambda E: E.matmul(pb2v[0:L, :, 0:L], lhsT=onesf[0:L, 0:L], rhs=diag[0:L, :, 0:L], start=True, stop=True), r=['onesf', 'diag'], w=['psB2'])
                for h in range(4):
                    P.add('dve', lambda E, h=h: E.tensor_tensor(out=tmpA[0:L, h, 0:L], in0=pb2v[0:L, h, 0:L], in1=maskC[0:L, 0:L], op=ALU.add), r=['psB2', 'ctf'], w=['tmpA'])
                P.add('dve', lambda E: E.tensor_reduce(out=gmax[0:L, :], in_=tmpA[0:L, :, 0:L], axis=AX.X, op=ALU.max), r=['tmpA'], w=['gmax'])
                P.add('dve', lambda E: E.tensor_tensor(out=g_all[0:L, ci, :], in0=gmax[0:L, :], in1=mprev[0:L, :], op=ALU.max), r=['gmax', mres], w=['g' + cr])
                for h in range(4):
                    P.add('dve', lambda E, h=h: E.tensor_scalar(diag[0:L, h, 0:L], identf[0:L, 0:L], g_all[0:L, ci, h:h + 1], None, ALU.mult), r=['ctf', 'g' + cr], w=['diag'])
                P.add('pe', lambda E: E.matmul(pb2v[0:L, :, 0:L], lhsT=onesf[0:L, 0:L], rhs=diag[0:L, :, 0:L], start=True, stop=True), r=['onesf', 'diag'], w=['psB2'])
                for h in range(4):
                    P.add('dve', lambda E, h=h: E.scalar_tensor_tensor(out=tmpA[0:L, h, 0:L], in0=pb2v[0:L, h, 0:L], scalar=-1.0, in1=maskT[0:L, 0:L], op0=ALU.mult, op1=ALU.add),
                          r=['psB2', 'ctf'], w=['tmpA'])
                for h in range(4):
                    P.add('act', lambda E, h=h: E.activation(out=Wt_ap(ci, L, h), in_=tmpA[0:L, h, 0:L], func=AF.Exp, bias=a_all[0:L, ci, h:h + 1], scale=1.0),
                          r=['tmpA', 'a' + cr], w=['Wt' + cr])
                P.add('dve', lambda E: E.tensor_tensor(out=bgg[0:L, 0:4], in0=g_all[0:L, ci, :], in1=pb1[0:L, 0:4], op=ALU.subtract), r=['g' + cr, 'psB1'], w=['bgg'])
                P.add('dve', lambda E: E.tensor_copy(out=bgg[0:L, 4:8], in_=g_all[0:L, ci, :]), r=['g' + cr], w=['bgg'])
                P.add('pe', lambda E: E.matmul(pb0[:, 0:8], lhsT=selL[L][0:L, :], rhs=bgg[0:L, :], start=True, stop=True), r=['ctf', 'bgg'], w=['psB0'])
                P.add('dve', lambda E: E.tensor_copy(out=glb[:, :], in_=pb0[:, 4:8]), r=['psB0'], w=['glb'])
                P.add('dve', lambda E: E.tensor_tensor(out=tmp4[0:L, :], in0=mprev[0:L, :], in1=g_all[0:L, ci, :], op=ALU.subtract), r=[mres, 'g' + cr], w=['tmp4'])
                P.add('act', lambda E: E.activation(out=wi_all[0:L, ci, :], in_=tmp4[0:L, :], func=AF.Exp), r=['tmp4'], w=['wi' + cr])
                P.add('dve', lambda E: E.tensor_tensor(out=tmp4[0:L, :], in0=a_all[0:L, ci, :], in1=glb[0:L, :], op=ALU.subtract), r=['a' + cr, 'glb'], w=['tmp4'])
                P.add('act', lambda E: E.activation(out=ws_all[0:L, ci, :], in_=tmp4[0:L, :], func=AF.Exp), r=['tmp4'], w=['ws' + cr])
                P.add('dve', lambda E: E.tensor_tensor(out=tmp4[:, :], in0=mprev[:, :], in1=glb[:, :], op=ALU.subtract), r=[mres, 'glb'], w=['tmp4'])
                P.add('act', lambda E: E.activation(out=dc_all[:, ci, :], in_=tmp4[:, :], func=AF.Exp), r=['tmp4'], w=['dc' + cr])
                P.add('act', lambda E: E.activation(out=em_all[0:L, ci, :], in_=bgg[0:L, 0:4], func=AF.Exp, scale=-1.0), r=['bgg'], w=['em' + cr])
                P.add('dve', lambda E: E.tensor_copy(out=mout, in_=pb0[:, 0:4]), r=['psB0', mres], w=[moutres])

            vtk1 = scrB[:, 17536:17536 + 512]
            kwt1 = scrB[:, 18048:18048 + 256]
            vtks = [vtk, vtk1]
            kwts = [kwt, kwt1]

            def mlstm_chunk_front(h, ci, L, t0, wqk, wqkr, wv_, wvr, par):
                cr = 'ck%d' % ci
                vtk = vtks[par]
                kwt = kwts[par]
                xw = ['CS0'] if par == 1 else []
                pv, pvn = rot_psA()
                pk, pkn = rot_psA()
                if t0 < NA:
                    P.add('pe', lambda E: E.matmul(pv[0:L, 0:512], lhsT=identb[0:NA, t0:t0 + L], rhs=vA_tok[0:NA, :], start=True, stop=True), r=['vA_tok', 'identb'], w=[pvn])
                    P.add('pe', lambda E: E.matmul(pk[0:L, 0:256], lhsT=identb[0:NA, t0:t0 + L], rhs=kA_tok[0:NA, :], start=True, stop=True), r=['kA_tok', 'identb'], w=[pkn])
                else:
                    def mm_v(E):
                        for c in range(KC):
                            ins = E.matmul(pv[0:L, 0:512], lhsT=xn[:, c, t0:t0 + L], rhs=wv_[:, c, :], start=(c == 0), stop=(c == KC - 1))
                        return ins
                    P.add('pe', mm_v, r=['xnall', wvr], w=[pvn])

                    def mm_k(E):
                        for c in range(KC):
                            ins = E.matmul(pk[0:L, 0:256], lhsT=xn[:, c, t0:t0 + L], rhs=wqk[:, c, 256:512], start=(c == 0), stop=(c == KC - 1))
                        return ins
                    P.add('pe', mm_k, r=['xnall', wqkr], w=[pkn])
                P.add('act', lambda E: E.activation(out=vtk[0:L, :], in_=pv[0:L, 0:512], func=AF.Copy), r=[pvn], w=['vtk%d' % par] + xw)
                P.add('dve', lambda E: E.tensor_scalar(kwt[0:L, :], pk[0:L, 0:256], ws_all[0:L, ci, h:h + 1], 1.0 / 16.0, ALU.mult, ALU.mult), r=[pkn, 'ws' + cr], w=['kwt%d' % par] + xw)

            def mlstm_head_chunk(h, ci, L, t0, wqk, wqkr, wv_, wvr, Cf, Cb, nf, nb, cres, par=0, front_done=False, mid_hook=None):
                cr = 'ck%d' % ci
                if not front_done:
                    mlstm_chunk_front(h, ci, L, t0, wqk, wqkr, wv_, wvr, par)
                vtk = vtks[par]
                kwt = kwts[par]
                vtkr = 'vtk%d' % par
                kwtr = 'kwt%d' % par
                pb0, pb1, pb2 = psB[0], psB[1], psB[2]

                def mm_s(E):
                    for c in range(2):
                        ins = E.matmul(pb0[0:L, 0:L], lhsT=kTh[:, c, t0:t0 + L], rhs=qTh[:, c, t0:t0 + L], start=(c == 0), stop=(c == 1))
                    return ins
                P.add('pe', mm_s, r=['qkT'], w=['psB0'])
                P.add('dve', lambda E: E.tensor_tensor(out=STb[0:L, 0:L], in0=pb0[0:L, 0:L], in1=Wt_ap(ci, L, h), op=ALU.mult), r=['psB0', 'Wt' + cr], w=['STb'])
                pn, pnn = rot_psA()
                P.add('pe', lambda E: E.matmul(pn[0:L, 0:512], lhsT=STb[0:L, 0:L], rhs=vtk[0:L, :], start=True, stop=True), r=['STb', vtkr], w=[pnn])

                def mm_d(E):
                    E.matmul(pb1[0:L, 0:1], lhsT=STb[0:L, 0:L], rhs=onesb[0:L, 0:1], start=True, stop=True)
                    for c in range(2):
                        ins = E.matmul(pb1[0:L, 1:2], lhsT=qTh[:, c, t0:t0 + L], rhs=nb[:, c:c + 1], start=(c == 0), stop=(c == 1))
                    return ins
                P.add('pe', mm_d, r=['STb', 'qkT', 'nb' + cres, 'onesb'], w=['psB1'])
                pi, pin = rot_psA()

                def mm_i(E):
                    for c in range(2):
                        ins = E.matmul(pi[0:L, 0:512], lhsT=qTh[:, c, t0:t0 + L], rhs=Cb[:, c, :], start=(c == 0), stop=(c == 1))
                    return ins
                P.add('pe', mm_i, r=['qkT', 'Cb' + cres], w=[pin])
                if mid_hook is not None:
                    mid_hook()
                P.add('act', lambda E: E.activation(out=numI[0:L, :], in_=pn[0:L, 0:512], func=AF.Copy), r=[pnn], w=['diag'])
                P.add('dve', lambda E: E.scalar_tensor_tensor(out=numT[0:L, :], in0=pi[0:L, 0:512], scalar=wi_all[0:L, ci, h:h + 1], in1=numI[0:L, :], op0=ALU.mult, op1=ALU.add),
                      r=[pin, 'wi' + cr, 'diag'], w=['tmpA'])
                P.add('act', lambda E: E.activation(out=den2[0:L, :], in_=pb1[0:L, 0:2], func=AF.Copy), r=['psB1'], w=['den2'])
                P.add('dve', lambda E: E.scalar_tensor_tensor(out=den[0:L, :], in0=den2[0:L, 1:2], scalar=wi_all[0:L, ci, h:h + 1], in1=den2[0:L, 0:1], op0=ALU.mult, op1=ALU.add),
                      r=['den2', 'wi' + cr], w=['den'])
                P.add('act', lambda E: E.activation(out=den[0:L, :], in_=den[0:L, :], func=AF.Abs), r=['den'], w=['den'])
                P.add('dve', lambda E: E.tensor_scalar(den[0:L, :], den[0:L, :], em_all[0:L, ci, h:h + 1], None, ALU.max), r=['den', 'em' + cr], w=['den'])
                P.add('dve', lambda E: E.reciprocal(rden[0:L, :], den[0:L, :]), r=['den'], w=['rden'])
                P.add('act', lambda E: E.activation(out=numI[0:L, :], in_=numT[0:L, :], func=AF.Square, scale=rden[0:L, 0:1], accum_out=ssq[0:L, :]), r=['tmpA', 'rden', 'diag'], w=['diag', 'ssq'])
                P.add('act', lambda E: E.activation(out=ssq[0:L, :], in_=ssq[0:L, :], func=AF.Sqrt, bias=epsb[0:L, :], scale=1.0 / 512.0), r=['ssq', 'epsb'], w=['ssq'])
                P.add('dve', lambda E: E.reciprocal(ssq[0:L, :], ssq[0:L, :]), r=['ssq'], w=['ssq'])
                P.add('dve', lambda E: E.tensor_tensor(out=scl[0:L, :], in0=ssq[0:L, :], in1=rden[0:L, :], op=ALU.mult), r=['ssq', 'rden'], w=['scl'])
                P.add('dve', lambda E: E.tensor_scalar(hnb[0:L, :], numT[0:L, :], scl[0:L, 0:1], None, ALU.mult), r=['tmpA', 'scl'], w=['hnb'])
                psTv = psT[:, 0:512].rearrange("p (j l) -> p j l", l=128)

                def mm_t(E):
                    for j in range(4):
                        ins = E.transpose(psTv[:, j, 0:L], hnb[0:L, 128 * j:128 * j + 128], identb[0:L, 0:L])
                    return ins
                P.add('pe', mm_t, r=['hnb', 'identb'], w=['psT'])
                for j in range(4):
                    P.add('dve', lambda E, j=j: E.scalar_tensor_tensor(out=actT[:, 4 * h + j, t0:t0 + L], in0=psTv[:, j, 0:L], scalar=gn[:, 7, 4 * h + j:4 * h + j + 1],
                                                                       in1=actT[:, 4 * h + j, t0:t0 + L], op0=ALU.mult, op1=ALU.mult),
                          r=['psT', 'gn', 'actT%d' % h], w=['actT%d' % h])
                pc0, pc0n = rot_psA()
                pc1, pc1n = rot_psA()

                def mm_c(E):
                    E.matmul(pc0[:, 0:512], lhsT=kwt[0:L, 0:128], rhs=vtk[0:L, :], start=True, stop=True)
                    ins = E.matmul(pc1[:, 0:512], lhsT=kwt[0:L, 128:256], rhs=vtk[0:L, :], start=True, stop=True)
                    return ins
                P.add('pe', mm_c, r=[kwtr, vtkr], w=[pc0n, pc1n])

                def mm_n(E):
                    E.matmul(pb2[:, 0:1], lhsT=kwt[0:L, 0:128], rhs=onesb[0:L, 0:1], start=True, stop=True)
                    ins = E.matmul(pb2[:, 1:2], lhsT=kwt[0:L, 128:256], rhs=onesb[0:L, 0:1], start=True, stop=True)
                    return ins
                P.add('pe', mm_n, r=[kwtr, 'onesb'], w=['psB2'])
                P.add('dve', lambda E: E.scalar_tensor_tensor(out=Cf[:, 0, :], in0=Cf[:, 0, :], scalar=dc_all[:, ci, h:h + 1], in1=pc0[:, 0:512], op0=ALU.mult, op1=ALU.add),
                      r=['C' + cres, 'dc' + cr, pc0n, 'Cb' + cres], w=['C' + cres])
                P.add('dve', lambda E: E.scalar_tensor_tensor(out=Cf[:, 1, :], in0=Cf[:, 1, :], scalar=dc_all[:, ci, h:h + 1], in1=pc1[:, 0:512], op0=ALU.mult, op1=ALU.add),
                      r=['C' + cres, 'dc' + cr, pc1n, 'Cb' + cres], w=['C' + cres])
                P.add('dve', lambda E: E.scalar_tensor_tensor(out=nf, in0=nf, scalar=dc_all[:, ci, h:h + 1], in1=pb2[:, 0:2], op0=ALU.mult, op1=ALU.add),
                      r=['n' + cres, 'dc' + cr, 'psB2'], w=['n' + cres])
                P.add('act', lambda E: E.activation(out=Cb, in_=Cf, func=AF.Copy), r=['C' + cres], w=['Cb' + cres])
                P.add('act', lambda E: E.activation(out=nb, in_=nf, func=AF.Copy), r=['n' + cres], w=['nb' + cres])

            def mlstm(seg, blocks):
                chunks = []
                if seg == 0:
                    chunks.append((0, 16, 0, 'p', None))
                    for b in range(16):
                        chunks.append((1 + b, 4, 16 + 4 * b, 's', b))
                for i in range(4):
                    chunks.append((17 + i, 128, NA + 128 * i, 'p', None))
                if seg == 0:
                    P.add('sp', lambda E: E.dma_start(out=nSall[:, :, :, :].rearrange("p b c h -> p b (c h)"), in_=stn.rearrange("b p x -> p b x")), w=['nS0', 'nS1'], stream='ldn')
                for (ci, L, t0, kind, b) in chunks:
                    if kind == 'p':
                        mlstm_gates(ci, L, t0, mbc, 'mbc', mbc[:, :], 'mbc')
                    else:
                        P.add('sp', lambda E, b=b: E.dma_start(out=mS[:, :], in_=stm[b:b + 1, :].partition_broadcast(128)), w=['mS'], stream='ldm')
                        mlstm_gates(ci, L, t0, mS, 'mS', mS[:, :], 'mS')
                        P.add('sp', lambda E, b=b: E.dma_start(out=oms[b:b + 1, :], in_=mS[0:1, :]), r=['mS'], stream='stm')
                for h in range(4):
                    wqk, wqkr = W.get(wblk[B_AQK + h], KC * 512)
                    wqk = wqk.rearrange("p (k n) -> p k n", n=512)
                    wv_, wvr = W.get(wblk[B_AV + h], KC * 512, hold=2)
                    wv_ = wv_.rearrange("p (k n) -> p k n", n=512)
                    wo_, wor = W.get(wblk[B_AO + h], KC * 512, hold=3)
                    wo_ = wo_.rearrange("p (k n) -> p k n", n=512)
                    for (b0, bn) in blocks:
                        bk = 'b%d' % b0
                        for c in range(4):
                            ps, psn = rot_psA()

                            def mmf(E, c=c, b0=b0, bn=bn, ps=ps, wqk=wqk):
                                for kc in range(KC):
                                    ins = E.matmul(ps[:, 0:bn], lhsT=wqk[:, kc, 128 * c:128 * c + 128], rhs=xn[:, kc, b0:b0 + bn], start=(kc == 0), stop=(kc == KC - 1))
                                return ins
                            P.add('pe', mmf, r=[wqkr, 'xn' + bk, 'xnall'], w=[psn])
                            if c < 2:
                                P.add('act', lambda E, c=c, b0=b0, bn=bn, ps=ps: E.activation(out=qTh[:, c, b0:b0 + bn], in_=ps[:, 0:bn], func=AF.Copy), r=[psn], w=['qkT'])
                            else:
                                P.add('act', lambda E, c=c, b0=b0, bn=bn, ps=ps: E.activation(out=kTh[:, c - 2, b0:b0 + bn], in_=ps[:, 0:bn], func=AF.Copy, scale=1.0 / 16.0), r=[psn], w=['qkT'])
                        for j in range(4):
                            ps, psn = rot_psA()

                            def mmf(E, j=j, b0=b0, bn=bn, ps=ps, wo_=wo_):
                                for kc in range(KC):
                                    ins = E.matmul(ps[:, 0:bn], lhsT=wo_[:, kc, 128 * j:128 * j + 128], rhs=xn[:, kc, b0:b0 + bn], start=(kc == 0), stop=(kc == KC - 1))
                                return ins
                            P.add('pe', mmf, r=[wor, 'xn' + bk, 'xnall'], w=[psn])
                            P.add('act', lambda E, j=j, b0=b0, bn=bn, ps=ps, h=h: E.activation(out=actT[:, 4 * h + j, b0:b0 + bn], in_=ps[:, 0:bn], func=AF.Sigmoid), r=[psn], w=['actT%d' % h])
                    if seg == 0:
                        pv_, pvn_ = rot_psA()
                        pk_, pkn_ = rot_psA()

                        def mm_va(E, pv_=pv_, pk_=pk_, wv_=wv_, wqk=wqk):
                            for c in range(KC):
                                E.matmul(pv_[0:NA, 0:512], lhsT=xn[:, c, 0:NA], rhs=wv_[:, c, :], start=(c == 0), stop=(c == KC - 1))
                            for c in range(KC):
                                ins = E.matmul(pk_[0:NA, 0:256], lhsT=xn[:, c, 0:NA], rhs=wqk[:, c, 256:512], start=(c == 0), stop=(c == KC - 1))
                            return ins
                        P.add('pe', mm_va, r=['xnall', wvr, wqkr], w=[pvn_, pkn_])
                        P.add('act', lambda E, pv_=pv_: E.activation(out=vA_tok[0:NA, :], in_=pv_[0:NA, 0:512], func=AF.Copy), r=[pvn_], w=['vA_tok'])
                        P.add('act', lambda E, pk_=pk_: E.activation(out=kA_tok[0:NA, :], in_=pk_[0:NA, 0:256], func=AF.Copy), r=[pkn_], w=['kA_tok'])
                    CsFs = [CsF, CsF2]
                    Csbs = [Csb, Csb2]
                    nsbs = [nsb, nsb2]

                    def ld_sample(b, h=h):
                        pb = b % 2
                        P.add('sp', lambda E: E.dma_start(out=CsFs[pb][:, :, :], in_=stC[b, h].rearrange("(c p) v -> p c v", p=128)), w=['CS%d' % pb] + (['vtk1', 'kwt1'] if pb == 0 else []), stream='ldC')
                    real = [c_ for c_ in chunks if c_[2] >= NA]
                    for (ci, L, t0, kind, b) in chunks:
                        if kind == 'p' and t0 >= NA:
                            ri = ci - 17
                            if ri == 0:
                                mlstm_chunk_front(h, ci, L, t0, wqk, wqkr, wv_, wvr, 0)
                            hook = None
                            if ri + 1 < 4:
                                nci, nL, nt0, _, _ = real[ri + 1]
                                hook = (lambda nci=nci, nL=nL, nt0=nt0, npar=(ri + 1) % 2, h=h, wqk=wqk, wqkr=wqkr, wv_=wv_, wvr=wvr:
                                        mlstm_chunk_front(h, nci, nL, nt0, wqk, wqkr, wv_, wvr, npar))
                            mlstm_head_chunk(h, ci, L, t0, wqk, wqkr, wv_, wvr, Cst[:, h, :, :], Cbf[:, h, :, :], nst[:, :, h], nbf[:, :, h], '%d' % h,
                                             par=ri % 2, front_done=True, mid_hook=hook)
                        elif kind == 'p':
                            mlstm_head_chunk(h, ci, L, t0, wqk, wqkr, wv_, wvr, Cst[:, h, :, :], Cbf[:, h, :, :], nst[:, :, h], nbf[:, :, h], '%d' % h)
                        else:
                            pb = b % 2
                            if b == 0:
                                ld_sample(0)
                            if b + 1 < 16:
                                ld_sample(b + 1)
                            nfS = nSall[:, b, :, h]
                            P.add('act', lambda E, pb=pb: E.activation(out=Csbs[pb][:, :, :], in_=CsFs[pb][:, :, :], func=AF.Copy), r=['CS%d' % pb], w=['CbS%d' % pb])
                            P.add('act', lambda E, nfS=nfS, pb=pb: E.activation(out=nsbs[pb][:, :, 0], in_=nfS, func=AF.Copy), r=['nS%d' % pb], w=['nbS%d' % pb])
                            mlstm_head_chunk(h, ci, L, t0, wqk, wqkr, wv_, wvr, CsFs[pb][:, :, :], Csbs[pb][:, :, :], nfS, nsbs[pb][:, :, 0], 'S%d' % pb)
                            P.add('sp', lambda E, b=b, h=h, pb=pb: E.dma_start(out=oCs[b, h].rearrange("(c p) v -> p c v", p=128), in_=CsFs[pb][:, :, :]), r=['CS%d' % pb], stream='stC')

            def mlstm_finish_sample():
                P.add('sp', lambda E: E.dma_start(out=ons.rearrange("b p x -> p b x"), in_=nSall[:, :, :, :].rearrange("p b c h -> p b (c h)")), r=['nS0', 'nS1'], stream='stn')


            def kv_phase(seg):
                wkv, wkvr = W.get(wblk[B_KV], KC * 512)
                wkv = wkv.rearrange("p (k n) -> p k n", n=512)
                kpad = scrB[:, 0:1024].rearrange("p (x d) -> p x d", d=128)
                knf = scrF[:, 600:856]
                vf = scrF[:, 856:1112]
                ssk = scrF[:, 1112:1116]
                junk = scrF[:, 1120:1184]
                psTv = psT[:, 0:1024].rearrange("p (x l) -> p x l", l=128)
                P.add('dve', lambda E: E.memset(kpad[:, :, :], 0.0), w=['kpad'])
                tiles = []
                if seg == 0:
                    tiles.append((NA, 0, 'A'))
                for i in range(4):
                    tiles.append((128, NA + 128 * i, i))
                for (L, t0, sl) in tiles:
                    ps, psn = rot_psA()

                    def mmf(E, L=L, t0=t0, ps=ps):
                        for kc in range(KC):
                            ins = E.matmul(ps[0:L, 0:512], lhsT=xn[:, kc, t0:t0 + L], rhs=wkv[:, kc, :], start=(kc == 0), stop=(kc == KC - 1))
                        return ins
                    P.add('pe', mmf, r=[wkvr, 'xnall'], w=[psn])
                    for g in range(4):
                        P.add('act', lambda E, g=g, L=L, ps=ps: E.activation(out=junk[0:L, :], in_=ps[0:L, 64 * g:64 * g + 64], func=AF.Square, accum_out=ssk[0:L, g:g + 1]),
                              r=[psn, 'junk'], w=['junk', 'ssk'])
                    P.add('act', lambda E, L=L: E.activation(out=ssk[0:L, :], in_=ssk[0:L, :], func=AF.Sqrt, bias=epsb[0:L, :], scale=1.0 / 64.0), r=['ssk', 'epsb'], w=['ssk'])
                    P.add('dve', lambda E, L=L: E.reciprocal(ssk[0:L, :], ssk[0:L, :]), r=['ssk'], w=['ssk'])
                    dk = knA if sl == 'A' else knf
                    dv = vA if sl == 'A' else vf
                    dkr = 'knA' if sl == 'A' else 'knf'
                    dvr = 'vA' if sl == 'A' else 'vf'
                    for g in range(4):
                        P.add('dve', lambda E, g=g, L=L, ps=ps, dk=dk: E.scalar_tensor_tensor(out=dk[0:L, 64 * g:64 * g + 64], in0=ps[0:L, 64 * g:64 * g + 64], scalar=ssk[0:L, g:g + 1],
                                                                                        in1=kg_bc[0:L, :], op0=ALU.mult, op1=ALU.mult), r=[psn, 'ssk', 'smc'], w=[dkr])
                    P.add('act', lambda E, L=L, ps=ps, dv=dv: E.activation(out=dv[0:L, :], in_=ps[0:L, 256:512], func=AF.Copy), r=[psn], w=[dvr])
                    if sl == 'A':
                        P.add('sp', lambda E: E.dma_start(out=okmp[:, :], in_=knA[0:16, :]), r=['knA'], stream='stkv')
                        P.add('sp', lambda E: E.dma_start(out=ovmp[:, :], in_=vA[0:16, :]), r=['vA'], stream='stkv2')
                    if seg == NSEG - 1 and sl == 3:
                        P.add('sp', lambda E: E.dma_start(out=okwp[:, :], in_=knf[:, :]), r=['knf'], stream='stkv')
                        P.add('sp', lambda E: E.dma_start(out=ovwp[:, :], in_=vf[:, :]), r=['vf'], stream='stkv2')
                    Lk = 16 if sl == 'A' else 128
                    dk3 = dk[0:Lk, :].rearrange("p (g d) -> p g d", d=64)
                    dv3 = dv[0:Lk, :].rearrange("p (g d) -> p g d", d=64)
                    kp4 = kpad[0:Lk, :, :].rearrange("p (g e) d -> p g e d", e=2)
                    P.add('dve', lambda E, kp4=kp4, dk3=dk3: E.tensor_copy(out=kp4[:, :, 0, 0:64], in_=dk3), r=[dkr], w=['kpad'])
                    P.add('dve', lambda E, kp4=kp4, dk3=dk3: E.tensor_copy(out=kp4[:, :, 1, 64:128], in_=dk3), r=[dkr], w=['kpad'])

                    def mm_t(E, Lk=Lk):
                        for x in range(8):
                            ins = E.transpose(psTv[:, x, 0:Lk], kpad[0:Lk, x, :], identb[0:Lk, 0:Lk])
                        return ins
                    P.add('pe', mm_t, r=['kpad', 'identb'], w=['psT'])
                    if sl == 'A':
                        P.add('act', lambda E: E.activation(out=kTm[:, :, :], in_=psTv[:, :, 0:16], func=AF.Copy), r=['psT'], w=['kTm'])
                        P.add('dve', lambda E, dv3=dv3: E.tensor_copy(out=vext[0:16, 0, :, 0:64], in_=dv3), r=[dvr], w=['vext0'])
                    else:
                        P.add('act', lambda E, sl=sl: E.activation(out=kTz[:, :, (1 + sl) * 128:(2 + sl) * 128], in_=psTv[:, :, :], func=AF.Copy), r=['psT'], w=['kTz%d' % (1 + sl)])
                        P.add('dve', lambda E, sl=sl, dv3=dv3: E.tensor_copy(out=vext[:, 2 + sl, :, 0:64], in_=dv3), r=[dvr], w=['vext%d' % (2 + sl)])

            qn = scrB[:, 0:KC * NT].rearrange("p (c t) -> p c t", t=NT)
            oT = scrB[:, KC * NT:2 * KC * NT].rearrange("p (c t) -> p c t", t=NT)
            ob = 2 * KC * NT
            otok = scrB[:, ob:ob + 2048]
            Pc = scrB[:, ob + 2048:ob + 2048 + 320]
            sqq = scrF[:, 0:256].bitcast(BF16)
            rq = scrF[:, 256:768]
            tmpS = scrF[:, 768:1280]
            Pcur = scrF[:, 1280:1536].bitcast(BF16)
            Pprev = scrF[:, 1536:1792].bitcast(BF16)
            Pmeta = scrF[:, 1792:2048].bitcast(BF16)
            dn = scrF[:, 2048:2052]
            qg8 = scrF[:, 2056:2057]
            kwin_f = scrF[:, 2064:2320]
            vwin_f = scrF[:, 2320:2576]
            k2_f = scrF[:, 2576:2832]
            sb2 = ob + 2048 + 320

            def q_proj(blocks):
                P.add('dve', lambda E: E.tensor_scalar(qg8[:, :], qgt[:, :], 0.125, None, ALU.mult), r=['qgt'], w=['qg8'])
                for jb in range(4):
                    wv, wr = W.get(wblk[B_Q + jb], KC * 512)
                    wv = wv.rearrange("p (k n) -> p k n", n=512)
                    for j in range(4):
                        c = 4 * jb + j
                        for (b0, bn) in blocks:
                            bk = 'b%d' % b0
                            ps, psn = rot_psA()

                            def mmf(E, wv=wv, j=j, b0=b0, bn=bn, ps=ps):
                                for kc in range(KC):
                                    ins = E.matmul(ps[:, 0:bn], lhsT=wv[:, kc, 128 * j:128 * j + 128], rhs=xn[:, kc, b0:b0 + bn], start=(kc == 0), stop=(kc == KC - 1))
                                return ins
                            P.add('pe', mmf, r=[wr, 'xn' + bk], w=[psn])
                            P.add('act', lambda E, bn=bn, ps=ps: E.activation(out=sqq[:, 0:bn], in_=ps[:, 0:bn], func=AF.Square), r=[psn], w=['sqq'])
                            ps2, ps2n = rot_psA()
                            P.add('pe', lambda E, bn=bn, ps2=ps2: E.matmul(ps2[:, 0:bn], lhsT=bd64[:, :], rhs=sqq[:, 0:bn], start=True, stop=True), r=['bd64', 'sqq'], w=[ps2n])
                            P.add('act', lambda E, bn=bn, ps2=ps2: E.activation(out=rq[:, 0:bn], in_=ps2[:, 0:bn], func=AF.Sqrt, bias=epsb[:, :], scale=1.0), r=[ps2n, 'epsb'], w=['rq'])
                            P.add('dve', lambda E, bn=bn: E.reciprocal(rq[:, 0:bn], rq[:, 0:bn]), r=['rq'], w=['rq'])
                            P.add('dve', lambda E, c=c, b0=b0, bn=bn, ps=ps: E.scalar_tensor_tensor(out=qn[:, c, b0:b0 + bn], in0=ps[:, 0:bn], scalar=qg8[:, 0:1], in1=rq[:, 0:bn],
                                                                                              op0=ALU.mult, op1=ALU.mult), r=[psn, 'qg8', 'rq'], w=['qn' + bk])

            def attn_core(nq, qcols, keysets, res_in, out_rows_ap, meta0=False):
                nk_list = [ks[1] for ks in keysets]
                Pt = [Pcur, Pprev, Pmeta]
                for g in range(4):
                    for e in range(2):
                        kx = 2 * g + e
                        heads = [8 * g + 2 * j + e for j in range(4)]
                        for si, (kT, nk, vx, Rt, kind) in enumerate(keysets):
                            pb = psB[si]
                            pbv = pb[:, 0:4 * nq].rearrange("p (j q) -> p j q", q=nq)
                            P.add('pe', lambda E, kT=kT, nk=nk, pbv=pbv, kx=kx, g=g: E.matmul(pbv[0:nk, :, :], lhsT=kT[:, kx, 0:nk], rhs=qn[:, 4 * g:4 * g + 4, qcols[0]:qcols[1]], start=True, stop=True),
                                  r=res_in + ['qnall'], w=['psB%d' % si])
                            Pv = Pt[si][:, 0:4 * nq].rearrange("p (j q) -> p j q", q=nq)
                            tv = tmpS[:, 0:4 * nq].rearrange("p (j q) -> p j q", q=nq)
                            if kind == 'R':
                                for j in range(4):
                                    P.add('dve', lambda E, j=j, nk=nk, Rt=Rt, pbv=pbv, tv=tv, heads=heads: E.scalar_tensor_tensor(out=tv[0:nk, j, :], in0=Rt[0:nk, 0:nq], scalar=float(SLOPES[heads[j]]),
                                                                                                                in1=pbv[0:nk, j, :], op0=ALU.mult, op1=ALU.add),
                                          r=['psB%d' % si, 'ctf'], w=['tmpS'])
                                P.add('act', lambda E, nk=nk, Pv=Pv, tv=tv: E.activation(out=Pv[0:nk, :, :], in_=tv[0:nk, :, :], func=AF.Exp), r=['tmpS'], w=['P%d' % si])
                            else:
                                for j in range(4):
                                    P.add('act', lambda E, j=j, nk=nk, Pv=Pv, pbv=pbv, heads=heads: E.activation(out=Pv[0:nk, j, :], in_=pbv[0:nk, j, :], func=AF.Exp,
                                                                                             bias=nslope128[0:nk, heads[j]:heads[j] + 1], scale=1.0),
                                          r=['psB%d' % si, 'smc'], w=['P%d' % si])
                        po, pon = rot_psA()

                        def mm_pv(E, po=po, g=g):
                            for j in range(4):
                                for si, (kT, nk, vx, Rt, kind) in enumerate(keysets):
                                    Pv = Pt[si][:, 0:4 * nq].rearrange("p (j q) -> p j q", q=nq)
                                    ins = E.matmul(po[0:nq, 65 * j:65 * j + 65], lhsT=Pv[0:nk, j, :], rhs=vx[0:nk, g, :], start=(si == 0), stop=(si == len(keysets) - 1))
                            return ins
                        P.add('pe', mm_pv, r=['P%d' % si for si in range(len(keysets))] + res_in, w=[pon])
                        pov = po[:, 0:260].rearrange("p (j d) -> p j d", d=65)
                        esv = esink[:, 8 * g:8 * g + 8].rearrange("p (j e) -> p j e", e=2)
                        P.add('dve', lambda E, pov=pov, esv=esv, e=e: E.tensor_tensor(out=dn[0:nq, :], in0=pov[0:nq, :, 64], in1=esv[0:nq, :, e], op=ALU.add), r=[pon, 'esink'], w=['dn'])
                        P.add('dve', lambda E: E.reciprocal(dn[0:nq, :], dn[0:nq, :]), r=['dn'], w=['dn'])
                        for j in range(4):
                            hh = heads[j]
                            P.add('dve', lambda E, j=j, hh=hh, pov=pov: E.tensor_scalar(otok[0:nq, 64 * hh:64 * hh + 64], pov[0:nq, j, 0:64], dn[0:nq, j:j + 1], None, ALU.mult),
                                  r=[pon, 'dn'], w=['otok'])

            def attn_prompt(seg):
                psTv = psT[:, 0:1024].rearrange("p (x l) -> p x l", l=128)
                def attn_block(i):
                    t0 = NA + 128 * i
                    first = (seg == 0 and i == 0)
                    keysets = [(kTz[:, :, (1 + i) * 128:(2 + i) * 128], 128, vext[:, 2 + i, :, :], Rcur, 'R')]
                    if not first:
                        keysets.append((kTz[:, :, i * 128:(1 + i) * 128], 128, vext[:, 1 + i, :, :], Rprev, 'R'))
                        keysets.append((kTm[:, :, :], 16, vext[:, 0, :, :], None, 'C'))
                    else:
                        keysets.append((kTm[:, :, :], 16, vext[:, 0, :, :], Rmeta0, 'R'))
                    psT32 = psT[:, :].bitcast(F32)
                    sets = [([psB[0], psB[1], psB[2]], ['psB0', 'psB1', 'psB2'], psA[3], 'psA3'),
                            ([psA[0], psA[1], psA[2]], ['psA0', 'psA1', 'psA2'], psT32, 'psT')]
                    Psets = [[Pcur, Pprev, Pmeta],
                             [scrF[:, 2064:2320].bitcast(BF16), scrF[:, 2320:2576].bitcast(BF16), scrF[:, 2576:2832].bitcast(BF16)]]
                    tmps = [tmpS, scrF[:, 0:512]]
                    dns = [scrF[:, 2048:2052], scrF[:, 2052:2056]]
                    nks = len(keysets)

                    def st_S(k):
                        g, e, p = k // 2, k % 2, k % 2
                        banks, bnames, _, _ = sets[p]
                        for si, (kT, nk, vx, Rt, kind) in enumerate(keysets):
                            pbv = banks[si][:, 0:512].rearrange("p (j q) -> p j q", q=128)
                            P.add('pe', lambda E, kT=kT, nk=nk, pbv=pbv, g=g, e=e: E.matmul(pbv[0:nk, :, :], lhsT=kT[:, 2 * g + e, 0:nk], rhs=qn[:, 4 * g:4 * g + 4, t0:t0 + 128], start=True, stop=True),
                                  r=['kTzall', 'qnall'], w=[bnames[si]])

                    def st_BX(k):
                        g, e, p = k // 2, k % 2, k % 2
                        banks, bnames, _, _ = sets[p]
                        heads = [8 * g + 2 * j + e for j in range(4)]
                        tv = tmps[p][:, 0:512].rearrange("p (j q) -> p j q", q=128)
                        for si, (kT, nk, vx, Rt, kind) in enumerate(keysets):
                            pbv = banks[si][:, 0:512].rearrange("p (j q) -> p j q", q=128)
                            Pv = Psets[p][si][:, 0:512].rearrange("p (j q) -> p j q", q=128)
                            if kind == 'R':
                                for j in range(4):
                                    P.add('dve', lambda E, j=j, nk=nk, Rt=Rt, pbv=pbv, tv=tv, heads=heads: E.scalar_tensor_tensor(
                                        out=tv[0:nk, j, :], in0=Rt[0:nk, 0:128], scalar=float(SLOPES[heads[j]]), in1=pbv[0:nk, j, :], op0=ALU.mult, op1=ALU.add),
                                        r=[bnames[si], 'ctf'], w=['tmpS%d' % p])
                                P.add('act', lambda E, nk=nk, Pv=Pv, tv=tv: E.activation(out=Pv[0:nk, :, :], in_=tv[0:nk, :, :], func=AF.Exp), r=['tmpS%d' % p], w=['P%d_%d' % (p, si)])
                            else:
                                for j in range(4):
                                    P.add('act', lambda E, j=j, nk=nk, Pv=Pv, pbv=pbv, heads=heads: E.activation(out=Pv[0:nk, j, :], in_=pbv[0:nk, j, :], func=AF.Exp,
                                                                                                          bias=nslope128[0:nk, heads[j]:heads[j] + 1], scale=1.0),
                                          r=[bnames[si], 'smc'], w=['P%d_%d' % (p, si)])

                    def st_PV(k):
                        g, e, p = k // 2, k % 2, k % 2
                        _, _, po, pon = sets[p]

                        def mm_pv(E, po=po, g=g, p=p):
                            for j in range(4):
                                for si, (kT, nk, vx, Rt, kind) in enumerate(keysets):
                                    Pv = Psets[p][si][:, 0:512].rearrange("p (j q) -> p j q", q=128)
                                    ins = E.matmul(po[:, 65 * j:65 * j + 65], lhsT=Pv[0:nk, j, :], rhs=vx[0:nk, g, :], start=(si == 0), stop=(si == nks - 1))
                            return ins
                        P.add('pe', mm_pv, r=['P%d_%d' % (p, si) for si in range(nks)] + ['kTzall'], w=[pon])

                    def st_E(k):
                        g, e, p = k // 2, k % 2, k % 2
                        _, _, po, pon = sets[p]
                        dn_ = dns[p]
                        heads = [8 * g + 2 * j + e for j in range(4)]
                        pov = po[:, 0:260].rearrange("p (j d) -> p j d", d=65)
                        esv = esink[:, 8 * g:8 * g + 8].rearrange("p (j e) -> p j e", e=2)
                        P.add('dve', lambda E, pov=pov, esv=esv, e=e, dn_=dn_: E.tensor_tensor(out=dn_[:, :], in0=pov[:, :, 64], in1=esv[:, :, e], op=ALU.add), r=[pon, 'esink'], w=['dn%d' % p])
                        P.add('dve', lambda E, dn_=dn_: E.reciprocal(dn_[:, :], dn_[:, :]), r=['dn%d' % p], w=['dn%d' % p])
                        for j in range(4):
                            hh = heads[j]
                            P.add('dve', lambda E, j=j, hh=hh, pov=pov, dn_=dn_: E.tensor_scalar(otok[:, 64 * hh:64 * hh + 64], pov[:, j, 0:64], dn_[:, j:j + 1], None, ALU.mult),
                                  r=[pon, 'dn%d' % p], w=['otok%d' % k])

                    st_S(0)
                    st_BX(0)
                    for k in range(8):
                        if k + 1 < 8:
                            st_S(k + 1)
                            st_BX(k + 1)
                        st_PV(k)
                        st_E(k)
                    for half in range(2):
                        def mm_t(E, half=half):
                            for x in range(8):
                                c = 8 * half + x
                                ins = E.transpose(psTv[:, x, :], otok[:, 128 * c:128 * c + 128], identb[:, :])
                            return ins
                        P.add('pe', mm_t, r=['otok%d' % k for k in range(8)] + ['identb'], w=['psT'])
                        P.add('act', lambda E, half=half, t0=t0: E.activation(out=oT[:, 8 * half:8 * half + 8, t0:t0 + 128], in_=psTv[:, :, :], func=AF.Copy), r=['psT'], w=['oT'])
                for i in range(4):
                    attn_block(i)
                P.add('dve', lambda E: E.tensor_copy(out=kTz[:, :, 0:128], in_=kTz[:, :, 512:640]), r=['kTzall'], w=['kTzall'])
                P.add('dve', lambda E: E.tensor_copy(out=vext[:, 1, :, :], in_=vext[:, 5, :, :]), r=['kTzall'], w=['kTzall'])

            def attn_sample():
                psTv = psT[:, 0:1024].rearrange("p (x l) -> p x l", l=128)
                xf = xn[:, :, :].rearrange("p c t -> p (c t)")
                kpadS = xf[:, 0:1024].rearrange("p (x d) -> p x d", d=128)
                kTzS = xf[:, 1024:2048].rearrange("p (x d) -> p x d", d=128)
                kpad2 = xf[:, 2048:3072].rearrange("p (x d) -> p x d", d=128)
                kTz2 = xf[:, 3072:3328].rearrange("p (x d) -> p x d", d=32)
                vxS1 = xf[:, 3328:3588].rearrange("p (g d) -> p g d", d=65)
                vxS2 = xf[:, 3588:3848].rearrange("p (g d) -> p g d", d=65)
                oS_all = xf[:, 3848:5896]
                v2_f = xf[:, 5896:6408].bitcast(F32)
                P.add('dve', lambda E: E.memset(xf[:, 0:3328], 0.0), w=['kpadS', 'kpad2', 'kTzS', 'kTz2'])
                P.add('dve', lambda E: E.memset(xf[:, 3328:3848], 1.0), w=['vxS1', 'vxS2'])
                SR1 = xf[:, 6408:6664].bitcast(F32)
                SR2 = xf[:, 6664:6920].bitcast(F32)
                esP = xf[:, 6920:6984].bitcast(F32)
                dnS = xf[:, 6984:7048].bitcast(F32)
                tmp1 = tmpS[:, 0:128]
                tmp2 = tmpS[:, 128:256]
                P1 = Pcur[:, 0:128]
                P2 = Pprev[:, 0:128]
                for x8 in range(8):
                    g_, e_ = x8 // 2, x8 % 2
                    for j in range(4):
                        hh = 8 * g_ + 2 * j + e_
                        sidx = 4 * x8 + j
                        P.add('dve', lambda E, sidx=sidx, hh=hh: E.tensor_scalar(SR1[:, 4 * sidx:4 * sidx + 4], RwinS, float(SLOPES[hh]), None, ALU.mult), r=['ctf'], w=['SR1'])
                        P.add('dve', lambda E, sidx=sidx, hh=hh: E.tensor_scalar(SR2[0:20, 4 * sidx:4 * sidx + 4], R2S[0:20, :], float(SLOPES[hh]), None, ALU.mult), r=['ctf'], w=['SR2'])
                    esv = esink[:, 8 * g_:8 * g_ + 8].rearrange("p (j e) -> p j e", e=2)
                    P.add('dve', lambda E, x8=x8, esv=esv, e_=e_: E.tensor_copy(out=esP[:, 4 * x8:4 * x8 + 4], in_=esv[:, :, e_]), r=['esink'], w=['esP'])
                banks = [(psA[0], 'psA0'), (psA[1], 'psA1'), (psA[2], 'psA2'), (psA[3], 'psA3'), (psB[2], 'psB2')]
                kpS4 = kpadS[:, :, :].rearrange("p (g e) d -> p g e d", e=2)
                kp24 = kpad2[0:20, :, :].rearrange("p (g e) d -> p g e d", e=2)
                kw3 = kwin_f[:, :].rearrange("p (g d) -> p g d", d=64)
                vw3 = vwin_f[:, :].rearrange("p (g d) -> p g d", d=64)
                k23 = k2_f[0:20, :].rearrange("p (g d) -> p g d", d=64)
                v23 = v2_f[0:20, :].rearrange("p (g d) -> p g d", d=64)
                for b in range(16):
                    r0 = 16 + 4 * b
                    P.add('sp', lambda E, b=b: E.dma_start(out=kwin_f[:, :], in_=ckw[b]), w=['kwin_f'], stream='ldk0')
                    P.add('sp', lambda E, b=b: E.dma_start(out=vwin_f[:, :], in_=cvw[b]), w=['vwin_f'], stream='ldk1')
                    P.add('sp', lambda E, b=b: E.dma_start(out=k2_f[0:16, :], in_=ckm[b]), w=['k2_f'], stream='ldk2')
                    P.add('sp', lambda E, r0=r0: E.dma_start(out=k2_f[16:20, :], in_=knA[r0:r0 + 4, :]), r=['knA'], w=['k2_f'], stream='ldk2')
                    P.add('sp', lambda E, b=b: E.dma_start(out=v2_f[0:16, :], in_=cvm[b]), w=['v2_f'], stream='ldk3')
                    P.add('sp', lambda E, r0=r0: E.dma_start(out=v2_f[16:20, :], in_=vA[r0:r0 + 4, :]), r=['vA'], w=['v2_f'], stream='ldk3')
                    P.add('sp', lambda E, b=b: E.dma_start(out=okws[b, 0:124, :], in_=kwin_f[4:128, :]), r=['kwin_f'], stream='stkw')
                    P.add('sp', lambda E, b=b, r0=r0: E.dma_start(out=okws[b, 124:128, :], in_=knA[r0:r0 + 4, :]), r=['knA'], stream='stkw')
                    P.add('sp', lambda E, b=b: E.dma_start(out=ovws[b, 0:124, :], in_=vwin_f[4:128, :]), r=['vwin_f'], stream='stkw2')
                    P.add('sp', lambda E, b=b, r0=r0: E.dma_start(out=ovws[b, 124:128, :], in_=vA[r0:r0 + 4, :]), r=['vA'], stream='stkw2')
                    P.add('dve', lambda E: E.tensor_copy(out=kpS4[:, :, 0, 0:64], in_=kw3), r=['kwin_f'], w=['kpadS'])
                    P.add('dve', lambda E: E.tensor_copy(out=kpS4[:, :, 1, 64:128], in_=kw3), r=['kwin_f'], w=['kpadS'])
                    P.add('dve', lambda E: E.tensor_copy(out=kp24[:, :, 0, 0:64], in_=k23), r=['k2_f'], w=['kpad2'])
                    P.add('dve', lambda E: E.tensor_copy(out=kp24[:, :, 1, 64:128], in_=k23), r=['k2_f'], w=['kpad2'])

                    def mm_t1(E):
                        for x in range(8):
                            ins = E.transpose(psTv[:, x, :], kpadS[:, x, :], identb[:, :])
                        return ins
                    P.add('pe', mm_t1, r=['kpadS', 'identb'], w=['psT'])
                    P.add('act', lambda E: E.activation(out=kTzS[:, :, :], in_=psTv[:, :, :], func=AF.Copy), r=['psT'], w=['kTzS'])

                    def mm_t2(E):
                        for x in range(8):
                            ins = E.transpose(psTv[:, x, 0:20], kpad2[0:20, x, :], identb[0:20, 0:20])
                        return ins
                    P.add('pe', mm_t2, r=['kpad2', 'identb'], w=['psT'])
                    P.add('act', lambda E: E.activation(out=kTz2[:, :, 0:20], in_=psTv[:, :, 0:20], func=AF.Copy), r=['psT'], w=['kTz2'])
                    P.add('dve', lambda E: E.tensor_copy(out=vxS1[:, :, 0:64], in_=vw3), r=['vwin_f'], w=['vxS1'])
                    P.add('dve', lambda E: E.tensor_copy(out=vxS2[0:20, :, 0:64], in_=v23), r=['v2_f'], w=['vxS2'])
                    def mm_sc(E, r0=r0):
                        for x8 in range(8):
                            g_ = x8 // 2
                            o0 = psB[0][:, 16 * x8:16 * x8 + 16].rearrange("p (j q) -> p j q", q=4)
                            o1 = psB[1][:, 16 * x8:16 * x8 + 16].rearrange("p (j q) -> p j q", q=4)
                            E.matmul(o0[:, :, :], lhsT=kTzS[:, x8, :], rhs=qn[:, 4 * g_:4 * g_ + 4, r0:r0 + 4], start=True, stop=True)
                            ins = E.matmul(o1[0:20, :, :], lhsT=kTz2[:, x8, 0:20], rhs=qn[:, 4 * g_:4 * g_ + 4, r0:r0 + 4], start=True, stop=True)
                        return ins
                    P.add('pe', mm_sc, r=['kTzS', 'kTz2', 'qnall'], w=['psB0', 'psB1'])
                    P.add('dve', lambda E: E.tensor_tensor(out=tmp1[:, :], in0=psB[0][:, 0:128], in1=SR1[:, :], op=ALU.add), r=['psB0', 'SR1'], w=['tmp1'])
                    P.add('dve', lambda E: E.tensor_tensor(out=tmp2[0:20, :], in0=psB[1][0:20, 0:128], in1=SR2[0:20, :], op=ALU.add), r=['psB1', 'SR2'], w=['tmp2'])
                    P.add('act', lambda E: E.activation(out=P1[:, :], in_=tmp1[:, :], func=AF.Exp), r=['tmp1'], w=['P1s'])
                    P.add('act', lambda E: E.activation(out=P2[0:20, :], in_=tmp2[0:20, :], func=AF.Exp), r=['tmp2'], w=['P2s'])

                    def mm_pvs(E):
                        for hh in range(32):
                            g_, j_, e_ = hh // 8, (hh % 8) // 2, hh % 2
                            sidx = 4 * (2 * g_ + e_) + j_
                            bank = banks[hh // 7][0]
                            col = (hh % 7) * 65
                            E.matmul(bank[0:4, col:col + 65], lhsT=P1[:, 4 * sidx:4 * sidx + 4], rhs=vxS1[:, g_, :], start=True, stop=False)
                            ins = E.matmul(bank[0:4, col:col + 65], lhsT=P2[0:20, 4 * sidx:4 * sidx + 4], rhs=vxS2[0:20, g_, :], start=False, stop=True)
                        return ins
                    P.add('pe', mm_pvs, r=['P1s', 'P2s', 'vxS1', 'vxS2'], w=[bn_ for (_, bn_) in banks])
                    for k, (bank, bname) in enumerate(banks):
                        s0 = 7 * k
                        nk_ = min(7, 32 - s0)
                        bv = bank[:, 0:nk_ * 65].rearrange("p (s d) -> p s d", d=65)
                        P.add('dve', lambda E, bv=bv, s0=s0, nk_=nk_: E.tensor_tensor(out=dnS[0:4, s0:s0 + nk_], in0=bv[0:4, :, 64], in1=esink[0:4, s0:s0 + nk_], op=ALU.add),
                              r=[bname, 'esink'], w=['dnS%d' % k])
                    P.add('dve', lambda E: E.reciprocal(dnS[0:4, 0:32], dnS[0:4, 0:32]), r=['dnS%d' % k for k in range(5)], w=['dnSr'])
                    for k, (bank, bname) in enumerate(banks):
                        s0 = 7 * k
                        nk_ = min(7, 32 - s0)
                        bv = bank[:, 0:nk_ * 65].rearrange("p (s d) -> p s d", d=65)
                        otv = otok[0:4, 64 * s0:64 * (s0 + nk_)].rearrange("p (s d) -> p s d", d=64)
                        P.add('dve', lambda E, bv=bv, s0=s0, nk_=nk_, otv=otv: E.tensor_tensor(out=otv, in0=bv[0:4, :, 0:64], in1=dnS[0:4, s0:s0 + nk_, None].broadcast_to([4, nk_, 64]), op=ALU.mult),
                              r=[bname, 'dnSr'], w=['otok'])
                    P.add('sp', lambda E, b=b: E.dma_start(out=oS_all[4 * b:4 * b + 4, :], in_=otok[0:4, :]), r=['otok'], w=['oS_all'], stream='mvo')
                for half in range(2):
                    def mm_t(E, half=half):
                        for x in range(8):
                            c = 8 * half + x
                            ins = E.transpose(psTv[:, x, 0:64], oS_all[0:64, 128 * c:128 * c + 128], identb[0:64, 0:64])
                        return ins
                    P.add('pe', mm_t, r=['oS_all', 'identb'], w=['psT'])
                    P.add('act', lambda E, half=half: E.activation(out=oT[:, 8 * half:8 * half + 8, 16:NA], in_=psTv[:, :, 0:64], func=AF.Copy), r=['psT'], w=['oT'])

            gen_state['kv_phase'] = kv_phase
            gen_state['mlstm'] = mlstm
            gen_state['rmsnorm'] = rmsnorm
            gen_state['ffn'] = ffn
            gen_state['out_proj'] = out_proj

            for seg in range(NSEG):
                blocks = [(NA, TS)] if seg > 0 else [(0, NA), (NA, TS)]
                allres = ['hTb%d' % b0 for (b0, _) in blocks]
                if seg == 0:
                    P.add('sp', lambda E: E.dma_start(out=hT[:, :, 0:NA], in_=xA[:, :, :]), w=['hTb0'], stream='ldx0')
                P.add('sp', lambda E, seg=seg: E.dma_start(out=hT[:, :, NA:NT], in_=xR[seg]), w=['hTb%d' % NA], stream='ldx1')
                P.barrier()
                rmsnorm(0, blocks)
                ffn(0, blocks, seg)
                rmsnorm(4, blocks)
                P.add('dve', lambda E: E.tensor_copy(out=scrF[0:1, 0:1], in_=scrF[0:1, 0:1]), r=['xnb%d' % b0 for (b0, _) in blocks], w=['xnall'])
                P.barrier()
                mlstm(seg, blocks)
                if seg == 0:
                    mlstm_finish_sample()
                P.barrier()
                out_proj(B_AOUT, actT, 'actTall', blocks)
                P.barrier()
                rmsnorm(1, blocks)
                ffn(1, blocks, seg)
                rmsnorm(6, blocks)
                P.add('dve', lambda E: E.tensor_copy(out=scrF[0:1, 0:1], in_=scrF[0:1, 0:1]), r=['xnb%d' % b0 for (b0, _) in blocks], w=['xnall'])
                P.barrier()
                kv_phase(seg)
                P.barrier()
                rmsnorm(2, blocks, reuse=True)
                ffn(2, blocks, seg)
                rmsnorm(5, blocks)
                P.barrier()
                q_proj(blocks)
                P.add('dve', lambda E: E.memset(oT[:, :, 0:NA], 0.0), w=['oT'])
                P.add('dve', lambda E: E.tensor_copy(out=scrF[0:1, 0:1], in_=scrF[0:1, 0:1]), r=['qnb%d' % b0 for (b0, _) in blocks] + ['kTz%d' % x for x in range(1, 5)] + ['vext%d' % x for x in range(2, 6)] + ['kTm', 'vext0'], w=['qnall', 'kTzall'])
                P.barrier()
                attn_prompt(seg)
                if seg == 0:
                    P.barrier()
                    attn_sample()
                P.barrier()
                out_proj(B_BOUT, oT, 'oTall', blocks)
                P.barrier()
                rmsnorm(3, blocks)
                ffn(3, blocks, seg)
                P.barrier()
                if seg == 0:
                    P.add('sp', lambda E: E.dma_start(out=yA[:, :, :], in_=hT[:, :, 0:NA]), r=['hTb0'], stream='sty0')
                P.add('sp', lambda E, seg=seg: E.dma_start(out=yR[seg], in_=hT[:, :, NA:NT]), r=['hTb%d' % NA], stream='sty1')
                P.barrier()
            for h in range(4):
                P.add('sp', lambda E, h=h: E.dma_start(out=oCp[h].rearrange("(c p) v -> p c v", p=128), in_=Cst[:, h, :, :]), r=['C%d' % h], stream='stCp')
            P.add('sp', lambda E: E.dma_start(out=onp[:, :], in_=nst[:, :, :].rearrange("p c h -> p (c h)")), r=['n0', 'n1', 'n2', 'n3'], stream='stnp')
            P.add('sp', lambda E: E.dma_start(out=omp[:, :], in_=mbc[0:1, :]), r=['mbc'], stream='stmp')

        Pd = Prog(nc, dry=True)
        Wd = WRing(Pd, slots)
        gen(Pd, Wd)
        P = Prog(nc)
        W = WRing(P, slots, schedule=Wd.record)
        gen(P, W)
        P.emit(st, final_streams=['sty0', 'sty1', 'stCp', 'stnp', 'stmp', 'stC', 'stn', 'stm', 'stkv', 'stkv2', 'stkw', 'stkw2'])
        build_program.stats = P.stats
    return nc


def _fm(x2d):
    T = x2d.shape[0]
    return np.ascontiguousarray(x2d.T.reshape(KC, 128, T).transpose(1, 0, 2))


def _blk(Wm, cols):
    return Wm[:, cols].reshape(KC, 128, len(cols)).transpose(1, 0, 2).reshape(128, KC * len(cols))


def _const_tables():
    t = np.zeros((128, 12, 128), np.float32)
    p = np.arange(128)[:, None]
    f = np.arange(128)[None, :]
    t[:, 0] = (p == f)
    t[:, 1] = (p <= f)
    t[:, 2] = np.where(f <= p, 0.0, NEG)
    t[:, 3] = np.where(p <= f, 0.0, NEG)
    t[:, 4] = (p == 127) * np.ones((1, 128))
    t[:, 5] = (p == 15) * np.ones((1, 128))
    t[:, 6] = (p == 3) * np.ones((1, 128))
    t[:, 7] = ((p // 64) == (f // 64)) / 64.0
    t[:, 8] = np.where(f >= p, -(f - p).astype(np.float32), NEG)
    t[:, 9] = np.where(p > f, -(f + 128 - p).astype(np.float32), NEG)
    t[:, 10] = -np.minimum(16 + f - p, 128).astype(np.float32)
    i4 = np.arange(4)[None, :]
    t[:, 11, 0:4] = np.where(p > i4, -(128 + i4 - p).astype(np.float32), NEG)
    r2 = np.full((128, 4), -128.0, np.float32)
    for j in range(4):
        r2[16 + j] = np.where(j <= np.arange(4), -(np.arange(4) - j).astype(np.float32), NEG)
    t[:, 11, 4:8] = r2
    return t


def _prep_shared(inp):
    w_in = inp['w_ffn_in']
    w_out = inp['w_ffn_out']
    wblk = np.empty((NBLK, 128, KC * 512), np.float32)
    woutb = np.empty((64, 128, FC * 128), np.float32)
    for l in range(2):
        for i in range(2):
            f = 2 * l + i
            Wm = w_in[l, i]
            for b in range(22):
                cols = np.concatenate([np.arange(256 * b, 256 * b + 256), DFF + np.arange(256 * b, 256 * b + 256)])
                wblk[B_FFN + 22 * f + b] = _blk(Wm, cols)
            Wo = w_out[l, i]
            for oc in range(16):
                woutb[16 * f + oc] = Wo[:, 128 * oc:128 * oc + 128].reshape(FC, 128, 128).transpose(1, 0, 2).reshape(128, FC * 128)
    wa = inp['w_a_in'][0]
    for h in range(4):
        wblk[B_AQK + h] = _blk(wa, np.concatenate([np.arange(256 * h, 256 * h + 256), 1024 + np.arange(256 * h, 256 * h + 256)]))
        wblk[B_AV + h] = _blk(wa, 2048 + np.arange(512 * h, 512 * h + 512))
        wblk[B_AO + h] = _blk(wa, 4096 + np.arange(512 * h, 512 * h + 512))
        wblk[B_AOUT + h] = _blk(inp['w_a_out'][0], np.arange(512 * h, 512 * h + 512))
        wblk[B_Q + h] = _blk(inp['w_q'][0], np.arange(512 * h, 512 * h + 512))
        wblk[B_BOUT + h] = _blk(inp['w_b_out'][0], np.arange(512 * h, 512 * h + 512))
    wblk[B_KV] = _blk(inp['w_kv'], np.arange(512))
    wgate = np.ascontiguousarray(wa[:, 6144:6152].reshape(KC, 128, 8).transpose(1, 0, 2).reshape(128, KC * 8))
    gl = [inp['ffn_norm'][0, 0], inp['ffn_norm'][0, 1], inp['ffn_norm'][1, 0], inp['ffn_norm'][1, 1],
          inp['mix_norm'][0], inp['mix_norm'][1], inp['kv_norm'], inp['a_head_norm'][0]]
    gains = np.ascontiguousarray(np.stack([g.reshape(KC, 128).T for g in gl], axis=1)).astype(np.float32)
    nsl = np.array([-128.0 * s for s in SLOPES], np.float32)
    smallc = np.concatenate([inp['b_a_gate'][0], inp['k_norm'], inp['sinks'][0], nsl]).astype(np.float32)[None, :]
    qg = np.ascontiguousarray(np.tile(inp['q_norm'][0], 2)[:, None]).astype(np.float32)
    return dict(wblk=wblk, wout=woutb, wgate=wgate, gains=gains, smallc=smallc, qg=qg, ctab=_const_tables())


def _prep_core(inp, c):
    xs = inp['x_sample'][16 * c:16 * c + 16].reshape(64, D)
    xA = _fm(np.concatenate([inp['meta_tokens'], xs], axis=0))
    xp = inp['x_prompt'][c]
    xR = np.stack([_fm(xp[TS * s:TS * s + TS]) for s in range(NSEG)])
    stn = inp['state_n'][0, 16 * c:16 * c + 16]
    stn = np.ascontiguousarray(stn.reshape(16, 4, 2, 128).transpose(0, 3, 2, 1).reshape(16, 128, 8))
    return dict(
        xA=xA, xR=xR,
        stC=np.ascontiguousarray(inp['state_C'][0, 16 * c:16 * c + 16]),
        stn=stn,
        stm=np.ascontiguousarray(inp['state_m'][0, 16 * c:16 * c + 16]),
        ckm=np.ascontiguousarray(inp['cache_k_meta'][16 * c:16 * c + 16].reshape(16, 16, 256)),
        cvm=np.ascontiguousarray(inp['cache_v_meta'][16 * c:16 * c + 16].reshape(16, 16, 256)),
        ckw=np.ascontiguousarray(inp['cache_k_win'][16 * c:16 * c + 16].reshape(16, 128, 256)),
        cvw=np.ascontiguousarray(inp['cache_v_win'][16 * c:16 * c + 16].reshape(16, 128, 256)),
    )


def _tm(a):
    T = a.shape[2]
    return a.transpose(1, 0, 2).reshape(D, T).T


def _assemble(results):
    n = len(results)
    y_prompt = np.empty((n, 2048, D), np.float32)
    y_sample = np.empty((16 * n, 4, D), np.float32)
    c_p = np.empty((1, n, 4, 256, 512), np.float32)
    n_p = np.empty((1, n, 4, 256), np.float32)
    m_p = np.empty((1, n, 4), np.float32)
    k_meta_p = np.empty((n, 16, 4, 64), np.float32)
    v_meta_p = np.empty((n, 16, 4, 64), np.float32)
    k_win_p = np.empty((n, 128, 4, 64), np.float32)
    v_win_p = np.empty((n, 128, 4, 64), np.float32)
    c_s = np.empty((1, 16 * n, 4, 256, 512), np.float32)
    n_s = np.empty((1, 16 * n, 4, 256), np.float32)
    m_s = np.empty((1, 16 * n, 4), np.float32)
    k_win_s = np.empty((16 * n, 128, 4, 64), np.float32)
    v_win_s = np.empty((16 * n, 128, 4, 64), np.float32)
    for c, r in enumerate(results):
        for s in range(NSEG):
            y_prompt[c, TS * s:TS * s + TS] = _tm(r['yR'][s])
        y_sample[16 * c:16 * c + 16] = _tm(r['yA'])[16:].reshape(16, 4, D)
        c_p[0, c] = r['oCp']
        n_p[0, c] = r['onp'].reshape(128, 2, 4).transpose(2, 1, 0).reshape(4, 256)
        m_p[0, c] = r['omp'][0]
        k_meta_p[c] = r['okmp'].reshape(16, 4, 64)
        v_meta_p[c] = r['ovmp'].reshape(16, 4, 64)
        k_win_p[c] = r['okwp'].reshape(128, 4, 64)
        v_win_p[c] = r['ovwp'].reshape(128, 4, 64)
        c_s[0, 16 * c:16 * c + 16] = r['oCs']
        n_s[0, 16 * c:16 * c + 16] = r['ons'].reshape(16, 128, 2, 4).transpose(0, 3, 2, 1).reshape(16, 4, 256)
        m_s[0, 16 * c:16 * c + 16] = r['oms']
        k_win_s[16 * c:16 * c + 16] = r['okws'].reshape(16, 128, 4, 64)
        v_win_s[16 * c:16 * c + 16] = r['ovws'].reshape(16, 128, 4, 64)
    return (y_prompt, y_sample, c_p, n_p, m_p, k_meta_p, v_meta_p, k_win_p, v_win_p, c_s, n_s, m_s, k_win_s, v_win_s)


def kernel(**inputs):
    inp = {k: np.asarray(v) for k, v in inputs.items()}
    n = 8
    shared = _prep_shared(inp)
    in_maps = []
    for c in range(n):
        m = dict(shared)
        m.update(_prep_core(inp, c))
        in_maps.append(m)
    nc = build_program()
    res = run_bass_kernel_spmd(nc, in_maps, core_ids=list(range(n)))
    return _assemble(res.results)
```

```python
from contextlib import ExitStack
import numpy as np
import concourse.bass as bass
import concourse.mybir as mybir
from concourse.bass_utils import run_bass_kernel_spmd

F32 = mybir.dt.float32
BF16 = mybir.dt.bfloat16
AF = mybir.ActivationFunctionType
ALU = mybir.AluOpType
AX = mybir.AxisListType

D = 2048
KC = 16
DFF = 5632
FC = 44
NA = 80
TS = 512
NSEG = 4
NT = NA + TS
NSLOT = 3
EPS = 1e-6
NEG = -1e30
SLOPES = [2.0 ** (-8.0 * (h + 1) / 32.0) for h in range(32)]
B_FFN = 0
B_AQK = 88
B_AV = 92
B_AO = 96
B_AOUT = 100
B_KV = 104
B_Q = 105
B_BOUT = 109
NBLK = 113


class Prog:
    def __init__(self, nc, dry=False):
        self.nc = nc
        self.dry = dry
        self.engs = {'pe': nc.tensor, 'act': nc.scalar, 'dve': nc.vector, 'pool': nc.gpsimd, 'sp': nc.sync}
        self.ops = []
        self.lw = {}
        self.rd = {}
        self.stream_last = {}
        self.fence = {}
        self.last_eng = {}

    def add(self, eng, fn, r=(), w=(), stream=None):
        if self.dry:
            return -1
        i = len(self.ops)
        deps = set()
        for x in r:
            if x in self.lw:
                deps.add(self.lw[x])
        for x in w:
            if x in self.lw:
                d = self.lw[x]
                de, _, _, dst = self.ops[d]
                if not (stream is None and dst is None and de == eng):
                    deps.add(d)
            for key, d in self.rd.get(x, {}).items():
                if stream is None and key == eng:
                    continue
                deps.add(d)
        if stream is not None and stream in self.stream_last:
            deps.add(self.stream_last[stream])
        if eng in self.fence:
            deps.update(self.fence.pop(eng))
        self.ops.append((eng, fn, deps, stream))
        for x in r:
            self.rd.setdefault(x, {})[eng if stream is None else (eng, i)] = i
        for x in w:
            self.lw[x] = i
            self.rd[x] = {}
        if stream is not None:
            self.stream_last[stream] = i
        else:
            self.last_eng[eng] = i
        return i

    def barrier(self, engines=('pe', 'act', 'dve', 'sp')):
        if self.dry:
            return
        front = set(self.last_eng.values()) | set(self.stream_last[s] for s in self.stream_last if not s.startswith('w'))
        for e in engines:
            self.fence.setdefault(e, set()).update(front)

    def emit(self, stack, final_streams):
        EP = 3000
        SEP = 200
        ops = self.ops
        n = len(ops)
        needed = [False] * n
        for i, (e, fn, deps, st) in enumerate(ops):
            for d in deps:
                de, _, _, dst = ops[d]
                if dst is None and not (de == 'pe' and e == 'pe' and st is None):
                    needed[d] = True
        cnt = {}
        ms = [0] * n
        for i, (e, fn, deps, st) in enumerate(ops):
            if st is not None:
                key = ('s', st)
                cnt[key] = cnt.get(key, 0) + 1
                ms[i] = cnt[key]
            elif needed[i]:
                key = ('e', e)
                cnt[key] = cnt.get(key, 0) + 1
                ms[i] = cnt[key]
        sems = {}

        def sem_for(key, count):
            if key[0] == 's':
                ep, v = (count - 1) // SEP, ((count - 1) % SEP + 1) * 16
            else:
                ep, v = (count - 1) // EP, (count - 1) % EP + 1
            k2 = (key, ep)
            if k2 not in sems:
                sems[k2] = stack.enter_context(self.nc.semaphore("sem_%s_%s_%d" % (key[0], str(key[1]), ep)))
            return sems[k2], v

        waited = {e: {} for e in self.engs}
        for i, (e, fn, deps, st) in enumerate(ops):
            E = self.engs[e]
            reqs = {}
            for d in deps:
                de, _, _, dst = ops[d]
                if dst is not None:
                    key = ('s', dst)
                elif de == 'pe' and e == 'pe' and st is None:
                    continue
                else:
                    key = ('e', de)
                if ms[d] > reqs.get(key, 0):
                    reqs[key] = ms[d]
            for key, val in reqs.items():
                if waited[e].get(key, 0) >= val:
                    continue
                sm, v = sem_for(key, val)
                E.wait_ge(sm, v)
                waited[e][key] = val
            ins = fn(E)
            if st is not None:
                sm, v = sem_for(('s', st), ms[i])
                ins.then_inc(sm, 16)
            elif needed[i]:
                sm, v = sem_for(('e', e), ms[i])
                ins.then_inc(sm, 1)
        sp = self.engs['sp']
        for s in final_streams:
            if ('s', s) in cnt:
                sm, v = sem_for(('s', s), cnt[('s', s)])
                sp.wait_ge(sm, v)
        self.stats = dict(n_ops=n, n_sems=len(sems), counts={str(k): v for k, v in cnt.items()})


class WRing:
    def __init__(self, P, slots, schedule=None):
        self.P = P
        self.slots = slots
        self.schedule = schedule
        self.record = [] if schedule is None else None
        self.n_get = 0
        self.n_issued = 0

    def _issue_upto(self, j):
        while self.n_issued <= j and self.n_issued < len(self.schedule):
            k = self.n_issued
            src, nfree, cache, mode, ckey = self.schedule[k]
            s = k % NSLOT
            dst = self.slots[s][:, 0:nfree]
            if mode == 'read':
                self.P.add('pool', (lambda E, dst=dst, cache=cache: E.dma_start(out=dst, in_=cache)),
                           r=[ckey], w=['wslot%d' % s], stream='w%d' % s)
            else:
                self.P.add('pool', (lambda E, dst=dst, src=src: E.dma_start(out=dst, in_=src)),
                           w=['wslot%d' % s], stream='w%d' % s)
                if mode == 'write':
                    self.P.add('sp', (lambda E, dst=dst, cache=cache: E.dma_start(out=cache, in_=dst)),
                               r=['wslot%d' % s], w=[ckey], stream='wb%d' % s)
            self.n_issued += 1

    def get(self, src, nfree, hold=1, cache=None, mode=None, ckey=None):
        k = self.n_get
        self.n_get += 1
        if self.record is not None:
            self.record.append((src, nfree, cache, mode, ckey))
            return self.slots[k % NSLOT][:, 0:nfree], 'wslot%d' % (k % NSLOT)
        self._issue_upto(k - hold + NSLOT)
        return self.slots[k % NSLOT][:, 0:nfree], 'wslot%d' % (k % NSLOT)


def build_program(debug=False):
    nc = bass.Bass("TRN2", target_bir_lowering=False)

    def din(name, shape, dt=F32):
        return nc.dram_tensor(name, list(shape), dt, kind="ExternalInput").ap()

    def dout(name, shape, dt=F32):
        return nc.dram_tensor(name, list(shape), dt, kind="ExternalOutput").ap()

    xA = din("xA", [128, KC, NA])
    xR = din("xR", [NSEG, 128, KC, TS])
    wblk = din("wblk", [NBLK, 128, KC * 512])
    wout = din("wout", [64, 128, FC * 128])
    wgate = din("wgate", [128, KC * 8])
    gains = din("gains", [128, 8, KC])
    smallc = din("smallc", [1, 8 + 64 + 32 + 32])
    qg = din("qg", [128, 1])
    ctab = din("ctab", [128, 12, 128])
    stC = din("stC", [16, 4, 256, 512])
    stn = din("stn", [16, 128, 8])
    stm = din("stm", [16, 4])
    ckm = din("ckm", [16, 16, 256])
    cvm = din("cvm", [16, 16, 256])
    ckw = din("ckw", [16, 128, 256])
    cvw = din("cvw", [16, 128, 256])

    wbf_in = nc.dram_tensor("wbf_in", [88, 128, KC * 512], BF16, kind="Internal").ap()
    wbf_out = nc.dram_tensor("wbf_out", [64, 128, FC * 128], BF16, kind="Internal").ap()

    yA = dout("yA", [128, KC, NA])
    yR = dout("yR", [NSEG, 128, KC, TS])
    oCp = dout("oCp", [4, 256, 512])
    onp = dout("onp", [128, 8])
    omp = dout("omp", [1, 4])
    okmp = dout("okmp", [16, 256])
    ovmp = dout("ovmp", [16, 256])
    okwp = dout("okwp", [128, 256])
    ovwp = dout("ovwp", [128, 256])
    oCs = dout("oCs", [16, 4, 256, 512])
    ons = dout("ons", [16, 128, 8])
    oms = dout("oms", [16, 4])
    okws = dout("okws", [16, 128, 256])
    ovws = dout("ovws", [16, 128, 256])

    with ExitStack() as st:
        def sb(name, shape, dt):
            return st.enter_context(nc.sbuf_tensor(name, list(shape), dt))

        def pst(name, shape, dt):
            return st.enter_context(nc.psum_tensor(name, list(shape), dt))

        hT = sb("hT", [128, KC, NT], F32)
        xn = sb("xn", [128, KC, NT], BF16)
        slots = [sb("wslot%d" % i, [128, KC * 512], BF16) for i in range(NSLOT)]
        scrB = sb("scrB", [128, 21312], BF16)
        scrF = sb("scrF", [128, 2880], F32)
        nSall = sb("nSall", [128, 16, 2, 4], F32)
        Cst = sb("Cst", [128, 4, 2, 512], F32)
        Cbf = sb("Cbf", [128, 4, 2, 512], BF16)
        nst = sb("nst", [128, 2, 4], F32)
        nbf = sb("nbf", [128, 2, 4], BF16)
        mbc = sb("mbc", [128, 4], F32)
        ctf = sb("ctf", [128, 12, 128], F32)
        identb = sb("identb", [128, 128], BF16)
        onesb = sb("onesb", [128, 128], BF16)
        onesD = sb("onesD", [128, 128], BF16)
        bd64 = sb("bd64", [128, 128], BF16)
        onesf = sb("onesf", [128, 128], F32)
        epsb = sb("epsb", [128, 1], F32)
        gn = sb("gn", [128, 8, KC], F32)
        wg = sb("wg", [128, KC * 8], BF16)
        smc = sb("smc", [128, 8 + 64 + 32 + 32], F32)
        esink = sb("esink", [128, 32], F32)
        qgt = sb("qgt", [128, 1], F32)
        kTz = sb("kTz", [128, 8, 5 * 128], BF16)
        kTm = sb("kTm", [128, 8, 16], BF16)
        vext = sb("vext", [128, 6, 4, 65], BF16)
        knA = sb("knA", [128, 256], F32)
        vA = sb("vA", [128, 256], F32)

        psA = [pst("psA%d" % i, [128, 512], F32) for i in range(4)]
        psB = [pst("psB%d" % i, [128, 512], F32) for i in range(3)]
        psT = pst("psT", [128, 1024], BF16)

        identf = ctf[:, 0, :]
        Umat = ctf[:, 1, :]
        maskC = ctf[:, 2, :]
        maskT = ctf[:, 3, :]
        selL = {128: ctf[:, 4, :], 16: ctf[:, 5, :], 4: ctf[:, 6, :]}
        Rcur = ctf[:, 8, :]
        Rprev = ctf[:, 9, :]
        Rmeta0 = ctf[:, 10, :]
        RwinS = ctf[:, 11, 0:4]
        R2S = ctf[:, 11, 4:8]
        bgate_bc = smc[:, 0:8]
        kg_bc = smc[:, 8:8 + 64]
        nslope128 = smc[:, 8 + 64 + 32: 8 + 64 + 64]

        gen_state = {}

        def gen(P, W):
            cntr = [0]

            def rot_psA():
                i = cntr[0] % 4
                cntr[0] += 1
                return psA[i], 'psA%d' % i

            P.add('sp', lambda E: E.dma_start(out=ctf[:], in_=ctab[:, :, :]), w=['ctf'], stream='ld0')
            P.add('sp', lambda E: E.dma_start(out=gn[:], in_=gains[:, :, :]), w=['gn'], stream='ld1')
            P.add('sp', lambda E: E.dma_start(out=smc[:], in_=smallc.partition_broadcast(128)), w=['smc'], stream='ld2')
            P.add('sp', lambda E: E.dma_start(out=qgt[:], in_=qg[:, :]), w=['qgt'], stream='ld3')
            P.add('pool', lambda E: E.dma_start(out=wg[:], in_=wgate[:, :]), w=['wg'], stream='ldw')
            P.add('dve', lambda E: E.memset(onesb[:], 1.0), w=['onesb'])
            P.add('dve', lambda E: E.memset(onesD[:], 1.0 / D), w=['onesD'])
            P.add('dve', lambda E: E.memset(onesf[:], 1.0), w=['onesf'])
            P.add('dve', lambda E: E.memset(epsb[:], EPS), w=['epsb'])
            P.add('dve', lambda E: E.memset(Cst[:], 0.0), w=['C0', 'C1', 'C2', 'C3'])
            P.add('dve', lambda E: E.memset(Cbf[:], 0.0), w=['Cb0', 'Cb1', 'Cb2', 'Cb3'])
            P.add('dve', lambda E: E.memset(nst[:], 0.0), w=['n0', 'n1', 'n2', 'n3'])
            P.add('dve', lambda E: E.memset(nbf[:], 0.0), w=['nb0', 'nb1', 'nb2', 'nb3'])
            P.add('dve', lambda E: E.memset(mbc[:], 0.0), w=['mbc'])
            P.add('dve', lambda E: E.memset(kTz[:], 0.0), w=['kTz'])
            P.add('dve', lambda E: E.memset(kTm[:], 0.0), w=['kTm'])
            P.add('dve', lambda E: E.memset(vext[:], 1.0), w=['vext'])
            P.add('dve', lambda E: E.tensor_copy(out=identb[:], in_=identf), r=['ctf'], w=['identb'])
            P.add('dve', lambda E: E.tensor_copy(out=bd64[:], in_=ctf[:, 7, :]), r=['ctf'], w=['bd64'])
            P.add('act', lambda E: E.activation(out=esink[:], in_=smc[:, 8 + 64: 8 + 64 + 32], func=AF.Exp),
                  r=['smc'], w=['esink'])

            def rmsnorm(gi, blocks, reuse=False):
                sq = scrB[:, 0:KC * NT].rearrange("p (c t) -> p c t", t=NT)
                rstd = scrF[:, 0:NT]
                for (b0, bn) in blocks:
                    bk = 'b%d' % b0
                    if reuse:
                        for c in range(KC):
                            P.add('dve', lambda E, c=c, b0=b0, bn=bn: E.scalar_tensor_tensor(
                                out=xn[:, c, b0:b0 + bn], in0=hT[:, c, b0:b0 + bn], scalar=gn[:, gi, c:c + 1], in1=rstd[:, b0:b0 + bn],
                                op0=ALU.mult, op1=ALU.mult), r=['hT' + bk, 'gn', 'rstd' + bk], w=['xn' + bk])
                        continue
                    P.add('act', lambda E, b0=b0, bn=bn: E.activation(out=sq[:, :, b0:b0 + bn], in_=hT[:, :, b0:b0 + bn], func=AF.Square),
                          r=['hT' + bk], w=['sq' + bk])
                    ps, psn = rot_psA()

                    def mmf(E, b0=b0, bn=bn, ps=ps):
                        for c in range(KC):
                            ins = E.matmul(ps[:, 0:bn], lhsT=onesD[:], rhs=sq[:, c, b0:b0 + bn], start=(c == 0), stop=(c == KC - 1))
                        return ins
                    P.add('pe', mmf, r=['onesD', 'sq' + bk], w=[psn])
                    P.add('act', lambda E, b0=b0, bn=bn, ps=ps: E.activation(out=rstd[:, b0:b0 + bn], in_=ps[:, 0:bn], func=AF.Sqrt, bias=epsb[:], scale=1.0),
                          r=[psn, 'epsb'], w=['rstd' + bk])
                    P.add('dve', lambda E, b0=b0, bn=bn: E.reciprocal(rstd[:, b0:b0 + bn], rstd[:, b0:b0 + bn]), r=['rstd' + bk], w=['rstd' + bk])
                    for c in range(KC):
                        P.add('dve', lambda E, c=c, b0=b0, bn=bn: E.scalar_tensor_tensor(
                            out=xn[:, c, b0:b0 + bn], in0=hT[:, c, b0:b0 + bn], scalar=gn[:, gi, c:c + 1], in1=rstd[:, b0:b0 + bn],
                            op0=ALU.mult, op1=ALU.mult), r=['hT' + bk, 'gn', 'rstd' + bk], w=['xn' + bk])

            def ffn(f, blocks, seg, stream_io=False):
                cmode = 'write' if seg == 0 else 'read'
                hidA = scrB[:, 0:36 * NT].rearrange("p (c t) -> p c t", t=NT)
                hidB = scrF[:, 512:512 + 4 * NT].bitcast(BF16).rearrange("p (c t) -> p c t", t=NT)

                def hid_ap(fc, b0, bn):
                    return hidA[:, fc, b0:b0 + bn] if fc < 36 else hidB[:, fc - 36, b0:b0 + bn]
                sg = scrF[:, 0:2 * 256].bitcast(BF16).rearrange("p (a t) -> p a t", a=2)
                it = 0
                for blk in range(22):
                    wv, wr = W.get(wblk[B_FFN + 22 * f + blk], KC * 512, cache=wbf_in[22 * f + blk], mode=cmode, ckey='wbi%d' % (22 * f + blk))
                    wv = wv.rearrange("p (k n) -> p k n", n=512)
                    for j in range(2):
                        fc = 2 * blk + j
                        for (b0, bn) in blocks:
                            bk = 'b%d' % b0
                            pg, pgn = rot_psA()
                            pu, pun = rot_psA()

                            def mmf(E, wv=wv, j=j, b0=b0, bn=bn, pg=pg, pu=pu):
                                for c in range(KC):
                                    E.matmul(pg[:, 0:bn], lhsT=wv[:, c, 128 * j:128 * j + 128], rhs=xn[:, c, b0:b0 + bn], start=(c == 0), stop=(c == KC - 1))
                                for c in range(KC):
                                    ins = E.matmul(pu[:, 0:bn], lhsT=wv[:, c, 256 + 128 * j:256 + 128 * j + 128], rhs=xn[:, c, b0:b0 + bn], start=(c == 0), stop=(c == KC - 1))
                                return ins
                            P.add('pe', mmf, r=[wr, 'xn' + bk], w=[pgn, pun])
                            sgi = it % 2
                            it += 1
                            P.add('act', lambda E, pg=pg, bn=bn, sgi=sgi: E.activation(out=sg[:, sgi, 0:bn], in_=pg[:, 0:bn], func=AF.Silu),
                                  r=[pgn], w=['sg%d' % sgi])
                            P.add('dve', lambda E, pu=pu, bn=bn, sgi=sgi, fc=fc, b0=b0: E.tensor_tensor(
                                out=hid_ap(fc, b0, bn), in0=pu[:, 0:bn], in1=sg[:, sgi, 0:bn], op=ALU.mult),
                                r=[pun, 'sg%d' % sgi], w=['hid%d' % fc + bk])
                for oc in range(KC):
                    wv, wr = W.get(wout[16 * f + oc], FC * 128, cache=wbf_out[16 * f + oc], mode=cmode, ckey='wbo%d' % (16 * f + oc))
                    wv = wv.rearrange("p (k n) -> p k n", n=128)
                    for (b0, bn) in blocks:
                        bk = 'b%d' % b0
                        ps, psn = rot_psA()

                        def mmf(E, wv=wv, b0=b0, bn=bn, ps=ps):
                            for c in range(FC):
                                ins = E.matmul(ps[:, 0:bn], lhsT=wv[:, c, :], rhs=hid_ap(c, b0, bn), start=(c == 0), stop=(c == FC - 1))
                            return ins
                        P.add('pe', mmf, r=[wr] + ['hid%d' % c + bk for c in range(FC)], w=[psn])
                        P.add('dve', lambda E, oc=oc, b0=b0, bn=bn, ps=ps: E.scalar_tensor_tensor(
                            out=hT[:, oc, b0:b0 + bn], in0=ps[:, 0:bn], scalar=0.5, in1=hT[:, oc, b0:b0 + bn], op0=ALU.mult, op1=ALU.add),
                            r=[psn, 'hT' + bk], w=['hT' + bk])
                        if stream_io and b0 == NA:
                            P.add('sp', lambda E, oc=oc, seg=seg: E.dma_start(out=yR[seg, :, oc, :], in_=hT[:, oc, NA:NT]), r=['hT' + bk], stream='sty1')
                            if seg + 1 < NSEG:
                                P.add('sp', lambda E, oc=oc, seg=seg: E.dma_start(out=hT[:, oc, NA:NT], in_=xR[seg + 1, :, oc, :]), w=['hT' + bk], stream='ldx1')

            def out_proj(bbase, src, srcres, blocks):
                for jb in range(4):
                    wv, wr = W.get(wblk[bbase + jb], KC * 512)
                    wv = wv.rearrange("p (k n) -> p k n", n=512)
                    for j in range(4):
                        oc = 4 * jb + j
                        for (b0, bn) in blocks:
                            bk = 'b%d' % b0
                            ps, psn = rot_psA()

                            def mmf(E, wv=wv, j=j, b0=b0, bn=bn, ps=ps):
                                for c in range(KC):
                                    ins = E.matmul(ps[:, 0:bn], lhsT=wv[:, c, 128 * j:128 * j + 128], rhs=src[:, c, b0:b0 + bn], start=(c == 0), stop=(c == KC - 1))
                                return ins
                            P.add('pe', mmf, r=[wr, srcres + bk], w=[psn])
                            P.add('dve', lambda E, oc=oc, b0=b0, bn=bn, ps=ps: E.tensor_tensor(
                                out=hT[:, oc, b0:b0 + bn], in0=ps[:, 0:bn], in1=hT[:, oc, b0:b0 + bn], op=ALU.add),
                                r=[psn, 'hT' + bk], w=['hT' + bk])

            actT = scrB[:, 0:KC * NT].rearrange("p (c t) -> p c t", t=NT)
            o1 = KC * NT
            qTh = scrB[:, o1:o1 + 2 * NT].rearrange("p (c t) -> p c t", t=NT)
            kTh = scrB[:, o1 + 2 * NT:o1 + 4 * NT].rearrange("p (c t) -> p c t", t=NT)
            o2 = o1 + 4 * NT
            vtk = scrB[:, o2:o2 + 512]
            kwt = scrB[:, o2 + 512:o2 + 768]
            STb = scrB[:, o2 + 768:o2 + 896]
            hnb = scrB[:, o2 + 896:o2 + 1408]
            Csb = scrB[:, o2 + 1408:o2 + 2432].rearrange("p (c v) -> p c v", v=512)
            nsb = scrB[:, o2 + 2432:o2 + 2440].rearrange("p (c h) -> p c h", h=4)
            WtBig = scrB[:, o2 + 2560:o2 + 2560 + 4 * 512].rearrange("p (k h l) -> p k h l", h=4, l=128)
            WtSm = scrB[:, o2 + 2560 + 2048:o2 + 2560 + 2048 + 17 * 64].rearrange("p (k h l) -> p k h l", h=4, l=16)

            def Wt_ap(ci, L, h):
                return WtBig[0:L, ci - 17, h, 0:L] if ci >= 17 else WtSm[0:L, ci, h, 0:L]
            gpre = scrF[:, 0:8]
            t1 = scrF[:, 8:16]
            ee = scrF[:, 16:20]
            spl = scrF[:, 20:24]
            gmax = scrF[:, 24:28]
            bgg = scrF[:, 28:36]
            glb = scrF[:, 36:40]
            tmp4 = scrF[:, 40:44]
            den2 = scrF[:, 44:46]
            den = scrF[:, 46:47]
            rden = scrF[:, 47:48]
            ssq = scrF[:, 48:49]
            scl = scrF[:, 49:50]
            mS = scrF[:, 52:56]
            a_all = scrF[:, 64:64 + 84].rearrange("p (k h) -> p k h", h=4)
            g_all = scrF[:, 148:148 + 84].rearrange("p (k h) -> p k h", h=4)
            wi_all = scrF[:, 232:232 + 84].rearrange("p (k h) -> p k h", h=4)
            ws_all = scrF[:, 316:316 + 84].rearrange("p (k h) -> p k h", h=4)
            dc_all = scrF[:, 400:400 + 84].rearrange("p (k h) -> p k h", h=4)
            em_all = scrF[:, 484:484 + 84].rearrange("p (k h) -> p k h", h=4)
            diag = scrF[:, 576:576 + 512].rearrange("p (h l) -> p h l", l=128)
            tmpA = scrF[:, 1088:1088 + 512].rearrange("p (h l) -> p h l", l=128)
            numI = scrF[:, 576:576 + 512]
            numT = scrF[:, 1088:1088 + 512]
            CsF2 = scrF[:, 1600:1600 + 1024].rearrange("p (c v) -> p c v", v=512)
            Csb2 = scrB[:, 19584:19584 + 1024].rearrange("p (c v) -> p c v", v=512)
            nsb2 = scrB[:, o2 + 2440:o2 + 2448].rearrange("p (c h) -> p c h", h=4)
            kA_tok = scrB[:, 20608:20608 + 256]
            vA_tok = scrF[:, 2624:2880].bitcast(BF16)
            CsF = scrB[:, 17536:17536 + 2048].bitcast(F32).rearrange("p (c v) -> p c v", v=512)

            def mlstm_gates(ci, L, t0, mprev, mres, mout, moutres):
                cr = 'ck%d' % ci
                pb0, pb1, pb2 = psB[0], psB[1], psB[2]

                def mm_g(E):
                    for c in range(KC):
                        ins = E.matmul(pb0[0:L, 0:8], lhsT=xn[:, c, t0:t0 + L], rhs=wg[:, 8 * c:8 * c + 8], start=(c == 0), stop=(c == KC - 1))
                    return ins
                P.add('pe', mm_g, r=['xnall', 'wg'], w=['psB0'])
                P.add('dve', lambda E: E.tensor_tensor(out=gpre[0:L, :], in0=pb0[0:L, 0:8], in1=bgate_bc[0:L, :], op=ALU.add), r=['psB0', 'smc'], w=['gpre'])
                P.add('act', lambda E: E.activation(out=t1[0:L, :], in_=gpre[0:L, :], func=AF.Tanh, scale=1.0 / 15.0), r=['gpre'], w=['t1'])
                P.add('act', lambda E: E.activation(out=ee[0:L, :], in_=t1[0:L, 4:8], func=AF.Exp, scale=-15.0), r=['t1'], w=['ee'])
                P.add('act', lambda E: E.activation(out=spl[0:L, :], in_=ee[0:L, :], func=AF.Ln, bias=onesf[0:L, 0:1], scale=1.0), r=['ee', 'onesf'], w=['spl'])
                P.add('pe', lambda E: E.matmul(pb1[0:L, 0:4], lhsT=Umat[0:L, 0:L], rhs=spl[0:L, :], start=True, stop=True), r=['ctf', 'spl'], w=['psB1'])
                P.add('dve', lambda E: E.scalar_tensor_tensor(out=a_all[0:L, ci, :], in0=t1[0:L, 0:4], scalar=15.0, in1=pb1[0:L, 0:4], op0=ALU.mult, op1=ALU.add),
                      r=['t1', 'psB1'], w=['a' + cr])
                for h in range(4):
                    P.add('dve', lambda E, h=h: E.tensor_scalar(diag[0:L, h, 0:L], identf[0:L, 0:L], a_all[0:L, ci, h:h + 1], None, ALU.mult), r=['ctf', 'a' + cr], w=['diag'])
                pb2v = pb2[:, :].rearrange("p (h l) -> p h l", l=128)
                P.add('pe', lambda E: E.matmul(pb2v[0:L, :, 0:L], lhsT=onesf[0:L, 0:L], rhs=diag[0:L, :, 0:L], start=True, stop=True), r=['onesf', 'diag'], w=['psB2'])
                for h in range(4):
                    P.add('dve', lambda E, h=h: E.tensor_tensor(out=tmpA[0:L, h, 0:L], in0=pb2v[0:L, h, 0:L], in1=maskC[0:L, 0:L], op=ALU.add), r=['psB2', 'ctf'], w=['tmpA'])
                P.add('dve', lambda E: E.tensor_reduce(out=gmax[0:L, :], in_=tmpA[0:L, :, 0:L], axis=AX.X, op=ALU.max), r=['tmpA'], w=['gmax'])
                P.add('dve', lambda E: E.tensor_tensor(out=g_all[0:L, ci, :], in0=gmax[0:L, :], in1=mprev[0:L, :], op=ALU.max), r=['gmax', mres], w=['g' + cr])
                for h in range(4):
                    P.add('dve', lambda E, h=h: E.tensor_scalar(diag[0:L, h, 0:L], identf[0:L, 0:L], g_all[0:L, ci, h:h + 1], None, ALU.mult), r=['ctf', 'g' + cr], w=['diag'])
                P.add('pe', lambda E: E.matmul(pb2v[0:L, :, 0:L], lhsT=onesf[0:L, 0:L], rhs=diag[0:L, :, 0:L], start=True, stop=True), r=['onesf', 'diag'], w=['psB2'])
                for h in range(4):
                    P.add('dve', lambda E, h=h: E.scalar_tensor_tensor(out=tmpA[0:L, h, 0:L], in0=pb2v[0:L, h, 0:L], scalar=-1.0, in1=maskT[0:L, 0:L], op0=ALU.mult, op1=ALU.add),
                          r=['psB2', 'ctf'], w=['tmpA'])
                for h in range(4):
                    P.add('act', lambda E, h=h: E.activation(out=Wt_ap(ci, L, h), in_=tmpA[0:L, h, 0:L], func=AF.Exp, bias=a_all[0:L, ci, h:h + 1], scale=1.0),
                          r=['tmpA', 'a' + cr], w=['Wt' + cr])
                P.add('dve', lambda E: E.tensor_tensor(out=bgg[0:L, 0:4], in0=g_all[0:L, ci, :], in1=pb1[0:L, 0:4], op=ALU.subtract), r=['g' + cr, 'psB1'], w=['bgg'])
                P.add('dve', lambda E: E.tensor_copy(out=bgg[0:L, 4:8], in_=g_all[0:L, ci, :]), r=['g' + cr], w=['bgg'])
                P.add('pe', lambda E: E.matmul(pb0[:, 0:8], lhsT=selL[L][0:L, :], rhs=bgg[0:L, :], start=True, stop=True), r=['ctf', 'bgg'], w=['psB0'])
                P.add('dve', lambda E: E.tensor_copy(out=glb[:, :], in_=pb0[:, 4:8]), r=['psB0'], w=['glb'])
                P.add('dve', lambda E: E.tensor_tensor(out=tmp4[0:L, :], in0=mprev[0:L, :], in1=g_all[0:L, ci, :], op=ALU.subtract), r=[mres, 'g' + cr], w=['tmp4'])
                P.add('act', lambda E: E.activation(out=wi_all[0:L, ci, :], in_=tmp4[0:L, :], func=AF.Exp), r=['tmp4'], w=['wi' + cr])
                P.add('dve', lambda E: E.tensor_tensor(out=tmp4[0:L, :], in0=a_all[0:L, ci, :], in1=glb[0:L, :], op=ALU.subtract), r=['a' + cr, 'glb'], w=['tmp4'])
                P.add('act', lambda E: E.activation(out=ws_all[0:L, ci, :], in_=tmp4[0:L, :], func=AF.Exp), r=['tmp4'], w=['ws' + cr])
                P.add('dve', lambda E: E.tensor_tensor(out=tmp4[:, :], in0=mprev[:, :], in1=glb[:, :], op=ALU.subtract), r=[mres, 'glb'], w=['tmp4'])
                P.add('act', lambda E: E.activation(out=dc_all[:, ci, :], in_=tmp4[:, :], func=AF.Exp), r=['tmp4'], w=['dc' + cr])
                P.add('act', lambda E: E.activation(out=em_all[0:L, ci, :], in_=bgg[0:L, 0:4], func=AF.Exp, scale=-1.0), r=['bgg'], w=['em' + cr])
                P.add('dve', lambda E: E.tensor_copy(out=mout, in_=pb0[:, 0:4]), r=['psB0', mres], w=[moutres])

            vtk1 = scrB[:, 17536:17536 + 512]
            kwt1 = scrB[:, 18048:18048 + 256]
            vtks = [vtk, vtk1]
            kwts = [kwt, kwt1]

            def mlstm_chunk_front(h, ci, L, t0, wqk, wqkr, wv_, wvr, par):
                cr = 'ck%d' % ci
                vtk = vtks[par]
                kwt = kwts[par]
                xw = ['CS0'] if par == 1 else []
                pv, pvn = rot_psA()
                pk, pkn = rot_psA()
                if t0 < NA:
                    P.add('pe', lambda E: E.matmul(pv[0:L, 0:512], lhsT=identb[0:NA, t0:t0 + L], rhs=vA_tok[0:NA, :], start=True, stop=True), r=['vA_tok', 'identb'], w=[pvn])
                    P.add('pe', lambda E: E.matmul(pk[0:L, 0:256], lhsT=identb[0:NA, t0:t0 + L], rhs=kA_tok[0:NA, :], start=True, stop=True), r=['kA_tok', 'identb'], w=[pkn])
                else:
                    def mm_v(E):
                        for c in range(KC):
                            ins = E.matmul(pv[0:L, 0:512], lhsT=xn[:, c, t0:t0 + L], rhs=wv_[:, c, :], start=(c == 0), stop=(c == KC - 1))
                        return ins
                    P.add('pe', mm_v, r=['xnall', wvr], w=[pvn])

                    def mm_k(E):
                        for c in range(KC):
                            ins = E.matmul(pk[0:L, 0:256], lhsT=xn[:, c, t0:t0 + L], rhs=wqk[:, c, 256:512], start=(c == 0), stop=(c == KC - 1))
                        return ins
                    P.add('pe', mm_k, r=['xnall', wqkr], w=[pkn])
                P.add('act', lambda E: E.activation(out=vtk[0:L, :], in_=pv[0:L, 0:512], func=AF.Copy), r=[pvn], w=['vtk%d' % par] + xw)
                P.add('dve', lambda E: E.tensor_scalar(kwt[0:L, :], pk[0:L, 0:256], ws_all[0:L, ci, h:h + 1], 1.0 / 16.0, ALU.mult, ALU.mult), r=[pkn, 'ws' + cr], w=['kwt%d' % par] + xw)

            def mlstm_head_chunk(h, ci, L, t0, wqk, wqkr, wv_, wvr, Cf, Cb, nf, nb, cres, par=0, front_done=False, mid_hook=None,
                                 cast_back=True, hook_cast=None, hook_front=None):
                cr = 'ck%d' % ci
                if not front_done:
                    mlstm_chunk_front(h, ci, L, t0, wqk, wqkr, wv_, wvr, par)
                vtk = vtks[par]
                kwt = kwts[par]
                vtkr = 'vtk%d' % par
                kwtr = 'kwt%d' % par
                pb0, pb1, pb2 = psB[0], psB[1], psB[2]

                def mm_s(E):
                    for c in range(2):
                        ins = E.matmul(pb0[0:L, 0:L], lhsT=kTh[:, c, t0:t0 + L], rhs=qTh[:, c, t0:t0 + L], start=(c == 0), stop=(c == 1))
                    return ins
                P.add('pe', mm_s, r=['qkT'], w=['psB0'])
                pc0, pc0n = rot_psA()
                pc1, pc1n = rot_psA()

                def mm_c(E):
                    E.matmul(pc0[:, 0:512], lhsT=kwt[0:L, 0:128], rhs=vtk[0:L, :], start=True, stop=True)
                    ins = E.matmul(pc1[:, 0:512], lhsT=kwt[0:L, 128:256], rhs=vtk[0:L, :], start=True, stop=True)
                    return ins
                P.add('pe', mm_c, r=[kwtr, vtkr], w=[pc0n, pc1n])

                def mm_n(E):
                    E.matmul(pb2[:, 0:1], lhsT=kwt[0:L, 0:128], rhs=onesb[0:L, 0:1], start=True, stop=True)
                    ins = E.matmul(pb2[:, 1:2], lhsT=kwt[0:L, 128:256], rhs=onesb[0:L, 0:1], start=True, stop=True)
                    return ins
                P.add('pe', mm_n, r=[kwtr, 'onesb'], w=['psB2'])
                P.add('dve', lambda E: E.tensor_tensor(out=STb[0:L, 0:L], in0=pb0[0:L, 0:L], in1=Wt_ap(ci, L, h), op=ALU.mult), r=['psB0', 'Wt' + cr], w=['STb'])
                pn, pnn = rot_psA()
                P.add('pe', lambda E: E.matmul(pn[0:L, 0:512], lhsT=STb[0:L, 0:L], rhs=vtk[0:L, :], start=True, stop=True), r=['STb', vtkr], w=[pnn])

                def mm_d(E):
                    E.matmul(pb1[0:L, 0:1], lhsT=STb[0:L, 0:L], rhs=onesb[0:L, 0:1], start=True, stop=True)
                    for c in range(2):
                        ins = E.matmul(pb1[0:L, 1:2], lhsT=qTh[:, c, t0:t0 + L], rhs=nb[:, c:c + 1], start=(c == 0), stop=(c == 1))
                    return ins
                P.add('pe', mm_d, r=['STb', 'qkT', 'nb' + cres, 'onesb'], w=['psB1'])
                pi, pin = rot_psA()

                def mm_i(E):
                    for c in range(2):
                        ins = E.matmul(pi[0:L, 0:512], lhsT=qTh[:, c, t0:t0 + L], rhs=Cb[:, c, :], start=(c == 0), stop=(c == 1))
                    return ins
                P.add('pe', mm_i, r=['qkT', 'Cb' + cres], w=[pin])
                P.add('dve', lambda E: E.scalar_tensor_tensor(out=Cf[:, 0, :], in0=Cf[:, 0, :], scalar=dc_all[:, ci, h:h + 1], in1=pc0[:, 0:512], op0=ALU.mult, op1=ALU.add),
                      r=['C' + cres, 'dc' + cr, pc0n, 'Cb' + cres], w=['C' + cres])
                P.add('dve', lambda E: E.scalar_tensor_tensor(out=Cf[:, 1, :], in0=Cf[:, 1, :], scalar=dc_all[:, ci, h:h + 1], in1=pc1[:, 0:512], op0=ALU.mult, op1=ALU.add),
                      r=['C' + cres, 'dc' + cr, pc1n, 'Cb' + cres], w=['C' + cres])
                P.add('dve', lambda E: E.scalar_tensor_tensor(out=nf, in0=nf, scalar=dc_all[:, ci, h:h + 1], in1=pb2[:, 0:2], op0=ALU.mult, op1=ALU.add),
                      r=['n' + cres, 'dc' + cr, 'psB2'], w=['n' + cres])
                if mid_hook is not None:
                    mid_hook()
                P.add('act', lambda E: E.activation(out=numI[0:L, :], in_=pn[0:L, 0:512], func=AF.Copy), r=[pnn], w=['diag'])
                P.add('dve', lambda E: E.scalar_tensor_tensor(out=numT[0:L, :], in0=pi[0:L, 0:512], scalar=wi_all[0:L, ci, h:h + 1], in1=numI[0:L, :], op0=ALU.mult, op1=ALU.add),
                      r=[pin, 'wi' + cr, 'diag'], w=['tmpA'])
                P.add('act', lambda E: E.activation(out=den2[0:L, :], in_=pb1[0:L, 0:2], func=AF.Copy), r=['psB1'], w=['den2'])
                if hook_cast is not None:
                    hook_cast()
                P.add('dve', lambda E: E.scalar_tensor_tensor(out=den[0:L, :], in0=den2[0:L, 1:2], scalar=wi_all[0:L, ci, h:h + 1], in1=den2[0:L, 0:1], op0=ALU.mult, op1=ALU.add),
                      r=['den2', 'wi' + cr], w=['den'])
                P.add('dve', lambda E: E.scalar_tensor_tensor(out=den2[0:L, 0:1], in0=den[0:L, :], scalar=-1.0, in1=den[0:L, :], op0=ALU.mult, op1=ALU.max), r=['den'], w=['den2'])
                P.add('dve', lambda E: E.tensor_scalar(den[0:L, :], den2[0:L, 0:1], em_all[0:L, ci, h:h + 1], None, ALU.max), r=['den2', 'em' + cr], w=['den'])
                P.add('dve', lambda E: E.reciprocal(rden[0:L, :], den[0:L, :]), r=['den'], w=['rden'])
                P.add('act', lambda E: E.activation(out=numI[0:L, :], in_=numT[0:L, :], func=AF.Square, scale=rden[0:L, 0:1], accum_out=ssq[0:L, :]), r=['tmpA', 'rden', 'diag'], w=['diag', 'ssq'])
                P.add('act', lambda E: E.activation(out=ssq[0:L, :], in_=ssq[0:L, :], func=AF.Sqrt, bias=epsb[0:L, :], scale=1.0 / 512.0), r=['ssq', 'epsb'], w=['ssq'])
                if cast_back:
                    P.add('act', lambda E: E.activation(out=Cb, in_=Cf, func=AF.Copy), r=['C' + cres], w=['Cb' + cres])
                    P.add('act', lambda E: E.activation(out=nb, in_=nf, func=AF.Copy), r=['n' + cres], w=['nb' + cres])
                if hook_front is not None:
                    hook_front()
                P.add('dve', lambda E: E.reciprocal(ssq[0:L, :], ssq[0:L, :]), r=['ssq'], w=['ssq'])
                P.add('dve', lambda E: E.tensor_tensor(out=scl[0:L, :], in0=ssq[0:L, :], in1=rden[0:L, :], op=ALU.mult), r=['ssq', 'rden'], w=['scl'])
                P.add('dve', lambda E: E.tensor_scalar(hnb[0:L, :], numT[0:L, :], scl[0:L, 0:1], None, ALU.mult), r=['tmpA', 'scl'], w=['hnb'])
                psTv = psT[:, 0:512].rearrange("p (j l) -> p j l", l=128)

                def mm_t(E):
                    for j in range(4):
                        ins = E.transpose(psTv[:, j, 0:L], hnb[0:L, 128 * j:128 * j + 128], identb[0:L, 0:L])
                    return ins
                P.add('pe', mm_t, r=['hnb', 'identb'], w=['psT'])
                P.add('dve', lambda E: E.tensor_tensor(out=actT[:, 4 * h:4 * h + 4, t0:t0 + L], in0=psTv[:, :, 0:L], in1=actT[:, 4 * h:4 * h + 4, t0:t0 + L], op=ALU.mult),
                      r=['psT', 'actT%d' % h], w=['actT%d' % h])

            def mlstm(seg, blocks):
                chunks = []
                if seg == 0:
                    chunks.append((0, 16, 0, 'p', None))
                    for b in range(16):
                        chunks.append((1 + b, 4, 16 + 4 * b, 's', b))
                for i in range(4):
                    chunks.append((17 + i, 128, NA + 128 * i, 'p', None))
                if seg == 0:
                    P.add('sp', lambda E: E.dma_start(out=nSall[:, :, :, :].rearrange("p b c h -> p b (c h)"), in_=stn.rearrange("b p x -> p b x")), w=['nS0', 'nS1'], stream='ldn')
                for (ci, L, t0, kind, b) in chunks:
                    if kind == 'p':
                        mlstm_gates(ci, L, t0, mbc, 'mbc', mbc[:, :], 'mbc')
                    else:
                        P.add('sp', lambda E, b=b: E.dma_start(out=mS[:, :], in_=stm[b:b + 1, :].partition_broadcast(128)), w=['mS'], stream='ldm')
                        mlstm_gates(ci, L, t0, mS, 'mS', mS[:, :], 'mS')
                        P.add('sp', lambda E, b=b: E.dma_start(out=oms[b:b + 1, :], in_=mS[0:1, :]), r=['mS'], stream='stm')
                for h in range(4):
                    wqk, wqkr = W.get(wblk[B_AQK + h], KC * 512)
                    wqk = wqk.rearrange("p (k n) -> p k n", n=512)
                    wv_, wvr = W.get(wblk[B_AV + h], KC * 512, hold=2)
                    wv_ = wv_.rearrange("p (k n) -> p k n", n=512)
                    wo_, wor = W.get(wblk[B_AO + h], KC * 512, hold=3)
                    wo_ = wo_.rearrange("p (k n) -> p k n", n=512)
                    for (b0, bn) in blocks:
                        bk = 'b%d' % b0
                        for c in range(4):
                            ps, psn = rot_psA()

                            def mmf(E, c=c, b0=b0, bn=bn, ps=ps, wqk=wqk):
                                for kc in range(KC):
                                    ins = E.matmul(ps[:, 0:bn], lhsT=wqk[:, kc, 128 * c:128 * c + 128], rhs=xn[:, kc, b0:b0 + bn], start=(kc == 0), stop=(kc == KC - 1))
                                return ins
                            P.add('pe', mmf, r=[wqkr, 'xn' + bk, 'xnall'], w=[psn])
                            if c < 2:
                                P.add('act', lambda E, c=c, b0=b0, bn=bn, ps=ps: E.activation(out=qTh[:, c, b0:b0 + bn], in_=ps[:, 0:bn], func=AF.Copy), r=[psn], w=['qkT'])
                            else:
                                P.add('act', lambda E, c=c, b0=b0, bn=bn, ps=ps: E.activation(out=kTh[:, c - 2, b0:b0 + bn], in_=ps[:, 0:bn], func=AF.Copy, scale=1.0 / 16.0), r=[psn], w=['qkT'])
                        for j in range(4):
                            ps, psn = rot_psA()

                            def mmf(E, j=j, b0=b0, bn=bn, ps=ps, wo_=wo_):
                                for kc in range(KC):
                                    ins = E.matmul(ps[:, 0:bn], lhsT=wo_[:, kc, 128 * j:128 * j + 128], rhs=xn[:, kc, b0:b0 + bn], start=(kc == 0), stop=(kc == KC - 1))
                                return ins
                            P.add('pe', mmf, r=[wor, 'xn' + bk, 'xnall'], w=[psn])
                            P.add('act', lambda E, j=j, b0=b0, bn=bn, ps=ps, h=h: E.activation(out=actT[:, 4 * h + j, b0:b0 + bn], in_=ps[:, 0:bn], func=AF.Sigmoid), r=[psn], w=['actT%d' % h])
                            P.add('dve', lambda E, j=j, b0=b0, bn=bn, h=h: E.tensor_scalar(actT[:, 4 * h + j, b0:b0 + bn], actT[:, 4 * h + j, b0:b0 + bn], gn[:, 7, 4 * h + j:4 * h + j + 1], None, ALU.mult),
                                  r=['actT%d' % h, 'gn'], w=['actT%d' % h])
                    if seg == 0:
                        pv_, pvn_ = rot_psA()
                        pk_, pkn_ = rot_psA()

                        def mm_va(E, pv_=pv_, pk_=pk_, wv_=wv_, wqk=wqk):
                            for c in range(KC):
                                E.matmul(pv_[0:NA, 0:512], lhsT=xn[:, c, 0:NA], rhs=wv_[:, c, :], start=(c == 0), stop=(c == KC - 1))
                            for c in range(KC):
                                ins = E.matmul(pk_[0:NA, 0:256], lhsT=xn[:, c, 0:NA], rhs=wqk[:, c, 256:512], start=(c == 0), stop=(c == KC - 1))
                            return ins
                        P.add('pe', mm_va, r=['xnall', wvr, wqkr], w=[pvn_, pkn_])
                        P.add('act', lambda E, pv_=pv_: E.activation(out=vA_tok[0:NA, :], in_=pv_[0:NA, 0:512], func=AF.Copy), r=[pvn_], w=['vA_tok'])
                        P.add('act', lambda E, pk_=pk_: E.activation(out=kA_tok[0:NA, :], in_=pk_[0:NA, 0:256], func=AF.Copy), r=[pkn_], w=['kA_tok'])
                    CsFs = [CsF, CsF2]
                    Csbs = [Csb, Csb2]
                    nsbs = [nsb, nsb2]

                    def ld_sample(b, h=h):
                        pb = b % 2
                        P.add('sp', lambda E: E.dma_start(out=CsFs[pb][:, :, :], in_=stC[b, h].rearrange("(c p) v -> p c v", p=128)), w=['CS%d' % pb] + (['vtk1', 'kwt1'] if pb == 0 else []), stream='ldC')
                    real = [c_ for c_ in chunks if c_[2] >= NA]
                    for (ci, L, t0, kind, b) in chunks:
                        if kind == 'p' and t0 >= NA:
                            ri = ci - 17
                            if ri == 0:
                                mlstm_chunk_front(h, ci, L, t0, wqk, wqkr, wv_, wvr, 0)
                            hook = None
                            if ri + 1 < 4:
                                nci, nL, nt0, _, _ = real[ri + 1]
                                hook = (lambda nci=nci, nL=nL, nt0=nt0, npar=(ri + 1) % 2, h=h, wqk=wqk, wqkr=wqkr, wv_=wv_, wvr=wvr:
                                        mlstm_chunk_front(h, nci, nL, nt0, wqk, wqkr, wv_, wvr, npar))
                            mlstm_head_chunk(h, ci, L, t0, wqk, wqkr, wv_, wvr, Cst[:, h, :, :], Cbf[:, h, :, :], nst[:, :, h], nbf[:, :, h], '%d' % h,
                                             par=ri % 2, front_done=True, mid_hook=hook)
                        elif kind == 'p':
                            mlstm_head_chunk(h, ci, L, t0, wqk, wqkr, wv_, wvr, Cst[:, h, :, :], Cbf[:, h, :, :], nst[:, :, h], nbf[:, :, h], '%d' % h)
                        else:
                            pb = b % 2

                            def cast_sample(bb, h=h):
                                pq = bb % 2
                                nq_ = nSall[:, bb, :, h]
                                P.add('act', lambda E: E.activation(out=Csbs[pq][:, :, :], in_=CsFs[pq][:, :, :], func=AF.Copy), r=['CS%d' % pq], w=['CbS%d' % pq])
                                P.add('act', lambda E: E.activation(out=nsbs[pq][:, :, 0], in_=nq_, func=AF.Copy), r=['nS%d' % pq], w=['nbS%d' % pq])
                            if b == 0:
                                ld_sample(0)
                                cast_sample(0)
                                mlstm_chunk_front(h, ci, L, t0, wqk, wqkr, wv_, wvr, 0)
                            hc = hf = None
                            if b + 1 < 16:
                                ld_sample(b + 1)
                                hc = (lambda b=b: cast_sample(b + 1))
                                hf = (lambda ci=ci, t0=t0, h=h, wqk=wqk, wqkr=wqkr, wv_=wv_, wvr=wvr: mlstm_chunk_front(h, ci + 1, 4, t0 + 4, wqk, wqkr, wv_, wvr, 0))
                            nfS = nSall[:, b, :, h]
                            mlstm_head_chunk(h, ci, L, t0, wqk, wqkr, wv_, wvr, CsFs[pb][:, :, :], Csbs[pb][:, :, :], nfS, nsbs[pb][:, :, 0], 'S%d' % pb,
                                             front_done=True, cast_back=False, hook_cast=hc, hook_front=hf)
                            P.add('sp', lambda E, b=b, h=h, pb=pb: E.dma_start(out=oCs[b, h].rearrange("(c p) v -> p c v", p=128), in_=CsFs[pb][:, :, :]), r=['CS%d' % pb], stream='stC')

            def mlstm_finish_sample():
                P.add('sp', lambda E: E.dma_start(out=ons.rearrange("b p x -> p b x"), in_=nSall[:, :, :, :].rearrange("p b c h -> p b (c h)")), r=['nS0', 'nS1'], stream='stn')


            def kv_phase(seg):
                wkv, wkvr = W.get(wblk[B_KV], KC * 512)
                wkv = wkv.rearrange("p (k n) -> p k n", n=512)
                kpad = scrB[:, 0:1024].rearrange("p (x d) -> p x d", d=128)
                knf = scrF[:, 600:856]
                vf = scrF[:, 856:1112]
                ssk = scrF[:, 1112:1116]
                junk = scrF[:, 1120:1184]
                psTv = psT[:, 0:1024].rearrange("p (x l) -> p x l", l=128)
                P.add('dve', lambda E: E.memset(kpad[:, :, :], 0.0), w=['kpad'])
                tiles = []
                if seg == 0:
                    tiles.append((NA, 0, 'A'))
                for i in range(4):
                    tiles.append((128, NA + 128 * i, i))
                for (L, t0, sl) in tiles:
                    ps, psn = rot_psA()

                    def mmf(E, L=L, t0=t0, ps=ps):
                        for kc in range(KC):
                            ins = E.matmul(ps[0:L, 0:512], lhsT=xn[:, kc, t0:t0 + L], rhs=wkv[:, kc, :], start=(kc == 0), stop=(kc == KC - 1))
                        return ins
                    P.add('pe', mmf, r=[wkvr, 'xnall'], w=[psn])
                    for g in range(4):
                        P.add('act', lambda E, g=g, L=L, ps=ps: E.activation(out=junk[0:L, :], in_=ps[0:L, 64 * g:64 * g + 64], func=AF.Square, accum_out=ssk[0:L, g:g + 1]),
                              r=[psn, 'junk'], w=['junk', 'ssk'])
                    P.add('act', lambda E, L=L: E.activation(out=ssk[0:L, :], in_=ssk[0:L, :], func=AF.Sqrt, bias=epsb[0:L, :], scale=1.0 / 64.0), r=['ssk', 'epsb'], w=['ssk'])
                    P.add('dve', lambda E, L=L: E.reciprocal(ssk[0:L, :], ssk[0:L, :]), r=['ssk'], w=['ssk'])
                    dk = knA if sl == 'A' else knf
                    dv = vA if sl == 'A' else vf
                    dkr = 'knA' if sl == 'A' else 'knf'
                    dvr = 'vA' if sl == 'A' else 'vf'
                    for g in range(4):
                        P.add('dve', lambda E, g=g, L=L, ps=ps, dk=dk: E.scalar_tensor_tensor(out=dk[0:L, 64 * g:64 * g + 64], in0=ps[0:L, 64 * g:64 * g + 64], scalar=ssk[0:L, g:g + 1],
                                                                                        in1=kg_bc[0:L, :], op0=ALU.mult, op1=ALU.mult), r=[psn, 'ssk', 'smc'], w=[dkr])
                    P.add('act', lambda E, L=L, ps=ps, dv=dv: E.activation(out=dv[0:L, :], in_=ps[0:L, 256:512], func=AF.Copy), r=[psn], w=[dvr])
                    if sl == 'A':
                        P.add('sp', lambda E: E.dma_start(out=okmp[:, :], in_=knA[0:16, :]), r=['knA'], stream='stkv')
                        P.add('sp', lambda E: E.dma_start(out=ovmp[:, :], in_=vA[0:16, :]), r=['vA'], stream='stkv2')
                    if seg == NSEG - 1 and sl == 3:
                        P.add('sp', lambda E: E.dma_start(out=okwp[:, :], in_=knf[:, :]), r=['knf'], stream='stkv')
                        P.add('sp', lambda E: E.dma_start(out=ovwp[:, :], in_=vf[:, :]), r=['vf'], stream='stkv2')
                    Lk = 16 if sl == 'A' else 128
                    dk3 = dk[0:Lk, :].rearrange("p (g d) -> p g d", d=64)
                    dv3 = dv[0:Lk, :].rearrange("p (g d) -> p g d", d=64)
                    kp4 = kpad[0:Lk, :, :].rearrange("p (g e) d -> p g e d", e=2)
                    P.add('dve', lambda E, kp4=kp4, dk3=dk3: E.tensor_copy(out=kp4[:, :, 0, 0:64], in_=dk3), r=[dkr], w=['kpad'])
                    P.add('dve', lambda E, kp4=kp4, dk3=dk3: E.tensor_copy(out=kp4[:, :, 1, 64:128], in_=dk3), r=[dkr], w=['kpad'])

                    def mm_t(E, Lk=Lk):
                        for x in range(8):
                            ins = E.transpose(psTv[:, x, 0:Lk], kpad[0:Lk, x, :], identb[0:Lk, 0:Lk])
                        return ins
                    P.add('pe', mm_t, r=['kpad', 'identb'], w=['psT'])
                    if sl == 'A':
                        P.add('act', lambda E: E.activation(out=kTm[:, :, :], in_=psTv[:, :, 0:16], func=AF.Copy), r=['psT'], w=['kTm'])
                        P.add('dve', lambda E, dv3=dv3: E.tensor_copy(out=vext[0:16, 0, :, 0:64], in_=dv3), r=[dvr], w=['vext0'])
                    else:
                        P.add('act', lambda E, sl=sl: E.activation(out=kTz[:, :, (1 + sl) * 128:(2 + sl) * 128], in_=psTv[:, :, :], func=AF.Copy), r=['psT'], w=['kTz%d' % (1 + sl)])
                        P.add('dve', lambda E, sl=sl, dv3=dv3: E.tensor_copy(out=vext[:, 2 + sl, :, 0:64], in_=dv3), r=[dvr], w=['vext%d' % (2 + sl)])

            qn = scrB[:, 0:KC * NT].rearrange("p (c t) -> p c t", t=NT)
            oT = scrB[:, KC * NT:2 * KC * NT].rearrange("p (c t) -> p c t", t=NT)
            ob = 2 * KC * NT
            otok = scrB[:, ob:ob + 2048]
            Pc = scrB[:, ob + 2048:ob + 2048 + 320]
            sqq = scrF[:, 0:256].bitcast(BF16)
            rq = scrF[:, 256:768]
            tmpS = scrF[:, 768:1280]
            Pcur = scrF[:, 1280:1536].bitcast(BF16)
            Pprev = scrF[:, 1536:1792].bitcast(BF16)
            Pmeta = scrF[:, 1792:2048].bitcast(BF16)
            dn = scrF[:, 2048:2052]
            qg8 = scrF[:, 2056:2057]
            kwin_f = scrF[:, 2064:2320]
            vwin_f = scrF[:, 2320:2576]
            k2_f = scrF[:, 2576:2832]
            sb2 = ob + 2048 + 320

            def q_proj(blocks):
                P.add('dve', lambda E: E.tensor_scalar(qg8[:, :], qgt[:, :], 0.125, None, ALU.mult), r=['qgt'], w=['qg8'])
                for jb in range(4):
                    wv, wr = W.get(wblk[B_Q + jb], KC * 512)
                    wv = wv.rearrange("p (k n) -> p k n", n=512)
                    for j in range(4):
                        c = 4 * jb + j
                        for (b0, bn) in blocks:
                            bk = 'b%d' % b0
                            ps, psn = rot_psA()

                            def mmf(E, wv=wv, j=j, b0=b0, bn=bn, ps=ps):
                                for kc in range(KC):
                                    ins = E.matmul(ps[:, 0:bn], lhsT=wv[:, kc, 128 * j:128 * j + 128], rhs=xn[:, kc, b0:b0 + bn], start=(kc == 0), stop=(kc == KC - 1))
                                return ins
                            P.add('pe', mmf, r=[wr, 'xn' + bk], w=[psn])
                            P.add('act', lambda E, bn=bn, ps=ps: E.activation(out=sqq[:, 0:bn], in_=ps[:, 0:bn], func=AF.Square), r=[psn], w=['sqq'])
                            ps2, ps2n = rot_psA()
                            P.add('pe', lambda E, bn=bn, ps2=ps2: E.matmul(ps2[:, 0:bn], lhsT=bd64[:, :], rhs=sqq[:, 0:bn], start=True, stop=True), r=['bd64', 'sqq'], w=[ps2n])
                            P.add('act', lambda E, bn=bn, ps2=ps2: E.activation(out=rq[:, 0:bn], in_=ps2[:, 0:bn], func=AF.Sqrt, bias=epsb[:, :], scale=1.0), r=[ps2n, 'epsb'], w=['rq'])
                            P.add('dve', lambda E, bn=bn: E.reciprocal(rq[:, 0:bn], rq[:, 0:bn]), r=['rq'], w=['rq'])
                            P.add('dve', lambda E, c=c, b0=b0, bn=bn, ps=ps: E.scalar_tensor_tensor(out=qn[:, c, b0:b0 + bn], in0=ps[:, 0:bn], scalar=qg8[:, 0:1], in1=rq[:, 0:bn],
                                                                                              op0=ALU.mult, op1=ALU.mult), r=[psn, 'qg8', 'rq'], w=['qn' + bk])

            def attn_core(nq, qcols, keysets, res_in, out_rows_ap, meta0=False):
                nk_list = [ks[1] for ks in keysets]
                Pt = [Pcur, Pprev, Pmeta]
                for g in range(4):
                    for e in range(2):
                        kx = 2 * g + e
                        heads = [8 * g + 2 * j + e for j in range(4)]
                        for si, (kT, nk, vx, Rt, kind) in enumerate(keysets):
                            pb = psB[si]
                            pbv = pb[:, 0:4 * nq].rearrange("p (j q) -> p j q", q=nq)
                            P.add('pe', lambda E, kT=kT, nk=nk, pbv=pbv, kx=kx, g=g: E.matmul(pbv[0:nk, :, :], lhsT=kT[:, kx, 0:nk], rhs=qn[:, 4 * g:4 * g + 4, qcols[0]:qcols[1]], start=True, stop=True),
                                  r=res_in + ['qnall'], w=['psB%d' % si])
                            Pv = Pt[si][:, 0:4 * nq].rearrange("p (j q) -> p j q", q=nq)
                            tv = tmpS[:, 0:4 * nq].rearrange("p (j q) -> p j q", q=nq)
                            if kind == 'R':
                                for j in range(4):
                                    P.add('dve', lambda E, j=j, nk=nk, Rt=Rt, pbv=pbv, tv=tv, heads=heads: E.scalar_tensor_tensor(out=tv[0:nk, j, :], in0=Rt[0:nk, 0:nq], scalar=float(SLOPES[heads[j]]),
                                                                                                                in1=pbv[0:nk, j, :], op0=ALU.mult, op1=ALU.add),
                                          r=['psB%d' % si, 'ctf'], w=['tmpS'])
                                P.add('act', lambda E, nk=nk, Pv=Pv, tv=tv: E.activation(out=Pv[0:nk, :, :], in_=tv[0:nk, :, :], func=AF.Exp), r=['tmpS'], w=['P%d' % si])
                            else:
                                for j in range(4):
                                    P.add('act', lambda E, j=j, nk=nk, Pv=Pv, pbv=pbv, heads=heads: E.activation(out=Pv[0:nk, j, :], in_=pbv[0:nk, j, :], func=AF.Exp,
                                                                                             bias=nslope128[0:nk, heads[j]:heads[j] + 1], scale=1.0),
                                          r=['psB%d' % si, 'smc'], w=['P%d' % si])
                        po, pon = rot_psA()

                        def mm_pv(E, po=po, g=g):
                            for j in range(4):
                                for si, (kT, nk, vx, Rt, kind) in enumerate(keysets):
                                    Pv = Pt[si][:, 0:4 * nq].rearrange("p (j q) -> p j q", q=nq)
                                    ins = E.matmul(po[0:nq, 65 * j:65 * j + 65], lhsT=Pv[0:nk, j, :], rhs=vx[0:nk, g, :], start=(si == 0), stop=(si == len(keysets) - 1))
                            return ins
                        P.add('pe', mm_pv, r=['P%d' % si for si in range(len(keysets))] + res_in, w=[pon])
                        pov = po[:, 0:260].rearrange("p (j d) -> p j d", d=65)
                        esv = esink[:, 8 * g:8 * g + 8].rearrange("p (j e) -> p j e", e=2)
                        P.add('dve', lambda E, pov=pov, esv=esv, e=e: E.tensor_tensor(out=dn[0:nq, :], in0=pov[0:nq, :, 64], in1=esv[0:nq, :, e], op=ALU.add), r=[pon, 'esink'], w=['dn'])
                        P.add('dve', lambda E: E.reciprocal(dn[0:nq, :], dn[0:nq, :]), r=['dn'], w=['dn'])
                        for j in range(4):
                            hh = heads[j]
                            P.add('dve', lambda E, j=j, hh=hh, pov=pov: E.tensor_scalar(otok[0:nq, 64 * hh:64 * hh + 64], pov[0:nq, j, 0:64], dn[0:nq, j:j + 1], None, ALU.mult),
                                  r=[pon, 'dn'], w=['otok'])

            def attn_prompt(seg):
                psTv = psT[:, 0:1024].rearrange("p (x l) -> p x l", l=128)
                def attn_block(i):
                    t0 = NA + 128 * i
                    first = (seg == 0 and i == 0)
                    keysets = [(kTz[:, :, (1 + i) * 128:(2 + i) * 128], 128, vext[:, 2 + i, :, :], Rcur, 'R')]
                    if not first:
                        keysets.append((kTz[:, :, i * 128:(1 + i) * 128], 128, vext[:, 1 + i, :, :], Rprev, 'R'))
                        keysets.append((kTm[:, :, :], 16, vext[:, 0, :, :], None, 'C'))
                    else:
                        keysets.append((kTm[:, :, :], 16, vext[:, 0, :, :], Rmeta0, 'R'))
                    psT32 = psT[:, :].bitcast(F32)
                    sets = [([psB[0], psB[1], psB[2]], ['psB0', 'psB1', 'psB2'], psA[3], 'psA3'),
                            ([psA[0], psA[1], psA[2]], ['psA0', 'psA1', 'psA2'], psT32, 'psT')]
                    Psets = [[Pcur, Pprev, Pmeta],
                             [scrF[:, 2064:2320].bitcast(BF16), scrF[:, 2320:2576].bitcast(BF16), scrF[:, 2576:2832].bitcast(BF16)]]
                    tmps = [tmpS, scrF[:, 0:512]]
                    dns = [scrF[:, 2048:2052], scrF[:, 2052:2056]]
                    nks = len(keysets)

                    def st_S(k):
                        g, e, p = k // 2, k % 2, k % 2
                        banks, bnames, _, _ = sets[p]
                        for si, (kT, nk, vx, Rt, kind) in enumerate(keysets):
                            pbv = banks[si][:, 0:512].rearrange("p (j q) -> p j q", q=128)
                            P.add('pe', lambda E, kT=kT, nk=nk, pbv=pbv, g=g, e=e: E.matmul(pbv[0:nk, :, :], lhsT=kT[:, 2 * g + e, 0:nk], rhs=qn[:, 4 * g:4 * g + 4, t0:t0 + 128], start=True, stop=True),
                                  r=['kTzall', 'qnall'], w=[bnames[si]])

                    def st_BX(k):
                        g, e, p = k // 2, k % 2, k % 2
                        banks, bnames, _, _ = sets[p]
                        heads = [8 * g + 2 * j + e for j in range(4)]
                        tv = tmps[p][:, 0:512].rearrange("p (j q) -> p j q", q=128)
                        for si, (kT, nk, vx, Rt, kind) in enumerate(keysets):
                            pbv = banks[si][:, 0:512].rearrange("p (j q) -> p j q", q=128)
                            Pv = Psets[p][si][:, 0:512].rearrange("p (j q) -> p j q", q=128)
                            if kind == 'R':
                                for j in range(4):
                                    P.add('dve', lambda E, j=j, nk=nk, Rt=Rt, pbv=pbv, tv=tv, heads=heads: E.scalar_tensor_tensor(
                                        out=tv[0:nk, j, :], in0=Rt[0:nk, 0:128], scalar=float(SLOPES[heads[j]]), in1=pbv[0:nk, j, :], op0=ALU.mult, op1=ALU.add),
                                        r=[bnames[si], 'ctf'], w=['tmpS%d' % p])
                                P.add('act', lambda E, nk=nk, Pv=Pv, tv=tv: E.activation(out=Pv[0:nk, :, :], in_=tv[0:nk, :, :], func=AF.Exp), r=['tmpS%d' % p], w=['P%d_%d' % (p, si)])
                            else:
                                for j in range(4):
                                    P.add('act', lambda E, j=j, nk=nk, Pv=Pv, pbv=pbv, heads=heads: E.activation(out=Pv[0:nk, j, :], in_=pbv[0:nk, j, :], func=AF.Exp,
                                                                                                          bias=nslope128[0:nk, heads[j]:heads[j] + 1], scale=1.0),
                                          r=[bnames[si], 'smc'], w=['P%d_%d' % (p, si)])

                    def st_PV(k):
                        g, e, p = k // 2, k % 2, k % 2
                        _, _, po, pon = sets[p]

                        def mm_pv(E, po=po, g=g, p=p):
                            for j in range(4):
                                for si, (kT, nk, vx, Rt, kind) in enumerate(keysets):
                                    Pv = Psets[p][si][:, 0:512].rearrange("p (j q) -> p j q", q=128)
                                    ins = E.matmul(po[:, 65 * j:65 * j + 65], lhsT=Pv[0:nk, j, :], rhs=vx[0:nk, g, :], start=(si == 0), stop=(si == nks - 1))
                            return ins
                        P.add('pe', mm_pv, r=['P%d_%d' % (p, si) for si in range(nks)] + ['kTzall'], w=[pon])

                    def st_E(k):
                        g, e, p = k // 2, k % 2, k % 2
                        _, _, po, pon = sets[p]
                        dn_ = dns[p]
                        heads = [8 * g + 2 * j + e for j in range(4)]
                        pov = po[:, 0:260].rearrange("p (j d) -> p j d", d=65)
                        esv = esink[:, 8 * g:8 * g + 8].rearrange("p (j e) -> p j e", e=2)
                        P.add('dve', lambda E, pov=pov, esv=esv, e=e, dn_=dn_: E.tensor_tensor(out=dn_[:, :], in0=pov[:, :, 64], in1=esv[:, :, e], op=ALU.add), r=[pon, 'esink'], w=['dn%d' % p])
                        P.add('dve', lambda E, dn_=dn_: E.reciprocal(dn_[:, :], dn_[:, :]), r=['dn%d' % p], w=['dn%d' % p])
                        for j in range(4):
                            hh = heads[j]
                            P.add('dve', lambda E, j=j, hh=hh, pov=pov, dn_=dn_: E.tensor_scalar(otok[:, 64 * hh:64 * hh + 64], pov[:, j, 0:64], dn_[:, j:j + 1], None, ALU.mult),
                                  r=[pon, 'dn%d' % p], w=['otok%d' % k])

                    st_S(0)
                    st_BX(0)
                    for k in range(8):
                        if k + 1 < 8:
                            st_S(k + 1)
                            st_BX(k + 1)
                        st_PV(k)
                        st_E(k)
                    for half in range(2):
                        def mm_t(E, half=half):
                            for x in range(8):
                                c = 8 * half + x
                                ins = E.transpose(psTv[:, x, :], otok[:, 128 * c:128 * c + 128], identb[:, :])
                            return ins
                        P.add('pe', mm_t, r=['otok%d' % k for k in range(8)] + ['identb'], w=['psT'])
                        P.add('act', lambda E, half=half, t0=t0: E.activation(out=oT[:, 8 * half:8 * half + 8, t0:t0 + 128], in_=psTv[:, :, :], func=AF.Copy), r=['psT'], w=['oT'])
                for i in range(4):
                    attn_block(i)
                P.add('dve', lambda E: E.tensor_copy(out=kTz[:, :, 0:128], in_=kTz[:, :, 512:640]), r=['kTzall'], w=['kTzall'])
                P.add('dve', lambda E: E.tensor_copy(out=vext[:, 1, :, :], in_=vext[:, 5, :, :]), r=['kTzall'], w=['kTzall'])

            def attn_sample():
                psTv = psT[:, 0:1024].rearrange("p (x l) -> p x l", l=128)
                xf = xn[:, :, :].rearrange("p c t -> p (c t)")
                kpadS = xf[:, 0:1024].rearrange("p (x d) -> p x d", d=128)
                kTzS = xf[:, 1024:2048].rearrange("p (x d) -> p x d", d=128)
                kpad2 = xf[:, 2048:3072].rearrange("p (x d) -> p x d", d=128)
                kTz2 = xf[:, 3072:3328].rearrange("p (x d) -> p x d", d=32)
                vxS1 = xf[:, 3328:3588].rearrange("p (g d) -> p g d", d=65)
                vxS2 = xf[:, 3588:3848].rearrange("p (g d) -> p g d", d=65)
                oS_all = xf[:, 3848:5896]
                v2_f = xf[:, 5896:6408].bitcast(F32)
                P.add('dve', lambda E: E.memset(xf[:, 0:3328], 0.0), w=['kpadS', 'kpad2', 'kTzS', 'kTz2'])
                P.add('dve', lambda E: E.memset(xf[:, 3328:3848], 1.0), w=['vxS1', 'vxS2'])
                SR1 = xf[:, 6408:6664].bitcast(F32)
                SR2 = xf[:, 6664:6920].bitcast(F32)
                esP = xf[:, 6920:6984].bitcast(F32)
                dnS = xf[:, 6984:7048].bitcast(F32)
                tmp1 = tmpS[:, 0:128]
                tmp2 = tmpS[:, 128:256]
                P1 = Pcur[:, 0:128]
                P2 = Pprev[:, 0:128]
                for x8 in range(8):
                    g_, e_ = x8 // 2, x8 % 2
                    for j in range(4):
                        hh = 8 * g_ + 2 * j + e_
                        sidx = 4 * x8 + j
                        P.add('dve', lambda E, sidx=sidx, hh=hh: E.tensor_scalar(SR1[:, 4 * sidx:4 * sidx + 4], RwinS, float(SLOPES[hh]), None, ALU.mult), r=['ctf'], w=['SR1'])
                        P.add('dve', lambda E, sidx=sidx, hh=hh: E.tensor_scalar(SR2[0:20, 4 * sidx:4 * sidx + 4], R2S[0:20, :], float(SLOPES[hh]), None, ALU.mult), r=['ctf'], w=['SR2'])
                    esv = esink[:, 8 * g_:8 * g_ + 8].rearrange("p (j e) -> p j e", e=2)
                    P.add('dve', lambda E, x8=x8, esv=esv, e_=e_: E.tensor_copy(out=esP[:, 4 * x8:4 * x8 + 4], in_=esv[:, :, e_]), r=['esink'], w=['esP'])
                banks = [(psA[0], 'psA0'), (psA[1], 'psA1'), (psA[2], 'psA2'), (psA[3], 'psA3'), (psB[2], 'psB2')]
                kpS4 = kpadS[:, :, :].rearrange("p (g e) d -> p g e d", e=2)
                kp24 = kpad2[0:20, :, :].rearrange("p (g e) d -> p g e d", e=2)
                kw3 = kwin_f[:, :].rearrange("p (g d) -> p g d", d=64)
                vw3 = vwin_f[:, :].rearrange("p (g d) -> p g d", d=64)
                k23 = k2_f[0:20, :].rearrange("p (g d) -> p g d", d=64)
                v23 = v2_f[0:20, :].rearrange("p (g d) -> p g d", d=64)
                for b in range(16):
                    r0 = 16 + 4 * b
                    P.add('sp', lambda E, b=b: E.dma_start(out=kwin_f[:, :], in_=ckw[b]), w=['kwin_f'], stream='ldk0')
                    P.add('sp', lambda E, b=b: E.dma_start(out=vwin_f[:, :], in_=cvw[b]), w=['vwin_f'], stream='ldk1')
                    P.add('sp', lambda E, b=b: E.dma_start(out=k2_f[0:16, :], in_=ckm[b]), w=['k2_f'], stream='ldk2')
                    P.add('sp', lambda E, r0=r0: E.dma_start(out=k2_f[16:20, :], in_=knA[r0:r0 + 4, :]), r=['knA'], w=['k2_f'], stream='ldk2')
                    P.add('sp', lambda E, b=b: E.dma_start(out=v2_f[0:16, :], in_=cvm[b]), w=['v2_f'], stream='ldk3')
                    P.add('sp', lambda E, r0=r0: E.dma_start(out=v2_f[16:20, :], in_=vA[r0:r0 + 4, :]), r=['vA'], w=['v2_f'], stream='ldk3')
                    P.add('sp', lambda E, b=b: E.dma_start(out=okws[b, 0:124, :], in_=kwin_f[4:128, :]), r=['kwin_f'], stream='stkw')
                    P.add('sp', lambda E, b=b, r0=r0: E.dma_start(out=okws[b, 124:128, :], in_=knA[r0:r0 + 4, :]), r=['knA'], stream='stkw')
                    P.add('sp', lambda E, b=b: E.dma_start(out=ovws[b, 0:124, :], in_=vwin_f[4:128, :]), r=['vwin_f'], stream='stkw2')
                    P.add('sp', lambda E, b=b, r0=r0: E.dma_start(out=ovws[b, 124:128, :], in_=vA[r0:r0 + 4, :]), r=['vA'], stream='stkw2')
                    P.add('dve', lambda E: E.tensor_copy(out=kpS4[:, :, 0, 0:64], in_=kw3), r=['kwin_f'], w=['kpadS'])
                    P.add('dve', lambda E: E.tensor_copy(out=kpS4[:, :, 1, 64:128], in_=kw3), r=['kwin_f'], w=['kpadS'])
                    P.add('dve', lambda E: E.tensor_copy(out=kp24[:, :, 0, 0:64], in_=k23), r=['k2_f'], w=['kpad2'])
                    P.add('dve', lambda E: E.tensor_copy(out=kp24[:, :, 1, 64:128], in_=k23), r=['k2_f'], w=['kpad2'])

                    def mm_t1(E):
                        for x in range(8):
                            ins = E.transpose(psTv[:, x, :], kpadS[:, x, :], identb[:, :])
                        return ins
                    P.add('pe', mm_t1, r=['kpadS', 'identb'], w=['psT'])
                    P.add('act', lambda E: E.activation(out=kTzS[:, :, :], in_=psTv[:, :, :], func=AF.Copy), r=['psT'], w=['kTzS'])

                    def mm_t2(E):
                        for x in range(8):
                            ins = E.transpose(psTv[:, x, 0:20], kpad2[0:20, x, :], identb[0:20, 0:20])
                        return ins
                    P.add('pe', mm_t2, r=['kpad2', 'identb'], w=['psT'])
                    P.add('act', lambda E: E.activation(out=kTz2[:, :, 0:20], in_=psTv[:, :, 0:20], func=AF.Copy), r=['psT'], w=['kTz2'])
                    P.add('dve', lambda E: E.tensor_copy(out=vxS1[:, :, 0:64], in_=vw3), r=['vwin_f'], w=['vxS1'])
                    P.add('dve', lambda E: E.tensor_copy(out=vxS2[0:20, :, 0:64], in_=v23), r=['v2_f'], w=['vxS2'])
                    def mm_sc(E, r0=r0):
                        for x8 in range(8):
                            g_ = x8 // 2
                            o0 = psB[0][:, 16 * x8:16 * x8 + 16].rearrange("p (j q) -> p j q", q=4)
                            o1 = psB[1][:, 16 * x8:16 * x8 + 16].rearrange("p (j q) -> p j q", q=4)
                            E.matmul(o0[:, :, :], lhsT=kTzS[:, x8, :], rhs=qn[:, 4 * g_:4 * g_ + 4, r0:r0 + 4], start=True, stop=True)
                            ins = E.matmul(o1[0:20, :, :], lhsT=kTz2[:, x8, 0:20], rhs=qn[:, 4 * g_:4 * g_ + 4, r0:r0 + 4], start=True, stop=True)
                        return ins
                    P.add('pe', mm_sc, r=['kTzS', 'kTz2', 'qnall'], w=['psB0', 'psB1'])
                    P.add('dve', lambda E: E.tensor_tensor(out=tmp1[:, :], in0=psB[0][:, 0:128], in1=SR1[:, :], op=ALU.add), r=['psB0', 'SR1'], w=['tmp1'])
                    P.add('dve', lambda E: E.tensor_tensor(out=tmp2[0:20, :], in0=psB[1][0:20, 0:128], in1=SR2[0:20, :], op=ALU.add), r=['psB1', 'SR2'], w=['tmp2'])
                    P.add('act', lambda E: E.activation(out=P1[:, :], in_=tmp1[:, :], func=AF.Exp), r=['tmp1'], w=['P1s'])
                    P.add('act', lambda E: E.activation(out=P2[0:20, :], in_=tmp2[0:20, :], func=AF.Exp), r=['tmp2'], w=['P2s'])

                    def mm_pvs(E):
                        for hh in range(32):
                            g_, j_, e_ = hh // 8, (hh % 8) // 2, hh % 2
                            sidx = 4 * (2 * g_ + e_) + j_
                            bank = banks[hh // 7][0]
                            col = (hh % 7) * 65
                            E.matmul(bank[0:4, col:col + 65], lhsT=P1[:, 4 * sidx:4 * sidx + 4], rhs=vxS1[:, g_, :], start=True, stop=False)
                            ins = E.matmul(bank[0:4, col:col + 65], lhsT=P2[0:20, 4 * sidx:4 * sidx + 4], rhs=vxS2[0:20, g_, :], start=False, stop=True)
                        return ins
                    P.add('pe', mm_pvs, r=['P1s', 'P2s', 'vxS1', 'vxS2'], w=[bn_ for (_, bn_) in banks])
                    for k, (bank, bname) in enumerate(banks):
                        s0 = 7 * k
                        nk_ = min(7, 32 - s0)
                        bv = bank[:, 0:nk_ * 65].rearrange("p (s d) -> p s d", d=65)
                        P.add('dve', lambda E, bv=bv, s0=s0, nk_=nk_: E.tensor_tensor(out=dnS[0:4, s0:s0 + nk_], in0=bv[0:4, :, 64], in1=esink[0:4, s0:s0 + nk_], op=ALU.add),
                              r=[bname, 'esink'], w=['dnS%d' % k])
                    P.add('dve', lambda E: E.reciprocal(dnS[0:4, 0:32], dnS[0:4, 0:32]), r=['dnS%d' % k for k in range(5)], w=['dnSr'])
                    for k, (bank, bname) in enumerate(banks):
                        s0 = 7 * k
                        nk_ = min(7, 32 - s0)
                        bv = bank[:, 0:nk_ * 65].rearrange("p (s d) -> p s d", d=65)
                        otv = otok[0:4, 64 * s0:64 * (s0 + nk_)].rearrange("p (s d) -> p s d", d=64)
                        P.add('dve', lambda E, bv=bv, s0=s0, nk_=nk_, otv=otv: E.tensor_tensor(out=otv, in0=bv[0:4, :, 0:64], in1=dnS[0:4, s0:s0 + nk_, None].broadcast_to([4, nk_, 64]), op=ALU.mult),
                              r=[bname, 'dnSr'], w=['otok'])
                    P.add('pool', lambda E, b=b: E.dma_start(out=oS_all[4 * b:4 * b + 4, :], in_=otok[0:4, :]), r=['otok'], w=['oS_all'], stream='mvo')
                for half in range(2):
                    def mm_t(E, half=half):
                        for x in range(8):
                            c = 8 * half + x
                            ins = E.transpose(psTv[:, x, 0:64], oS_all[0:64, 128 * c:128 * c + 128], identb[0:64, 0:64])
                        return ins
                    P.add('pe', mm_t, r=['oS_all', 'identb'], w=['psT'])
                    P.add('act', lambda E, half=half: E.activation(out=oT[:, 8 * half:8 * half + 8, 16:NA], in_=psTv[:, :, 0:64], func=AF.Copy), r=['psT'], w=['oT'])

            gen_state['kv_phase'] = kv_phase
            gen_state['mlstm'] = mlstm
            gen_state['rmsnorm'] = rmsnorm
            gen_state['ffn'] = ffn
            gen_state['out_proj'] = out_proj

            for seg in range(NSEG):
                blocks = [(NA, TS)] if seg > 0 else [(0, NA), (NA, TS)]
                allres = ['hTb%d' % b0 for (b0, _) in blocks]
                if seg == 0:
                    P.add('sp', lambda E: E.dma_start(out=hT[:, :, 0:NA], in_=xA[:, :, :]), w=['hTb0'], stream='ldx0')
                    P.add('sp', lambda E, seg=seg: E.dma_start(out=hT[:, :, NA:NT], in_=xR[seg]), w=['hTb%d' % NA], stream='ldx1')
                P.barrier()
                rmsnorm(0, blocks)
                ffn(0, blocks, seg)
                rmsnorm(4, blocks)
                P.add('dve', lambda E: E.tensor_copy(out=scrF[0:1, 0:1], in_=scrF[0:1, 0:1]), r=['xnb%d' % b0 for (b0, _) in blocks], w=['xnall'])
                P.barrier()
                mlstm(seg, blocks)
                if seg == 0:
                    mlstm_finish_sample()
                P.barrier()
                out_proj(B_AOUT, actT, 'actTall', blocks)
                P.barrier()
                rmsnorm(1, blocks)
                ffn(1, blocks, seg)
                rmsnorm(6, blocks)
                P.add('dve', lambda E: E.tensor_copy(out=scrF[0:1, 0:1], in_=scrF[0:1, 0:1]), r=['xnb%d' % b0 for (b0, _) in blocks], w=['xnall'])
                P.barrier()
                kv_phase(seg)
                P.barrier()
                rmsnorm(2, blocks, reuse=True)
                ffn(2, blocks, seg)
                rmsnorm(5, blocks)
                P.barrier()
                q_proj(blocks)
                P.add('dve', lambda E: E.memset(oT[:, :, 0:NA], 0.0), w=['oT'])
                P.add('dve', lambda E: E.tensor_copy(out=scrF[0:1, 0:1], in_=scrF[0:1, 0:1]), r=['qnb%d' % b0 for (b0, _) in blocks] + ['kTz%d' % x for x in range(1, 5)] + ['vext%d' % x for x in range(2, 6)] + ['kTm', 'vext0'], w=['qnall', 'kTzall'])
                P.barrier()
                attn_prompt(seg)
                if seg == 0:
                    P.barrier()
                    attn_sample()
                P.barrier()
                out_proj(B_BOUT, oT, 'oTall', blocks)
                P.barrier()
                rmsnorm(3, blocks)
                ffn(3, blocks, seg, stream_io=True)
                P.barrier()
                if seg == 0:
                    P.add('sp', lambda E: E.dma_start(out=yA[:, :, :], in_=hT[:, :, 0:NA]), r=['hTb0'], stream='sty0')
                P.barrier()
            for h in range(4):
                P.add('sp', lambda E, h=h: E.dma_start(out=oCp[h].rearrange("(c p) v -> p c v", p=128), in_=Cst[:, h, :, :]), r=['C%d' % h], stream='stCp')
            P.add('sp', lambda E: E.dma_start(out=onp[:, :], in_=nst[:, :, :].rearrange("p c h -> p (c h)")), r=['n0', 'n1', 'n2', 'n3'], stream='stnp')
            P.add('sp', lambda E: E.dma_start(out=omp[:, :], in_=mbc[0:1, :]), r=['mbc'], stream='stmp')

        Pd = Prog(nc, dry=True)
        Wd = WRing(Pd, slots)
        gen(Pd, Wd)
        P = Prog(nc)
        W = WRing(P, slots, schedule=Wd.record)
        gen(P, W)
        P.emit(st, final_streams=['sty0', 'sty1', 'stCp', 'stnp', 'stmp', 'stC', 'stn', 'stm', 'stkv', 'stkv2', 'stkw', 'stkw2'])
        build_program.stats = P.stats
    return nc


def _fm(x2d):
    T = x2d.shape[0]
    return np.ascontiguousarray(x2d.T.reshape(KC, 128, T).transpose(1, 0, 2))


def _blk(Wm, cols):
    return Wm[:, cols].reshape(KC, 128, len(cols)).transpose(1, 0, 2).reshape(128, KC * len(cols))


def _const_tables():
    t = np.zeros((128, 12, 128), np.float32)
    p = np.arange(128)[:, None]
    f = np.arange(128)[None, :]
    t[:, 0] = (p == f)
    t[:, 1] = (p <= f)
    t[:, 2] = np.where(f <= p, 0.0, NEG)
    t[:, 3] = np.where(p <= f, 0.0, NEG)
    t[:, 4] = (p == 127) * np.ones((1, 128))
    t[:, 5] = (p == 15) * np.ones((1, 128))
    t[:, 6] = (p == 3) * np.ones((1, 128))
    t[:, 7] = ((p // 64) == (f // 64)) / 64.0
    t[:, 8] = np.where(f >= p, -(f - p).astype(np.float32), NEG)
    t[:, 9] = np.where(p > f, -(f + 128 - p).astype(np.float32), NEG)
    t[:, 10] = -np.minimum(16 + f - p, 128).astype(np.float32)
    i4 = np.arange(4)[None, :]
    t[:, 11, 0:4] = np.where(p > i4, -(128 + i4 - p).astype(np.float32), NEG)
    r2 = np.full((128, 4), -128.0, np.float32)
    for j in range(4):
        r2[16 + j] = np.where(j <= np.arange(4), -(np.arange(4) - j).astype(np.float32), NEG)
    t[:, 11, 4:8] = r2
    return t


def _prep_shared(inp):
    w_in = inp['w_ffn_in']
    w_out = inp['w_ffn_out']
    wblk = np.empty((NBLK, 128, KC * 512), np.float32)
    woutb = np.empty((64, 128, FC * 128), np.float32)
    for l in range(2):
        for i in range(2):
            f = 2 * l + i
            Wm = w_in[l, i]
            for b in range(22):
                cols = np.concatenate([np.arange(256 * b, 256 * b + 256), DFF + np.arange(256 * b, 256 * b + 256)])
                wblk[B_FFN + 22 * f + b] = _blk(Wm, cols)
            Wo = w_out[l, i]
            for oc in range(16):
                woutb[16 * f + oc] = Wo[:, 128 * oc:128 * oc + 128].reshape(FC, 128, 128).transpose(1, 0, 2).reshape(128, FC * 128)
    wa = inp['w_a_in'][0]
    for h in range(4):
        wblk[B_AQK + h] = _blk(wa, np.concatenate([np.arange(256 * h, 256 * h + 256), 1024 + np.arange(256 * h, 256 * h + 256)]))
        wblk[B_AV + h] = _blk(wa, 2048 + np.arange(512 * h, 512 * h + 512))
        wblk[B_AO + h] = _blk(wa, 4096 + np.arange(512 * h, 512 * h + 512))
        wblk[B_AOUT + h] = _blk(inp['w_a_out'][0], np.arange(512 * h, 512 * h + 512))
        wblk[B_Q + h] = _blk(inp['w_q'][0], np.arange(512 * h, 512 * h + 512))
        wblk[B_BOUT + h] = _blk(inp['w_b_out'][0], np.arange(512 * h, 512 * h + 512))
    wblk[B_KV] = _blk(inp['w_kv'], np.arange(512))
    wgate = np.ascontiguousarray(wa[:, 6144:6152].reshape(KC, 128, 8).transpose(1, 0, 2).reshape(128, KC * 8))
    gl = [inp['ffn_norm'][0, 0], inp['ffn_norm'][0, 1], inp['ffn_norm'][1, 0], inp['ffn_norm'][1, 1],
          inp['mix_norm'][0], inp['mix_norm'][1], inp['kv_norm'], inp['a_head_norm'][0]]
    gains = np.ascontiguousarray(np.stack([g.reshape(KC, 128).T for g in gl], axis=1)).astype(np.float32)
    nsl = np.array([-128.0 * s for s in SLOPES], np.float32)
    smallc = np.concatenate([inp['b_a_gate'][0], inp['k_norm'], inp['sinks'][0], nsl]).astype(np.float32)[None, :]
    qg = np.ascontiguousarray(np.tile(inp['q_norm'][0], 2)[:, None]).astype(np.float32)
    return dict(wblk=wblk, wout=woutb, wgate=wgate, gains=gains, smallc=smallc, qg=qg, ctab=_const_tables())


def _prep_core(inp, c):
    xs = inp['x_sample'][16 * c:16 * c + 16].reshape(64, D)
    xA = _fm(np.concatenate([inp['meta_tokens'], xs], axis=0))
    xp = inp['x_prompt'][c]
    xR = np.stack([_fm(xp[TS * s:TS * s + TS]) for s in range(NSEG)])
    stn = inp['state_n'][0, 16 * c:16 * c + 16]
    stn = np.ascontiguousarray(stn.reshape(16, 4, 2, 128).transpose(0, 3, 2, 1).reshape(16, 128, 8))
    return dict(
        xA=xA, xR=xR,
        stC=np.ascontiguousarray(inp['state_C'][0, 16 * c:16 * c + 16]),
        stn=stn,
        stm=np.ascontiguousarray(inp['state_m'][0, 16 * c:16 * c + 16]),
        ckm=np.ascontiguousarray(inp['cache_k_meta'][16 * c:16 * c + 16].reshape(16, 16, 256)),
        cvm=np.ascontiguousarray(inp['cache_v_meta'][16 * c:16 * c + 16].reshape(16, 16, 256)),
        ckw=np.ascontiguousarray(inp['cache_k_win'][16 * c:16 * c + 16].reshape(16, 128, 256)),
        cvw=np.ascontiguousarray(inp['cache_v_win'][16 * c:16 * c + 16].reshape(16, 128, 256)),
    )


def _tm(a):
    T = a.shape[2]
    return a.transpose(1, 0, 2).reshape(D, T).T


def _assemble(results):
    n = len(results)
    y_prompt = np.empty((n, 2048, D), np.float32)
    y_sample = np.empty((16 * n, 4, D), np.float32)
    c_p = np.empty((1, n, 4, 256, 512), np.float32)
    n_p = np.empty((1, n, 4, 256), np.float32)
    m_p = np.empty((1, n, 4), np.float32)
    k_meta_p = np.empty((n, 16, 4, 64), np.float32)
    v_meta_p = np.empty((n, 16, 4, 64), np.float32)
    k_win_p = np.empty((n, 128, 4, 64), np.float32)
    v_win_p = np.empty((n, 128, 4, 64), np.float32)
    c_s = np.empty((1, 16 * n, 4, 256, 512), np.float32)
    n_s = np.empty((1, 16 * n, 4, 256), np.float32)
    m_s = np.empty((1, 16 * n, 4), np.float32)
    k_win_s = np.empty((16 * n, 128, 4, 64), np.float32)
    v_win_s = np.empty((16 * n, 128, 4, 64), np.float32)
    for c, r in enumerate(results):
        for s in range(NSEG):
            y_prompt[c, TS * s:TS * s + TS] = _tm(r['yR'][s])
        y_sample[16 * c:16 * c + 16] = _tm(r['yA'])[16:].reshape(16, 4, D)
        c_p[0, c] = r['oCp']
        n_p[0, c] = r['onp'].reshape(128, 2, 4).transpose(2, 1, 0).reshape(4, 256)
        m_p[0, c] = r['omp'][0]
        k_meta_p[c] = r['okmp'].reshape(16, 4, 64)
        v_meta_p[c] = r['ovmp'].reshape(16, 4, 64)
        k_win_p[c] = r['okwp'].reshape(128, 4, 64)
        v_win_p[c] = r['ovwp'].reshape(128, 4, 64)
        c_s[0, 16 * c:16 * c + 16] = r['oCs']
        n_s[0, 16 * c:16 * c + 16] = r['ons'].reshape(16, 128, 2, 4).transpose(0, 3, 2, 1).reshape(16, 4, 256)
        m_s[0, 16 * c:16 * c + 16] = r['oms']
        k_win_s[16 * c:16 * c + 16] = r['okws'].reshape(16, 128, 4, 64)
        v_win_s[16 * c:16 * c + 16] = r['ovws'].reshape(16, 128, 4, 64)
    return (y_prompt, y_sample, c_p, n_p, m_p, k_meta_p, v_meta_p, k_win_p, v_win_p, c_s, n_s, m_s, k_win_s, v_win_s)


def kernel(**inputs):
    inp = {k: np.asarray(v) for k, v in inputs.items()}
    n = 8
    shared = _prep_shared(inp)
    in_maps = []
    for c in range(n):
        m = dict(shared)
        m.update(_prep_core(inp, c))
        in_maps.append(m)
    nc = build_program()
    res = run_bass_kernel_spmd(nc, in_maps, core_ids=list(range(n)))
    return _assemble(res.results)
```

```python
from contextlib import ExitStack
import numpy as np
import concourse.bass as bass
import concourse.mybir as mybir
from concourse.bass_utils import run_bass_kernel_spmd

F32 = mybir.dt.float32
BF16 = mybir.dt.bfloat16
AF = mybir.ActivationFunctionType
ALU = mybir.AluOpType
AX = mybir.AxisListType

D = 2048
KC = 16
DFF = 5632
FC = 44
NA = 80
TS = 512
NSEG = 4
NT = NA + TS
NSLOT = 3
EPS = 1e-6
NEG = -1e30
SLOPES = [2.0 ** (-8.0 * (h + 1) / 32.0) for h in range(32)]
B_FFN = 0
B_AQK = 88
B_AV = 92
B_AO = 96
B_AOUT = 100
B_KV = 104
B_Q = 105
B_BOUT = 109
NBLK = 113


class Prog:
    def __init__(self, nc, dry=False):
        self.nc = nc
        self.dry = dry
        self.engs = {'pe': nc.tensor, 'act': nc.scalar, 'dve': nc.vector, 'pool': nc.gpsimd, 'sp': nc.sync}
        self.ops = []
        self.lw = {}
        self.rd = {}
        self.stream_last = {}
        self.fence = {}
        self.last_eng = {}

    def add(self, eng, fn, r=(), w=(), stream=None):
        if self.dry:
            return -1
        i = len(self.ops)
        deps = set()
        for x in r:
            if x in self.lw:
                deps.add(self.lw[x])
        for x in w:
            if x in self.lw:
                d = self.lw[x]
                de, _, _, dst = self.ops[d]
                if not (stream is None and dst is None and de == eng):
                    deps.add(d)
            for key, d in self.rd.get(x, {}).items():
                if stream is None and key == eng:
                    continue
                deps.add(d)
        if stream is not None and stream in self.stream_last:
            deps.add(self.stream_last[stream])
        if eng in self.fence:
            deps.update(self.fence.pop(eng))
        self.ops.append((eng, fn, deps, stream))
        for x in r:
            self.rd.setdefault(x, {})[eng if stream is None else (eng, i)] = i
        for x in w:
            self.lw[x] = i
            self.rd[x] = {}
        if stream is not None:
            self.stream_last[stream] = i
        else:
            self.last_eng[eng] = i
        return i

    def barrier(self, engines=('pe', 'act', 'dve', 'sp')):
        if self.dry:
            return
        front = set(self.last_eng.values()) | set(self.stream_last[s] for s in self.stream_last if not s.startswith('w'))
        for e in engines:
            self.fence.setdefault(e, set()).update(front)

    def emit(self, stack, final_streams):
        EP = 3000
        SEP = 200
        ops = self.ops
        n = len(ops)
        needed = [False] * n
        for i, (e, fn, deps, st) in enumerate(ops):
            for d in deps:
                de, _, _, dst = ops[d]
                if dst is None and not (de == 'pe' and e == 'pe' and st is None):
                    needed[d] = True
        cnt = {}
        ms = [0] * n
        for i, (e, fn, deps, st) in enumerate(ops):
            if st is not None:
                key = ('s', st)
                cnt[key] = cnt.get(key, 0) + 1
                ms[i] = cnt[key]
            elif needed[i]:
                key = ('e', e)
                cnt[key] = cnt.get(key, 0) + 1
                ms[i] = cnt[key]
        sems = {}

        def sem_for(key, count):
            if key[0] == 's':
                ep, v = (count - 1) // SEP, ((count - 1) % SEP + 1) * 16
            else:
                ep, v = (count - 1) // EP, (count - 1) % EP + 1
            k2 = (key, ep)
            if k2 not in sems:
                sems[k2] = stack.enter_context(self.nc.semaphore("sem_%s_%s_%d" % (key[0], str(key[1]), ep)))
            return sems[k2], v

        waited = {e: {} for e in self.engs}
        for i, (e, fn, deps, st) in enumerate(ops):
            E = self.engs[e]
            reqs = {}
            for d in deps:
                de, _, _, dst = ops[d]
                if dst is not None:
                    key = ('s', dst)
                elif de == 'pe' and e == 'pe' and st is None:
                    continue
                else:
                    key = ('e', de)
                if ms[d] > reqs.get(key, 0):
                    reqs[key] = ms[d]
            for key, val in reqs.items():
                if waited[e].get(key, 0) >= val:
                    continue
                sm, v = sem_for(key, val)
                E.wait_ge(sm, v)
                waited[e][key] = val
            ins = fn(E)
            if st is not None:
                sm, v = sem_for(('s', st), ms[i])
                ins.then_inc(sm, 16)
            elif needed[i]:
                sm, v = sem_for(('e', e), ms[i])
                ins.then_inc(sm, 1)
        sp = self.engs['sp']
        for s in final_streams:
            if ('s', s) in cnt:
                sm, v = sem_for(('s', s), cnt[('s', s)])
                sp.wait_ge(sm, v)
        self.stats = dict(n_ops=n, n_sems=len(sems), counts={str(k): v for k, v in cnt.items()})


class WRing:
    def __init__(self, P, slots, schedule=None):
        self.P = P
        self.slots = slots
        self.schedule = schedule
        self.record = [] if schedule is None else None
        self.n_get = 0
        self.n_issued = 0

    def _issue_upto(self, j):
        while self.n_issued <= j and self.n_issued < len(self.schedule):
            k = self.n_issued
            src, nfree, cache, mode, ckey = self.schedule[k]
            s = k % NSLOT
            dst = self.slots[s][:, 0:nfree]
            if mode == 'read':
                self.P.add('pool', (lambda E, dst=dst, cache=cache: E.dma_start(out=dst, in_=cache)),
                           r=[ckey], w=['wslot%d' % s], stream='w%d' % s)
            else:
                self.P.add('pool', (lambda E, dst=dst, src=src: E.dma_start(out=dst, in_=src)),
                           w=['wslot%d' % s], stream='w%d' % s)
                if mode == 'write':
                    self.P.add('sp', (lambda E, dst=dst, cache=cache: E.dma_start(out=cache, in_=dst)),
                               r=['wslot%d' % s], w=[ckey], stream='wb%d' % s)
            self.n_issued += 1

    def get(self, src, nfree, hold=1, cache=None, mode=None, ckey=None):
        k = self.n_get
        self.n_get += 1
        if self.record is not None:
            self.record.append((src, nfree, cache, mode, ckey))
            return self.slots[k % NSLOT][:, 0:nfree], 'wslot%d' % (k % NSLOT)
        self._issue_upto(k - hold + NSLOT)
        return self.slots[k % NSLOT][:, 0:nfree], 'wslot%d' % (k % NSLOT)


def build_program(debug=False):
    nc = bass.Bass("TRN2", target_bir_lowering=False)

    def din(name, shape, dt=F32):
        return nc.dram_tensor(name, list(shape), dt, kind="ExternalInput").ap()

    def dout(name, shape, dt=F32):
        return nc.dram_tensor(name, list(shape), dt, kind="ExternalOutput").ap()

    xA = din("xA", [128, KC, NA])
    xR = din("xR", [NSEG, 128, KC, TS])
    wblk = din("wblk", [NBLK, 128, KC * 512])
    wout = din("wout", [64, 128, FC * 128])
    wgate = din("wgate", [128, KC * 8])
    gains = din("gains", [128, 8, KC])
    smallc = din("smallc", [1, 8 + 64 + 32 + 32])
    qg = din("qg", [128, 1])
    ctab = din("ctab", [128, 12, 128])
    stC = din("stC", [16, 4, 256, 512])
    stn = din("stn", [16, 128, 8])
    stm = din("stm", [16, 4])
    ckm = din("ckm", [16, 16, 256])
    cvm = din("cvm", [16, 16, 256])
    ckw = din("ckw", [16, 128, 256])
    cvw = din("cvw", [16, 128, 256])

    wbf_in = nc.dram_tensor("wbf_in", [88, 128, KC * 512], BF16, kind="Internal").ap()
    wbf_out = nc.dram_tensor("wbf_out", [64, 128, FC * 128], BF16, kind="Internal").ap()

    yA = dout("yA", [128, KC, NA])
    yR = dout("yR", [NSEG, 128, KC, TS])
    oCp = dout("oCp", [4, 256, 512])
    onp = dout("onp", [128, 8])
    omp = dout("omp", [1, 4])
    okmp = dout("okmp", [16, 256])
    ovmp = dout("ovmp", [16, 256])
    okwp = dout("okwp", [128, 256])
    ovwp = dout("ovwp", [128, 256])
    oCs = dout("oCs", [16, 4, 256, 512])
    ons = dout("ons", [16, 128, 8])
    oms = dout("oms", [16, 4])
    okws = dout("okws", [16, 128, 256])
    ovws = dout("ovws", [16, 128, 256])

    with ExitStack() as st:
        def sb(name, shape, dt):
            return st.enter_context(nc.sbuf_tensor(name, list(shape), dt))

        def pst(name, shape, dt):
            return st.enter_context(nc.psum_tensor(name, list(shape), dt))

        hT = sb("hT", [128, KC, NT], F32)
        xn = sb("xn", [128, KC, NT], BF16)
        slots = [sb("wslot%d" % i, [128, KC * 512], BF16) for i in range(NSLOT)]
        scrB = sb("scrB", [128, 21312], BF16)
        scrF = sb("scrF", [128, 2880], F32)
        nSall = sb("nSall", [128, 16, 2, 4], F32)
        Cst = sb("Cst", [128, 4, 2, 512], F32)
        Cbf = sb("Cbf", [128, 4, 2, 512], BF16)
        nst = sb("nst", [128, 2, 4], F32)
        nbf = sb("nbf", [128, 2, 4], BF16)
        mbc = sb("mbc", [128, 4], F32)
        ctf = sb("ctf", [128, 12, 128], F32)
        identb = sb("identb", [128, 128], BF16)
        onesb = sb("onesb", [128, 128], BF16)
        onesD = sb("onesD", [128, 128], BF16)
        bd64 = sb("bd64", [128, 128], BF16)
        onesf = sb("onesf", [128, 128], F32)
        epsb = sb("epsb", [128, 1], F32)
        gn = sb("gn", [128, 8, KC], F32)
        wg = sb("wg", [128, KC * 8], BF16)
        smc = sb("smc", [128, 8 + 64 + 32 + 32], F32)
        esink = sb("esink", [128, 32], F32)
        qgt = sb("qgt", [128, 1], F32)
        kTz = sb("kTz", [128, 8, 5 * 128], BF16)
        kTm = sb("kTm", [128, 8, 16], BF16)
        vext = sb("vext", [128, 6, 4, 65], BF16)
        knA = sb("knA", [128, 256], F32)
        vA = sb("vA", [128, 256], F32)

        psA = [pst("psA%d" % i, [128, 512], F32) for i in range(4)]
        psB = [pst("psB%d" % i, [128, 512], F32) for i in range(3)]
        psT = pst("psT", [128, 1024], BF16)

        identf = ctf[:, 0, :]
        Umat = ctf[:, 1, :]
        maskC = ctf[:, 2, :]
        maskT = ctf[:, 3, :]
        selL = {128: ctf[:, 4, :], 16: ctf[:, 5, :], 4: ctf[:, 6, :]}
        Rcur = ctf[:, 8, :]
        Rprev = ctf[:, 9, :]
        Rmeta0 = ctf[:, 10, :]
        RwinS = ctf[:, 11, 0:4]
        R2S = ctf[:, 11, 4:8]
        bgate_bc = smc[:, 0:8]
        kg_bc = smc[:, 8:8 + 64]
        nslope128 = smc[:, 8 + 64 + 32: 8 + 64 + 64]

        gen_state = {}

        def gen(P, W):
            cntr = [0]

            def rot_psA():
                i = cntr[0] % 4
                cntr[0] += 1
                return psA[i], 'psA%d' % i

            P.add('sp', lambda E: E.dma_start(out=ctf[:], in_=ctab[:, :, :]), w=['ctf'], stream='ld0')
            P.add('sp', lambda E: E.dma_start(out=gn[:], in_=gains[:, :, :]), w=['gn'], stream='ld1')
            P.add('sp', lambda E: E.dma_start(out=smc[:], in_=smallc.partition_broadcast(128)), w=['smc'], stream='ld2')
            P.add('sp', lambda E: E.dma_start(out=qgt[:], in_=qg[:, :]), w=['qgt'], stream='ld3')
            P.add('pool', lambda E: E.dma_start(out=wg[:], in_=wgate[:, :]), w=['wg'], stream='ldw')
            P.add('dve', lambda E: E.memset(onesb[:], 1.0), w=['onesb'])
            P.add('dve', lambda E: E.memset(onesD[:], 1.0 / D), w=['onesD'])
            P.add('dve', lambda E: E.memset(onesf[:], 1.0), w=['onesf'])
            P.add('dve', lambda E: E.memset(epsb[:], EPS), w=['epsb'])
            P.add('dve', lambda E: E.memset(Cst[:], 0.0), w=['C0', 'C1', 'C2', 'C3'])
            P.add('dve', lambda E: E.memset(Cbf[:], 0.0), w=['Cb0', 'Cb1', 'Cb2', 'Cb3'])
            P.add('dve', lambda E: E.memset(nst[:], 0.0), w=['n0', 'n1', 'n2', 'n3'])
            P.add('dve', lambda E: E.memset(nbf[:], 0.0), w=['nb0', 'nb1', 'nb2', 'nb3'])
            P.add('dve', lambda E: E.memset(mbc[:], 0.0), w=['mbc'])
            P.add('dve', lambda E: E.memset(kTz[:], 0.0), w=['kTz'])
            P.add('dve', lambda E: E.memset(kTm[:], 0.0), w=['kTm'])
            P.add('dve', lambda E: E.memset(vext[:], 1.0), w=['vext'])
            P.add('dve', lambda E: E.tensor_copy(out=identb[:], in_=identf), r=['ctf'], w=['identb'])
            P.add('dve', lambda E: E.tensor_copy(out=bd64[:], in_=ctf[:, 7, :]), r=['ctf'], w=['bd64'])
            P.add('act', lambda E: E.activation(out=esink[:], in_=smc[:, 8 + 64: 8 + 64 + 32], func=AF.Exp),
                  r=['smc'], w=['esink'])

            def rmsnorm(gi, blocks, reuse=False):
                sq = scrB[:, 0:KC * NT].rearrange("p (c t) -> p c t", t=NT)
                rstd = scrF[:, 0:NT]
                for (b0, bn) in blocks:
                    bk = 'b%d' % b0
                    if reuse:
                        for c in range(KC):
                            P.add('dve', lambda E, c=c, b0=b0, bn=bn: E.scalar_tensor_tensor(
                                out=xn[:, c, b0:b0 + bn], in0=hT[:, c, b0:b0 + bn], scalar=gn[:, gi, c:c + 1], in1=rstd[:, b0:b0 + bn],
                                op0=ALU.mult, op1=ALU.mult), r=['hT' + bk, 'gn', 'rstd' + bk], w=['xn' + bk])
                        continue
                    P.add('act', lambda E, b0=b0, bn=bn: E.activation(out=sq[:, :, b0:b0 + bn], in_=hT[:, :, b0:b0 + bn], func=AF.Square),
                          r=['hT' + bk], w=['sq' + bk])
                    ps, psn = rot_psA()

                    def mmf(E, b0=b0, bn=bn, ps=ps):
                        for c in range(KC):
                            ins = E.matmul(ps[:, 0:bn], lhsT=onesD[:], rhs=sq[:, c, b0:b0 + bn], start=(c == 0), stop=(c == KC - 1))
                        return ins
                    P.add('pe', mmf, r=['onesD', 'sq' + bk], w=[psn])
                    P.add('act', lambda E, b0=b0, bn=bn, ps=ps: E.activation(out=rstd[:, b0:b0 + bn], in_=ps[:, 0:bn], func=AF.Sqrt, bias=epsb[:], scale=1.0),
                          r=[psn, 'epsb'], w=['rstd' + bk])
                    P.add('dve', lambda E, b0=b0, bn=bn: E.reciprocal(rstd[:, b0:b0 + bn], rstd[:, b0:b0 + bn]), r=['rstd' + bk], w=['rstd' + bk])
                    for c in range(KC):
                        P.add('dve', lambda E, c=c, b0=b0, bn=bn: E.scalar_tensor_tensor(
                            out=xn[:, c, b0:b0 + bn], in0=hT[:, c, b0:b0 + bn], scalar=gn[:, gi, c:c + 1], in1=rstd[:, b0:b0 + bn],
                            op0=ALU.mult, op1=ALU.mult), r=['hT' + bk, 'gn', 'rstd' + bk], w=['xn' + bk])

            def ffn(f, blocks, seg, stream_io=False):
                cmode = 'write' if seg == 0 else 'read'
                hidA = scrB[:, 0:36 * NT].rearrange("p (c t) -> p c t", t=NT)
                hidB = scrF[:, 512:512 + 4 * NT].bitcast(BF16).rearrange("p (c t) -> p c t", t=NT)

                def hid_ap(fc, b0, bn):
                    return hidA[:, fc, b0:b0 + bn] if fc < 36 else hidB[:, fc - 36, b0:b0 + bn]
                sg = scrF[:, 0:2 * 256].bitcast(BF16).rearrange("p (a t) -> p a t", a=2)
                it = 0
                for blk in range(22):
                    wv, wr = W.get(wblk[B_FFN + 22 * f + blk], KC * 512, cache=wbf_in[22 * f + blk], mode=cmode, ckey='wbi%d' % (22 * f + blk))
                    wv = wv.rearrange("p (k n) -> p k n", n=512)
                    for j in range(2):
                        fc = 2 * blk + j
                        for (b0, bn) in blocks:
                            bk = 'b%d' % b0
                            pg, pgn = rot_psA()
                            pu, pun = rot_psA()

                            def mmf(E, wv=wv, j=j, b0=b0, bn=bn, pg=pg, pu=pu):
                                for c in range(KC):
                                    E.matmul(pg[:, 0:bn], lhsT=wv[:, c, 128 * j:128 * j + 128], rhs=xn[:, c, b0:b0 + bn], start=(c == 0), stop=(c == KC - 1))
                                for c in range(KC):
                                    ins = E.matmul(pu[:, 0:bn], lhsT=wv[:, c, 256 + 128 * j:256 + 128 * j + 128], rhs=xn[:, c, b0:b0 + bn], start=(c == 0), stop=(c == KC - 1))
                                return ins
                            P.add('pe', mmf, r=[wr, 'xn' + bk], w=[pgn, pun])
                            sgi = it % 2
                            it += 1
                            P.add('act', lambda E, pg=pg, bn=bn, sgi=sgi: E.activation(out=sg[:, sgi, 0:bn], in_=pg[:, 0:bn], func=AF.Silu),
                                  r=[pgn], w=['sg%d' % sgi])
                            P.add('dve', lambda E, pu=pu, bn=bn, sgi=sgi, fc=fc, b0=b0: E.tensor_tensor(
                                out=hid_ap(fc, b0, bn), in0=pu[:, 0:bn], in1=sg[:, sgi, 0:bn], op=ALU.mult),
                                r=[pun, 'sg%d' % sgi], w=['hid%d' % fc + bk])
                for oc in range(KC):
                    wv, wr = W.get(wout[16 * f + oc], FC * 128, cache=wbf_out[16 * f + oc], mode=cmode, ckey='wbo%d' % (16 * f + oc))
                    wv = wv.rearrange("p (k n) -> p k n", n=128)
                    for (b0, bn) in blocks:
                        bk = 'b%d' % b0
                        ps, psn = rot_psA()

                        def mmf(E, wv=wv, b0=b0, bn=bn, ps=ps):
                            for c in range(FC):
                                ins = E.matmul(ps[:, 0:bn], lhsT=wv[:, c, :], rhs=hid_ap(c, b0, bn), start=(c == 0), stop=(c == FC - 1))
                            return ins
                        P.add('pe', mmf, r=[wr] + ['hid%d' % c + bk for c in range(FC)], w=[psn])
                        P.add('dve', lambda E, oc=oc, b0=b0, bn=bn, ps=ps: E.scalar_tensor_tensor(
                            out=hT[:, oc, b0:b0 + bn], in0=ps[:, 0:bn], scalar=0.5, in1=hT[:, oc, b0:b0 + bn], op0=ALU.mult, op1=ALU.add),
                            r=[psn, 'hT' + bk], w=['hT' + bk])
                        if stream_io and b0 == NA:
                            P.add('sp', lambda E, oc=oc, seg=seg: E.dma_start(out=yR[seg, :, oc, :], in_=hT[:, oc, NA:NT]), r=['hT' + bk], stream='sty1')
                            if seg + 1 < NSEG:
                                P.add('sp', lambda E, oc=oc, seg=seg: E.dma_start(out=hT[:, oc, NA:NT], in_=xR[seg + 1, :, oc, :]), w=['hT' + bk], stream='ldx1')

            def out_proj(bbase, src, srcres, blocks):
                for jb in range(4):
                    wv, wr = W.get(wblk[bbase + jb], KC * 512)
                    wv = wv.rearrange("p (k n) -> p k n", n=512)
                    for j in range(4):
                        oc = 4 * jb + j
                        for (b0, bn) in blocks:
                            bk = 'b%d' % b0
                            ps, psn = rot_psA()

                            def mmf(E, wv=wv, j=j, b0=b0, bn=bn, ps=ps):
                                for c in range(KC):
                                    ins = E.matmul(ps[:, 0:bn], lhsT=wv[:, c, 128 * j:128 * j + 128], rhs=src[:, c, b0:b0 + bn], start=(c == 0), stop=(c == KC - 1))
                                return ins
                            P.add('pe', mmf, r=[wr, srcres + bk], w=[psn])
                            P.add('dve', lambda E, oc=oc, b0=b0, bn=bn, ps=ps: E.tensor_tensor(
                                out=hT[:, oc, b0:b0 + bn], in0=ps[:, 0:bn], in1=hT[:, oc, b0:b0 + bn], op=ALU.add),
                                r=[psn, 'hT' + bk], w=['hT' + bk])

            actT = scrB[:, 0:KC * NT].rearrange("p (c t) -> p c t", t=NT)
            o1 = KC * NT
            qTh = scrB[:, o1:o1 + 2 * NT].rearrange("p (c t) -> p c t", t=NT)
            kTh = scrB[:, o1 + 2 * NT:o1 + 4 * NT].rearrange("p (c t) -> p c t", t=NT)
            o2 = o1 + 4 * NT
            vtk = scrB[:, o2:o2 + 512]
            kwt = scrB[:, o2 + 512:o2 + 768]
            STb = scrB[:, o2 + 768:o2 + 896]
            hnb = scrB[:, o2 + 896:o2 + 1408]
            Csb = scrB[:, o2 + 1408:o2 + 2432].rearrange("p (c v) -> p c v", v=512)
            nsb = scrB[:, o2 + 2432:o2 + 2440].rearrange("p (c h) -> p c h", h=4)
            WtBig = scrB[:, o2 + 2560:o2 + 2560 + 4 * 512].rearrange("p (k h l) -> p k h l", h=4, l=128)
            WtSm = scrB[:, o2 + 2560 + 2048:o2 + 2560 + 2048 + 17 * 64].rearrange("p (k h l) -> p k h l", h=4, l=16)

            def Wt_ap(ci, L, h):
                return WtBig[0:L, ci - 17, h, 0:L] if ci >= 17 else WtSm[0:L, ci, h, 0:L]
            gpre = scrF[:, 0:8]
            t1 = scrF[:, 8:16]
            ee = scrF[:, 16:20]
            spl = scrF[:, 20:24]
            gmax = scrF[:, 24:28]
            bgg = scrF[:, 28:36]
            glb = scrF[:, 36:40]
            tmp4 = scrF[:, 40:44]
            den2 = scrF[:, 44:46]
            den = scrF[:, 46:47]
            rden = scrF[:, 47:48]
            ssq = scrF[:, 48:49]
            scl = scrF[:, 49:50]
            mS = scrF[:, 52:56]
            a_all = scrF[:, 64:64 + 84].rearrange("p (k h) -> p k h", h=4)
            g_all = scrF[:, 148:148 + 84].rearrange("p (k h) -> p k h", h=4)
            wi_all = scrF[:, 232:232 + 84].rearrange("p (k h) -> p k h", h=4)
            ws_all = scrF[:, 316:316 + 84].rearrange("p (k h) -> p k h", h=4)
            dc_all = scrF[:, 400:400 + 84].rearrange("p (k h) -> p k h", h=4)
            em_all = scrF[:, 484:484 + 84].rearrange("p (k h) -> p k h", h=4)
            diag = scrF[:, 576:576 + 512].rearrange("p (h l) -> p h l", l=128)
            tmpA = scrF[:, 1088:1088 + 512].rearrange("p (h l) -> p h l", l=128)
            numI = scrF[:, 576:576 + 512]
            numT = scrF[:, 1088:1088 + 512]
            CsF2 = scrF[:, 1600:1600 + 1024].rearrange("p (c v) -> p c v", v=512)
            Csb2 = scrB[:, 19584:19584 + 1024].rearrange("p (c v) -> p c v", v=512)
            nsb2 = scrB[:, o2 + 2440:o2 + 2448].rearrange("p (c h) -> p c h", h=4)
            kA_tok = scrB[:, 20608:20608 + 256]
            vA_tok = scrF[:, 2624:2880].bitcast(BF16)
            CsF = scrB[:, 17536:17536 + 2048].bitcast(F32).rearrange("p (c v) -> p c v", v=512)

            gpre0, t10, ee0, spl0, gmax0, bgg0, glb0, tmp40, diag0, tmpA0 = gpre, t1, ee, spl, gmax, bgg, glb, tmp4, diag, tmpA

            def mlstm_gates(ci, L, t0, mprev, mres, mout, moutres, gs=0, PP=None):
                cr = 'ck%d' % ci
                PP = P if PP is None else PP
                if gs == 0:
                    pb0, pb1, pb2 = psB[0], psB[1], psB[2]
                    nb0, nb1, nb2 = 'psB0', 'psB1', 'psB2'
                    gpre, t1, ee, spl, gmax, bgg, glb, tmp4, diag, tmpA = gpre0, t10, ee0, spl0, gmax0, bgg0, glb0, tmp40, diag0, tmpA0
                else:
                    pb0, pb1, pb2 = psA[0], psA[1], psA[2]
                    nb0, nb1, nb2 = 'psA0', 'psA1', 'psA2'
                    sm1 = scrF[:, 2624:2672]
                    gpre, t1, ee, spl, gmax, bgg, glb, tmp4 = sm1[:, 0:8], sm1[:, 8:16], sm1[:, 16:20], sm1[:, 20:24], sm1[:, 24:28], sm1[:, 28:36], sm1[:, 36:40], sm1[:, 40:44]
                    diag = scrF[:, 1600:2112].rearrange("p (h l) -> p h l", l=128)
                    tmpA = scrF[:, 2112:2624].rearrange("p (h l) -> p h l", l=128)
                sfx = '' if gs == 0 else '_g1'

                def mm_g(E):
                    for c in range(KC):
                        ins = E.matmul(pb0[0:L, 0:8], lhsT=xn[:, c, t0:t0 + L], rhs=wg[:, 8 * c:8 * c + 8], start=(c == 0), stop=(c == KC - 1))
                    return ins
                PP.add('pe', mm_g, r=['xnall', 'wg'], w=[nb0])
                PP.add('dve', lambda E: E.tensor_tensor(out=gpre[0:L, :], in0=pb0[0:L, 0:8], in1=bgate_bc[0:L, :], op=ALU.add), r=[nb0, 'smc'], w=['gpre' + sfx])
                PP.add('act', lambda E: E.activation(out=t1[0:L, :], in_=gpre[0:L, :], func=AF.Tanh, scale=1.0 / 15.0), r=['gpre' + sfx], w=['t1' + sfx])
                PP.add('act', lambda E: E.activation(out=ee[0:L, :], in_=t1[0:L, 4:8], func=AF.Exp, scale=-15.0), r=['t1' + sfx], w=['ee' + sfx])
                PP.add('act', lambda E: E.activation(out=spl[0:L, :], in_=ee[0:L, :], func=AF.Ln, bias=onesf[0:L, 0:1], scale=1.0), r=['ee' + sfx, 'onesf'], w=['spl' + sfx])
                PP.add('pe', lambda E: E.matmul(pb1[0:L, 0:4], lhsT=Umat[0:L, 0:L], rhs=spl[0:L, :], start=True, stop=True), r=['ctf', 'spl' + sfx], w=[nb1])
                PP.add('dve', lambda E: E.scalar_tensor_tensor(out=a_all[0:L, ci, :], in0=t1[0:L, 0:4], scalar=15.0, in1=pb1[0:L, 0:4], op0=ALU.mult, op1=ALU.add),
                      r=['t1' + sfx, nb1], w=['a' + cr])
                for h in range(4):
                    PP.add('dve', lambda E, h=h: E.tensor_scalar(diag[0:L, h, 0:L], identf[0:L, 0:L], a_all[0:L, ci, h:h + 1], None, ALU.mult), r=['ctf', 'a' + cr], w=['diag' + sfx])
                pb2v = pb2[:, :].rearrange("p (h l) -> p h l", l=128)
                PP.add('pe', lambda E: E.matmul(pb2v[0:L, :, 0:L], lhsT=onesf[0:L, 0:L], rhs=diag[0:L, :, 0:L], start=True, stop=True), r=['onesf', 'diag' + sfx], w=[nb2])
                for h in range(4):
                    PP.add('dve', lambda E, h=h: E.tensor_tensor(out=tmpA[0:L, h, 0:L], in0=pb2v[0:L, h, 0:L], in1=maskC[0:L, 0:L], op=ALU.add), r=[nb2, 'ctf'], w=['tmpA' + sfx])
                PP.add('dve', lambda E: E.tensor_reduce(out=gmax[0:L, :], in_=tmpA[0:L, :, 0:L], axis=AX.X, op=ALU.max), r=['tmpA' + sfx], w=['gmax' + sfx])
                PP.add('dve', lambda E: E.tensor_tensor(out=g_all[0:L, ci, :], in0=gmax[0:L, :], in1=mprev[0:L, :], op=ALU.max), r=['gmax' + sfx, mres], w=['g' + cr])
                for h in range(4):
                    PP.add('dve', lambda E, h=h: E.tensor_scalar(diag[0:L, h, 0:L], identf[0:L, 0:L], g_all[0:L, ci, h:h + 1], None, ALU.mult), r=['ctf', 'g' + cr], w=['diag' + sfx])
                PP.add('pe', lambda E: E.matmul(pb2v[0:L, :, 0:L], lhsT=onesf[0:L, 0:L], rhs=diag[0:L, :, 0:L], start=True, stop=True), r=['onesf', 'diag' + sfx], w=[nb2])
                for h in range(4):
                    PP.add('dve', lambda E, h=h: E.scalar_tensor_tensor(out=tmpA[0:L, h, 0:L], in0=pb2v[0:L, h, 0:L], scalar=-1.0, in1=maskT[0:L, 0:L], op0=ALU.mult, op1=ALU.add),
                          r=[nb2, 'ctf'], w=['tmpA' + sfx])
                for h in range(4):
                    PP.add('act', lambda E, h=h: E.activation(out=Wt_ap(ci, L, h), in_=tmpA[0:L, h, 0:L], func=AF.Exp, bias=a_all[0:L, ci, h:h + 1], scale=1.0),
                          r=['tmpA' + sfx, 'a' + cr], w=['Wt' + cr])
                PP.add('dve', lambda E: E.tensor_tensor(out=bgg[0:L, 0:4], in0=g_all[0:L, ci, :], in1=pb1[0:L, 0:4], op=ALU.subtract), r=['g' + cr, nb1], w=['bgg' + sfx])
                PP.add('dve', lambda E: E.tensor_copy(out=bgg[0:L, 4:8], in_=g_all[0:L, ci, :]), r=['g' + cr], w=['bgg' + sfx])
                PP.add('pe', lambda E: E.matmul(pb0[:, 0:8], lhsT=selL[L][0:L, :], rhs=bgg[0:L, :], start=True, stop=True), r=['ctf', 'bgg' + sfx], w=[nb0])
                PP.add('dve', lambda E: E.tensor_copy(out=glb[:, :], in_=pb0[:, 4:8]), r=[nb0], w=['glb' + sfx])
                PP.add('dve', lambda E: E.tensor_tensor(out=tmp4[0:L, :], in0=mprev[0:L, :], in1=g_all[0:L, ci, :], op=ALU.subtract), r=[mres, 'g' + cr], w=['tmp4' + sfx])
                PP.add('act', lambda E: E.activation(out=wi_all[0:L, ci, :], in_=tmp4[0:L, :], func=AF.Exp), r=['tmp4' + sfx], w=['wi' + cr])
                PP.add('dve', lambda E: E.tensor_tensor(out=tmp4[0:L, :], in0=a_all[0:L, ci, :], in1=glb[0:L, :], op=ALU.subtract), r=['a' + cr, 'glb' + sfx], w=['tmp4' + sfx])
                PP.add('act', lambda E: E.activation(out=ws_all[0:L, ci, :], in_=tmp4[0:L, :], func=AF.Exp), r=['tmp4' + sfx], w=['ws' + cr])
                PP.add('dve', lambda E: E.tensor_tensor(out=tmp4[:, :], in0=mprev[:, :], in1=glb[:, :], op=ALU.subtract), r=[mres, 'glb' + sfx], w=['tmp4' + sfx])
                PP.add('act', lambda E: E.activation(out=dc_all[:, ci, :], in_=tmp4[:, :], func=AF.Exp), r=['tmp4' + sfx], w=['dc' + cr])
                PP.add('act', lambda E: E.activation(out=em_all[0:L, ci, :], in_=bgg[0:L, 0:4], func=AF.Exp, scale=-1.0), r=['bgg' + sfx], w=['em' + cr])
                PP.add('dve', lambda E: E.tensor_copy(out=mout, in_=pb0[:, 0:4]), r=[nb0, mres], w=[moutres])

            vtk1 = scrB[:, 17536:17536 + 512]
            kwt1 = scrB[:, 18048:18048 + 256]
            vtks = [vtk, vtk1]
            kwts = [kwt, kwt1]

            def mlstm_chunk_front(h, ci, L, t0, wqk, wqkr, wv_, wvr, par):
                cr = 'ck%d' % ci
                vtk = vtks[par]
                kwt = kwts[par]
                xw = ['CS0'] if par == 1 else []
                pv, pvn = rot_psA()
                pk, pkn = rot_psA()
                if t0 < NA:
                    P.add('pe', lambda E: E.matmul(pv[0:L, 0:512], lhsT=identb[0:NA, t0:t0 + L], rhs=vA_tok[0:NA, :], start=True, stop=True), r=['vA_tok', 'identb'], w=[pvn])
                    P.add('pe', lambda E: E.matmul(pk[0:L, 0:256], lhsT=identb[0:NA, t0:t0 + L], rhs=kA_tok[0:NA, :], start=True, stop=True), r=['kA_tok', 'identb'], w=[pkn])
                else:
                    def mm_v(E):
                        for c in range(KC):
                            ins = E.matmul(pv[0:L, 0:512], lhsT=xn[:, c, t0:t0 + L], rhs=wv_[:, c, :], start=(c == 0), stop=(c == KC - 1))
                        return ins
                    P.add('pe', mm_v, r=['xnall', wvr], w=[pvn])

                    def mm_k(E):
                        for c in range(KC):
                            ins = E.matmul(pk[0:L, 0:256], lhsT=xn[:, c, t0:t0 + L], rhs=wqk[:, c, 256:512], start=(c == 0), stop=(c == KC - 1))
                        return ins
                    P.add('pe', mm_k, r=['xnall', wqkr], w=[pkn])
                P.add('act', lambda E: E.activation(out=vtk[0:L, :], in_=pv[0:L, 0:512], func=AF.Copy), r=[pvn], w=['vtk%d' % par] + xw)
                P.add('dve', lambda E: E.tensor_scalar(kwt[0:L, :], pk[0:L, 0:256], ws_all[0:L, ci, h:h + 1], 1.0 / 16.0, ALU.mult, ALU.mult), r=[pkn, 'ws' + cr], w=['kwt%d' % par] + xw)

            def mlstm_head_chunk(h, ci, L, t0, wqk, wqkr, wv_, wvr, Cf, Cb, nf, nb, cres, par=0, front_done=False, mid_hook=None,
                                 cast_back=True, hook_cast=None, hook_front=None):
                cr = 'ck%d' % ci
                if not front_done:
                    mlstm_chunk_front(h, ci, L, t0, wqk, wqkr, wv_, wvr, par)
                vtk = vtks[par]
                kwt = kwts[par]
                vtkr = 'vtk%d' % par
                kwtr = 'kwt%d' % par
                pb0, pb1, pb2 = psB[0], psB[1], psB[2]

                def mm_s(E):
                    for c in range(2):
                        ins = E.matmul(pb0[0:L, 0:L], lhsT=kTh[:, c, t0:t0 + L], rhs=qTh[:, c, t0:t0 + L], start=(c == 0), stop=(c == 1))
                    return ins
                P.add('pe', mm_s, r=['qkT'], w=['psB0'])
                pc0, pc0n = rot_psA()
                pc1, pc1n = rot_psA()

                def mm_c(E):
                    E.matmul(pc0[:, 0:512], lhsT=kwt[0:L, 0:128], rhs=vtk[0:L, :], start=True, stop=True)
                    ins = E.matmul(pc1[:, 0:512], lhsT=kwt[0:L, 128:256], rhs=vtk[0:L, :], start=True, stop=True)
                    return ins
                P.add('pe', mm_c, r=[kwtr, vtkr], w=[pc0n, pc1n])

                def mm_n(E):
                    E.matmul(pb2[:, 0:1], lhsT=kwt[0:L, 0:128], rhs=onesb[0:L, 0:1], start=True, stop=True)
                    ins = E.matmul(pb2[:, 1:2], lhsT=kwt[0:L, 128:256], rhs=onesb[0:L, 0:1], start=True, stop=True)
                    return ins
                P.add('pe', mm_n, r=[kwtr, 'onesb'], w=['psB2'])
                P.add('dve', lambda E: E.tensor_tensor(out=STb[0:L, 0:L], in0=pb0[0:L, 0:L], in1=Wt_ap(ci, L, h), op=ALU.mult), r=['psB0', 'Wt' + cr], w=['STb'])
                pn, pnn = rot_psA()
                P.add('pe', lambda E: E.matmul(pn[0:L, 0:512], lhsT=STb[0:L, 0:L], rhs=vtk[0:L, :], start=True, stop=True), r=['STb', vtkr], w=[pnn])

                def mm_d(E):
                    E.matmul(pb1[0:L, 0:1], lhsT=STb[0:L, 0:L], rhs=onesb[0:L, 0:1], start=True, stop=True)
                    for c in range(2):
                        ins = E.matmul(pb1[0:L, 1:2], lhsT=qTh[:, c, t0:t0 + L], rhs=nb[:, c:c + 1], start=(c == 0), stop=(c == 1))
                    return ins
                P.add('pe', mm_d, r=['STb', 'qkT', 'nb' + cres, 'onesb'], w=['psB1'])
                pi, pin = rot_psA()

                def mm_i(E):
                    for c in range(2):
                        ins = E.matmul(pi[0:L, 0:512], lhsT=qTh[:, c, t0:t0 + L], rhs=Cb[:, c, :], start=(c == 0), stop=(c == 1))
                    return ins
                P.add('pe', mm_i, r=['qkT', 'Cb' + cres], w=[pin])
                P.add('dve', lambda E: E.scalar_tensor_tensor(out=Cf[:, 0, :], in0=Cf[:, 0, :], scalar=dc_all[:, ci, h:h + 1], in1=pc0[:, 0:512], op0=ALU.mult, op1=ALU.add),
                      r=['C' + cres, 'dc' + cr, pc0n, 'Cb' + cres], w=['C' + cres])
                P.add('dve', lambda E: E.scalar_tensor_tensor(out=Cf[:, 1, :], in0=Cf[:, 1, :], scalar=dc_all[:, ci, h:h + 1], in1=pc1[:, 0:512], op0=ALU.mult, op1=ALU.add),
                      r=['C' + cres, 'dc' + cr, pc1n, 'Cb' + cres], w=['C' + cres])
                P.add('dve', lambda E: E.scalar_tensor_tensor(out=nf, in0=nf, scalar=dc_all[:, ci, h:h + 1], in1=pb2[:, 0:2], op0=ALU.mult, op1=ALU.add),
                      r=['n' + cres, 'dc' + cr, 'psB2'], w=['n' + cres])
                if mid_hook is not None:
                    mid_hook()
                P.add('act', lambda E: E.activation(out=numI[0:L, :], in_=pn[0:L, 0:512], func=AF.Copy), r=[pnn], w=['diag'])
                P.add('dve', lambda E: E.scalar_tensor_tensor(out=numT[0:L, :], in0=pi[0:L, 0:512], scalar=wi_all[0:L, ci, h:h + 1], in1=numI[0:L, :], op0=ALU.mult, op1=ALU.add),
                      r=[pin, 'wi' + cr, 'diag'], w=['tmpA'])
                P.add('act', lambda E: E.activation(out=den2[0:L, :], in_=pb1[0:L, 0:2], func=AF.Copy), r=['psB1'], w=['den2'])
                if hook_cast is not None:
                    hook_cast()
                P.add('dve', lambda E: E.scalar_tensor_tensor(out=den[0:L, :], in0=den2[0:L, 1:2], scalar=wi_all[0:L, ci, h:h + 1], in1=den2[0:L, 0:1], op0=ALU.mult, op1=ALU.add),
                      r=['den2', 'wi' + cr], w=['den'])
                P.add('dve', lambda E: E.scalar_tensor_tensor(out=den2[0:L, 0:1], in0=den[0:L, :], scalar=-1.0, in1=den[0:L, :], op0=ALU.mult, op1=ALU.max), r=['den'], w=['den2'])
                P.add('dve', lambda E: E.tensor_scalar(den[0:L, :], den2[0:L, 0:1], em_all[0:L, ci, h:h + 1], None, ALU.max), r=['den2', 'em' + cr], w=['den'])
                P.add('dve', lambda E: E.reciprocal(rden[0:L, :], den[0:L, :]), r=['den'], w=['rden'])
                P.add('act', lambda E: E.activation(out=numI[0:L, :], in_=numT[0:L, :], func=AF.Square, scale=rden[0:L, 0:1], accum_out=ssq[0:L, :]), r=['tmpA', 'rden', 'diag'], w=['diag', 'ssq'])
                P.add('act', lambda E: E.activation(out=ssq[0:L, :], in_=ssq[0:L, :], func=AF.Sqrt, bias=epsb[0:L, :], scale=1.0 / 512.0), r=['ssq', 'epsb'], w=['ssq'])
                if cast_back:
                    P.add('act', lambda E: E.activation(out=Cb, in_=Cf, func=AF.Copy), r=['C' + cres], w=['Cb' + cres])
                    P.add('act', lambda E: E.activation(out=nb, in_=nf, func=AF.Copy), r=['n' + cres], w=['nb' + cres])
                if hook_front is not None:
                    hook_front()
                P.add('dve', lambda E: E.reciprocal(ssq[0:L, :], ssq[0:L, :]), r=['ssq'], w=['ssq'])
                P.add('dve', lambda E: E.tensor_tensor(out=scl[0:L, :], in0=ssq[0:L, :], in1=rden[0:L, :], op=ALU.mult), r=['ssq', 'rden'], w=['scl'])
                P.add('dve', lambda E: E.tensor_scalar(hnb[0:L, :], numT[0:L, :], scl[0:L, 0:1], None, ALU.mult), r=['tmpA', 'scl'], w=['hnb'])
                psTv = psT[:, 0:512].rearrange("p (j l) -> p j l", l=128)

                def mm_t(E):
                    for j in range(4):
                        ins = E.transpose(psTv[:, j, 0:L], hnb[0:L, 128 * j:128 * j + 128], identb[0:L, 0:L])
                    return ins
                P.add('pe', mm_t, r=['hnb', 'identb'], w=['psT'])
                P.add('dve', lambda E: E.tensor_tensor(out=actT[:, 4 * h:4 * h + 4, t0:t0 + L], in0=psTv[:, :, 0:L], in1=actT[:, 4 * h:4 * h + 4, t0:t0 + L], op=ALU.mult),
                      r=['psT', 'actT%d' % h], w=['actT%d' % h])

            def mlstm(seg, blocks):
                chunks = []
                if seg == 0:
                    chunks.append((0, 16, 0, 'p', None))
                    for b in range(16):
                        chunks.append((1 + b, 4, 16 + 4 * b, 's', b))
                for i in range(4):
                    chunks.append((17 + i, 128, NA + 128 * i, 'p', None))
                if seg == 0:
                    P.add('sp', lambda E: E.dma_start(out=nSall[:, :, :, :].rearrange("p b c h -> p b (c h)"), in_=stn.rearrange("b p x -> p b x")), w=['nS0', 'nS1'], stream='ldn')
                class Lane:
                    def __init__(self):
                        self.ops = []

                    def add(self, *a, **k):
                        self.ops.append((a, k))
                mS1 = scrF[:, 2672:2676]

                def gate_chunk(PP, gs, ci, L, t0, kind, b):
                    if kind == 'p':
                        if isinstance(PP, Lane):
                            PP.ops.append(None)
                        mlstm_gates(ci, L, t0, mbc, 'mbc', mbc[:, :], 'mbc', gs=gs, PP=PP)
                        if isinstance(PP, Lane):
                            PP.ops.append(None)
                    else:
                        mSx, mSn = (mS, 'mS') if gs == 0 else (mS1, 'mS_g1')
                        PP.add('sp', lambda E, b=b: E.dma_start(out=mSx[:, :], in_=stm[b:b + 1, :].partition_broadcast(128)), w=[mSn], stream='ldm%d' % gs)
                        mlstm_gates(ci, L, t0, mSx, mSn, mSx[:, :], mSn, gs=gs, PP=PP)
                        PP.add('sp', lambda E, b=b: E.dma_start(out=oms[b:b + 1, :], in_=mSx[0:1, :]), r=[mSn], stream='stm%d' % gs)
                if seg == 0:
                    lanes = [Lane(), Lane()]
                    samp = [c_ for c_ in chunks if c_[3] == 's']
                    prm = [c_ for c_ in chunks if c_[3] == 'p']
                    for c_ in prm + samp[11:]:
                        gate_chunk(lanes[0], 0, *c_)
                    for c_ in samp[:11]:
                        gate_chunk(lanes[1], 1, *c_)
                    for i in range(max(len(lanes[0].ops), len(lanes[1].ops))):
                        for ln in lanes:
                            if i < len(ln.ops) and ln.ops[i] is not None:
                                P.add(*ln.ops[i][0], **ln.ops[i][1])
                    P.barrier()
                else:
                    for c_ in chunks:
                        gate_chunk(P, 0, *c_)
                for h in range(4):
                    wqk, wqkr = W.get(wblk[B_AQK + h], KC * 512)
                    wqk = wqk.rearrange("p (k n) -> p k n", n=512)
                    wv_, wvr = W.get(wblk[B_AV + h], KC * 512, hold=2)
                    wv_ = wv_.rearrange("p (k n) -> p k n", n=512)
                    wo_, wor = W.get(wblk[B_AO + h], KC * 512, hold=3)
                    wo_ = wo_.rearrange("p (k n) -> p k n", n=512)
                    for (b0, bn) in blocks:
                        bk = 'b%d' % b0
                        for c in range(4):
                            ps, psn = rot_psA()

                            def mmf(E, c=c, b0=b0, bn=bn, ps=ps, wqk=wqk):
                                for kc in range(KC):
                                    ins = E.matmul(ps[:, 0:bn], lhsT=wqk[:, kc, 128 * c:128 * c + 128], rhs=xn[:, kc, b0:b0 + bn], start=(kc == 0), stop=(kc == KC - 1))
                                return ins
                            P.add('pe', mmf, r=[wqkr, 'xn' + bk, 'xnall'], w=[psn])
                            if c < 2:
                                P.add('act', lambda E, c=c, b0=b0, bn=bn, ps=ps: E.activation(out=qTh[:, c, b0:b0 + bn], in_=ps[:, 0:bn], func=AF.Copy), r=[psn], w=['qkT'])
                            else:
                                P.add('act', lambda E, c=c, b0=b0, bn=bn, ps=ps: E.activation(out=kTh[:, c - 2, b0:b0 + bn], in_=ps[:, 0:bn], func=AF.Copy, scale=1.0 / 16.0), r=[psn], w=['qkT'])
                        for j in range(4):
                            ps, psn = rot_psA()

                            def mmf(E, j=j, b0=b0, bn=bn, ps=ps, wo_=wo_):
                                for kc in range(KC):
                                    ins = E.matmul(ps[:, 0:bn], lhsT=wo_[:, kc, 128 * j:128 * j + 128], rhs=xn[:, kc, b0:b0 + bn], start=(kc == 0), stop=(kc == KC - 1))
                                return ins
                            P.add('pe', mmf, r=[wor, 'xn' + bk, 'xnall'], w=[psn])
                            P.add('act', lambda E, j=j, b0=b0, bn=bn, ps=ps, h=h: E.activation(out=actT[:, 4 * h + j, b0:b0 + bn], in_=ps[:, 0:bn], func=AF.Sigmoid), r=[psn], w=['actT%d' % h])
                            P.add('dve', lambda E, j=j, b0=b0, bn=bn, h=h: E.tensor_scalar(actT[:, 4 * h + j, b0:b0 + bn], actT[:, 4 * h + j, b0:b0 + bn], gn[:, 7, 4 * h + j:4 * h + j + 1], None, ALU.mult),
                                  r=['actT%d' % h, 'gn'], w=['actT%d' % h])
                    if seg == 0:
                        pv_, pvn_ = rot_psA()
                        pk_, pkn_ = rot_psA()

                        def mm_va(E, pv_=pv_, pk_=pk_, wv_=wv_, wqk=wqk):
                            for c in range(KC):
                                E.matmul(pv_[0:NA, 0:512], lhsT=xn[:, c, 0:NA], rhs=wv_[:, c, :], start=(c == 0), stop=(c == KC - 1))
                            for c in range(KC):
                                ins = E.matmul(pk_[0:NA, 0:256], lhsT=xn[:, c, 0:NA], rhs=wqk[:, c, 256:512], start=(c == 0), stop=(c == KC - 1))
                            return ins
                        P.add('pe', mm_va, r=['xnall', wvr, wqkr], w=[pvn_, pkn_])
                        P.add('act', lambda E, pv_=pv_: E.activation(out=vA_tok[0:NA, :], in_=pv_[0:NA, 0:512], func=AF.Copy), r=[pvn_], w=['vA_tok'])
                        P.add('act', lambda E, pk_=pk_: E.activation(out=kA_tok[0:NA, :], in_=pk_[0:NA, 0:256], func=AF.Copy), r=[pkn_], w=['kA_tok'])
                    CsFs = [CsF, CsF2]
                    Csbs = [Csb, Csb2]
                    nsbs = [nsb, nsb2]

                    def ld_sample(b, h=h):
                        pb = b % 2
                        P.add('sp', lambda E: E.dma_start(out=CsFs[pb][:, :, :], in_=stC[b, h].rearrange("(c p) v -> p c v", p=128)), w=['CS%d' % pb] + (['vtk1', 'kwt1'] if pb == 0 else []), stream='ldC')
                    real = [c_ for c_ in chunks if c_[2] >= NA]
                    for (ci, L, t0, kind, b) in chunks:
                        if kind == 'p' and t0 >= NA:
                            ri = ci - 17
                            if ri == 0:
                                mlstm_chunk_front(h, ci, L, t0, wqk, wqkr, wv_, wvr, 0)
                            hook = None
                            if ri + 1 < 4:
                                nci, nL, nt0, _, _ = real[ri + 1]
                                hook = (lambda nci=nci, nL=nL, nt0=nt0, npar=(ri + 1) % 2, h=h, wqk=wqk, wqkr=wqkr, wv_=wv_, wvr=wvr:
                                        mlstm_chunk_front(h, nci, nL, nt0, wqk, wqkr, wv_, wvr, npar))
                            mlstm_head_chunk(h, ci, L, t0, wqk, wqkr, wv_, wvr, Cst[:, h, :, :], Cbf[:, h, :, :], nst[:, :, h], nbf[:, :, h], '%d' % h,
                                             par=ri % 2, front_done=True, mid_hook=hook)
                        elif kind == 'p':
                            mlstm_head_chunk(h, ci, L, t0, wqk, wqkr, wv_, wvr, Cst[:, h, :, :], Cbf[:, h, :, :], nst[:, :, h], nbf[:, :, h], '%d' % h)
                        else:
                            pb = b % 2

                            def cast_sample(bb, h=h):
                                pq = bb % 2
                                nq_ = nSall[:, bb, :, h]
                                P.add('act', lambda E: E.activation(out=Csbs[pq][:, :, :], in_=CsFs[pq][:, :, :], func=AF.Copy), r=['CS%d' % pq], w=['CbS%d' % pq])
                                P.add('act', lambda E: E.activation(out=nsbs[pq][:, :, 0], in_=nq_, func=AF.Copy), r=['nS%d' % pq], w=['nbS%d' % pq])
                            if b == 0:
                                ld_sample(0)
                                cast_sample(0)
                                mlstm_chunk_front(h, ci, L, t0, wqk, wqkr, wv_, wvr, 0)
                            hc = hf = None
                            if b + 1 < 16:
                                ld_sample(b + 1)
                                hc = (lambda b=b: cast_sample(b + 1))
                                hf = (lambda ci=ci, t0=t0, h=h, wqk=wqk, wqkr=wqkr, wv_=wv_, wvr=wvr: mlstm_chunk_front(h, ci + 1, 4, t0 + 4, wqk, wqkr, wv_, wvr, 0))
                            nfS = nSall[:, b, :, h]
                            mlstm_head_chunk(h, ci, L, t0, wqk, wqkr, wv_, wvr, CsFs[pb][:, :, :], Csbs[pb][:, :, :], nfS, nsbs[pb][:, :, 0], 'S%d' % pb,
                                             front_done=True, cast_back=False, hook_cast=hc, hook_front=hf)
                            P.add('sp', lambda E, b=b, h=h, pb=pb: E.dma_start(out=oCs[b, h].rearrange("(c p) v -> p c v", p=128), in_=CsFs[pb][:, :, :]), r=['CS%d' % pb], stream='stC')

            def mlstm_finish_sample():
                P.add('sp', lambda E: E.dma_start(out=ons.rearrange("b p x -> p b x"), in_=nSall[:, :, :, :].rearrange("p b c h -> p b (c h)")), r=['nS0', 'nS1'], stream='stn')


            def kv_phase(seg):
                wkv, wkvr = W.get(wblk[B_KV], KC * 512)
                wkv = wkv.rearrange("p (k n) -> p k n", n=512)
                kpad = scrB[:, 0:1024].rearrange("p (x d) -> p x d", d=128)
                knf = scrF[:, 600:856]
                vf = scrF[:, 856:1112]
                ssk = scrF[:, 1112:1116]
                junk = scrF[:, 1120:1184]
                psTv = psT[:, 0:1024].rearrange("p (x l) -> p x l", l=128)
                P.add('dve', lambda E: E.memset(kpad[:, :, :], 0.0), w=['kpad'])
                tiles = []
                if seg == 0:
                    tiles.append((NA, 0, 'A'))
                for i in range(4):
                    tiles.append((128, NA + 128 * i, i))
                for (L, t0, sl) in tiles:
                    ps, psn = rot_psA()

                    def mmf(E, L=L, t0=t0, ps=ps):
                        for kc in range(KC):
                            ins = E.matmul(ps[0:L, 0:512], lhsT=xn[:, kc, t0:t0 + L], rhs=wkv[:, kc, :], start=(kc == 0), stop=(kc == KC - 1))
                        return ins
                    P.add('pe', mmf, r=[wkvr, 'xnall'], w=[psn])
                    for g in range(4):
                        P.add('act', lambda E, g=g, L=L, ps=ps: E.activation(out=junk[0:L, :], in_=ps[0:L, 64 * g:64 * g + 64], func=AF.Square, accum_out=ssk[0:L, g:g + 1]),
                              r=[psn, 'junk'], w=['junk', 'ssk'])
                    P.add('act', lambda E, L=L: E.activation(out=ssk[0:L, :], in_=ssk[0:L, :], func=AF.Sqrt, bias=epsb[0:L, :], scale=1.0 / 64.0), r=['ssk', 'epsb'], w=['ssk'])
                    P.add('dve', lambda E, L=L: E.reciprocal(ssk[0:L, :], ssk[0:L, :]), r=['ssk'], w=['ssk'])
                    dk = knA if sl == 'A' else knf
                    dv = vA if sl == 'A' else vf
                    dkr = 'knA' if sl == 'A' else 'knf'
                    dvr = 'vA' if sl == 'A' else 'vf'
                    for g in range(4):
                        P.add('dve', lambda E, g=g, L=L, ps=ps, dk=dk: E.scalar_tensor_tensor(out=dk[0:L, 64 * g:64 * g + 64], in0=ps[0:L, 64 * g:64 * g + 64], scalar=ssk[0:L, g:g + 1],
                                                                                        in1=kg_bc[0:L, :], op0=ALU.mult, op1=ALU.mult), r=[psn, 'ssk', 'smc'], w=[dkr])
                    P.add('act', lambda E, L=L, ps=ps, dv=dv: E.activation(out=dv[0:L, :], in_=ps[0:L, 256:512], func=AF.Copy), r=[psn], w=[dvr])
                    if sl == 'A':
                        P.add('sp', lambda E: E.dma_start(out=okmp[:, :], in_=knA[0:16, :]), r=['knA'], stream='stkv')
                        P.add('sp', lambda E: E.dma_start(out=ovmp[:, :], in_=vA[0:16, :]), r=['vA'], stream='stkv2')
                    if seg == NSEG - 1 and sl == 3:
                        P.add('sp', lambda E: E.dma_start(out=okwp[:, :], in_=knf[:, :]), r=['knf'], stream='stkv')
                        P.add('sp', lambda E: E.dma_start(out=ovwp[:, :], in_=vf[:, :]), r=['vf'], stream='stkv2')
                    Lk = 16 if sl == 'A' else 128
                    dk3 = dk[0:Lk, :].rearrange("p (g d) -> p g d", d=64)
                    dv3 = dv[0:Lk, :].rearrange("p (g d) -> p g d", d=64)
                    kp4 = kpad[0:Lk, :, :].rearrange("p (g e) d -> p g e d", e=2)
                    P.add('dve', lambda E, kp4=kp4, dk3=dk3: E.tensor_copy(out=kp4[:, :, 0, 0:64], in_=dk3), r=[dkr], w=['kpad'])
                    P.add('dve', lambda E, kp4=kp4, dk3=dk3: E.tensor_copy(out=kp4[:, :, 1, 64:128], in_=dk3), r=[dkr], w=['kpad'])

                    def mm_t(E, Lk=Lk):
                        for x in range(8):
                            ins = E.transpose(psTv[:, x, 0:Lk], kpad[0:Lk, x, :], identb[0:Lk, 0:Lk])
                        return ins
                    P.add('pe', mm_t, r=['kpad', 'identb'], w=['psT'])
                    if sl == 'A':
                        P.add('act', lambda E: E.activation(out=kTm[:, :, :], in_=psTv[:, :, 0:16], func=AF.Copy), r=['psT'], w=['kTm'])
                        P.add('dve', lambda E, dv3=dv3: E.tensor_copy(out=vext[0:16, 0, :, 0:64], in_=dv3), r=[dvr], w=['vext0'])
                    else:
                        P.add('act', lambda E, sl=sl: E.activation(out=kTz[:, :, (1 + sl) * 128:(2 + sl) * 128], in_=psTv[:, :, :], func=AF.Copy), r=['psT'], w=['kTz%d' % (1 + sl)])
                        P.add('dve', lambda E, sl=sl, dv3=dv3: E.tensor_copy(out=vext[:, 2 + sl, :, 0:64], in_=dv3), r=[dvr], w=['vext%d' % (2 + sl)])

            qn = scrB[:, 0:KC * NT].rearrange("p (c t) -> p c t", t=NT)
            oT = scrB[:, KC * NT:2 * KC * NT].rearrange("p (c t) -> p c t", t=NT)
            ob = 2 * KC * NT
            otok = scrB[:, ob:ob + 2048]
            Pc = scrB[:, ob + 2048:ob + 2048 + 320]
            sqq = scrF[:, 0:256].bitcast(BF16)
            rq = scrF[:, 256:768]
            tmpS = scrF[:, 768:1280]
            Pcur = scrF[:, 1280:1536].bitcast(BF16)
            Pprev = scrF[:, 1536:1792].bitcast(BF16)
            Pmeta = scrF[:, 1792:2048].bitcast(BF16)
            dn = scrF[:, 2048:2052]
            qg8 = scrF[:, 2056:2057]
            kwin_f = scrF[:, 2064:2320]
            vwin_f = scrF[:, 2320:2576]
            k2_f = scrF[:, 2576:2832]
            sb2 = ob + 2048 + 320

            def q_proj(blocks):
                P.add('dve', lambda E: E.tensor_scalar(qg8[:, :], qgt[:, :], 0.125, None, ALU.mult), r=['qgt'], w=['qg8'])
                for jb in range(4):
                    wv, wr = W.get(wblk[B_Q + jb], KC * 512)
                    wv = wv.rearrange("p (k n) -> p k n", n=512)
                    for j in range(4):
                        c = 4 * jb + j
                        for (b0, bn) in blocks:
                            bk = 'b%d' % b0
                            ps, psn = rot_psA()

                            def mmf(E, wv=wv, j=j, b0=b0, bn=bn, ps=ps):
                                for kc in range(KC):
                                    ins = E.matmul(ps[:, 0:bn], lhsT=wv[:, kc, 128 * j:128 * j + 128], rhs=xn[:, kc, b0:b0 + bn], start=(kc == 0), stop=(kc == KC - 1))
                                return ins
                            P.add('pe', mmf, r=[wr, 'xn' + bk], w=[psn])
                            P.add('act', lambda E, bn=bn, ps=ps: E.activation(out=sqq[:, 0:bn], in_=ps[:, 0:bn], func=AF.Square), r=[psn], w=['sqq'])
                            ps2, ps2n = rot_psA()
                            P.add('pe', lambda E, bn=bn, ps2=ps2: E.matmul(ps2[:, 0:bn], lhsT=bd64[:, :], rhs=sqq[:, 0:bn], start=True, stop=True), r=['bd64', 'sqq'], w=[ps2n])
                            P.add('act', lambda E, bn=bn, ps2=ps2: E.activation(out=rq[:, 0:bn], in_=ps2[:, 0:bn], func=AF.Sqrt, bias=epsb[:, :], scale=1.0), r=[ps2n, 'epsb'], w=['rq'])
                            P.add('dve', lambda E, bn=bn: E.reciprocal(rq[:, 0:bn], rq[:, 0:bn]), r=['rq'], w=['rq'])
                            P.add('dve', lambda E, c=c, b0=b0, bn=bn, ps=ps: E.scalar_tensor_tensor(out=qn[:, c, b0:b0 + bn], in0=ps[:, 0:bn], scalar=qg8[:, 0:1], in1=rq[:, 0:bn],
                                                                                              op0=ALU.mult, op1=ALU.mult), r=[psn, 'qg8', 'rq'], w=['qn' + bk])

            def attn_core(nq, qcols, keysets, res_in, out_rows_ap, meta0=False):
                nk_list = [ks[1] for ks in keysets]
                Pt = [Pcur, Pprev, Pmeta]
                for g in range(4):
                    for e in range(2):
                        kx = 2 * g + e
                        heads = [8 * g + 2 * j + e for j in range(4)]
                        for si, (kT, nk, vx, Rt, kind) in enumerate(keysets):
                            pb = psB[si]
                            pbv = pb[:, 0:4 * nq].rearrange("p (j q) -> p j q", q=nq)
                            P.add('pe', lambda E, kT=kT, nk=nk, pbv=pbv, kx=kx, g=g: E.matmul(pbv[0:nk, :, :], lhsT=kT[:, kx, 0:nk], rhs=qn[:, 4 * g:4 * g + 4, qcols[0]:qcols[1]], start=True, stop=True),
                                  r=res_in + ['qnall'], w=['psB%d' % si])
                            Pv = Pt[si][:, 0:4 * nq].rearrange("p (j q) -> p j q", q=nq)
                            tv = tmpS[:, 0:4 * nq].rearrange("p (j q) -> p j q", q=nq)
                            if kind == 'R':
                                for j in range(4):
                                    P.add('dve', lambda E, j=j, nk=nk, Rt=Rt, pbv=pbv, tv=tv, heads=heads: E.scalar_tensor_tensor(out=tv[0:nk, j, :], in0=Rt[0:nk, 0:nq], scalar=float(SLOPES[heads[j]]),
                                                                                                                in1=pbv[0:nk, j, :], op0=ALU.mult, op1=ALU.add),
                                          r=['psB%d' % si, 'ctf'], w=['tmpS'])
                                P.add('act', lambda E, nk=nk, Pv=Pv, tv=tv: E.activation(out=Pv[0:nk, :, :], in_=tv[0:nk, :, :], func=AF.Exp), r=['tmpS'], w=['P%d' % si])
                            else:
                                for j in range(4):
                                    P.add('act', lambda E, j=j, nk=nk, Pv=Pv, pbv=pbv, heads=heads: E.activation(out=Pv[0:nk, j, :], in_=pbv[0:nk, j, :], func=AF.Exp,
                                                                                             bias=nslope128[0:nk, heads[j]:heads[j] + 1], scale=1.0),
                                          r=['psB%d' % si, 'smc'], w=['P%d' % si])
                        po, pon = rot_psA()

                        def mm_pv(E, po=po, g=g):
                            for j in range(4):
                                for si, (kT, nk, vx, Rt, kind) in enumerate(keysets):
                                    Pv = Pt[si][:, 0:4 * nq].rearrange("p (j q) -> p j q", q=nq)
                                    ins = E.matmul(po[0:nq, 65 * j:65 * j + 65], lhsT=Pv[0:nk, j, :], rhs=vx[0:nk, g, :], start=(si == 0), stop=(si == len(keysets) - 1))
                            return ins
                        P.add('pe', mm_pv, r=['P%d' % si for si in range(len(keysets))] + res_in, w=[pon])
                        pov = po[:, 0:260].rearrange("p (j d) -> p j d", d=65)
                        esv = esink[:, 8 * g:8 * g + 8].rearrange("p (j e) -> p j e", e=2)
                        P.add('dve', lambda E, pov=pov, esv=esv, e=e: E.tensor_tensor(out=dn[0:nq, :], in0=pov[0:nq, :, 64], in1=esv[0:nq, :, e], op=ALU.add), r=[pon, 'esink'], w=['dn'])
                        P.add('dve', lambda E: E.reciprocal(dn[0:nq, :], dn[0:nq, :]), r=['dn'], w=['dn'])
                        for j in range(4):
                            hh = heads[j]
                            P.add('dve', lambda E, j=j, hh=hh, pov=pov: E.tensor_scalar(otok[0:nq, 64 * hh:64 * hh + 64], pov[0:nq, j, 0:64], dn[0:nq, j:j + 1], None, ALU.mult),
                                  r=[pon, 'dn'], w=['otok'])

            def attn_prompt(seg):
                psTv = psT[:, 0:1024].rearrange("p (x l) -> p x l", l=128)
                def attn_block(i):
                    t0 = NA + 128 * i
                    first = (seg == 0 and i == 0)
                    keysets = [(kTz[:, :, (1 + i) * 128:(2 + i) * 128], 128, vext[:, 2 + i, :, :], Rcur, 'R')]
                    if not first:
                        keysets.append((kTz[:, :, i * 128:(1 + i) * 128], 128, vext[:, 1 + i, :, :], Rprev, 'R'))
                        keysets.append((kTm[:, :, :], 16, vext[:, 0, :, :], None, 'C'))
                    else:
                        keysets.append((kTm[:, :, :], 16, vext[:, 0, :, :], Rmeta0, 'R'))
                    psT32 = psT[:, :].bitcast(F32)
                    sets = [([psB[0], psB[1], psB[2]], ['psB0', 'psB1', 'psB2'], psA[3], 'psA3'),
                            ([psA[0], psA[1], psA[2]], ['psA0', 'psA1', 'psA2'], psT32, 'psT')]
                    Psets = [[Pcur, Pprev, Pmeta],
                             [scrF[:, 2064:2320].bitcast(BF16), scrF[:, 2320:2576].bitcast(BF16), scrF[:, 2576:2832].bitcast(BF16)]]
                    tmps = [tmpS, scrF[:, 0:512]]
                    dns = [scrF[:, 2048:2052], scrF[:, 2052:2056]]
                    nks = len(keysets)

                    def st_S(k):
                        g, e, p = k // 2, k % 2, k % 2
                        banks, bnames, _, _ = sets[p]
                        for si, (kT, nk, vx, Rt, kind) in enumerate(keysets):
                            pbv = banks[si][:, 0:512].rearrange("p (j q) -> p j q", q=128)
                            P.add('pe', lambda E, kT=kT, nk=nk, pbv=pbv, g=g, e=e: E.matmul(pbv[0:nk, :, :], lhsT=kT[:, 2 * g + e, 0:nk], rhs=qn[:, 4 * g:4 * g + 4, t0:t0 + 128], start=True, stop=True),
                                  r=['kTzall', 'qnall'], w=[bnames[si]])

                    def st_BX(k):
                        g, e, p = k // 2, k % 2, k % 2
                        banks, bnames, _, _ = sets[p]
                        heads = [8 * g + 2 * j + e for j in range(4)]
                        tv = tmps[p][:, 0:512].rearrange("p (j q) -> p j q", q=128)
                        for si, (kT, nk, vx, Rt, kind) in enumerate(keysets):
                            pbv = banks[si][:, 0:512].rearrange("p (j q) -> p j q", q=128)
                            Pv = Psets[p][si][:, 0:512].rearrange("p (j q) -> p j q", q=128)
                            if kind == 'R':
                                for j in range(4):
                                    P.add('dve', lambda E, j=j, nk=nk, Rt=Rt, pbv=pbv, tv=tv, heads=heads: E.scalar_tensor_tensor(
                                        out=tv[0:nk, j, :], in0=Rt[0:nk, 0:128], scalar=float(SLOPES[heads[j]]), in1=pbv[0:nk, j, :], op0=ALU.mult, op1=ALU.add),
                                        r=[bnames[si], 'ctf'], w=['tmpS%d' % p])
                                P.add('act', lambda E, nk=nk, Pv=Pv, tv=tv: E.activation(out=Pv[0:nk, :, :], in_=tv[0:nk, :, :], func=AF.Exp), r=['tmpS%d' % p], w=['P%d_%d' % (p, si)])
                            else:
                                for j in range(4):
                                    P.add('act', lambda E, j=j, nk=nk, Pv=Pv, pbv=pbv, heads=heads: E.activation(out=Pv[0:nk, j, :], in_=pbv[0:nk, j, :], func=AF.Exp,
                                                                                                          bias=nslope128[0:nk, heads[j]:heads[j] + 1], scale=1.0),
                                          r=[bnames[si], 'smc'], w=['P%d_%d' % (p, si)])

                    def st_PV(k):
                        g, e, p = k // 2, k % 2, k % 2
                        _, _, po, pon = sets[p]

                        def mm_pv(E, po=po, g=g, p=p):
                            for j in range(4):
                                for si, (kT, nk, vx, Rt, kind) in enumerate(keysets):
                                    Pv = Psets[p][si][:, 0:512].rearrange("p (j q) -> p j q", q=128)
                                    ins = E.matmul(po[:, 65 * j:65 * j + 65], lhsT=Pv[0:nk, j, :], rhs=vx[0:nk, g, :], start=(si == 0), stop=(si == nks - 1))
                            return ins
                        P.add('pe', mm_pv, r=['P%d_%d' % (p, si) for si in range(nks)] + ['kTzall'], w=[pon])

                    def st_E(k):
                        g, e, p = k // 2, k % 2, k % 2
                        _, _, po, pon = sets[p]
                        dn_ = dns[p]
                        heads = [8 * g + 2 * j + e for j in range(4)]
                        pov = po[:, 0:260].rearrange("p (j d) -> p j d", d=65)
                        esv = esink[:, 8 * g:8 * g + 8].rearrange("p (j e) -> p j e", e=2)
                        P.add('dve', lambda E, pov=pov, esv=esv, e=e, dn_=dn_: E.tensor_tensor(out=dn_[:, :], in0=pov[:, :, 64], in1=esv[:, :, e], op=ALU.add), r=[pon, 'esink'], w=['dn%d' % p])
                        P.add('dve', lambda E, dn_=dn_: E.reciprocal(dn_[:, :], dn_[:, :]), r=['dn%d' % p], w=['dn%d' % p])
                        for j in range(4):
                            hh = heads[j]
                            P.add('dve', lambda E, j=j, hh=hh, pov=pov, dn_=dn_: E.tensor_scalar(otok[:, 64 * hh:64 * hh + 64], pov[:, j, 0:64], dn_[:, j:j + 1], None, ALU.mult),
                                  r=[pon, 'dn%d' % p], w=['otok%d' % k])

                    st_S(0)
                    st_BX(0)
                    for k in range(8):
                        if k + 1 < 8:
                            st_S(k + 1)
                            st_BX(k + 1)
                        st_PV(k)
                        st_E(k)
                    for half in range(2):
                        def mm_t(E, half=half):
                            for x in range(8):
                                c = 8 * half + x
                                ins = E.transpose(psTv[:, x, :], otok[:, 128 * c:128 * c + 128], identb[:, :])
                            return ins
                        P.add('pe', mm_t, r=['otok%d' % k for k in range(8)] + ['identb'], w=['psT'])
                        P.add('act', lambda E, half=half, t0=t0: E.activation(out=oT[:, 8 * half:8 * half + 8, t0:t0 + 128], in_=psTv[:, :, :], func=AF.Copy), r=['psT'], w=['oT'])
                for i in range(4):
                    attn_block(i)
                P.add('dve', lambda E: E.tensor_copy(out=kTz[:, :, 0:128], in_=kTz[:, :, 512:640]), r=['kTzall'], w=['kTzall'])
                P.add('dve', lambda E: E.tensor_copy(out=vext[:, 1, :, :], in_=vext[:, 5, :, :]), r=['kTzall'], w=['kTzall'])

            def attn_sample():
                psTv = psT[:, 0:1024].rearrange("p (x l) -> p x l", l=128)
                xf = xn[:, :, :].rearrange("p c t -> p (c t)")
                kpadS = xf[:, 0:1024].rearrange("p (x d) -> p x d", d=128)
                kTzS = xf[:, 1024:2048].rearrange("p (x d) -> p x d", d=128)
                kpad2 = xf[:, 2048:3072].rearrange("p (x d) -> p x d", d=128)
                kTz2 = xf[:, 3072:3328].rearrange("p (x d) -> p x d", d=32)
                vxS1 = xf[:, 3328:3588].rearrange("p (g d) -> p g d", d=65)
                vxS2 = xf[:, 3588:3848].rearrange("p (g d) -> p g d", d=65)
                oS_all = xf[:, 3848:5896]
                v2_f = xf[:, 5896:6408].bitcast(F32)
                P.add('dve', lambda E: E.memset(xf[:, 0:3328], 0.0), w=['kpadS', 'kpad2', 'kTzS', 'kTz2'])
                P.add('dve', lambda E: E.memset(xf[:, 3328:3848], 1.0), w=['vxS1', 'vxS2'])
                SR1 = xf[:, 6408:6664].bitcast(F32)
                SR2 = xf[:, 6664:6920].bitcast(F32)
                esP = xf[:, 6920:6984].bitcast(F32)
                dnS = xf[:, 6984:7048].bitcast(F32)
                tmp1 = tmpS[:, 0:128]
                tmp2 = tmpS[:, 128:256]
                P1 = Pcur[:, 0:128]
                P2 = Pprev[:, 0:128]
                for x8 in range(8):
                    g_, e_ = x8 // 2, x8 % 2
                    for j in range(4):
                        hh = 8 * g_ + 2 * j + e_
                        sidx = 4 * x8 + j
                        P.add('dve', lambda E, sidx=sidx, hh=hh: E.tensor_scalar(SR1[:, 4 * sidx:4 * sidx + 4], RwinS, float(SLOPES[hh]), None, ALU.mult), r=['ctf'], w=['SR1'])
                        P.add('dve', lambda E, sidx=sidx, hh=hh: E.tensor_scalar(SR2[0:20, 4 * sidx:4 * sidx + 4], R2S[0:20, :], float(SLOPES[hh]), None, ALU.mult), r=['ctf'], w=['SR2'])
                    esv = esink[:, 8 * g_:8 * g_ + 8].rearrange("p (j e) -> p j e", e=2)
                    P.add('dve', lambda E, x8=x8, esv=esv, e_=e_: E.tensor_copy(out=esP[:, 4 * x8:4 * x8 + 4], in_=esv[:, :, e_]), r=['esink'], w=['esP'])
                banks = [(psA[0], 'psA0'), (psA[1], 'psA1'), (psA[2], 'psA2'), (psA[3], 'psA3'), (psB[2], 'psB2')]
                kpS4 = kpadS[:, :, :].rearrange("p (g e) d -> p g e d", e=2)
                kp24 = kpad2[0:20, :, :].rearrange("p (g e) d -> p g e d", e=2)
                kw3 = kwin_f[:, :].rearrange("p (g d) -> p g d", d=64)
                vw3 = vwin_f[:, :].rearrange("p (g d) -> p g d", d=64)
                k23 = k2_f[0:20, :].rearrange("p (g d) -> p g d", d=64)
                v23 = v2_f[0:20, :].rearrange("p (g d) -> p g d", d=64)
                for b in range(16):
                    r0 = 16 + 4 * b
                    P.add('sp', lambda E, b=b: E.dma_start(out=kwin_f[:, :], in_=ckw[b]), w=['kwin_f'], stream='ldk0')
                    P.add('sp', lambda E, b=b: E.dma_start(out=vwin_f[:, :], in_=cvw[b]), w=['vwin_f'], stream='ldk1')
                    P.add('sp', lambda E, b=b: E.dma_start(out=k2_f[0:16, :], in_=ckm[b]), w=['k2_f'], stream='ldk2')
                    P.add('sp', lambda E, r0=r0: E.dma_start(out=k2_f[16:20, :], in_=knA[r0:r0 + 4, :]), r=['knA'], w=['k2_f'], stream='ldk2')
                    P.add('sp', lambda E, b=b: E.dma_start(out=v2_f[0:16, :], in_=cvm[b]), w=['v2_f'], stream='ldk3')
                    P.add('sp', lambda E, r0=r0: E.dma_start(out=v2_f[16:20, :], in_=vA[r0:r0 + 4, :]), r=['vA'], w=['v2_f'], stream='ldk3')
                    P.add('sp', lambda E, b=b: E.dma_start(out=okws[b, 0:124, :], in_=kwin_f[4:128, :]), r=['kwin_f'], stream='stkw')
                    P.add('sp', lambda E, b=b, r0=r0: E.dma_start(out=okws[b, 124:128, :], in_=knA[r0:r0 + 4, :]), r=['knA'], stream='stkw')
                    P.add('sp', lambda E, b=b: E.dma_start(out=ovws[b, 0:124, :], in_=vwin_f[4:128, :]), r=['vwin_f'], stream='stkw2')
                    P.add('sp', lambda E, b=b, r0=r0: E.dma_start(out=ovws[b, 124:128, :], in_=vA[r0:r0 + 4, :]), r=['vA'], stream='stkw2')
                    P.add('dve', lambda E: E.tensor_copy(out=kpS4[:, :, 0, 0:64], in_=kw3), r=['kwin_f'], w=['kpadS'])
                    P.add('dve', lambda E: E.tensor_copy(out=kpS4[:, :, 1, 64:128], in_=kw3), r=['kwin_f'], w=['kpadS'])
                    P.add('dve', lambda E: E.tensor_copy(out=kp24[:, :, 0, 0:64], in_=k23), r=['k2_f'], w=['kpad2'])
                    P.add('dve', lambda E: E.tensor_copy(out=kp24[:, :, 1, 64:128], in_=k23), r=['k2_f'], w=['kpad2'])

                    def mm_t1(E):
                        for x in range(8):
                            ins = E.transpose(psTv[:, x, :], kpadS[:, x, :], identb[:, :])
                        return ins
                    P.add('pe', mm_t1, r=['kpadS', 'identb'], w=['psT'])
                    P.add('act', lambda E: E.activation(out=kTzS[:, :, :], in_=psTv[:, :, :], func=AF.Copy), r=['psT'], w=['kTzS'])

                    def mm_t2(E):
                        for x in range(8):
                            ins = E.transpose(psTv[:, x, 0:20], kpad2[0:20, x, :], identb[0:20, 0:20])
                        return ins
                    P.add('pe', mm_t2, r=['kpad2', 'identb'], w=['psT'])
                    P.add('act', lambda E: E.activation(out=kTz2[:, :, 0:20], in_=psTv[:, :, 0:20], func=AF.Copy), r=['psT'], w=['kTz2'])
                    P.add('dve', lambda E: E.tensor_copy(out=vxS1[:, :, 0:64], in_=vw3), r=['vwin_f'], w=['vxS1'])
                    P.add('dve', lambda E: E.tensor_copy(out=vxS2[0:20, :, 0:64], in_=v23), r=['v2_f'], w=['vxS2'])
                    def mm_sc(E, r0=r0):
                        for x8 in range(8):
                            g_ = x8 // 2
                            o0 = psB[0][:, 16 * x8:16 * x8 + 16].rearrange("p (j q) -> p j q", q=4)
                            o1 = psB[1][:, 16 * x8:16 * x8 + 16].rearrange("p (j q) -> p j q", q=4)
                            E.matmul(o0[:, :, :], lhsT=kTzS[:, x8, :], rhs=qn[:, 4 * g_:4 * g_ + 4, r0:r0 + 4], start=True, stop=True)
                            ins = E.matmul(o1[0:20, :, :], lhsT=kTz2[:, x8, 0:20], rhs=qn[:, 4 * g_:4 * g_ + 4, r0:r0 + 4], start=True, stop=True)
                        return ins
                    P.add('pe', mm_sc, r=['kTzS', 'kTz2', 'qnall'], w=['psB0', 'psB1'])
                    P.add('dve', lambda E: E.tensor_tensor(out=tmp1[:, :], in0=psB[0][:, 0:128], in1=SR1[:, :], op=ALU.add), r=['psB0', 'SR1'], w=['tmp1'])
                    P.add('dve', lambda E: E.tensor_tensor(out=tmp2[0:20, :], in0=psB[1][0:20, 0:128], in1=SR2[0:20, :], op=ALU.add), r=['psB1', 'SR2'], w=['tmp2'])
                    P.add('act', lambda E: E.activation(out=P1[:, :], in_=tmp1[:, :], func=AF.Exp), r=['tmp1'], w=['P1s'])
                    P.add('act', lambda E: E.activation(out=P2[0:20, :], in_=tmp2[0:20, :], func=AF.Exp), r=['tmp2'], w=['P2s'])

                    def mm_pvs(E):
                        for hh in range(32):
                            g_, j_, e_ = hh // 8, (hh % 8) // 2, hh % 2
                            sidx = 4 * (2 * g_ + e_) + j_
                            bank = banks[hh // 7][0]
                            col = (hh % 7) * 65
                            E.matmul(bank[0:4, col:col + 65], lhsT=P1[:, 4 * sidx:4 * sidx + 4], rhs=vxS1[:, g_, :], start=True, stop=False)
                            ins = E.matmul(bank[0:4, col:col + 65], lhsT=P2[0:20, 4 * sidx:4 * sidx + 4], rhs=vxS2[0:20, g_, :], start=False, stop=True)
                        return ins
                    P.add('pe', mm_pvs, r=['P1s', 'P2s', 'vxS1', 'vxS2'], w=[bn_ for (_, bn_) in banks])
                    for k, (bank, bname) in enumerate(banks):
                        s0 = 7 * k
                        nk_ = min(7, 32 - s0)
                        bv = bank[:, 0:nk_ * 65].rearrange("p (s d) -> p s d", d=65)
                        P.add('dve', lambda E, bv=bv, s0=s0, nk_=nk_: E.tensor_tensor(out=dnS[0:4, s0:s0 + nk_], in0=bv[0:4, :, 64], in1=esink[0:4, s0:s0 + nk_], op=ALU.add),
                              r=[bname, 'esink'], w=['dnS%d' % k])
                    P.add('dve', lambda E: E.reciprocal(dnS[0:4, 0:32], dnS[0:4, 0:32]), r=['dnS%d' % k for k in range(5)], w=['dnSr'])
                    for k, (bank, bname) in enumerate(banks):
                        s0 = 7 * k
                        nk_ = min(7, 32 - s0)
                        bv = bank[:, 0:nk_ * 65].rearrange("p (s d) -> p s d", d=65)
                        otv = otok[0:4, 64 * s0:64 * (s0 + nk_)].rearrange("p (s d) -> p s d", d=64)
                        P.add('dve', lambda E, bv=bv, s0=s0, nk_=nk_, otv=otv: E.tensor_tensor(out=otv, in0=bv[0:4, :, 0:64], in1=dnS[0:4, s0:s0 + nk_, None].broadcast_to([4, nk_, 64]), op=ALU.mult),
                              r=[bname, 'dnSr'], w=['otok'])
                    P.add('pool', lambda E, b=b: E.dma_start(out=oS_all[4 * b:4 * b + 4, :], in_=otok[0:4, :]), r=['otok'], w=['oS_all'], stream='mvo')
                for half in range(2):
                    def mm_t(E, half=half):
                        for x in range(8):
                            c = 8 * half + x
                            ins = E.transpose(psTv[:, x, 0:64], oS_all[0:64, 128 * c:128 * c + 128], identb[0:64, 0:64])
                        return ins
                    P.add('pe', mm_t, r=['oS_all', 'identb'], w=['psT'])
                    P.add('act', lambda E, half=half: E.activation(out=oT[:, 8 * half:8 * half + 8, 16:NA], in_=psTv[:, :, 0:64], func=AF.Copy), r=['psT'], w=['oT'])

            gen_state['kv_phase'] = kv_phase
            gen_state['mlstm'] = mlstm
            gen_state['rmsnorm'] = rmsnorm
            gen_state['ffn'] = ffn
            gen_state['out_proj'] = out_proj

            for seg in range(NSEG):
                blocks = [(NA, TS)] if seg > 0 else [(0, NA), (NA, TS)]
                allres = ['hTb%d' % b0 for (b0, _) in blocks]
                if seg == 0:
                    P.add('sp', lambda E: E.dma_start(out=hT[:, :, 0:NA], in_=xA[:, :, :]), w=['hTb0'], stream='ldx0')
                    P.add('sp', lambda E, seg=seg: E.dma_start(out=hT[:, :, NA:NT], in_=xR[seg]), w=['hTb%d' % NA], stream='ldx1')
                P.barrier()
                rmsnorm(0, blocks)
                ffn(0, blocks, seg)
                rmsnorm(4, blocks)
                P.add('dve', lambda E: E.tensor_copy(out=scrF[0:1, 0:1], in_=scrF[0:1, 0:1]), r=['xnb%d' % b0 for (b0, _) in blocks], w=['xnall'])
                P.barrier()
                mlstm(seg, blocks)
                if seg == 0:
                    mlstm_finish_sample()
                P.barrier()
                out_proj(B_AOUT, actT, 'actTall', blocks)
                P.barrier()
                rmsnorm(1, blocks)
                ffn(1, blocks, seg)
                rmsnorm(6, blocks)
                P.add('dve', lambda E: E.tensor_copy(out=scrF[0:1, 0:1], in_=scrF[0:1, 0:1]), r=['xnb%d' % b0 for (b0, _) in blocks], w=['xnall'])
                P.barrier()
                kv_phase(seg)
                P.barrier()
                rmsnorm(2, blocks, reuse=True)
                ffn(2, blocks, seg)
                rmsnorm(5, blocks)
                P.barrier()
                q_proj(blocks)
                P.add('dve', lambda E: E.memset(oT[:, :, 0:NA], 0.0), w=['oT'])
                P.add('dve', lambda E: E.tensor_copy(out=scrF[0:1, 0:1], in_=scrF[0:1, 0:1]), r=['qnb%d' % b0 for (b0, _) in blocks] + ['kTz%d' % x for x in range(1, 5)] + ['vext%d' % x for x in range(2, 6)] + ['kTm', 'vext0'], w=['qnall', 'kTzall'])
                P.barrier()
                attn_prompt(seg)
                if seg == 0:
                    P.barrier()
                    attn_sample()
                P.barrier()
                out_proj(B_BOUT, oT, 'oTall', blocks)
                P.barrier()
                rmsnorm(3, blocks)
                ffn(3, blocks, seg, stream_io=True)
                P.barrier()
                if seg == 0:
                    P.add('sp', lambda E: E.dma_start(out=yA[:, :, :], in_=hT[:, :, 0:NA]), r=['hTb0'], stream='sty0')
                P.barrier()
            for h in range(4):
                P.add('sp', lambda E, h=h: E.dma_start(out=oCp[h].rearrange("(c p) v -> p c v", p=128), in_=Cst[:, h, :, :]), r=['C%d' % h], stream='stCp')
            P.add('sp', lambda E: E.dma_start(out=onp[:, :], in_=nst[:, :, :].rearrange("p c h -> p (c h)")), r=['n0', 'n1', 'n2', 'n3'], stream='stnp')
            P.add('sp', lambda E: E.dma_start(out=omp[:, :], in_=mbc[0:1, :]), r=['mbc'], stream='stmp')

        Pd = Prog(nc, dry=True)
        Wd = WRing(Pd, slots)
        gen(Pd, Wd)
        P = Prog(nc)
        W = WRing(P, slots, schedule=Wd.record)
        gen(P, W)
        P.emit(st, final_streams=['sty0', 'sty1', 'stCp', 'stnp', 'stmp', 'stC', 'stn', 'stm0', 'stm1', 'stkv', 'stkv2', 'stkw', 'stkw2'])
        build_program.stats = P.stats
    return nc


def _fm(x2d):
    T = x2d.shape[0]
    return np.ascontiguousarray(x2d.T.reshape(KC, 128, T).transpose(1, 0, 2))


def _blk(Wm, cols):
    return Wm[:, cols].reshape(KC, 128, len(cols)).transpose(1, 0, 2).reshape(128, KC * len(cols))


def _const_tables():
    t = np.zeros((128, 12, 128), np.float32)
    p = np.arange(128)[:, None]
    f = np.arange(128)[None, :]
    t[:, 0] = (p == f)
    t[:, 1] = (p <= f)
    t[:, 2] = np.where(f <= p, 0.0, NEG)
    t[:, 3] = np.where(p <= f, 0.0, NEG)
    t[:, 4] = (p == 127) * np.ones((1, 128))
    t[:, 5] = (p == 15) * np.ones((1, 128))
    t[:, 6] = (p == 3) * np.ones((1, 128))
    t[:, 7] = ((p // 64) == (f // 64)) / 64.0
    t[:, 8] = np.where(f >= p, -(f - p).astype(np.float32), NEG)
    t[:, 9] = np.where(p > f, -(f + 128 - p).astype(np.float32), NEG)
    t[:, 10] = -np.minimum(16 + f - p, 128).astype(np.float32)
    i4 = np.arange(4)[None, :]
    t[:, 11, 0:4] = np.where(p > i4, -(128 + i4 - p).astype(np.float32), NEG)
    r2 = np.full((128, 4), -128.0, np.float32)
    for j in range(4):
        r2[16 + j] = np.where(j <= np.arange(4), -(np.arange(4) - j).astype(np.float32), NEG)
    t[:, 11, 4:8] = r2
    return t


def _prep_shared(inp):
    w_in = inp['w_ffn_in']
    w_out = inp['w_ffn_out']
    wblk = np.empty((NBLK, 128, KC * 512), np.float32)
    woutb = np.empty((64, 128, FC * 128), np.float32)
    for l in range(2):
        for i in range(2):
            f = 2 * l + i
            Wm = w_in[l, i]
            for b in range(22):
                cols = np.concatenate([np.arange(256 * b, 256 * b + 256), DFF + np.arange(256 * b, 256 * b + 256)])
                wblk[B_FFN + 22 * f + b] = _blk(Wm, cols)
            Wo = w_out[l, i]
            for oc in range(16):
                woutb[16 * f + oc] = Wo[:, 128 * oc:128 * oc + 128].reshape(FC, 128, 128).transpose(1, 0, 2).reshape(128, FC * 128)
    wa = inp['w_a_in'][0]
    for h in range(4):
        wblk[B_AQK + h] = _blk(wa, np.concatenate([np.arange(256 * h, 256 * h + 256), 1024 + np.arange(256 * h, 256 * h + 256)]))
        wblk[B_AV + h] = _blk(wa, 2048 + np.arange(512 * h, 512 * h + 512))
        wblk[B_AO + h] = _blk(wa, 4096 + np.arange(512 * h, 512 * h + 512))
        wblk[B_AOUT + h] = _blk(inp['w_a_out'][0], np.arange(512 * h, 512 * h + 512))
        wblk[B_Q + h] = _blk(inp['w_q'][0], np.arange(512 * h, 512 * h + 512))
        wblk[B_BOUT + h] = _blk(inp['w_b_out'][0], np.arange(512 * h, 512 * h + 512))
    wblk[B_KV] = _blk(inp['w_kv'], np.arange(512))
    wgate = np.ascontiguousarray(wa[:, 6144:6152].reshape(KC, 128, 8).transpose(1, 0, 2).reshape(128, KC * 8))
    gl = [inp['ffn_norm'][0, 0], inp['ffn_norm'][0, 1], inp['ffn_norm'][1, 0], inp['ffn_norm'][1, 1],
          inp['mix_norm'][0], inp['mix_norm'][1], inp['kv_norm'], inp['a_head_norm'][0]]
    gains = np.ascontiguousarray(np.stack([g.reshape(KC, 128).T for g in gl], axis=1)).astype(np.float32)
    nsl = np.array([-128.0 * s for s in SLOPES], np.float32)
    smallc = np.concatenate([inp['b_a_gate'][0], inp['k_norm'], inp['sinks'][0], nsl]).astype(np.float32)[None, :]
    qg = np.ascontiguousarray(np.tile(inp['q_norm'][0], 2)[:, None]).astype(np.float32)
    return dict(wblk=wblk, wout=woutb, wgate=wgate, gains=gains, smallc=smallc, qg=qg, ctab=_const_tables())


def _prep_core(inp, c):
    xs = inp['x_sample'][16 * c:16 * c + 16].reshape(64, D)
    xA = _fm(np.concatenate([inp['meta_tokens'], xs], axis=0))
    xp = inp['x_prompt'][c]
    xR = np.stack([_fm(xp[TS * s:TS * s + TS]) for s in range(NSEG)])
    stn = inp['state_n'][0, 16 * c:16 * c + 16]
    stn = np.ascontiguousarray(stn.reshape(16, 4, 2, 128).transpose(0, 3, 2, 1).reshape(16, 128, 8))
    return dict(
        xA=xA, xR=xR,
        stC=np.ascontiguousarray(inp['state_C'][0, 16 * c:16 * c + 16]),
        stn=stn,
        stm=np.ascontiguousarray(inp['state_m'][0, 16 * c:16 * c + 16]),
        ckm=np.ascontiguousarray(inp['cache_k_meta'][16 * c:16 * c + 16].reshape(16, 16, 256)),
        cvm=np.ascontiguousarray(inp['cache_v_meta'][16 * c:16 * c + 16].reshape(16, 16, 256)),
        ckw=np.ascontiguousarray(inp['cache_k_win'][16 * c:16 * c + 16].reshape(16, 128, 256)),
        cvw=np.ascontiguousarray(inp['cache_v_win'][16 * c:16 * c + 16].reshape(16, 128, 256)),
    )


def _tm(a):
    T = a.shape[2]
    return a.transpose(1, 0, 2).reshape(D, T).T


def _assemble(results):
    n = len(results)
    y_prompt = np.empty((n, 2048, D), np.float32)
    y_sample = np.empty((16 * n, 4, D), np.float32)
    c_p = np.empty((1, n, 4, 256, 512), np.float32)
    n_p = np.empty((1, n, 4, 256), np.float32)
    m_p = np.empty((1, n, 4), np.float32)
    k_meta_p = np.empty((n, 16, 4, 64), np.float32)
    v_meta_p = np.empty((n, 16, 4, 64), np.float32)
    k_win_p = np.empty((n, 128, 4, 64), np.float32)
    v_win_p = np.empty((n, 128, 4, 64), np.float32)
    c_s = np.empty((1, 16 * n, 4, 256, 512), np.float32)
    n_s = np.empty((1, 16 * n, 4, 256), np.float32)
    m_s = np.empty((1, 16 * n, 4), np.float32)
    k_win_s = np.empty((16 * n, 128, 4, 64), np.float32)
    v_win_s = np.empty((16 * n, 128, 4, 64), np.float32)
    for c, r in enumerate(results):
        for s in range(NSEG):
            y_prompt[c, TS * s:TS * s + TS] = _tm(r['yR'][s])
        y_sample[16 * c:16 * c + 16] = _tm(r['yA'])[16:].reshape(16, 4, D)
        c_p[0, c] = r['oCp']
        n_p[0, c] = r['onp'].reshape(128, 2, 4).transpose(2, 1, 0).reshape(4, 256)
        m_p[0, c] = r['omp'][0]
        k_meta_p[c] = r['okmp'].reshape(16, 4, 64)
        v_meta_p[c] = r['ovmp'].reshape(16, 4, 64)
        k_win_p[c] = r['okwp'].reshape(128, 4, 64)
        v_win_p[c] = r['ovwp'].reshape(128, 4, 64)
        c_s[0, 16 * c:16 * c + 16] = r['oCs']
        n_s[0, 16 * c:16 * c + 16] = r['ons'].reshape(16, 128, 2, 4).transpose(0, 3, 2, 1).reshape(16, 4, 256)
        m_s[0, 16 * c:16 * c + 16] = r['oms']
        k_win_s[16 * c:16 * c + 16] = r['okws'].reshape(16, 128, 4, 64)
        v_win_s[16 * c:16 * c + 16] = r['ovws'].reshape(16, 128, 4, 64)
    return (y_prompt, y_sample, c_p, n_p, m_p, k_meta_p, v_meta_p, k_win_p, v_win_p, c_s, n_s, m_s, k_win_s, v_win_s)


def kernel(**inputs):
    inp = {k: np.asarray(v) for k, v in inputs.items()}
    n = 8
    shared = _prep_shared(inp)
    in_maps = []
    for c in range(n):
        m = dict(shared)
        m.update(_prep_core(inp, c))
        in_maps.append(m)
    nc = build_program()
    res = run_bass_kernel_spmd(nc, in_maps, core_ids=list(range(n)))
    return _assemble(res.results)
```

```python
from contextlib import ExitStack
import numpy as np
import concourse.bass as bass
import concourse.mybir as mybir
from concourse.bass_utils import run_bass_kernel_spmd

F32 = mybir.dt.float32
BF16 = mybir.dt.bfloat16
AF = mybir.ActivationFunctionType
ALU = mybir.AluOpType
AX = mybir.AxisListType

D = 2048
KC = 16
DFF = 5632
FC = 44
NA = 80
TS = 512
NSEG = 4
NT = NA + TS
NSLOT = 3
EPS = 1e-6
NEG = -1e30
SLOPES = [2.0 ** (-8.0 * (h + 1) / 32.0) for h in range(32)]
B_FFN = 0
B_AQK = 88
B_AV = 92
B_AO = 96
B_AOUT = 100
B_KV = 104
B_Q = 105
B_BOUT = 109
NBLK = 113


class Prog:
    def __init__(self, nc, dry=False):
        self.nc = nc
        self.dry = dry
        self.engs = {'pe': nc.tensor, 'act': nc.scalar, 'dve': nc.vector, 'pool': nc.gpsimd, 'sp': nc.sync}
        self.ops = []
        self.lw = {}
        self.rd = {}
        self.stream_last = {}
        self.fence = {}
        self.last_eng = {}

    def add(self, eng, fn, r=(), w=(), stream=None):
        if self.dry:
            return -1
        i = len(self.ops)
        deps = set()
        for x in r:
            if x in self.lw:
                deps.add(self.lw[x])
        for x in w:
            if x in self.lw:
                d = self.lw[x]
                de, _, _, dst = self.ops[d]
                if not (stream is None and dst is None and de == eng):
                    deps.add(d)
            for key, d in self.rd.get(x, {}).items():
                if stream is None and key == eng:
                    continue
                deps.add(d)
        if stream is not None and stream in self.stream_last:
            deps.add(self.stream_last[stream])
        if eng in self.fence:
            deps.update(self.fence.pop(eng))
        self.ops.append((eng, fn, deps, stream))
        for x in r:
            self.rd.setdefault(x, {})[eng if stream is None else (eng, i)] = i
        for x in w:
            self.lw[x] = i
            self.rd[x] = {}
        if stream is not None:
            self.stream_last[stream] = i
        else:
            self.last_eng[eng] = i
        return i

    def barrier(self, engines=('pe', 'act', 'dve', 'sp')):
        if self.dry:
            return
        front = set(self.last_eng.values()) | set(self.stream_last[s] for s in self.stream_last if not s.startswith('w'))
        for e in engines:
            self.fence.setdefault(e, set()).update(front)

    def emit(self, stack, final_streams):
        EP = 3000
        SEP = 200
        ops = self.ops
        n = len(ops)
        needed = [False] * n
        for i, (e, fn, deps, st) in enumerate(ops):
            for d in deps:
                de, _, _, dst = ops[d]
                if dst is None and not (de == 'pe' and e == 'pe' and st is None):
                    needed[d] = True
        cnt = {}
        ms = [0] * n
        for i, (e, fn, deps, st) in enumerate(ops):
            if st is not None:
                key = ('s', st)
                cnt[key] = cnt.get(key, 0) + 1
                ms[i] = cnt[key]
            elif needed[i]:
                key = ('e', e)
                cnt[key] = cnt.get(key, 0) + 1
                ms[i] = cnt[key]
        sems = {}

        def sem_for(key, count):
            if key[0] == 's':
                ep, v = (count - 1) // SEP, ((count - 1) % SEP + 1) * 16
            else:
                ep, v = (count - 1) // EP, (count - 1) % EP + 1
            k2 = (key, ep)
            if k2 not in sems:
                sems[k2] = stack.enter_context(self.nc.semaphore("sem_%s_%s_%d" % (key[0], str(key[1]), ep)))
            return sems[k2], v

        waited = {e: {} for e in self.engs}
        for i, (e, fn, deps, st) in enumerate(ops):
            E = self.engs[e]
            reqs = {}
            for d in deps:
                de, _, _, dst = ops[d]
                if dst is not None:
                    key = ('s', dst)
                elif de == 'pe' and e == 'pe' and st is None:
                    continue
                else:
                    key = ('e', de)
                if ms[d] > reqs.get(key, 0):
                    reqs[key] = ms[d]
            for key, val in reqs.items():
                if waited[e].get(key, 0) >= val:
                    continue
                sm, v = sem_for(key, val)
                E.wait_ge(sm, v)
                waited[e][key] = val
            ins = fn(E)
            if st is not None:
                sm, v = sem_for(('s', st), ms[i])
                ins.then_inc(sm, 16)
            elif needed[i]:
                sm, v = sem_for(('e', e), ms[i])
                ins.then_inc(sm, 1)
        sp = self.engs['sp']
        for s in final_streams:
            if ('s', s) in cnt:
                sm, v = sem_for(('s', s), cnt[('s', s)])
                sp.wait_ge(sm, v)
        self.stats = dict(n_ops=n, n_sems=len(sems), counts={str(k): v for k, v in cnt.items()})


class WRing:
    def __init__(self, P, slots, schedule=None):
        self.P = P
        self.slots = slots
        self.schedule = schedule
        self.record = [] if schedule is None else None
        self.n_get = 0
        self.n_issued = 0

    def _issue_upto(self, j):
        while self.n_issued <= j and self.n_issued < len(self.schedule):
            k = self.n_issued
            src, nfree, cache, mode, ckey = self.schedule[k]
            s = k % NSLOT
            dst = self.slots[s][:, 0:nfree]
            if mode == 'read':
                self.P.add('pool', (lambda E, dst=dst, cache=cache: E.dma_start(out=dst, in_=cache)),
                           r=[ckey], w=['wslot%d' % s], stream='w%d' % s)
            else:
                self.P.add('pool', (lambda E, dst=dst, src=src: E.dma_start(out=dst, in_=src)),
                           w=['wslot%d' % s], stream='w%d' % s)
                if mode == 'write':
                    self.P.add('sp', (lambda E, dst=dst, cache=cache: E.dma_start(out=cache, in_=dst)),
                               r=['wslot%d' % s], w=[ckey], stream='wb%d' % s)
            self.n_issued += 1

    def get(self, src, nfree, hold=1, cache=None, mode=None, ckey=None):
        k = self.n_get
        self.n_get += 1
        if self.record is not None:
            self.record.append((src, nfree, cache, mode, ckey))
            return self.slots[k % NSLOT][:, 0:nfree], 'wslot%d' % (k % NSLOT)
        self._issue_upto(k - hold + NSLOT)
        return self.slots[k % NSLOT][:, 0:nfree], 'wslot%d' % (k % NSLOT)


def build_program(debug=False):
    nc = bass.Bass("TRN2", target_bir_lowering=False)

    def din(name, shape, dt=F32):
        return nc.dram_tensor(name, list(shape), dt, kind="ExternalInput").ap()

    def dout(name, shape, dt=F32):
        return nc.dram_tensor(name, list(shape), dt, kind="ExternalOutput").ap()

    xA = din("xA", [128, KC, NA])
    xR = din("xR", [NSEG, 128, KC, TS])
    wblk = din("wblk", [NBLK, 128, KC * 512])
    wout = din("wout", [64, 128, FC * 128])
    wgate = din("wgate", [128, KC * 8])
    gains = din("gains", [128, 8, KC])
    smallc = din("smallc", [1, 8 + 64 + 32 + 32])
    qg = din("qg", [128, 1])
    ctab = din("ctab", [128, 12, 128])
    stC = din("stC", [16, 4, 256, 512])
    stn = din("stn", [16, 128, 8])
    stm = din("stm", [16, 4])
    ckm = din("ckm", [16, 16, 256])
    cvm = din("cvm", [16, 16, 256])
    ckw = din("ckw", [16, 128, 256])
    cvw = din("cvw", [16, 128, 256])

    wbf_in = nc.dram_tensor("wbf_in", [88, 128, KC * 512], BF16, kind="Internal").ap()
    wbf_out = nc.dram_tensor("wbf_out", [64, 128, FC * 128], BF16, kind="Internal").ap()

    yA = dout("yA", [128, KC, NA])
    yR = dout("yR", [NSEG, 128, KC, TS])
    oCp = dout("oCp", [4, 256, 512])
    onp = dout("onp", [128, 8])
    omp = dout("omp", [1, 4])
    okmp = dout("okmp", [16, 256])
    ovmp = dout("ovmp", [16, 256])
    okwp = dout("okwp", [128, 256])
    ovwp = dout("ovwp", [128, 256])
    oCs = dout("oCs", [16, 4, 256, 512])
    ons = dout("ons", [16, 128, 8])
    oms = dout("oms", [16, 4])
    okws = dout("okws", [16, 128, 256])
    ovws = dout("ovws", [16, 128, 256])

    with ExitStack() as st:
        def sb(name, shape, dt):
            return st.enter_context(nc.sbuf_tensor(name, list(shape), dt))

        def pst(name, shape, dt):
            return st.enter_context(nc.psum_tensor(name, list(shape), dt))

        hT = sb("hT", [128, KC, NT], F32)
        xn = sb("xn", [128, KC, NT], BF16)
        slots = [sb("wslot%d" % i, [128, KC * 512], BF16) for i in range(NSLOT)]
        scrB = sb("scrB", [128, 21312], BF16)
        scrF = sb("scrF", [128, 2880], F32)
        nSall = sb("nSall", [128, 16, 2, 4], F32)
        Cst = sb("Cst", [128, 4, 2, 512], F32)
        Cbf = sb("Cbf", [128, 4, 2, 512], BF16)
        nst = sb("nst", [128, 2, 4], F32)
        nbf = sb("nbf", [128, 2, 4], BF16)
        mbc = sb("mbc", [128, 4], F32)
        ctf = sb("ctf", [128, 12, 128], F32)
        identb = sb("identb", [128, 128], BF16)
        onesb = sb("onesb", [128, 128], BF16)
        onesD = sb("onesD", [128, 128], BF16)
        bd64 = sb("bd64", [128, 128], BF16)
        onesf = sb("onesf", [128, 128], F32)
        epsb = sb("epsb", [128, 1], F32)
        gn = sb("gn", [128, 8, KC], F32)
        wg = sb("wg", [128, KC * 8], BF16)
        smc = sb("smc", [128, 8 + 64 + 32 + 32], F32)
        esink = sb("esink", [128, 32], F32)
        qgt = sb("qgt", [128, 1], F32)
        kTz = sb("kTz", [128, 8, 5 * 128], BF16)
        kTm = sb("kTm", [128, 8, 16], BF16)
        vext = sb("vext", [128, 6, 4, 65], BF16)
        knA = sb("knA", [128, 256], F32)
        vA = sb("vA", [128, 256], F32)

        psA = [pst("psA%d" % i, [128, 512], F32) for i in range(4)]
        psB = [pst("psB%d" % i, [128, 512], F32) for i in range(3)]
        psT = pst("psT", [128, 1024], BF16)

        identf = ctf[:, 0, :]
        Umat = ctf[:, 1, :]
        maskC = ctf[:, 2, :]
        maskT = ctf[:, 3, :]
        selL = {128: ctf[:, 4, :], 16: ctf[:, 5, :], 4: ctf[:, 6, :]}
        Rcur = ctf[:, 8, :]
        Rprev = ctf[:, 9, :]
        Rmeta0 = ctf[:, 10, :]
        RwinS = ctf[:, 11, 0:4]
        R2S = ctf[:, 11, 4:8]
        bgate_bc = smc[:, 0:8]
        kg_bc = smc[:, 8:8 + 64]
        nslope128 = smc[:, 8 + 64 + 32: 8 + 64 + 64]

        gen_state = {}

        def gen(P, W):
            cntr = [0]

            def rot_psA():
                i = cntr[0] % 4
                cntr[0] += 1
                return psA[i], 'psA%d' % i

            P.add('sp', lambda E: E.dma_start(out=ctf[:], in_=ctab[:, :, :]), w=['ctf'], stream='ld0')
            P.add('sp', lambda E: E.dma_start(out=gn[:], in_=gains[:, :, :]), w=['gn'], stream='ld1')
            P.add('sp', lambda E: E.dma_start(out=smc[:], in_=smallc.partition_broadcast(128)), w=['smc'], stream='ld2')
            P.add('sp', lambda E: E.dma_start(out=qgt[:], in_=qg[:, :]), w=['qgt'], stream='ld3')
            P.add('pool', lambda E: E.dma_start(out=wg[:], in_=wgate[:, :]), w=['wg'], stream='ldw')
            P.add('dve', lambda E: E.memset(onesb[:], 1.0), w=['onesb'])
            P.add('dve', lambda E: E.memset(onesD[:], 1.0 / D), w=['onesD'])
            P.add('dve', lambda E: E.memset(onesf[:], 1.0), w=['onesf'])
            P.add('dve', lambda E: E.memset(epsb[:], EPS), w=['epsb'])
            P.add('dve', lambda E: E.memset(Cst[:], 0.0), w=['C0', 'C1', 'C2', 'C3'])
            P.add('dve', lambda E: E.memset(Cbf[:], 0.0), w=['Cb0', 'Cb1', 'Cb2', 'Cb3'])
            P.add('dve', lambda E: E.memset(nst[:], 0.0), w=['n0', 'n1', 'n2', 'n3'])
            P.add('dve', lambda E: E.memset(nbf[:], 0.0), w=['nb0', 'nb1', 'nb2', 'nb3'])
            P.add('dve', lambda E: E.memset(mbc[:], 0.0), w=['mbc'])
            P.add('dve', lambda E: E.memset(kTz[:], 0.0), w=['kTz'])
            P.add('dve', lambda E: E.memset(kTm[:], 0.0), w=['kTm'])
            P.add('dve', lambda E: E.memset(vext[:], 1.0), w=['vext'])
            P.add('dve', lambda E: E.tensor_copy(out=identb[:], in_=identf), r=['ctf'], w=['identb'])
            P.add('dve', lambda E: E.tensor_copy(out=bd64[:], in_=ctf[:, 7, :]), r=['ctf'], w=['bd64'])
            P.add('act', lambda E: E.activation(out=esink[:], in_=smc[:, 8 + 64: 8 + 64 + 32], func=AF.Exp),
                  r=['smc'], w=['esink'])

            def rmsnorm(gi, blocks, reuse=False):
                sq = scrB[:, 0:KC * NT].rearrange("p (c t) -> p c t", t=NT)
                rstd = scrF[:, 0:NT]
                for (b0, bn) in blocks:
                    bk = 'b%d' % b0
                    if reuse:
                        for c in range(KC):
                            P.add('dve', lambda E, c=c, b0=b0, bn=bn: E.scalar_tensor_tensor(
                                out=xn[:, c, b0:b0 + bn], in0=hT[:, c, b0:b0 + bn], scalar=gn[:, gi, c:c + 1], in1=rstd[:, b0:b0 + bn],
                                op0=ALU.mult, op1=ALU.mult), r=['hT' + bk, 'gn', 'rstd' + bk], w=['xn' + bk])
                        continue
                    P.add('act', lambda E, b0=b0, bn=bn: E.activation(out=sq[:, :, b0:b0 + bn], in_=hT[:, :, b0:b0 + bn], func=AF.Square),
                          r=['hT' + bk], w=['sq' + bk])
                    ps, psn = rot_psA()

                    def mmf(E, b0=b0, bn=bn, ps=ps):
                        for c in range(KC):
                            ins = E.matmul(ps[:, 0:bn], lhsT=onesD[:], rhs=sq[:, c, b0:b0 + bn], start=(c == 0), stop=(c == KC - 1))
                        return ins
                    P.add('pe', mmf, r=['onesD', 'sq' + bk], w=[psn])
                    P.add('act', lambda E, b0=b0, bn=bn, ps=ps: E.activation(out=rstd[:, b0:b0 + bn], in_=ps[:, 0:bn], func=AF.Sqrt, bias=epsb[:], scale=1.0),
                          r=[psn, 'epsb'], w=['rstd' + bk])
                    P.add('dve', lambda E, b0=b0, bn=bn: E.reciprocal(rstd[:, b0:b0 + bn], rstd[:, b0:b0 + bn]), r=['rstd' + bk], w=['rstd' + bk])
                    for c in range(KC):
                        P.add('dve', lambda E, c=c, b0=b0, bn=bn: E.scalar_tensor_tensor(
                            out=xn[:, c, b0:b0 + bn], in0=hT[:, c, b0:b0 + bn], scalar=gn[:, gi, c:c + 1], in1=rstd[:, b0:b0 + bn],
                            op0=ALU.mult, op1=ALU.mult), r=['hT' + bk, 'gn', 'rstd' + bk], w=['xn' + bk])

            def ffn(f, blocks, seg, stream_io=False):
                cmode = 'write' if seg == 0 else 'read'
                hidA = scrB[:, 0:36 * NT].rearrange("p (c t) -> p c t", t=NT)
                hidB = scrF[:, 512:512 + 4 * NT].bitcast(BF16).rearrange("p (c t) -> p c t", t=NT)

                def hid_ap(fc, b0, bn):
                    return hidA[:, fc, b0:b0 + bn] if fc < 36 else hidB[:, fc - 36, b0:b0 + bn]
                sg = scrF[:, 0:2 * 256].bitcast(BF16).rearrange("p (a t) -> p a t", a=2)
                it = 0
                for blk in range(22):
                    wv, wr = W.get(wblk[B_FFN + 22 * f + blk], KC * 512, cache=wbf_in[22 * f + blk], mode=cmode, ckey='wbi%d' % (22 * f + blk))
                    wv = wv.rearrange("p (k n) -> p k n", n=512)
                    for j in range(2):
                        fc = 2 * blk + j
                        for (b0, bn) in blocks:
                            bk = 'b%d' % b0
                            pg, pgn = rot_psA()
                            pu, pun = rot_psA()

                            def mmf(E, wv=wv, j=j, b0=b0, bn=bn, pg=pg, pu=pu):
                                for c in range(KC):
                                    E.matmul(pg[:, 0:bn], lhsT=wv[:, c, 128 * j:128 * j + 128], rhs=xn[:, c, b0:b0 + bn], start=(c == 0), stop=(c == KC - 1))
                                for c in range(KC):
                                    ins = E.matmul(pu[:, 0:bn], lhsT=wv[:, c, 256 + 128 * j:256 + 128 * j + 128], rhs=xn[:, c, b0:b0 + bn], start=(c == 0), stop=(c == KC - 1))
                                return ins
                            P.add('pe', mmf, r=[wr, 'xn' + bk], w=[pgn, pun])
                            sgi = it % 2
                            it += 1
                            P.add('act', lambda E, pg=pg, bn=bn, sgi=sgi: E.activation(out=sg[:, sgi, 0:bn], in_=pg[:, 0:bn], func=AF.Silu),
                                  r=[pgn], w=['sg%d' % sgi])
                            P.add('dve', lambda E, pu=pu, bn=bn, sgi=sgi, fc=fc, b0=b0: E.tensor_tensor(
                                out=hid_ap(fc, b0, bn), in0=pu[:, 0:bn], in1=sg[:, sgi, 0:bn], op=ALU.mult),
                                r=[pun, 'sg%d' % sgi], w=['hid%d' % fc + bk])
                for oc in range(KC):
                    wv, wr = W.get(wout[16 * f + oc], FC * 128, cache=wbf_out[16 * f + oc], mode=cmode, ckey='wbo%d' % (16 * f + oc))
                    wv = wv.rearrange("p (k n) -> p k n", n=128)
                    for (b0, bn) in blocks:
                        bk = 'b%d' % b0
                        ps, psn = rot_psA()

                        def mmf(E, wv=wv, b0=b0, bn=bn, ps=ps):
                            for c in range(FC):
                                ins = E.matmul(ps[:, 0:bn], lhsT=wv[:, c, :], rhs=hid_ap(c, b0, bn), start=(c == 0), stop=(c == FC - 1))
                            return ins
                        P.add('pe', mmf, r=[wr] + ['hid%d' % c + bk for c in range(FC)], w=[psn])
                        P.add('dve', lambda E, oc=oc, b0=b0, bn=bn, ps=ps: E.scalar_tensor_tensor(
                            out=hT[:, oc, b0:b0 + bn], in0=ps[:, 0:bn], scalar=0.5, in1=hT[:, oc, b0:b0 + bn], op0=ALU.mult, op1=ALU.add),
                            r=[psn, 'hT' + bk], w=['hT' + bk])
                        if stream_io and b0 == NA:
                            P.add('sp', lambda E, oc=oc, seg=seg: E.dma_start(out=yR[seg, :, oc, :], in_=hT[:, oc, NA:NT]), r=['hT' + bk], stream='sty1')
                            if seg + 1 < NSEG:
                                P.add('sp', lambda E, oc=oc, seg=seg: E.dma_start(out=hT[:, oc, NA:NT], in_=xR[seg + 1, :, oc, :]), w=['hT' + bk], stream='ldx1')

            def out_proj(bbase, src, srcres, blocks):
                for jb in range(4):
                    wv, wr = W.get(wblk[bbase + jb], KC * 512)
                    wv = wv.rearrange("p (k n) -> p k n", n=512)
                    for j in range(4):
                        oc = 4 * jb + j
                        for (b0, bn) in blocks:
                            bk = 'b%d' % b0
                            ps, psn = rot_psA()

                            def mmf(E, wv=wv, j=j, b0=b0, bn=bn, ps=ps):
                                for c in range(KC):
                                    ins = E.matmul(ps[:, 0:bn], lhsT=wv[:, c, 128 * j:128 * j + 128], rhs=src[:, c, b0:b0 + bn], start=(c == 0), stop=(c == KC - 1))
                                return ins
                            P.add('pe', mmf, r=[wr, srcres + bk], w=[psn])
                            P.add('dve', lambda E, oc=oc, b0=b0, bn=bn, ps=ps: E.tensor_tensor(
                                out=hT[:, oc, b0:b0 + bn], in0=ps[:, 0:bn], in1=hT[:, oc, b0:b0 + bn], op=ALU.add),
                                r=[psn, 'hT' + bk], w=['hT' + bk])

            actT = scrB[:, 0:KC * NT].rearrange("p (c t) -> p c t", t=NT)
            o1 = KC * NT
            qTh = scrB[:, o1:o1 + 2 * NT].rearrange("p (c t) -> p c t", t=NT)
            kTh = scrB[:, o1 + 2 * NT:o1 + 4 * NT].rearrange("p (c t) -> p c t", t=NT)
            o2 = o1 + 4 * NT
            vtk = scrB[:, o2:o2 + 512]
            kwt = scrB[:, o2 + 512:o2 + 768]
            STb = scrB[:, o2 + 768:o2 + 896]
            hnb = scrB[:, o2 + 896:o2 + 1408]
            Csb = scrB[:, o2 + 1408:o2 + 2432].rearrange("p (c v) -> p c v", v=512)
            nsb = scrB[:, o2 + 2432:o2 + 2440].rearrange("p (c h) -> p c h", h=4)
            WtBig = scrB[:, o2 + 2560:o2 + 2560 + 4 * 512].rearrange("p (k h l) -> p k h l", h=4, l=128)
            WtSm = scrB[:, o2 + 2560 + 2048:o2 + 2560 + 2048 + 17 * 64].rearrange("p (k h l) -> p k h l", h=4, l=16)

            def Wt_ap(ci, L, h):
                return WtBig[0:L, ci - 17, h, 0:L] if ci >= 17 else WtSm[0:L, ci, h, 0:L]
            gpre = scrF[:, 0:8]
            t1 = scrF[:, 8:16]
            ee = scrF[:, 16:20]
            spl = scrF[:, 20:24]
            gmax = scrF[:, 24:28]
            bgg = scrF[:, 28:36]
            glb = scrF[:, 36:40]
            tmp4 = scrF[:, 40:44]
            den2 = scrF[:, 44:46]
            den = scrF[:, 46:47]
            rden = scrF[:, 47:48]
            ssq = scrF[:, 48:49]
            scl = scrF[:, 49:50]
            mS = scrF[:, 52:56]
            a_all = scrF[:, 64:64 + 84].rearrange("p (k h) -> p k h", h=4)
            g_all = scrF[:, 148:148 + 84].rearrange("p (k h) -> p k h", h=4)
            wi_all = scrF[:, 232:232 + 84].rearrange("p (k h) -> p k h", h=4)
            ws_all = scrF[:, 316:316 + 84].rearrange("p (k h) -> p k h", h=4)
            dc_all = scrF[:, 400:400 + 84].rearrange("p (k h) -> p k h", h=4)
            em_all = scrF[:, 484:484 + 84].rearrange("p (k h) -> p k h", h=4)
            diag = scrF[:, 576:576 + 512].rearrange("p (h l) -> p h l", l=128)
            tmpA = scrF[:, 1088:1088 + 512].rearrange("p (h l) -> p h l", l=128)
            numI = scrF[:, 576:576 + 512]
            numT = scrF[:, 1088:1088 + 512]
            CsF2 = scrF[:, 1600:1600 + 1024].rearrange("p (c v) -> p c v", v=512)
            Csb2 = scrB[:, 19584:19584 + 1024].rearrange("p (c v) -> p c v", v=512)
            nsb2 = scrB[:, o2 + 2440:o2 + 2448].rearrange("p (c h) -> p c h", h=4)
            kA_tok = scrB[:, 20608:20608 + 256]
            vA_tok = scrF[:, 2624:2880].bitcast(BF16)
            CsF = scrB[:, 17536:17536 + 2048].bitcast(F32).rearrange("p (c v) -> p c v", v=512)

            gpre0, t10, ee0, spl0, gmax0, bgg0, glb0, tmp40, diag0, tmpA0 = gpre, t1, ee, spl, gmax, bgg, glb, tmp4, diag, tmpA

            def mlstm_gates(ci, L, t0, mprev, mres, mout, moutres, gs=0, PP=None):
                cr = 'ck%d' % ci
                PP = P if PP is None else PP
                if gs == 0:
                    pb0, pb1, pb2 = psB[0], psB[1], psB[2]
                    nb0, nb1, nb2 = 'psB0', 'psB1', 'psB2'
                    gpre, t1, ee, spl, gmax, bgg, glb, tmp4, diag, tmpA = gpre0, t10, ee0, spl0, gmax0, bgg0, glb0, tmp40, diag0, tmpA0
                else:
                    pb0, pb1, pb2 = psA[0], psA[1], psA[2]
                    nb0, nb1, nb2 = 'psA0', 'psA1', 'psA2'
                    sm1 = scrF[:, 2624:2672]
                    gpre, t1, ee, spl, gmax, bgg, glb, tmp4 = sm1[:, 0:8], sm1[:, 8:16], sm1[:, 16:20], sm1[:, 20:24], sm1[:, 24:28], sm1[:, 28:36], sm1[:, 36:40], sm1[:, 40:44]
                    diag = scrF[:, 1600:2112].rearrange("p (h l) -> p h l", l=128)
                    tmpA = scrF[:, 2112:2624].rearrange("p (h l) -> p h l", l=128)
                sfx = '' if gs == 0 else '_g1'

                def mm_g(E):
                    for c in range(KC):
                        ins = E.matmul(pb0[0:L, 0:8], lhsT=xn[:, c, t0:t0 + L], rhs=wg[:, 8 * c:8 * c + 8], start=(c == 0), stop=(c == KC - 1))
                    return ins
                PP.add('pe', mm_g, r=['xnall', 'wg'], w=[nb0])
                PP.add('dve', lambda E: E.tensor_tensor(out=gpre[0:L, :], in0=pb0[0:L, 0:8], in1=bgate_bc[0:L, :], op=ALU.add), r=[nb0, 'smc'], w=['gpre' + sfx])
                PP.add('act', lambda E: E.activation(out=t1[0:L, :], in_=gpre[0:L, :], func=AF.Exp, scale=2.0 / 15.0), r=['gpre' + sfx], w=['t1' + sfx])
                PP.add('dve', lambda E: E.tensor_scalar(t1[0:L, :], t1[0:L, :], 1.0, None, ALU.add), r=['t1' + sfx], w=['t1' + sfx])
                PP.add('dve', lambda E: E.reciprocal(t1[0:L, :], t1[0:L, :]), r=['t1' + sfx], w=['t1' + sfx])
                PP.add('dve', lambda E: E.tensor_scalar(t1[0:L, :], t1[0:L, :], -2.0, 1.0, ALU.mult, ALU.add), r=['t1' + sfx], w=['t1' + sfx])
                PP.add('act', lambda E: E.activation(out=ee[0:L, :], in_=t1[0:L, 4:8], func=AF.Exp, scale=-15.0), r=['t1' + sfx], w=['ee' + sfx])
                PP.add('act', lambda E: E.activation(out=spl[0:L, :], in_=ee[0:L, :], func=AF.Ln, bias=onesf[0:L, 0:1], scale=1.0), r=['ee' + sfx, 'onesf'], w=['spl' + sfx])
                PP.add('pe', lambda E: E.matmul(pb1[0:L, 0:4], lhsT=Umat[0:L, 0:L], rhs=spl[0:L, :], start=True, stop=True), r=['ctf', 'spl' + sfx], w=[nb1])
                PP.add('dve', lambda E: E.scalar_tensor_tensor(out=a_all[0:L, ci, :], in0=t1[0:L, 0:4], scalar=15.0, in1=pb1[0:L, 0:4], op0=ALU.mult, op1=ALU.add),
                      r=['t1' + sfx, nb1], w=['a' + cr])
                for h in range(4):
                    PP.add('dve', lambda E, h=h: E.tensor_scalar(diag[0:L, h, 0:L], identf[0:L, 0:L], a_all[0:L, ci, h:h + 1], None, ALU.mult), r=['ctf', 'a' + cr], w=['diag' + sfx])
                pb2v = pb2[:, :].rearrange("p (h l) -> p h l", l=128)
                PP.add('pe', lambda E: E.matmul(pb2v[0:L, :, 0:L], lhsT=onesf[0:L, 0:L], rhs=diag[0:L, :, 0:L], start=True, stop=True), r=['onesf', 'diag' + sfx], w=[nb2])
                for h in range(4):
                    PP.add('dve', lambda E, h=h: E.tensor_tensor(out=tmpA[0:L, h, 0:L], in0=pb2v[0:L, h, 0:L], in1=maskC[0:L, 0:L], op=ALU.add), r=[nb2, 'ctf'], w=['tmpA' + sfx])
                PP.add('dve', lambda E: E.tensor_reduce(out=gmax[0:L, :], in_=tmpA[0:L, :, 0:L], axis=AX.X, op=ALU.max), r=['tmpA' + sfx], w=['gmax' + sfx])
                PP.add('dve', lambda E: E.tensor_tensor(out=g_all[0:L, ci, :], in0=gmax[0:L, :], in1=mprev[0:L, :], op=ALU.max), r=['gmax' + sfx, mres], w=['g' + cr])
                for h in range(4):
                    PP.add('dve', lambda E, h=h: E.tensor_scalar(diag[0:L, h, 0:L], identf[0:L, 0:L], g_all[0:L, ci, h:h + 1], None, ALU.mult), r=['ctf', 'g' + cr], w=['diag' + sfx])
                PP.add('pe', lambda E: E.matmul(pb2v[0:L, :, 0:L], lhsT=onesf[0:L, 0:L], rhs=diag[0:L, :, 0:L], start=True, stop=True), r=['onesf', 'diag' + sfx], w=[nb2])
                for h in range(4):
                    PP.add('dve', lambda E, h=h: E.scalar_tensor_tensor(out=tmpA[0:L, h, 0:L], in0=pb2v[0:L, h, 0:L], scalar=-1.0, in1=maskT[0:L, 0:L], op0=ALU.mult, op1=ALU.add),
                          r=[nb2, 'ctf'], w=['tmpA' + sfx])
                for h in range(4):
                    PP.add('act', lambda E, h=h: E.activation(out=Wt_ap(ci, L, h), in_=tmpA[0:L, h, 0:L], func=AF.Exp, bias=a_all[0:L, ci, h:h + 1], scale=1.0),
                          r=['tmpA' + sfx, 'a' + cr], w=['Wt' + cr])
                PP.add('dve', lambda E: E.tensor_tensor(out=bgg[0:L, 0:4], in0=g_all[0:L, ci, :], in1=pb1[0:L, 0:4], op=ALU.subtract), r=['g' + cr, nb1], w=['bgg' + sfx])
                PP.add('dve', lambda E: E.tensor_copy(out=bgg[0:L, 4:8], in_=g_all[0:L, ci, :]), r=['g' + cr], w=['bgg' + sfx])
                PP.add('pe', lambda E: E.matmul(pb0[:, 0:8], lhsT=selL[L][0:L, :], rhs=bgg[0:L, :], start=True, stop=True), r=['ctf', 'bgg' + sfx], w=[nb0])
                PP.add('dve', lambda E: E.tensor_copy(out=glb[:, :], in_=pb0[:, 4:8]), r=[nb0], w=['glb' + sfx])
                PP.add('dve', lambda E: E.tensor_tensor(out=tmp4[0:L, :], in0=mprev[0:L, :], in1=g_all[0:L, ci, :], op=ALU.subtract), r=[mres, 'g' + cr], w=['tmp4' + sfx])
                PP.add('act', lambda E: E.activation(out=wi_all[0:L, ci, :], in_=tmp4[0:L, :], func=AF.Exp), r=['tmp4' + sfx], w=['wi' + cr])
                PP.add('dve', lambda E: E.tensor_tensor(out=tmp4[0:L, :], in0=a_all[0:L, ci, :], in1=glb[0:L, :], op=ALU.subtract), r=['a' + cr, 'glb' + sfx], w=['tmp4' + sfx])
                PP.add('act', lambda E: E.activation(out=ws_all[0:L, ci, :], in_=tmp4[0:L, :], func=AF.Exp), r=['tmp4' + sfx], w=['ws' + cr])
                PP.add('dve', lambda E: E.tensor_tensor(out=tmp4[:, :], in0=mprev[:, :], in1=glb[:, :], op=ALU.subtract), r=[mres, 'glb' + sfx], w=['tmp4' + sfx])
                PP.add('act', lambda E: E.activation(out=dc_all[:, ci, :], in_=tmp4[:, :], func=AF.Exp), r=['tmp4' + sfx], w=['dc' + cr])
                PP.add('act', lambda E: E.activation(out=em_all[0:L, ci, :], in_=bgg[0:L, 0:4], func=AF.Exp, scale=-1.0), r=['bgg' + sfx], w=['em' + cr])
                PP.add('dve', lambda E: E.tensor_copy(out=mout, in_=pb0[:, 0:4]), r=[nb0, mres], w=[moutres])

            vtk1 = scrB[:, 17536:17536 + 512]
            kwt1 = scrB[:, 18048:18048 + 256]
            vtks = [vtk, vtk1]
            kwts = [kwt, kwt1]

            def mlstm_chunk_front(h, ci, L, t0, wqk, wqkr, wv_, wvr, par):
                cr = 'ck%d' % ci
                vtk = vtks[par]
                kwt = kwts[par]
                xw = ['CS0'] if par == 1 else []
                pv, pvn = rot_psA()
                pk, pkn = rot_psA()
                if t0 < NA:
                    P.add('pe', lambda E: E.matmul(pv[0:L, 0:512], lhsT=identb[0:NA, t0:t0 + L], rhs=vA_tok[0:NA, :], start=True, stop=True), r=['vA_tok', 'identb'], w=[pvn])
                    P.add('pe', lambda E: E.matmul(pk[0:L, 0:256], lhsT=identb[0:NA, t0:t0 + L], rhs=kA_tok[0:NA, :], start=True, stop=True), r=['kA_tok', 'identb'], w=[pkn])
                else:
                    def mm_v(E):
                        for c in range(KC):
                            ins = E.matmul(pv[0:L, 0:512], lhsT=xn[:, c, t0:t0 + L], rhs=wv_[:, c, :], start=(c == 0), stop=(c == KC - 1))
                        return ins
                    P.add('pe', mm_v, r=['xnall', wvr], w=[pvn])

                    def mm_k(E):
                        for c in range(KC):
                            ins = E.matmul(pk[0:L, 0:256], lhsT=xn[:, c, t0:t0 + L], rhs=wqk[:, c, 256:512], start=(c == 0), stop=(c == KC - 1))
                        return ins
                    P.add('pe', mm_k, r=['xnall', wqkr], w=[pkn])
                P.add('act', lambda E: E.activation(out=vtk[0:L, :], in_=pv[0:L, 0:512], func=AF.Copy), r=[pvn], w=['vtk%d' % par] + xw)
                P.add('dve', lambda E: E.tensor_scalar(kwt[0:L, :], pk[0:L, 0:256], ws_all[0:L, ci, h:h + 1], 1.0 / 16.0, ALU.mult, ALU.mult), r=[pkn, 'ws' + cr], w=['kwt%d' % par] + xw)

            def mlstm_head_chunk(h, ci, L, t0, wqk, wqkr, wv_, wvr, Cf, Cb, nf, nb, cres, par=0, front_done=False, mid_hook=None,
                                 cast_back=True, hook_cast=None, hook_front=None):
                cr = 'ck%d' % ci
                if not front_done:
                    mlstm_chunk_front(h, ci, L, t0, wqk, wqkr, wv_, wvr, par)
                vtk = vtks[par]
                kwt = kwts[par]
                vtkr = 'vtk%d' % par
                kwtr = 'kwt%d' % par
                pb0, pb1, pb2 = psB[0], psB[1], psB[2]

                def mm_s(E):
                    for c in range(2):
                        ins = E.matmul(pb0[0:L, 0:L], lhsT=kTh[:, c, t0:t0 + L], rhs=qTh[:, c, t0:t0 + L], start=(c == 0), stop=(c == 1))
                    return ins
                P.add('pe', mm_s, r=['qkT'], w=['psB0'])
                pc0, pc0n = rot_psA()
                pc1, pc1n = rot_psA()

                def mm_c(E):
                    E.matmul(pc0[:, 0:512], lhsT=kwt[0:L, 0:128], rhs=vtk[0:L, :], start=True, stop=True)
                    ins = E.matmul(pc1[:, 0:512], lhsT=kwt[0:L, 128:256], rhs=vtk[0:L, :], start=True, stop=True)
                    return ins
                P.add('pe', mm_c, r=[kwtr, vtkr], w=[pc0n, pc1n])

                def mm_n(E):
                    E.matmul(pb2[:, 0:1], lhsT=kwt[0:L, 0:128], rhs=onesb[0:L, 0:1], start=True, stop=True)
                    ins = E.matmul(pb2[:, 1:2], lhsT=kwt[0:L, 128:256], rhs=onesb[0:L, 0:1], start=True, stop=True)
                    return ins
                P.add('pe', mm_n, r=[kwtr, 'onesb'], w=['psB2'])
                P.add('dve', lambda E: E.tensor_tensor(out=STb[0:L, 0:L], in0=pb0[0:L, 0:L], in1=Wt_ap(ci, L, h), op=ALU.mult), r=['psB0', 'Wt' + cr], w=['STb'])
                pn, pnn = rot_psA()
                P.add('pe', lambda E: E.matmul(pn[0:L, 0:512], lhsT=STb[0:L, 0:L], rhs=vtk[0:L, :], start=True, stop=True), r=['STb', vtkr], w=[pnn])

                def mm_d(E):
                    E.matmul(pb1[0:L, 0:1], lhsT=STb[0:L, 0:L], rhs=onesb[0:L, 0:1], start=True, stop=True)
                    for c in range(2):
                        ins = E.matmul(pb1[0:L, 1:2], lhsT=qTh[:, c, t0:t0 + L], rhs=nb[:, c:c + 1], start=(c == 0), stop=(c == 1))
                    return ins
                P.add('pe', mm_d, r=['STb', 'qkT', 'nb' + cres, 'onesb'], w=['psB1'])
                pi, pin = rot_psA()

                def mm_i(E):
                    for c in range(2):
                        ins = E.matmul(pi[0:L, 0:512], lhsT=qTh[:, c, t0:t0 + L], rhs=Cb[:, c, :], start=(c == 0), stop=(c == 1))
                    return ins
                P.add('pe', mm_i, r=['qkT', 'Cb' + cres], w=[pin])
                P.add('dve', lambda E: E.scalar_tensor_tensor(out=Cf[:, 0, :], in0=Cf[:, 0, :], scalar=dc_all[:, ci, h:h + 1], in1=pc0[:, 0:512], op0=ALU.mult, op1=ALU.add),
                      r=['C' + cres, 'dc' + cr, pc0n, 'Cb' + cres], w=['C' + cres])
                P.add('dve', lambda E: E.scalar_tensor_tensor(out=Cf[:, 1, :], in0=Cf[:, 1, :], scalar=dc_all[:, ci, h:h + 1], in1=pc1[:, 0:512], op0=ALU.mult, op1=ALU.add),
                      r=['C' + cres, 'dc' + cr, pc1n, 'Cb' + cres], w=['C' + cres])
                P.add('dve', lambda E: E.scalar_tensor_tensor(out=nf, in0=nf, scalar=dc_all[:, ci, h:h + 1], in1=pb2[:, 0:2], op0=ALU.mult, op1=ALU.add),
                      r=['n' + cres, 'dc' + cr, 'psB2'], w=['n' + cres])
                if mid_hook is not None:
                    mid_hook()
                P.add('act', lambda E: E.activation(out=numI[0:L, :], in_=pn[0:L, 0:512], func=AF.Copy), r=[pnn], w=['diag'])
                P.add('dve', lambda E: E.scalar_tensor_tensor(out=numT[0:L, :], in0=pi[0:L, 0:512], scalar=wi_all[0:L, ci, h:h + 1], in1=numI[0:L, :], op0=ALU.mult, op1=ALU.add),
                      r=[pin, 'wi' + cr, 'diag'], w=['tmpA'])
                P.add('act', lambda E: E.activation(out=den2[0:L, :], in_=pb1[0:L, 0:2], func=AF.Copy), r=['psB1'], w=['den2'])
                if hook_cast is not None:
                    hook_cast()
                P.add('dve', lambda E: E.scalar_tensor_tensor(out=den[0:L, :], in0=den2[0:L, 1:2], scalar=wi_all[0:L, ci, h:h + 1], in1=den2[0:L, 0:1], op0=ALU.mult, op1=ALU.add),
                      r=['den2', 'wi' + cr], w=['den'])
                P.add('dve', lambda E: E.scalar_tensor_tensor(out=den2[0:L, 0:1], in0=den[0:L, :], scalar=-1.0, in1=den[0:L, :], op0=ALU.mult, op1=ALU.max), r=['den'], w=['den2'])
                P.add('dve', lambda E: E.tensor_scalar(den[0:L, :], den2[0:L, 0:1], em_all[0:L, ci, h:h + 1], None, ALU.max), r=['den2', 'em' + cr], w=['den'])
                P.add('dve', lambda E: E.reciprocal(rden[0:L, :], den[0:L, :]), r=['den'], w=['rden'])
                P.add('act', lambda E: E.activation(out=numI[0:L, :], in_=numT[0:L, :], func=AF.Square, scale=rden[0:L, 0:1], accum_out=ssq[0:L, :]), r=['tmpA', 'rden', 'diag'], w=['diag', 'ssq'])
                P.add('act', lambda E: E.activation(out=ssq[0:L, :], in_=ssq[0:L, :], func=AF.Sqrt, bias=epsb[0:L, :], scale=1.0 / 512.0), r=['ssq', 'epsb'], w=['ssq'])
                if cast_back:
                    P.add('act', lambda E: E.activation(out=Cb, in_=Cf, func=AF.Copy), r=['C' + cres], w=['Cb' + cres])
                    P.add('act', lambda E: E.activation(out=nb, in_=nf, func=AF.Copy), r=['n' + cres], w=['nb' + cres])
                if hook_front is not None:
                    hook_front()
                P.add('dve', lambda E: E.reciprocal(ssq[0:L, :], ssq[0:L, :]), r=['ssq'], w=['ssq'])
                P.add('dve', lambda E: E.tensor_tensor(out=scl[0:L, :], in0=ssq[0:L, :], in1=rden[0:L, :], op=ALU.mult), r=['ssq', 'rden'], w=['scl'])
                P.add('dve', lambda E: E.tensor_scalar(hnb[0:L, :], numT[0:L, :], scl[0:L, 0:1], None, ALU.mult), r=['tmpA', 'scl'], w=['hnb'])
                psTv = psT[:, 0:512].rearrange("p (j l) -> p j l", l=128)

                def mm_t(E):
                    for j in range(4):
                        ins = E.transpose(psTv[:, j, 0:L], hnb[0:L, 128 * j:128 * j + 128], identb[0:L, 0:L])
                    return ins
                P.add('pe', mm_t, r=['hnb', 'identb'], w=['psT'])
                P.add('dve', lambda E: E.tensor_tensor(out=actT[:, 4 * h:4 * h + 4, t0:t0 + L], in0=psTv[:, :, 0:L], in1=actT[:, 4 * h:4 * h + 4, t0:t0 + L], op=ALU.mult),
                      r=['psT', 'actT%d' % h], w=['actT%d' % h])

            def mlstm(seg, blocks):
                chunks = []
                if seg == 0:
                    chunks.append((0, 16, 0, 'p', None))
                    for b in range(16):
                        chunks.append((1 + b, 4, 16 + 4 * b, 's', b))
                for i in range(4):
                    chunks.append((17 + i, 128, NA + 128 * i, 'p', None))
                if seg == 0:
                    P.add('sp', lambda E: E.dma_start(out=nSall[:, :, :, :].rearrange("p b c h -> p b (c h)"), in_=stn.rearrange("b p x -> p b x")), w=['nS0', 'nS1'], stream='ldn')
                class Lane:
                    def __init__(self):
                        self.ops = []

                    def add(self, *a, **k):
                        self.ops.append((a, k))
                mS1 = scrF[:, 2672:2676]

                def gate_chunk(PP, gs, ci, L, t0, kind, b):
                    if kind == 'p':
                        if isinstance(PP, Lane):
                            PP.ops.append(None)
                        mlstm_gates(ci, L, t0, mbc, 'mbc', mbc[:, :], 'mbc', gs=gs, PP=PP)
                        if isinstance(PP, Lane):
                            PP.ops.append(None)
                    else:
                        mSx, mSn = (mS, 'mS') if gs == 0 else (mS1, 'mS_g1')
                        PP.add('sp', lambda E, b=b: E.dma_start(out=mSx[:, :], in_=stm[b:b + 1, :].partition_broadcast(128)), w=[mSn], stream='ldm%d' % gs)
                        mlstm_gates(ci, L, t0, mSx, mSn, mSx[:, :], mSn, gs=gs, PP=PP)
                        PP.add('sp', lambda E, b=b: E.dma_start(out=oms[b:b + 1, :], in_=mSx[0:1, :]), r=[mSn], stream='stm%d' % gs)
                if seg == 0:
                    lanes = [Lane(), Lane()]
                    samp = [c_ for c_ in chunks if c_[3] == 's']
                    prm = [c_ for c_ in chunks if c_[3] == 'p']
                    for c_ in prm + samp[11:]:
                        gate_chunk(lanes[0], 0, *c_)
                    for c_ in samp[:11]:
                        gate_chunk(lanes[1], 1, *c_)
                    for i in range(max(len(lanes[0].ops), len(lanes[1].ops))):
                        for ln in lanes:
                            if i < len(ln.ops) and ln.ops[i] is not None:
                                P.add(*ln.ops[i][0], **ln.ops[i][1])
                    P.barrier()
                else:
                    for c_ in chunks:
                        gate_chunk(P, 0, *c_)
                for h in range(4):
                    wqk, wqkr = W.get(wblk[B_AQK + h], KC * 512)
                    wqk = wqk.rearrange("p (k n) -> p k n", n=512)
                    wv_, wvr = W.get(wblk[B_AV + h], KC * 512, hold=2)
                    wv_ = wv_.rearrange("p (k n) -> p k n", n=512)
                    wo_, wor = W.get(wblk[B_AO + h], KC * 512, hold=3)
                    wo_ = wo_.rearrange("p (k n) -> p k n", n=512)
                    for (b0, bn) in blocks:
                        bk = 'b%d' % b0
                        for c in range(4):
                            ps, psn = rot_psA()

                            def mmf(E, c=c, b0=b0, bn=bn, ps=ps, wqk=wqk):
                                for kc in range(KC):
                                    ins = E.matmul(ps[:, 0:bn], lhsT=wqk[:, kc, 128 * c:128 * c + 128], rhs=xn[:, kc, b0:b0 + bn], start=(kc == 0), stop=(kc == KC - 1))
                                return ins
                            P.add('pe', mmf, r=[wqkr, 'xn' + bk, 'xnall'], w=[psn])
                            if c < 2:
                                P.add('act', lambda E, c=c, b0=b0, bn=bn, ps=ps: E.activation(out=qTh[:, c, b0:b0 + bn], in_=ps[:, 0:bn], func=AF.Copy), r=[psn], w=['qkT'])
                            else:
                                P.add('act', lambda E, c=c, b0=b0, bn=bn, ps=ps: E.activation(out=kTh[:, c - 2, b0:b0 + bn], in_=ps[:, 0:bn], func=AF.Copy, scale=1.0 / 16.0), r=[psn], w=['qkT'])
                        for j in range(4):
                            ps, psn = rot_psA()

                            def mmf(E, j=j, b0=b0, bn=bn, ps=ps, wo_=wo_):
                                for kc in range(KC):
                                    ins = E.matmul(ps[:, 0:bn], lhsT=wo_[:, kc, 128 * j:128 * j + 128], rhs=xn[:, kc, b0:b0 + bn], start=(kc == 0), stop=(kc == KC - 1))
                                return ins
                            P.add('pe', mmf, r=[wor, 'xn' + bk, 'xnall'], w=[psn])
                            P.add('act', lambda E, j=j, b0=b0, bn=bn, ps=ps, h=h: E.activation(out=actT[:, 4 * h + j, b0:b0 + bn], in_=ps[:, 0:bn], func=AF.Sigmoid), r=[psn], w=['actT%d' % h])
                            P.add('dve', lambda E, j=j, b0=b0, bn=bn, h=h: E.tensor_scalar(actT[:, 4 * h + j, b0:b0 + bn], actT[:, 4 * h + j, b0:b0 + bn], gn[:, 7, 4 * h + j:4 * h + j + 1], None, ALU.mult),
                                  r=['actT%d' % h, 'gn'], w=['actT%d' % h])
                    if seg == 0:
                        pv_, pvn_ = rot_psA()
                        pk_, pkn_ = rot_psA()

                        def mm_va(E, pv_=pv_, pk_=pk_, wv_=wv_, wqk=wqk):
                            for c in range(KC):
                                E.matmul(pv_[0:NA, 0:512], lhsT=xn[:, c, 0:NA], rhs=wv_[:, c, :], start=(c == 0), stop=(c == KC - 1))
                            for c in range(KC):
                                ins = E.matmul(pk_[0:NA, 0:256], lhsT=xn[:, c, 0:NA], rhs=wqk[:, c, 256:512], start=(c == 0), stop=(c == KC - 1))
                            return ins
                        P.add('pe', mm_va, r=['xnall', wvr, wqkr], w=[pvn_, pkn_])
                        P.add('act', lambda E, pv_=pv_: E.activation(out=vA_tok[0:NA, :], in_=pv_[0:NA, 0:512], func=AF.Copy), r=[pvn_], w=['vA_tok'])
                        P.add('act', lambda E, pk_=pk_: E.activation(out=kA_tok[0:NA, :], in_=pk_[0:NA, 0:256], func=AF.Copy), r=[pkn_], w=['kA_tok'])
                    CsFs = [CsF, CsF2]
                    Csbs = [Csb, Csb2]
                    nsbs = [nsb, nsb2]

                    def ld_sample(b, h=h):
                        pb = b % 2
                        P.add('sp', lambda E: E.dma_start(out=CsFs[pb][:, :, :], in_=stC[b, h].rearrange("(c p) v -> p c v", p=128)), w=['CS%d' % pb] + (['vtk1', 'kwt1'] if pb == 0 else []), stream='ldC')
                    real = [c_ for c_ in chunks if c_[2] >= NA]
                    for (ci, L, t0, kind, b) in chunks:
                        if kind == 'p' and t0 >= NA:
                            ri = ci - 17
                            if ri == 0:
                                mlstm_chunk_front(h, ci, L, t0, wqk, wqkr, wv_, wvr, 0)
                            hook = None
                            if ri + 1 < 4:
                                nci, nL, nt0, _, _ = real[ri + 1]
                                hook = (lambda nci=nci, nL=nL, nt0=nt0, npar=(ri + 1) % 2, h=h, wqk=wqk, wqkr=wqkr, wv_=wv_, wvr=wvr:
                                        mlstm_chunk_front(h, nci, nL, nt0, wqk, wqkr, wv_, wvr, npar))
                            mlstm_head_chunk(h, ci, L, t0, wqk, wqkr, wv_, wvr, Cst[:, h, :, :], Cbf[:, h, :, :], nst[:, :, h], nbf[:, :, h], '%d' % h,
                                             par=ri % 2, front_done=True, mid_hook=hook)
                        elif kind == 'p':
                            mlstm_head_chunk(h, ci, L, t0, wqk, wqkr, wv_, wvr, Cst[:, h, :, :], Cbf[:, h, :, :], nst[:, :, h], nbf[:, :, h], '%d' % h)
                        else:
                            pb = b % 2

                            def cast_sample(bb, h=h):
                                pq = bb % 2
                                nq_ = nSall[:, bb, :, h]
                                P.add('act', lambda E: E.activation(out=Csbs[pq][:, :, :], in_=CsFs[pq][:, :, :], func=AF.Copy), r=['CS%d' % pq], w=['CbS%d' % pq])
                                P.add('act', lambda E: E.activation(out=nsbs[pq][:, :, 0], in_=nq_, func=AF.Copy), r=['nS%d' % pq], w=['nbS%d' % pq])
                            if b == 0:
                                ld_sample(0)
                                cast_sample(0)
                                mlstm_chunk_front(h, ci, L, t0, wqk, wqkr, wv_, wvr, 0)
                            hc = hf = None
                            if b + 1 < 16:
                                ld_sample(b + 1)
                                hc = (lambda b=b: cast_sample(b + 1))
                                hf = (lambda ci=ci, t0=t0, h=h, wqk=wqk, wqkr=wqkr, wv_=wv_, wvr=wvr: mlstm_chunk_front(h, ci + 1, 4, t0 + 4, wqk, wqkr, wv_, wvr, 0))
                            nfS = nSall[:, b, :, h]
                            mlstm_head_chunk(h, ci, L, t0, wqk, wqkr, wv_, wvr, CsFs[pb][:, :, :], Csbs[pb][:, :, :], nfS, nsbs[pb][:, :, 0], 'S%d' % pb,
                                             front_done=True, cast_back=False, hook_cast=hc, hook_front=hf)
                            P.add('sp', lambda E, b=b, h=h, pb=pb: E.dma_start(out=oCs[b, h].rearrange("(c p) v -> p c v", p=128), in_=CsFs[pb][:, :, :]), r=['CS%d' % pb], stream='stC')

            def mlstm_finish_sample():
                P.add('sp', lambda E: E.dma_start(out=ons.rearrange("b p x -> p b x"), in_=nSall[:, :, :, :].rearrange("p b c h -> p b (c h)")), r=['nS0', 'nS1'], stream='stn')


            def kv_phase(seg):
                wkv, wkvr = W.get(wblk[B_KV], KC * 512)
                wkv = wkv.rearrange("p (k n) -> p k n", n=512)
                kpad = scrB[:, 0:1024].rearrange("p (x d) -> p x d", d=128)
                knf = scrF[:, 600:856]
                vf = scrF[:, 856:1112]
                ssk = scrF[:, 1112:1116]
                junk = scrF[:, 1120:1184]
                psTv = psT[:, 0:1024].rearrange("p (x l) -> p x l", l=128)
                P.add('dve', lambda E: E.memset(kpad[:, :, :], 0.0), w=['kpad'])
                tiles = []
                if seg == 0:
                    tiles.append((NA, 0, 'A'))
                for i in range(4):
                    tiles.append((128, NA + 128 * i, i))
                for (L, t0, sl) in tiles:
                    ps, psn = rot_psA()

                    def mmf(E, L=L, t0=t0, ps=ps):
                        for kc in range(KC):
                            ins = E.matmul(ps[0:L, 0:512], lhsT=xn[:, kc, t0:t0 + L], rhs=wkv[:, kc, :], start=(kc == 0), stop=(kc == KC - 1))
                        return ins
                    P.add('pe', mmf, r=[wkvr, 'xnall'], w=[psn])
                    for g in range(4):
                        P.add('act', lambda E, g=g, L=L, ps=ps: E.activation(out=junk[0:L, :], in_=ps[0:L, 64 * g:64 * g + 64], func=AF.Square, accum_out=ssk[0:L, g:g + 1]),
                              r=[psn, 'junk'], w=['junk', 'ssk'])
                    P.add('act', lambda E, L=L: E.activation(out=ssk[0:L, :], in_=ssk[0:L, :], func=AF.Sqrt, bias=epsb[0:L, :], scale=1.0 / 64.0), r=['ssk', 'epsb'], w=['ssk'])
                    P.add('dve', lambda E, L=L: E.reciprocal(ssk[0:L, :], ssk[0:L, :]), r=['ssk'], w=['ssk'])
                    dk = knA if sl == 'A' else knf
                    dv = vA if sl == 'A' else vf
                    dkr = 'knA' if sl == 'A' else 'knf'
                    dvr = 'vA' if sl == 'A' else 'vf'
                    for g in range(4):
                        P.add('dve', lambda E, g=g, L=L, ps=ps, dk=dk: E.scalar_tensor_tensor(out=dk[0:L, 64 * g:64 * g + 64], in0=ps[0:L, 64 * g:64 * g + 64], scalar=ssk[0:L, g:g + 1],
                                                                                        in1=kg_bc[0:L, :], op0=ALU.mult, op1=ALU.mult), r=[psn, 'ssk', 'smc'], w=[dkr])
                    P.add('act', lambda E, L=L, ps=ps, dv=dv: E.activation(out=dv[0:L, :], in_=ps[0:L, 256:512], func=AF.Copy), r=[psn], w=[dvr])
                    if sl == 'A':
                        P.add('sp', lambda E: E.dma_start(out=okmp[:, :], in_=knA[0:16, :]), r=['knA'], stream='stkv')
                        P.add('sp', lambda E: E.dma_start(out=ovmp[:, :], in_=vA[0:16, :]), r=['vA'], stream='stkv2')
                    if seg == NSEG - 1 and sl == 3:
                        P.add('sp', lambda E: E.dma_start(out=okwp[:, :], in_=knf[:, :]), r=['knf'], stream='stkv')
                        P.add('sp', lambda E: E.dma_start(out=ovwp[:, :], in_=vf[:, :]), r=['vf'], stream='stkv2')
                    Lk = 16 if sl == 'A' else 128
                    dk3 = dk[0:Lk, :].rearrange("p (g d) -> p g d", d=64)
                    dv3 = dv[0:Lk, :].rearrange("p (g d) -> p g d", d=64)
                    kp4 = kpad[0:Lk, :, :].rearrange("p (g e) d -> p g e d", e=2)
                    P.add('dve', lambda E, kp4=kp4, dk3=dk3: E.tensor_copy(out=kp4[:, :, 0, 0:64], in_=dk3), r=[dkr], w=['kpad'])
                    P.add('dve', lambda E, kp4=kp4, dk3=dk3: E.tensor_copy(out=kp4[:, :, 1, 64:128], in_=dk3), r=[dkr], w=['kpad'])

                    def mm_t(E, Lk=Lk):
                        for x in range(8):
                            ins = E.transpose(psTv[:, x, 0:Lk], kpad[0:Lk, x, :], identb[0:Lk, 0:Lk])
                        return ins
                    P.add('pe', mm_t, r=['kpad', 'identb'], w=['psT'])
                    if sl == 'A':
                        P.add('act', lambda E: E.activation(out=kTm[:, :, :], in_=psTv[:, :, 0:16], func=AF.Copy), r=['psT'], w=['kTm'])
                        P.add('dve', lambda E, dv3=dv3: E.tensor_copy(out=vext[0:16, 0, :, 0:64], in_=dv3), r=[dvr], w=['vext0'])
                    else:
                        P.add('act', lambda E, sl=sl: E.activation(out=kTz[:, :, (1 + sl) * 128:(2 + sl) * 128], in_=psTv[:, :, :], func=AF.Copy), r=['psT'], w=['kTz%d' % (1 + sl)])
                        P.add('dve', lambda E, sl=sl, dv3=dv3: E.tensor_copy(out=vext[:, 2 + sl, :, 0:64], in_=dv3), r=[dvr], w=['vext%d' % (2 + sl)])

            qn = scrB[:, 0:KC * NT].rearrange("p (c t) -> p c t", t=NT)
            oT = scrB[:, KC * NT:2 * KC * NT].rearrange("p (c t) -> p c t", t=NT)
            ob = 2 * KC * NT
            otok = scrB[:, ob:ob + 2048]
            Pc = scrB[:, ob + 2048:ob + 2048 + 320]
            sqq = scrF[:, 0:256].bitcast(BF16)
            rq = scrF[:, 256:768]
            tmpS = scrF[:, 768:1280]
            Pcur = scrF[:, 1280:1536].bitcast(BF16)
            Pprev = scrF[:, 1536:1792].bitcast(BF16)
            Pmeta = scrF[:, 1792:2048].bitcast(BF16)
            dn = scrF[:, 2048:2052]
            qg8 = scrF[:, 2056:2057]
            kwin_f = scrF[:, 2064:2320]
            vwin_f = scrF[:, 2320:2576]
            k2_f = scrF[:, 2576:2832]
            sb2 = ob + 2048 + 320

            def q_proj(blocks):
                P.add('dve', lambda E: E.tensor_scalar(qg8[:, :], qgt[:, :], 0.125, None, ALU.mult), r=['qgt'], w=['qg8'])
                for jb in range(4):
                    wv, wr = W.get(wblk[B_Q + jb], KC * 512)
                    wv = wv.rearrange("p (k n) -> p k n", n=512)
                    for j in range(4):
                        c = 4 * jb + j
                        for (b0, bn) in blocks:
                            bk = 'b%d' % b0
                            ps, psn = rot_psA()

                            def mmf(E, wv=wv, j=j, b0=b0, bn=bn, ps=ps):
                                for kc in range(KC):
                                    ins = E.matmul(ps[:, 0:bn], lhsT=wv[:, kc, 128 * j:128 * j + 128], rhs=xn[:, kc, b0:b0 + bn], start=(kc == 0), stop=(kc == KC - 1))
                                return ins
                            P.add('pe', mmf, r=[wr, 'xn' + bk], w=[psn])
                            P.add('act', lambda E, bn=bn, ps=ps: E.activation(out=sqq[:, 0:bn], in_=ps[:, 0:bn], func=AF.Square), r=[psn], w=['sqq'])
                            ps2, ps2n = rot_psA()
                            P.add('pe', lambda E, bn=bn, ps2=ps2: E.matmul(ps2[:, 0:bn], lhsT=bd64[:, :], rhs=sqq[:, 0:bn], start=True, stop=True), r=['bd64', 'sqq'], w=[ps2n])
                            P.add('act', lambda E, bn=bn, ps2=ps2: E.activation(out=rq[:, 0:bn], in_=ps2[:, 0:bn], func=AF.Sqrt, bias=epsb[:, :], scale=1.0), r=[ps2n, 'epsb'], w=['rq'])
                            P.add('dve', lambda E, bn=bn: E.reciprocal(rq[:, 0:bn], rq[:, 0:bn]), r=['rq'], w=['rq'])
                            P.add('dve', lambda E, c=c, b0=b0, bn=bn, ps=ps: E.scalar_tensor_tensor(out=qn[:, c, b0:b0 + bn], in0=ps[:, 0:bn], scalar=qg8[:, 0:1], in1=rq[:, 0:bn],
                                                                                              op0=ALU.mult, op1=ALU.mult), r=[psn, 'qg8', 'rq'], w=['qn' + bk])

            def attn_core(nq, qcols, keysets, res_in, out_rows_ap, meta0=False):
                nk_list = [ks[1] for ks in keysets]
                Pt = [Pcur, Pprev, Pmeta]
                for g in range(4):
                    for e in range(2):
                        kx = 2 * g + e
                        heads = [8 * g + 2 * j + e for j in range(4)]
                        for si, (kT, nk, vx, Rt, kind) in enumerate(keysets):
                            pb = psB[si]
                            pbv = pb[:, 0:4 * nq].rearrange("p (j q) -> p j q", q=nq)
                            P.add('pe', lambda E, kT=kT, nk=nk, pbv=pbv, kx=kx, g=g: E.matmul(pbv[0:nk, :, :], lhsT=kT[:, kx, 0:nk], rhs=qn[:, 4 * g:4 * g + 4, qcols[0]:qcols[1]], start=True, stop=True),
                                  r=res_in + ['qnall'], w=['psB%d' % si])
                            Pv = Pt[si][:, 0:4 * nq].rearrange("p (j q) -> p j q", q=nq)
                            tv = tmpS[:, 0:4 * nq].rearrange("p (j q) -> p j q", q=nq)
                            if kind == 'R':
                                for j in range(4):
                                    P.add('dve', lambda E, j=j, nk=nk, Rt=Rt, pbv=pbv, tv=tv, heads=heads: E.scalar_tensor_tensor(out=tv[0:nk, j, :], in0=Rt[0:nk, 0:nq], scalar=float(SLOPES[heads[j]]),
                                                                                                                in1=pbv[0:nk, j, :], op0=ALU.mult, op1=ALU.add),
                                          r=['psB%d' % si, 'ctf'], w=['tmpS'])
                                P.add('act', lambda E, nk=nk, Pv=Pv, tv=tv: E.activation(out=Pv[0:nk, :, :], in_=tv[0:nk, :, :], func=AF.Exp), r=['tmpS'], w=['P%d' % si])
                            else:
                                for j in range(4):
                                    P.add('act', lambda E, j=j, nk=nk, Pv=Pv, pbv=pbv, heads=heads: E.activation(out=Pv[0:nk, j, :], in_=pbv[0:nk, j, :], func=AF.Exp,
                                                                                             bias=nslope128[0:nk, heads[j]:heads[j] + 1], scale=1.0),
                                          r=['psB%d' % si, 'smc'], w=['P%d' % si])
                        po, pon = rot_psA()

                        def mm_pv(E, po=po, g=g):
                            for j in range(4):
                                for si, (kT, nk, vx, Rt, kind) in enumerate(keysets):
                                    Pv = Pt[si][:, 0:4 * nq].rearrange("p (j q) -> p j q", q=nq)
                                    ins = E.matmul(po[0:nq, 65 * j:65 * j + 65], lhsT=Pv[0:nk, j, :], rhs=vx[0:nk, g, :], start=(si == 0), stop=(si == len(keysets) - 1))
                            return ins
                        P.add('pe', mm_pv, r=['P%d' % si for si in range(len(keysets))] + res_in, w=[pon])
                        pov = po[:, 0:260].rearrange("p (j d) -> p j d", d=65)
                        esv = esink[:, 8 * g:8 * g + 8].rearrange("p (j e) -> p j e", e=2)
                        P.add('dve', lambda E, pov=pov, esv=esv, e=e: E.tensor_tensor(out=dn[0:nq, :], in0=pov[0:nq, :, 64], in1=esv[0:nq, :, e], op=ALU.add), r=[pon, 'esink'], w=['dn'])
                        P.add('dve', lambda E: E.reciprocal(dn[0:nq, :], dn[0:nq, :]), r=['dn'], w=['dn'])
                        for j in range(4):
                            hh = heads[j]
                            P.add('dve', lambda E, j=j, hh=hh, pov=pov: E.tensor_scalar(otok[0:nq, 64 * hh:64 * hh + 64], pov[0:nq, j, 0:64], dn[0:nq, j:j + 1], None, ALU.mult),
                                  r=[pon, 'dn'], w=['otok'])

            def attn_prompt(seg):
                psTv = psT[:, 0:1024].rearrange("p (x l) -> p x l", l=128)
                def attn_block(i):
                    t0 = NA + 128 * i
                    first = (seg == 0 and i == 0)
                    keysets = [(kTz[:, :, (1 + i) * 128:(2 + i) * 128], 128, vext[:, 2 + i, :, :], Rcur, 'R')]
                    if not first:
                        keysets.append((kTz[:, :, i * 128:(1 + i) * 128], 128, vext[:, 1 + i, :, :], Rprev, 'R'))
                        keysets.append((kTm[:, :, :], 16, vext[:, 0, :, :], None, 'C'))
                    else:
                        keysets.append((kTm[:, :, :], 16, vext[:, 0, :, :], Rmeta0, 'R'))
                    psT32 = psT[:, :].bitcast(F32)
                    sets = [([psB[0], psB[1], psB[2]], ['psB0', 'psB1', 'psB2'], psA[3], 'psA3'),
                            ([psA[0], psA[1], psA[2]], ['psA0', 'psA1', 'psA2'], psT32, 'psT')]
                    Psets = [[Pcur, Pprev, Pmeta],
                             [scrF[:, 2064:2320].bitcast(BF16), scrF[:, 2320:2576].bitcast(BF16), scrF[:, 2576:2832].bitcast(BF16)]]
                    tmps = [tmpS, scrF[:, 0:512]]
                    dns = [scrF[:, 2048:2052], scrF[:, 2052:2056]]
                    nks = len(keysets)

                    def st_S(k):
                        g, e, p = k // 2, k % 2, k % 2
                        banks, bnames, _, _ = sets[p]
                        for si, (kT, nk, vx, Rt, kind) in enumerate(keysets):
                            pbv = banks[si][:, 0:512].rearrange("p (j q) -> p j q", q=128)
                            P.add('pe', lambda E, kT=kT, nk=nk, pbv=pbv, g=g, e=e: E.matmul(pbv[0:nk, :, :], lhsT=kT[:, 2 * g + e, 0:nk], rhs=qn[:, 4 * g:4 * g + 4, t0:t0 + 128], start=True, stop=True),
                                  r=['kTzall', 'qnall'], w=[bnames[si]])

                    def st_BX(k):
                        g, e, p = k // 2, k % 2, k % 2
                        banks, bnames, _, _ = sets[p]
                        heads = [8 * g + 2 * j + e for j in range(4)]
                        tv = tmps[p][:, 0:512].rearrange("p (j q) -> p j q", q=128)
                        for si, (kT, nk, vx, Rt, kind) in enumerate(keysets):
                            pbv = banks[si][:, 0:512].rearrange("p (j q) -> p j q", q=128)
                            Pv = Psets[p][si][:, 0:512].rearrange("p (j q) -> p j q", q=128)
                            if kind == 'R':
                                for j in range(4):
                                    P.add('dve', lambda E, j=j, nk=nk, Rt=Rt, pbv=pbv, tv=tv, heads=heads: E.scalar_tensor_tensor(
                                        out=tv[0:nk, j, :], in0=Rt[0:nk, 0:128], scalar=float(SLOPES[heads[j]]), in1=pbv[0:nk, j, :], op0=ALU.mult, op1=ALU.add),
                                        r=[bnames[si], 'ctf'], w=['tmpS%d' % p])
                                P.add('act', lambda E, nk=nk, Pv=Pv, tv=tv: E.activation(out=Pv[0:nk, :, :], in_=tv[0:nk, :, :], func=AF.Exp), r=['tmpS%d' % p], w=['P%d_%d' % (p, si)])
                            else:
                                for j in range(4):
                                    P.add('act', lambda E, j=j, nk=nk, Pv=Pv, pbv=pbv, heads=heads: E.activation(out=Pv[0:nk, j, :], in_=pbv[0:nk, j, :], func=AF.Exp,
                                                                                                          bias=nslope128[0:nk, heads[j]:heads[j] + 1], scale=1.0),
                                          r=[bnames[si], 'smc'], w=['P%d_%d' % (p, si)])

                    def st_PV(k):
                        g, e, p = k // 2, k % 2, k % 2
                        _, _, po, pon = sets[p]

                        def mm_pv(E, po=po, g=g, p=p):
                            for j in range(4):
                                for si, (kT, nk, vx, Rt, kind) in enumerate(keysets):
                                    Pv = Psets[p][si][:, 0:512].rearrange("p (j q) -> p j q", q=128)
                                    ins = E.matmul(po[:, 65 * j:65 * j + 65], lhsT=Pv[0:nk, j, :], rhs=vx[0:nk, g, :], start=(si == 0), stop=(si == nks - 1))
                            return ins
                        P.add('pe', mm_pv, r=['P%d_%d' % (p, si) for si in range(nks)] + ['kTzall'], w=[pon])

                    def st_E(k):
                        g, e, p = k // 2, k % 2, k % 2
                        _, _, po, pon = sets[p]
                        dn_ = dns[p]
                        heads = [8 * g + 2 * j + e for j in range(4)]
                        pov = po[:, 0:260].rearrange("p (j d) -> p j d", d=65)
                        esv = esink[:, 8 * g:8 * g + 8].rearrange("p (j e) -> p j e", e=2)
                        P.add('dve', lambda E, pov=pov, esv=esv, e=e, dn_=dn_: E.tensor_tensor(out=dn_[:, :], in0=pov[:, :, 64], in1=esv[:, :, e], op=ALU.add), r=[pon, 'esink'], w=['dn%d' % p])
                        P.add('dve', lambda E, dn_=dn_: E.reciprocal(dn_[:, :], dn_[:, :]), r=['dn%d' % p], w=['dn%d' % p])
                        for j in range(4):
                            hh = heads[j]
                            P.add('dve', lambda E, j=j, hh=hh, pov=pov, dn_=dn_: E.tensor_scalar(otok[:, 64 * hh:64 * hh + 64], pov[:, j, 0:64], dn_[:, j:j + 1], None, ALU.mult),
                                  r=[pon, 'dn%d' % p], w=['otok%d' % k])

                    st_S(0)
                    st_BX(0)
                    for k in range(8):
                        if k + 1 < 8:
                            st_S(k + 1)
                            st_BX(k + 1)
                        st_PV(k)
                        st_E(k)
                    for half in range(2):
                        def mm_t(E, half=half):
                            for x in range(8):
                                c = 8 * half + x
                                ins = E.transpose(psTv[:, x, :], otok[:, 128 * c:128 * c + 128], identb[:, :])
                            return ins
                        P.add('pe', mm_t, r=['otok%d' % k for k in range(8)] + ['identb'], w=['psT'])
                        P.add('act', lambda E, half=half, t0=t0: E.activation(out=oT[:, 8 * half:8 * half + 8, t0:t0 + 128], in_=psTv[:, :, :], func=AF.Copy), r=['psT'], w=['oT'])
                for i in range(4):
                    attn_block(i)
                P.add('dve', lambda E: E.tensor_copy(out=kTz[:, :, 0:128], in_=kTz[:, :, 512:640]), r=['kTzall'], w=['kTzall'])
                P.add('dve', lambda E: E.tensor_copy(out=vext[:, 1, :, :], in_=vext[:, 5, :, :]), r=['kTzall'], w=['kTzall'])

            def attn_sample():
                psTv = psT[:, 0:1024].rearrange("p (x l) -> p x l", l=128)
                xf = xn[:, :, :].rearrange("p c t -> p (c t)")
                kpadS = xf[:, 0:1024].rearrange("p (x d) -> p x d", d=128)
                kTzS = xf[:, 1024:2048].rearrange("p (x d) -> p x d", d=128)
                kpad2 = xf[:, 2048:3072].rearrange("p (x d) -> p x d", d=128)
                kTz2 = xf[:, 3072:3328].rearrange("p (x d) -> p x d", d=32)
                vxS1 = xf[:, 3328:3588].rearrange("p (g d) -> p g d", d=65)
                vxS2 = xf[:, 3588:3848].rearrange("p (g d) -> p g d", d=65)
                oS_all = xf[:, 3848:5896]
                v2_f = xf[:, 5896:6408].bitcast(F32)
                P.add('dve', lambda E: E.memset(xf[:, 0:3328], 0.0), w=['kpadS', 'kpad2', 'kTzS', 'kTz2'])
                P.add('dve', lambda E: E.memset(xf[:, 3328:3848], 1.0), w=['vxS1', 'vxS2'])
                SR1 = xf[:, 6408:6664].bitcast(F32)
                SR2 = xf[:, 6664:6920].bitcast(F32)
                esP = xf[:, 6920:6984].bitcast(F32)
                dnS = xf[:, 6984:7048].bitcast(F32)
                tmp1 = tmpS[:, 0:128]
                tmp2 = tmpS[:, 128:256]
                P1 = Pcur[:, 0:128]
                P2 = Pprev[:, 0:128]
                for x8 in range(8):
                    g_, e_ = x8 // 2, x8 % 2
                    for j in range(4):
                        hh = 8 * g_ + 2 * j + e_
                        sidx = 4 * x8 + j
                        P.add('dve', lambda E, sidx=sidx, hh=hh: E.tensor_scalar(SR1[:, 4 * sidx:4 * sidx + 4], RwinS, float(SLOPES[hh]), None, ALU.mult), r=['ctf'], w=['SR1'])
                        P.add('dve', lambda E, sidx=sidx, hh=hh: E.tensor_scalar(SR2[0:20, 4 * sidx:4 * sidx + 4], R2S[0:20, :], float(SLOPES[hh]), None, ALU.mult), r=['ctf'], w=['SR2'])
                    esv = esink[:, 8 * g_:8 * g_ + 8].rearrange("p (j e) -> p j e", e=2)
                    P.add('dve', lambda E, x8=x8, esv=esv, e_=e_: E.tensor_copy(out=esP[:, 4 * x8:4 * x8 + 4], in_=esv[:, :, e_]), r=['esink'], w=['esP'])
                banks = [(psA[0], 'psA0'), (psA[1], 'psA1'), (psA[2], 'psA2'), (psA[3], 'psA3'), (psB[2], 'psB2')]
                kpS4 = kpadS[:, :, :].rearrange("p (g e) d -> p g e d", e=2)
                kp24 = kpad2[0:20, :, :].rearrange("p (g e) d -> p g e d", e=2)
                kw3 = kwin_f[:, :].rearrange("p (g d) -> p g d", d=64)
                vw3 = vwin_f[:, :].rearrange("p (g d) -> p g d", d=64)
                k23 = k2_f[0:20, :].rearrange("p (g d) -> p g d", d=64)
                v23 = v2_f[0:20, :].rearrange("p (g d) -> p g d", d=64)
                for b in range(16):
                    r0 = 16 + 4 * b
                    P.add('sp', lambda E, b=b: E.dma_start(out=kwin_f[:, :], in_=ckw[b]), w=['kwin_f'], stream='ldk0')
                    P.add('sp', lambda E, b=b: E.dma_start(out=vwin_f[:, :], in_=cvw[b]), w=['vwin_f'], stream='ldk1')
                    P.add('sp', lambda E, b=b: E.dma_start(out=k2_f[0:16, :], in_=ckm[b]), w=['k2_f'], stream='ldk2')
                    P.add('sp', lambda E, r0=r0: E.dma_start(out=k2_f[16:20, :], in_=knA[r0:r0 + 4, :]), r=['knA'], w=['k2_f'], stream='ldk2')
                    P.add('sp', lambda E, b=b: E.dma_start(out=v2_f[0:16, :], in_=cvm[b]), w=['v2_f'], stream='ldk3')
                    P.add('sp', lambda E, r0=r0: E.dma_start(out=v2_f[16:20, :], in_=vA[r0:r0 + 4, :]), r=['vA'], w=['v2_f'], stream='ldk3')
                    P.add('sp', lambda E, b=b: E.dma_start(out=okws[b, 0:124, :], in_=kwin_f[4:128, :]), r=['kwin_f'], stream='stkw')
                    P.add('sp', lambda E, b=b, r0=r0: E.dma_start(out=okws[b, 124:128, :], in_=knA[r0:r0 + 4, :]), r=['knA'], stream='stkw')
                    P.add('sp', lambda E, b=b: E.dma_start(out=ovws[b, 0:124, :], in_=vwin_f[4:128, :]), r=['vwin_f'], stream='stkw2')
                    P.add('sp', lambda E, b=b, r0=r0: E.dma_start(out=ovws[b, 124:128, :], in_=vA[r0:r0 + 4, :]), r=['vA'], stream='stkw2')
                    P.add('dve', lambda E: E.tensor_copy(out=kpS4[:, :, 0, 0:64], in_=kw3), r=['kwin_f'], w=['kpadS'])
                    P.add('dve', lambda E: E.tensor_copy(out=kpS4[:, :, 1, 64:128], in_=kw3), r=['kwin_f'], w=['kpadS'])
                    P.add('dve', lambda E: E.tensor_copy(out=kp24[:, :, 0, 0:64], in_=k23), r=['k2_f'], w=['kpad2'])
                    P.add('dve', lambda E: E.tensor_copy(out=kp24[:, :, 1, 64:128], in_=k23), r=['k2_f'], w=['kpad2'])

                    def mm_t1(E):
                        for x in range(8):
                            ins = E.transpose(psTv[:, x, :], kpadS[:, x, :], identb[:, :])
                        return ins
                    P.add('pe', mm_t1, r=['kpadS', 'identb'], w=['psT'])
                    P.add('act', lambda E: E.activation(out=kTzS[:, :, :], in_=psTv[:, :, :], func=AF.Copy), r=['psT'], w=['kTzS'])

                    def mm_t2(E):
                        for x in range(8):
                            ins = E.transpose(psTv[:, x, 0:20], kpad2[0:20, x, :], identb[0:20, 0:20])
                        return ins
                    P.add('pe', mm_t2, r=['kpad2', 'identb'], w=['psT'])
                    P.add('act', lambda E: E.activation(out=kTz2[:, :, 0:20], in_=psTv[:, :, 0:20], func=AF.Copy), r=['psT'], w=['kTz2'])
                    P.add('dve', lambda E: E.tensor_copy(out=vxS1[:, :, 0:64], in_=vw3), r=['vwin_f'], w=['vxS1'])
                    P.add('dve', lambda E: E.tensor_copy(out=vxS2[0:20, :, 0:64], in_=v23), r=['v2_f'], w=['vxS2'])
                    def mm_sc(E, r0=r0):
                        for x8 in range(8):
                            g_ = x8 // 2
                            o0 = psB[0][:, 16 * x8:16 * x8 + 16].rearrange("p (j q) -> p j q", q=4)
                            o1 = psB[1][:, 16 * x8:16 * x8 + 16].rearrange("p (j q) -> p j q", q=4)
                            E.matmul(o0[:, :, :], lhsT=kTzS[:, x8, :], rhs=qn[:, 4 * g_:4 * g_ + 4, r0:r0 + 4], start=True, stop=True)
                            ins = E.matmul(o1[0:20, :, :], lhsT=kTz2[:, x8, 0:20], rhs=qn[:, 4 * g_:4 * g_ + 4, r0:r0 + 4], start=True, stop=True)
                        return ins
                    P.add('pe', mm_sc, r=['kTzS', 'kTz2', 'qnall'], w=['psB0', 'psB1'])
                    P.add('dve', lambda E: E.tensor_tensor(out=tmp1[:, :], in0=psB[0][:, 0:128], in1=SR1[:, :], op=ALU.add), r=['psB0', 'SR1'], w=['tmp1'])
                    P.add('dve', lambda E: E.tensor_tensor(out=tmp2[0:20, :], in0=psB[1][0:20, 0:128], in1=SR2[0:20, :], op=ALU.add), r=['psB1', 'SR2'], w=['tmp2'])
                    P.add('act', lambda E: E.activation(out=P1[:, :], in_=tmp1[:, :], func=AF.Exp), r=['tmp1'], w=['P1s'])
                    P.add('act', lambda E: E.activation(out=P2[0:20, :], in_=tmp2[0:20, :], func=AF.Exp), r=['tmp2'], w=['P2s'])

                    def mm_pvs(E):
                        for hh in range(32):
                            g_, j_, e_ = hh // 8, (hh % 8) // 2, hh % 2
                            sidx = 4 * (2 * g_ + e_) + j_
                            bank = banks[hh // 7][0]
                            col = (hh % 7) * 65
                            E.matmul(bank[0:4, col:col + 65], lhsT=P1[:, 4 * sidx:4 * sidx + 4], rhs=vxS1[:, g_, :], start=True, stop=False)
                            ins = E.matmul(bank[0:4, col:col + 65], lhsT=P2[0:20, 4 * sidx:4 * sidx + 4], rhs=vxS2[0:20, g_, :], start=False, stop=True)
                        return ins
                    P.add('pe', mm_pvs, r=['P1s', 'P2s', 'vxS1', 'vxS2'], w=[bn_ for (_, bn_) in banks])
                    for k, (bank, bname) in enumerate(banks):
                        s0 = 7 * k
                        nk_ = min(7, 32 - s0)
                        bv = bank[:, 0:nk_ * 65].rearrange("p (s d) -> p s d", d=65)
                        P.add('dve', lambda E, bv=bv, s0=s0, nk_=nk_: E.tensor_tensor(out=dnS[0:4, s0:s0 + nk_], in0=bv[0:4, :, 64], in1=esink[0:4, s0:s0 + nk_], op=ALU.add),
                              r=[bname, 'esink'], w=['dnS%d' % k])
                    P.add('dve', lambda E: E.reciprocal(dnS[0:4, 0:32], dnS[0:4, 0:32]), r=['dnS%d' % k for k in range(5)], w=['dnSr'])
                    for k, (bank, bname) in enumerate(banks):
                        s0 = 7 * k
                        nk_ = min(7, 32 - s0)
                        bv = bank[:, 0:nk_ * 65].rearrange("p (s d) -> p s d", d=65)
                        otv = otok[0:4, 64 * s0:64 * (s0 + nk_)].rearrange("p (s d) -> p s d", d=64)
                        P.add('dve', lambda E, bv=bv, s0=s0, nk_=nk_, otv=otv: E.tensor_tensor(out=otv, in0=bv[0:4, :, 0:64], in1=dnS[0:4, s0:s0 + nk_, None].broadcast_to([4, nk_, 64]), op=ALU.mult),
                              r=[bname, 'dnSr'], w=['otok'])
                    P.add('pool', lambda E, b=b: E.dma_start(out=oS_all[4 * b:4 * b + 4, :], in_=otok[0:4, :]), r=['otok'], w=['oS_all'], stream='mvo')
                for half in range(2):
                    def mm_t(E, half=half):
                        for x in range(8):
                            c = 8 * half + x
                            ins = E.transpose(psTv[:, x, 0:64], oS_all[0:64, 128 * c:128 * c + 128], identb[0:64, 0:64])
                        return ins
                    P.add('pe', mm_t, r=['oS_all', 'identb'], w=['psT'])
                    P.add('act', lambda E, half=half: E.activation(out=oT[:, 8 * half:8 * half + 8, 16:NA], in_=psTv[:, :, 0:64], func=AF.Copy), r=['psT'], w=['oT'])

            gen_state['kv_phase'] = kv_phase
            gen_state['mlstm'] = mlstm
            gen_state['rmsnorm'] = rmsnorm
            gen_state['ffn'] = ffn
            gen_state['out_proj'] = out_proj

            for seg in range(NSEG):
                blocks = [(NA, TS)] if seg > 0 else [(0, NA), (NA, TS)]
                allres = ['hTb%d' % b0 for (b0, _) in blocks]
                if seg == 0:
                    P.add('sp', lambda E: E.dma_start(out=hT[:, :, 0:NA], in_=xA[:, :, :]), w=['hTb0'], stream='ldx0')
                    P.add('sp', lambda E, seg=seg: E.dma_start(out=hT[:, :, NA:NT], in_=xR[seg]), w=['hTb%d' % NA], stream='ldx1')
                P.barrier()
                rmsnorm(0, blocks)
                ffn(0, blocks, seg)
                rmsnorm(4, blocks)
                P.add('dve', lambda E: E.tensor_copy(out=scrF[0:1, 0:1], in_=scrF[0:1, 0:1]), r=['xnb%d' % b0 for (b0, _) in blocks], w=['xnall'])
                P.barrier()
                mlstm(seg, blocks)
                if seg == 0:
                    mlstm_finish_sample()
                P.barrier()
                out_proj(B_AOUT, actT, 'actTall', blocks)
                P.barrier()
                rmsnorm(1, blocks)
                ffn(1, blocks, seg)
                rmsnorm(6, blocks)
                P.add('dve', lambda E: E.tensor_copy(out=scrF[0:1, 0:1], in_=scrF[0:1, 0:1]), r=['xnb%d' % b0 for (b0, _) in blocks], w=['xnall'])
                P.barrier()
                kv_phase(seg)
                P.barrier()
                rmsnorm(2, blocks, reuse=True)
                ffn(2, blocks, seg)
                rmsnorm(5, blocks)
                P.barrier()
                q_proj(blocks)
                P.add('dve', lambda E: E.memset(oT[:, :, 0:NA], 0.0), w=['oT'])
                P.add('dve', lambda E: E.tensor_copy(out=scrF[0:1, 0:1], in_=scrF[0:1, 0:1]), r=['qnb%d' % b0 for (b0, _) in blocks] + ['kTz%d' % x for x in range(1, 5)] + ['vext%d' % x for x in range(2, 6)] + ['kTm', 'vext0'], w=['qnall', 'kTzall'])
                P.barrier()
                attn_prompt(seg)
                if seg == 0:
                    P.barrier()
                    attn_sample()
                P.barrier()
                out_proj(B_BOUT, oT, 'oTall', blocks)
                P.barrier()
                rmsnorm(3, blocks)
                ffn(3, blocks, seg, stream_io=True)
                P.barrier()
                if seg == 0:
                    P.add('sp', lambda E: E.dma_start(out=yA[:, :, :], in_=hT[:, :, 0:NA]), r=['hTb0'], stream='sty0')
                P.barrier()
            for h in range(4):
                P.add('sp', lambda E, h=h: E.dma_start(out=oCp[h].rearrange("(c p) v -> p c v", p=128), in_=Cst[:, h, :, :]), r=['C%d' % h], stream='stCp')
            P.add('sp', lambda E: E.dma_start(out=onp[:, :], in_=nst[:, :, :].rearrange("p c h -> p (c h)")), r=['n0', 'n1', 'n2', 'n3'], stream='stnp')
            P.add('sp', lambda E: E.dma_start(out=omp[:, :], in_=mbc[0:1, :]), r=['mbc'], stream='stmp')

        Pd = Prog(nc, dry=True)
        Wd = WRing(Pd, slots)
        gen(Pd, Wd)
        P = Prog(nc)
        W = WRing(P, slots, schedule=Wd.record)
        gen(P, W)
        P.emit(st, final_streams=['sty0', 'sty1', 'stCp', 'stnp', 'stmp', 'stC', 'stn', 'stm0', 'stm1', 'stkv', 'stkv2', 'stkw', 'stkw2'])
        build_program.stats = P.stats
    return nc


def _fm(x2d):
    T = x2d.shape[0]
    return np.ascontiguousarray(x2d.T.reshape(KC, 128, T).transpose(1, 0, 2))


def _blk(Wm, cols):
    return Wm[:, cols].reshape(KC, 128, len(cols)).transpose(1, 0, 2).reshape(128, KC * len(cols))


def _const_tables():
    t = np.zeros((128, 12, 128), np.float32)
    p = np.arange(128)[:, None]
    f = np.arange(128)[None, :]
    t[:, 0] = (p == f)
    t[:, 1] = (p <= f)
    t[:, 2] = np.where(f <= p, 0.0, NEG)
    t[:, 3] = np.where(p <= f, 0.0, NEG)
    t[:, 4] = (p == 127) * np.ones((1, 128))
    t[:, 5] = (p == 15) * np.ones((1, 128))
    t[:, 6] = (p == 3) * np.ones((1, 128))
    t[:, 7] = ((p // 64) == (f // 64)) / 64.0
    t[:, 8] = np.where(f >= p, -(f - p).astype(np.float32), NEG)
    t[:, 9] = np.where(p > f, -(f + 128 - p).astype(np.float32), NEG)
    t[:, 10] = -np.minimum(16 + f - p, 128).astype(np.float32)
    i4 = np.arange(4)[None, :]
    t[:, 11, 0:4] = np.where(p > i4, -(128 + i4 - p).astype(np.float32), NEG)
    r2 = np.full((128, 4), -128.0, np.float32)
    for j in range(4):
        r2[16 + j] = np.where(j <= np.arange(4), -(np.arange(4) - j).astype(np.float32), NEG)
    t[:, 11, 4:8] = r2
    return t


def _prep_shared(inp):
    w_in = inp['w_ffn_in']
    w_out = inp['w_ffn_out']
    wblk = np.empty((NBLK, 128, KC * 512), np.float32)
    woutb = np.empty((64, 128, FC * 128), np.float32)
    for l in range(2):
        for i in range(2):
            f = 2 * l + i
            Wm = w_in[l, i]
            for b in range(22):
                cols = np.concatenate([np.arange(256 * b, 256 * b + 256), DFF + np.arange(256 * b, 256 * b + 256)])
                wblk[B_FFN + 22 * f + b] = _blk(Wm, cols)
            Wo = w_out[l, i]
            for oc in range(16):
                woutb[16 * f + oc] = Wo[:, 128 * oc:128 * oc + 128].reshape(FC, 128, 128).transpose(1, 0, 2).reshape(128, FC * 128)
    wa = inp['w_a_in'][0]
    for h in range(4):
        wblk[B_AQK + h] = _blk(wa, np.concatenate([np.arange(256 * h, 256 * h + 256), 1024 + np.arange(256 * h, 256 * h + 256)]))
        wblk[B_AV + h] = _blk(wa, 2048 + np.arange(512 * h, 512 * h + 512))
        wblk[B_AO + h] = _blk(wa, 4096 + np.arange(512 * h, 512 * h + 512))
        wblk[B_AOUT + h] = _blk(inp['w_a_out'][0], np.arange(512 * h, 512 * h + 512))
        wblk[B_Q + h] = _blk(inp['w_q'][0], np.arange(512 * h, 512 * h + 512))
        wblk[B_BOUT + h] = _blk(inp['w_b_out'][0], np.arange(512 * h, 512 * h + 512))
    wblk[B_KV] = _blk(inp['w_kv'], np.arange(512))
    wgate = np.ascontiguousarray(wa[:, 6144:6152].reshape(KC, 128, 8).transpose(1, 0, 2).reshape(128, KC * 8))
    gl = [inp['ffn_norm'][0, 0], inp['ffn_norm'][0, 1], inp['ffn_norm'][1, 0], inp['ffn_norm'][1, 1],
          inp['mix_norm'][0], inp['mix_norm'][1], inp['kv_norm'], inp['a_head_norm'][0]]
    gains = np.ascontiguousarray(np.stack([g.reshape(KC, 128).T for g in gl], axis=1)).astype(np.float32)
    nsl = np.array([-128.0 * s for s in SLOPES], np.float32)
    smallc = np.concatenate([inp['b_a_gate'][0], inp['k_norm'], inp['sinks'][0], nsl]).astype(np.float32)[None, :]
    qg = np.ascontiguousarray(np.tile(inp['q_norm'][0], 2)[:, None]).astype(np.float32)
    return dict(wblk=wblk, wout=woutb, wgate=wgate, gains=gains, smallc=smallc, qg=qg, ctab=_const_tables())


def _prep_core(inp, c):
    xs = inp['x_sample'][16 * c:16 * c + 16].reshape(64, D)
    xA = _fm(np.concatenate([inp['meta_tokens'], xs], axis=0))
    xp = inp['x_prompt'][c]
    xR = np.stack([_fm(xp[TS * s:TS * s + TS]) for s in range(NSEG)])
    stn = inp['state_n'][0, 16 * c:16 * c + 16]
    stn = np.ascontiguousarray(stn.reshape(16, 4, 2, 128).transpose(0, 3, 2, 1).reshape(16, 128, 8))
    return dict(
        xA=xA, xR=xR,
        stC=np.ascontiguousarray(inp['state_C'][0, 16 * c:16 * c + 16]),
        stn=stn,
        stm=np.ascontiguousarray(inp['state_m'][0, 16 * c:16 * c + 16]),
        ckm=np.ascontiguousarray(inp['cache_k_meta'][16 * c:16 * c + 16].reshape(16, 16, 256)),
        cvm=np.ascontiguousarray(inp['cache_v_meta'][16 * c:16 * c + 16].reshape(16, 16, 256)),
        ckw=np.ascontiguousarray(inp['cache_k_win'][16 * c:16 * c + 16].reshape(16, 128, 256)),
        cvw=np.ascontiguousarray(inp['cache_v_win'][16 * c:16 * c + 16].reshape(16, 128, 256)),
    )


def _tm(a):
    T = a.shape[2]
    return a.transpose(1, 0, 2).reshape(D, T).T


def _assemble(results):
    n = len(results)
    y_prompt = np.empty((n, 2048, D), np.float32)
    y_sample = np.empty((16 * n, 4, D), np.float32)
    c_p = np.empty((1, n, 4, 256, 512), np.float32)
    n_p = np.empty((1, n, 4, 256), np.float32)
    m_p = np.empty((1, n, 4), np.float32)
    k_meta_p = np.empty((n, 16, 4, 64), np.float32)
    v_meta_p = np.empty((n, 16, 4, 64), np.float32)
    k_win_p = np.empty((n, 128, 4, 64), np.float32)
    v_win_p = np.empty((n, 128, 4, 64), np.float32)
    c_s = np.empty((1, 16 * n, 4, 256, 512), np.float32)
    n_s = np.empty((1, 16 * n, 4, 256), np.float32)
    m_s = np.empty((1, 16 * n, 4), np.float32)
    k_win_s = np.empty((16 * n, 128, 4, 64), np.float32)
    v_win_s = np.empty((16 * n, 128, 4, 64), np.float32)
    for c, r in enumerate(results):
        for s in range(NSEG):
            y_prompt[c, TS * s:TS * s + TS] = _tm(r['yR'][s])
        y_sample[16 * c:16 * c + 16] = _tm(r['yA'])[16:].reshape(16, 4, D)
        c_p[0, c] = r['oCp']
        n_p[0, c] = r['onp'].reshape(128, 2, 4).transpose(2, 1, 0).reshape(4, 256)
        m_p[0, c] = r['omp'][0]
        k_meta_p[c] = r['okmp'].reshape(16, 4, 64)
        v_meta_p[c] = r['ovmp'].reshape(16, 4, 64)
        k_win_p[c] = r['okwp'].reshape(128, 4, 64)
        v_win_p[c] = r['ovwp'].reshape(128, 4, 64)
        c_s[0, 16 * c:16 * c + 16] = r['oCs']
        n_s[0, 16 * c:16 * c + 16] = r['ons'].reshape(16, 128, 2, 4).transpose(0, 3, 2, 1).reshape(16, 4, 256)
        m_s[0, 16 * c:16 * c + 16] = r['oms']
        k_win_s[16 * c:16 * c + 16] = r['okws'].reshape(16, 128, 4, 64)
        v_win_s[16 * c:16 * c + 16] = r['ovws'].reshape(16, 128, 4, 64)
    return (y_prompt, y_sample, c_p, n_p, m_p, k_meta_p, v_meta_p, k_win_p, v_win_p, c_s, n_s, m_s, k_win_s, v_win_s)


def kernel(**inputs):
    inp = {k: np.asarray(v) for k, v in inputs.items()}
    n = 8
    shared = _prep_shared(inp)
    in_maps = []
    for c in range(n):
        m = dict(shared)
        m.update(_prep_core(inp, c))
        in_maps.append(m)
    nc = build_program()
    res = run_bass_kernel_spmd(nc, in_maps, core_ids=list(range(n)))
    return _assemble(res.results)
```
